# Optimizing a Trainium2 kernel written in Bass

```python
import math
import jax, jax.numpy as jnp
from jax import lax
import numpy as np

D_MODEL = 2048
BATCH = 16
SEQ = 2048
DEPTH = 1

RET_HEADS = 8
RET_DK = 128
RET_DV = 128
RET_WIDTH = RET_HEADS * RET_DV
RET_CHUNK = 128

NSA_HEADS = 8
NSA_GROUPS = 2
NSA_HD = 128
NSA_WIDTH = NSA_HEADS * NSA_HD
NSA_KV_WIDTH = NSA_GROUPS * NSA_HD
CMP_BLOCK = 32
CMP_STRIDE = 16
CMP_HIDDEN = 256
SEL_BLOCK = 64
SEL_TOPN = 16
SEL_QBLK = 32
WINDOW = 512
WIN_QBLK = 128
FORCE_BONUS = 1.0e4

ROPE_THETA = 10000.0
EPS = 1e-6
NEG = -1.0e30

IN_SPLITS = (RET_HEADS * RET_DK, RET_HEADS * RET_DK, RET_WIDTH, RET_WIDTH,
             NSA_WIDTH, NSA_KV_WIDTH, NSA_KV_WIDTH, NSA_KV_WIDTH, NSA_KV_WIDTH,
             NSA_KV_WIDTH, NSA_KV_WIDTH, NSA_WIDTH, NSA_HEADS * 3, D_MODEL, D_MODEL)
IN_WIDTH = sum(IN_SPLITS)

kernel_name = "gated_parallel_retention_nsa_block"


def rmsnorm(x, g):
    xf = x.astype(jnp.float32)
    y = xf * lax.rsqrt(jnp.mean(xf * xf, -1, keepdims=True) + EPS)
    return (y * g.astype(jnp.float32)).astype(x.dtype)


def rope_tables(pos, hd):
    inv = jnp.exp(jnp.arange(0, hd, 2, dtype=jnp.float32) * (-math.log(ROPE_THETA) / hd))
    ang = pos.astype(jnp.float32)[..., None] * inv
    ang = jnp.concatenate([ang, ang], -1)[:, None]
    return jnp.cos(ang), jnp.sin(ang)


def apply_rope(x, cos, sin):
    xf = x.astype(jnp.float32)
    half = xf.shape[-1] // 2
    rot = jnp.concatenate([-xf[..., half:], xf[..., :half]], -1)
    return (xf * cos + rot * sin).astype(x.dtype)


def heads(t, n):
    B, S, W = t.shape
    return t.reshape(B, S, n, W // n).transpose(0, 2, 1, 3)


def retention(q, k, v, g):
    f32 = jnp.float32
    B, H, S, DK = q.shape
    DV = v.shape[-1]
    C = RET_CHUNK
    N = S // C
    log_gamma = jnp.log1p(-jnp.exp2(-5.0 - jnp.arange(H, dtype=f32)))
    idx = jnp.arange(C, dtype=f32)
    diff = idx[:, None] - idx[None, :]
    dmask = jnp.where(diff >= 0, jnp.exp(log_gamma[:, None, None] * jnp.maximum(diff, 0.0)), 0.0)
    zeta = jnp.exp(log_gamma[:, None] * (C - 1 - idx))
    xi = jnp.exp(log_gamma[:, None] * (idx + 1))
    chunk_decay = jnp.exp(log_gamma * C)
    qc = q.astype(f32).reshape(B, H, N, C, DK)
    kc = k.astype(f32).reshape(B, H, N, C, DK) * (DK ** -0.5)
    vc = v.astype(f32).reshape(B, H, N, C, DV)
    sc = jnp.einsum('bhncd,bhnmd->bhncm', qc, kc) * dmask[None, :, None]
    inner = jnp.einsum('bhncm,bhnme->bhnce', sc, vc)
    kv = jnp.einsum('bhnmd,hm,bhnme->bhnde', kc, zeta, vc)

    def step(state, kv_n):
        return state * chunk_decay[None, :, None, None] + kv_n, state

    _, prev = lax.scan(step, jnp.zeros((B, H, DK, DV), f32), jnp.moveaxis(kv, 2, 0))
    prev = jnp.moveaxis(prev, 0, 2)
    cross = jnp.einsum('bhncd,bhnde->bhnce', qc, prev) * xi[None, :, None, :, None]
    o = (inner + cross).reshape(B, H, S, DV)
    mu = jnp.mean(o, -1, keepdims=True)
    var = jnp.mean(jnp.square(o - mu), -1, keepdims=True)
    o = (o - mu) * lax.rsqrt(var + EPS)
    o = o.transpose(0, 2, 1, 3).reshape(B, S, H * DV) * g.astype(f32)
    return o.astype(v.dtype)


def compress_blocks(kraw, cidx, pe, w1, w2):
    B, G, _, hd = kraw.shape
    blk = kraw[:, :, cidx] + pe
    flat = blk.reshape(B, G, cidx.shape[0], CMP_BLOCK * hd)
    return jax.nn.silu(flat @ w1) @ w2


def nsa_attention(q, k_cmp, v_cmp, k_sel, v_sel, k_win, v_win, gates, positions,
                  w_ck1, w_ck2, pe_ck, w_cv1, w_cv2, pe_cv):
    f32 = jnp.float32
    B, H, S, hd = q.shape
    G = k_sel.shape[1]
    R = H // G
    scale = hd ** -0.5
    qg = q.reshape(B, G, R, S, hd)
    t = jnp.arange(S)

    ncmp = (S - CMP_BLOCK) // CMP_STRIDE + 1
    cidx = jnp.arange(ncmp)[:, None] * CMP_STRIDE + jnp.arange(CMP_BLOCK)[None, :]
    cend = cidx[:, -1]
    kc = compress_blocks(k_cmp, cidx, pe_ck, w_ck1, w_ck2)
    vc = compress_blocks(v_cmp, cidx, pe_cv, w_cv1, w_cv2)
    ccos, csin = rope_tables(positions[:, cend], hd)
    kc = apply_rope(kc, ccos, csin)
    s_c = jnp.einsum('bgrtd,bgnd->bgrtn', qg, kc).astype(f32) * scale
    valid_c = cend[None, :] <= t[:, None]
    p_c = jax.nn.softmax(jnp.where(valid_c, s_c, NEG), -1) * valid_c.any(-1)[:, None]
    o_cmp = jnp.einsum('bgrtn,bgnd->bgrtd', p_c.astype(vc.dtype), vc)

    nsel = S // SEL_BLOCK
    topn = min(SEL_TOPN, nsel)
    cstart = jnp.arange(ncmp) * CMP_STRIDE
    sstart = jnp.arange(nsel) * SEL_BLOCK
    overlap = jnp.clip(jnp.minimum(cstart[:, None] + CMP_BLOCK, sstart[None, :] + SEL_BLOCK)
                       - jnp.maximum(cstart[:, None], sstart[None, :]), 0, None).astype(f32) / CMP_BLOCK
    imp = jnp.einsum('bgrtn,nj->bgtj', p_c, overlap)
    tblk = t // SEL_BLOCK
    j = jnp.arange(nsel)
    forced = (j[None, :] == 0) | (j[None, :] == tblk[:, None]) | (j[None, :] == tblk[:, None] - 1)
    causal_b = j[None, :] <= tblk[:, None]
    scores = jnp.where(causal_b, imp + FORCE_BONUS * forced, NEG)
    top_v, top_i = lax.top_k(scores, topn)
    top_ok = top_v > NEG * 0.5
    kb = k_sel.reshape(B, G, nsel, SEL_BLOCK, hd)
    vb = v_sel.reshape(B, G, nsel, SEL_BLOCK, hd)
    nqb = S // SEL_QBLK
    bi = jnp.arange(B)[:, None, None, None]
    gi = jnp.arange(G)[None, :, None, None]

    def sel_block(args):
        qb, ib, okb, tb = args
        kg = kb[bi, gi, ib]
        vg = vb[bi, gi, ib]
        s = jnp.einsum('bgrqd,bgqnkd->bgrqnk', qb, kg).astype(f32) * scale
        kpos = ib[..., None] * SEL_BLOCK + jnp.arange(SEL_BLOCK)
        m = okb[..., None] & (kpos <= tb[None, None, :, None, None])
        s = jnp.where(m[:, :, None], s, NEG)
        p = jax.nn.softmax(s.reshape(s.shape[:4] + (-1,)), -1).reshape(s.shape)
        return jnp.einsum('bgrqnk,bgqnkd->bgrqd', p.astype(vg.dtype), vg)

    xs = (jnp.moveaxis(qg.reshape(B, G, R, nqb, SEL_QBLK, hd), 3, 0),
          jnp.moveaxis(top_i.reshape(B, G, nqb, SEL_QBLK, topn), 2, 0),
          jnp.moveaxis(top_ok.reshape(B, G, nqb, SEL_QBLK, topn), 2, 0),
          t.reshape(nqb, SEL_QBLK))
    o_sel = jnp.moveaxis(lax.map(sel_block, xs), 0, 3).reshape(B, G, R, S, hd)

    nw = S // WIN_QBLK
    span = WINDOW + WIN_QBLK
    widx = jnp.arange(nw)[:, None] * WIN_QBLK + jnp.arange(span)[None, :]
    pad = ((0, 0), (0, 0), (WINDOW, 0), (0, 0))
    kw = jnp.pad(k_win, pad)[:, :, widx]
    vw = jnp.pad(v_win, pad)[:, :, widx]
    qw = qg.reshape(B, G, R, nw, WIN_QBLK, hd)
    s_w = jnp.einsum('bgrnqd,bgnkd->bgrnqk', qw, kw).astype(f32) * scale
    kpos = widx - WINDOW
    qpos = t.reshape(nw, WIN_QBLK)
    d = qpos[:, :, None] - kpos[:, None, :]
    m_w = (kpos[:, None, :] >= 0) & (d >= 0) & (d < WINDOW)
    p_w = jax.nn.softmax(jnp.where(m_w, s_w, NEG), -1)
    o_win = jnp.einsum('bgrnqk,bgnkd->bgrnqd', p_w.astype(vw.dtype), vw).reshape(B, G, R, S, hd)

    gt = gates.reshape(B, S, G, R, 3).transpose(0, 2, 3, 1, 4)
    o = gt[..., 0:1] * o_cmp + gt[..., 1:2] * o_sel + gt[..., 2:3] * o_win
    return o.transpose(0, 3, 1, 2, 4).reshape(B, S, H * hd)


def setup_inputs(seed: int = 0) -> dict:
    key = jax.random.key(seed)
    ks = jax.random.split(key, 18)
    f32 = jnp.float32
    nrm = lambda k, shape, s: jax.random.normal(k, shape, f32) * s
    x = jax.random.normal(ks[0], (BATCH, SEQ, D_MODEL), f32)
    c = jax.random.normal(ks[1], (BATCH, D_MODEL), f32)
    offset = jax.random.randint(ks[2], (BATCH, 1), 0, 4096, dtype=jnp.int32)
    positions = offset + jnp.arange(SEQ, dtype=jnp.int32)[None, :]
    return {
        "x": x,
        "c": c,
        "positions": positions,
        "w_ada": nrm(ks[3], (DEPTH, D_MODEL, 3 * D_MODEL), D_MODEL ** -0.5),
        "b_ada": nrm(ks[4], (DEPTH, 3 * D_MODEL), 0.01),
        "g_norm": 1.0 + nrm(ks[5], (DEPTH, D_MODEL), 0.02),
        "w_in": nrm(ks[6], (DEPTH, D_MODEL, IN_WIDTH), D_MODEL ** -0.5),
        "g_ret": 1.0 + nrm(ks[7], (DEPTH, RET_WIDTH), 0.02),
        "w_ck1": nrm(ks[8], (DEPTH, CMP_BLOCK * NSA_HD, CMP_HIDDEN), (CMP_BLOCK * NSA_HD) ** -0.5),
        "w_ck2": nrm(ks[9], (DEPTH, CMP_HIDDEN, NSA_HD), CMP_HIDDEN ** -0.5),
        "pe_ck": nrm(ks[10], (DEPTH, CMP_BLOCK, NSA_HD), 0.1),
        "w_cv1": nrm(ks[11], (DEPTH, CMP_BLOCK * NSA_HD, CMP_HIDDEN), (CMP_BLOCK * NSA_HD) ** -0.5),
        "w_cv2": nrm(ks[12], (DEPTH, CMP_HIDDEN, NSA_HD), CMP_HIDDEN ** -0.5),
        "pe_cv": nrm(ks[13], (DEPTH, CMP_BLOCK, NSA_HD), 0.1),
        "w_up_ret": nrm(ks[14], (DEPTH, RET_WIDTH, D_MODEL), RET_WIDTH ** -0.5),
        "w_up_nsa": nrm(ks[15], (DEPTH, NSA_WIDTH, D_MODEL), NSA_WIDTH ** -0.5),
        "w_out": nrm(ks[16], (DEPTH, D_MODEL, D_MODEL), D_MODEL ** -0.5),
        "g_final": 1.0 + nrm(ks[17], (D_MODEL,), 0.02),
    }


def reference(x, c, positions, w_ada, b_ada, g_norm, w_in, g_ret, w_ck1, w_ck2, pe_ck,
              w_cv1, w_cv2, pe_cv, w_up_ret, w_up_nsa, w_out, g_final):
    cos, sin = rope_tables(positions, NSA_HD)
    split_at = [int(v) for v in np.cumsum(IN_SPLITS)[:-1]]
    for l in range(DEPTH):
        mod = jax.nn.silu(c) @ w_ada[l] + b_ada[l]
        shift, scl, gate = jnp.split(mod, 3, -1)
        h = rmsnorm(x, g_norm[l]) * (1.0 + scl[:, None]) + shift[:, None]
        (r_q, r_k, r_v, r_g, n_q, n_kc, n_vc, n_ks, n_vs, n_kw, n_vw, n_g, n_bg,
         m_a, m_b) = jnp.split(h @ w_in[l], split_at, -1)
        y_ret = retention(apply_rope(heads(r_q, RET_HEADS), cos, sin),
                          apply_rope(heads(r_k, RET_HEADS), cos, sin),
                          heads(r_v, RET_HEADS), g_ret[l]) * jax.nn.silu(r_g)
        y_nsa = nsa_attention(apply_rope(heads(n_q, NSA_HEADS), cos, sin),
                              heads(n_kc, NSA_GROUPS), heads(n_vc, NSA_GROUPS),
                              apply_rope(heads(n_ks, NSA_GROUPS), cos, sin), heads(n_vs, NSA_GROUPS),
                              apply_rope(heads(n_kw, NSA_GROUPS), cos, sin), heads(n_vw, NSA_GROUPS),
                              jax.nn.sigmoid(n_bg), positions,
                              w_ck1[l], w_ck2[l], pe_ck[l], w_cv1[l], w_cv2[l], pe_cv[l]) * jax.nn.silu(n_g)
        merged = jax.nn.sigmoid(m_a) * (y_ret @ w_up_ret[l]) + jax.nn.sigmoid(m_b) * (y_nsa @ w_up_nsa[l])
        x = x + gate[:, None] * (merged @ w_out[l])
    return rmsnorm(x, g_final)
```

```python
import contextlib
import math
import numpy as np
import ml_dtypes
import concourse.bass as bass
import concourse.mybir as mybir
from concourse.bass_utils import run_bass_kernel_spmd

F32 = mybir.dt.float32
BF16 = mybir.dt.bfloat16
I32 = mybir.dt.int32
ALU = mybir.AluOpType
AF = mybir.ActivationFunctionType

D = 2048
S = 2048
NB = 16
NCORES = 8
SEQ_PER_CORE = 2
TT = 512
NT = S // TT
INW = 11800
O_RQ, O_RK, O_RV, O_RG, O_NQ = 0, 1024, 2048, 3072, 4096
O_KC, O_VC, O_KS, O_VS, O_KW, O_VW = 5120, 5376, 5632, 5888, 6144, 6400
O_NG, O_BG, O_MA, O_MB = 6656, 7680, 7704, 9752
EPS = 1e-6
SCALE = 128.0 ** -0.5
NEGM = -30000.0
ENGS = ("pe", "act", "dve", "pool", "sp")
PSUM_KEYS = {"pp0", "pp1", "psc0", "psc1", "po0", "po1", "ptr0", "ptr1"}


class _Op:
    __slots__ = ("eng", "fn", "deps", "sig", "semkey", "inc", "tick")


class Prog:
    def __init__(self, nc):
        self.nc = nc
        self.ops = []
        self.last_w = {}
        self.readers = {}
        self.dry = False

    def _add(self, eng, fn, reads, writes, semkey, inc, sig):
        if self.dry:
            return
        writes = list(writes) + [k for k in reads if k in PSUM_KEYS]
        deps = set()
        lw = self.last_w
        for k in reads:
            w = lw.get(k)
            if w is not None:
                deps.add(w)
        for k in writes:
            w = lw.get(k)
            if w is not None:
                deps.add(w)
            for r in self.readers.get(k, ()):
                deps.add(r)
        o = _Op()
        o.eng, o.fn, o.deps, o.sig, o.semkey, o.inc = eng, fn, deps, sig, semkey, inc
        idx = len(self.ops)
        self.ops.append(o)
        for k in reads:
            self.readers.setdefault(k, []).append(idx)
        for k in writes:
            lw[k] = idx
            self.readers[k] = []

    def op(self, eng, fn, reads=(), writes=()):
        self._add(eng, fn, reads, writes, eng, 1, False)

    def dma(self, eng, fn, reads=(), writes=(), semkey=None, n=1):
        self._add(eng, fn, reads, writes, semkey, 16 * n, True)

    def emit(self):
        nc = self.nc
        ops = self.ops
        for o in ops:
            for d in o.deps:
                ops[d].sig = True
        counts = {}
        for o in ops:
            if o.sig:
                counts[o.semkey] = counts.get(o.semkey, 0) + o.inc
                o.tick = counts[o.semkey]
            else:
                o.tick = None
        semkeys = list(counts.keys())
        for e in ENGS:
            if e not in semkeys:
                semkeys.append(e)
        self.stats = dict(counts)
        with contextlib.ExitStack() as st:
            sems = {k: st.enter_context(nc.semaphore("s_" + str(k))) for k in semkeys}
            block = st.enter_context(nc.Block())
            per_eng = {e: [o for o in ops if o.eng == e] for e in ENGS}

            def run(engine, ename):
                waited = {}
                for o in per_eng[ename]:
                    need = {}
                    for d in o.deps:
                        dd = ops[d]
                        if dd.tick > need.get(dd.semkey, 0):
                            need[dd.semkey] = dd.tick
                    for k, v in need.items():
                        if v > waited.get(k, 0):
                            engine.wait_ge(sems[k], v)
                            waited[k] = v
                    if o.semkey == ename:
                        ins = o.fn(engine)
                        if o.sig:
                            ins.then_inc(sems[ename], 1)
                    else:
                        o.fn(engine, sems[o.semkey])
                if ename == "sp":
                    for k, v in counts.items():
                        if v > waited.get(k, 0):
                            engine.wait_ge(sems[k], v)

            block.tensor(lambda e: run(e, "pe"))
            block.scalar(lambda e: run(e, "act"))
            block.vector(lambda e: run(e, "dve"))
            block.gpsimd(lambda e: run(e, "pool"))
            block.sync(lambda e: run(e, "sp"))


def _consts():
    bf = ml_dtypes.bfloat16
    H = 8
    lg = np.log1p(-np.exp2(-5.0 - np.arange(H, dtype=np.float64)))
    idx = np.arange(128, dtype=np.float64)
    cb = np.zeros((128, 0), np.float32)
    parts = {}
    ident = np.eye(128, dtype=np.float32)
    swp = np.zeros((128, 128), np.float32)
    for d in range(128):
        swp[(d + 64) % 128, d] = 1.0
    k = idx[:, None]
    q = idx[None, :]
    tri = np.where(k <= q, 0.0, NEGM).astype(np.float32)
    anti = np.where(k > q, 0.0, NEGM).astype(np.float32)
    dm = np.zeros((128, H, 128), np.float32)
    xi = np.zeros((128, H, 128), np.float32)
    for h in range(H):
        diff = idx[None, :] - idx[:, None]
        dm[:, h, :] = np.where(diff >= 0, np.exp(lg[h] * np.maximum(diff, 0.0)), 0.0) * SCALE
        xi[:, h, :] = np.exp(lg[h] * (idx + 1))[None, :]
    E = np.zeros((128, 16, 128), np.float32)
    for kt in range(16):
        for kk in range(128):
            E[2 * kt + kk // 64, kt, kk] = 1.0
    t = np.arange(S)[None, :]
    n1 = np.arange(128)[:, None]
    validT = np.where((n1 >= 1) & (16 * n1 + 15 <= t), 0.0, NEGM).astype(np.float32)
    negb = np.zeros((128, 2, 4, 128), np.float32)
    bonus = np.zeros((128, 2, 4, 32), np.float32)
    for tm in range(2):
        for st in range(4):
            tt = 1024 + tm * 512 + st * 128 + np.arange(128)
            nn = np.arange(128)[None, :]
            v = (nn >= 1) & (16 * nn + 15 <= tt[:, None])
            negb[:, tm, st, :] = np.where(v, 0.0, NEGM)
            tb = (tt // 64)[:, None]
            j = np.arange(32)[None, :]
            forced = (j == 0) | (j == tb) | (j == tb - 1)
            bonus[:, tm, st, :] = np.where(j <= tb, np.where(forced, 1.0e4, 0.0), -1.0e30)
    cbf = np.concatenate([ident, swp, tri, anti, dm.reshape(128, -1), xi.reshape(128, -1),
                          E.reshape(128, -1), validT, negb.reshape(128, -1)], axis=1).astype(bf)
    inv = np.exp(np.arange(0, 128, 2, dtype=np.float32) * np.float32(-math.log(10000.0) / 128)).astype(np.float32)
    inv = np.concatenate([inv, inv])
    sgn = np.concatenate([-np.ones(64), np.ones(64)]).astype(np.float32)
    zeta = np.zeros((128, H), np.float32)
    for h in range(H):
        zeta[:, h] = np.exp(lg[h] * (127 - idx)) * SCALE
    oh = np.zeros((128, 256), np.float32)
    oh[0, 0:128] = 1.0
    oh[1, 128:256] = 1.0
    cf = np.concatenate([inv[:, None], sgn[:, None], zeta, bonus.reshape(128, -1), oh], axis=1).astype(np.float32)
    decay = [float(np.exp(lg[h] * 128)) for h in range(H)]
    return cbf, cf, decay


CB_ID, CB_SW, CB_TRI, CB_ANTI, CB_DM, CB_XI, CB_E, CB_VT, CB_NB = 0, 128, 256, 384, 512, 1536, 2560, 4608, 6656
CB_N = 6656 + 1024
CF_INV, CF_SGN, CF_ZETA, CF_BON, CF_OH = 0, 1, 2, 10, 266
CF_N = 266 + 256


def build_nc(nseq=SEQ_PER_CORE, debug=False):
    nc = bass.Bass("TRN2", target_bir_lowering=False)
    _, _, DECAY = _consts()

    def din(name, shape, dt=F32):
        return nc.dram_tensor(name, list(shape), dt, kind="ExternalInput").ap()

    x_d = din("x", [nseq, S, D])
    cT_d = din("cT", [128, nseq * 16])
    pos_d = din("pos", [nseq, S], I32)
    w_ada = din("w_ada", [D, 3 * D])
    b_adaT = din("b_adaT", [128, 48])
    b_gate = din("b_gate", [1, D])
    g_normT = din("g_normT", [128, 16])
    w_in = din("w_in", [D, INW])
    g_retT = din("g_retT", [128, 8])
    w_ck1 = din("w_ck1", [4096, 256])
    w_ck2 = din("w_ck2", [256, 128])
    pe_ckT = din("pe_ckT", [128, 32])
    w_cv1 = din("w_cv1", [4096, 256])
    w_cv2 = din("w_cv2", [256, 128])
    pe_cvT = din("pe_cvT", [128, 32])
    w_ur = din("w_up_ret", [1024, D])
    w_un = din("w_up_nsa", [1024, D])
    w_out = din("w_out", [D, D])
    g_fin = din("g_final", [1, D])
    cbf_d = din("cbf", [128, CB_N], BF16)
    cf_d = din("cf", [128, CF_N])
    out_d = nc.dram_tensor("out", [nseq, S, D], F32, kind="ExternalOutput").ap()
    gsc_d = nc.dram_tensor("gsc", [nseq, D], F32, kind="Internal").ap()
    dbg = {}
    if debug:
        for nm, shp in (("d_hT", [128, 16 * TT]), ("d_yT", [128, 16 * TT]), ("d_m2", [128, 16 * TT])):
            dbg[nm] = nc.dram_tensor(nm, shp, F32, kind="ExternalOutput").ap()

    P = Prog(nc)
    base = [16512]

    def sb(name, shape, dt):
        nbytes = int(np.prod(shape[1:])) * (4 if dt in (F32, I32) else 2)
        off = base[0]
        base[0] = (off + nbytes + 31) // 32 * 32
        assert base[0] <= 229344, (name, base[0])
        return nc.alloc_sbuf_tensor_at(name, list(shape), dt, offset=off)

    def psum(name, shape, dt):
        return nc.alloc_psum_tensor(name, list(shape), dt)

    cbf = sb("cbf", [128, CB_N], BF16)
    cf = sb("cf", [128, CF_N], F32)
    hT_off = base[0]
    hT = sb("hT", [128, 16, TT], BF16)
    yT = sb("yT", [128, 16, TT], BF16)
    xt = sb("xt", [128, 4, D], F32)
    xn = sb("xn", [128, 4, D], BF16)
    wreg = base[0]
    wst = [sb("wst%d" % i, [128, 2048], F32) for i in range(2)]
    wbf = [sb("wbf%d" % i, [128, 2048], BF16) for i in range(2)]
    NSLOT = 6
    assert base[0] - wreg == NSLOT * 4096
    wsl = [nc.alloc_sbuf_tensor_at("wsl%d" % i, [128, 2048], BF16, offset=wreg + i * 4096) for i in range(NSLOT)]
    gate_bc1 = sb("gate_bc", [128, D], F32)
    gate_bc = [gate_bc1 for b in range(nseq)]
    gfin_bc = nc.alloc_sbuf_tensor_at("gfin_bc", [128, D], F32, offset=hT_off)
    cosT = sb("cosT", [128, TT], F32)
    sinT = sb("sinT", [128, TT], F32)
    ksT = sb("ksT", [128, 2, S], BF16)
    kwT = sb("kwT", [128, 2, 2 * TT], BF16)
    Vs = sb("Vs", [128, 2, 16, 130], BF16)
    Vw = sb("Vw", [128, 2, 8, 130], BF16)
    kroll = sb("kroll", [128, 2, 16 + TT], BF16)
    vroll = sb("vroll", [128, 2, 16 + TT], BF16)
    KcT = sb("KcT", [128, 2, 128], BF16)
    Vc = sb("Vc", [128, 2, 130], BF16)
    state = sb("state", [128, 8, 128], F32)
    state_b = sb("state_b", [128, 8, 128], BF16)
    modT = sb("modT", [128, 32, nseq], F32)
    sc1 = sb("sc1", [128, 16, nseq], F32)
    gnT = sb("gnT", [128, 16], F32)
    grT = sb("grT", [128, 8], F32)
    badaT = sb("badaT", [128, 48], F32)
    cT = sb("cT", [128, nseq * 16], F32)
    cTt = sb("cTt", [128, nseq * 16], F32)
    siluc = sb("siluc", [128, 16, nseq], BF16)
    peT = [sb("peT%d" % i, [128, 32], F32) for i in range(2)]
    peTb = [sb("peTb%d" % i, [128, 32], BF16) for i in range(2)]
    w2st = [sb("w2st%d" % i, [128, 2, 128], F32) for i in range(2)]
    w2bf = [sb("w2bf%d" % i, [128, 2, 128], BF16) for i in range(2)]
    cpi = sb("cpi", [128, 1], F32)
    ones_b = sb("ones_b", [128, 128], BF16)
    ceps = sb("ceps", [128, 1], F32)
    ss4 = sb("ss4", [128, 4], F32)
    rs4 = sb("rs4", [128, 4], F32)
    angI = sb("angI", [128, TT], I32)
    pos_i = angI
    qT = [sb("qT%d" % i, [128, TT], BF16) for i in range(4)]
    kT = sb("kT", [128, TT], BF16)
    qxT = sb("qxT", [128, TT], BF16)
    vT = sb("vT", [128, TT], BF16)
    xb = sb("xb", [128, TT], BF16)
    r1 = sb("r1", [128, TT], F32)
    r2 = sb("r2", [128, TT], F32)
    Vr = sb("Vr", [128, 4, 128], BF16)
    Kz = sb("Kz", [128, 4, 128], BF16)
    scm4 = sb("scm4", [128, 4, 128], BF16)
    sb3 = sb("sb3", [128, 3, 128], BF16)
    bst = sb("bst", [128, 4, 6], F32)
    mv = sb("mv", [128, 4, 2], F32)
    nmr = sb("nmr", [128, 4], F32)
    on = sb("on", [128, 4, 128], BF16)
    tg = sb("tg", [128, TT], F32)
    gg = sb("gg", [128, TT], BF16)
    gates = sb("gates", [128, 4, 24], F32)
    hb = sb("hb", [128, 2, 2], F32)
    hbh = sb("hbh", [128, 2, 2], F32)
    hx = sb("hx", [128, 64], F32)
    ht = sb("ht", [128, 64], F32)
    shT = sb("shT", [128, 2, 64], BF16)
    kcx = sb("kcx", [128, 64], BF16)
    vct = sb("vct", [32, 2, 130], BF16)
    PT = [sb("PT%d" % i, [128, TT], BF16) for i in range(2)]
    stmp = sb("stmp", [128, 128], F32)
    etmp = sb("etmp", [128, 128], F32)
    rsum = sb("rsum", [128, 1], F32)
    rinv = sb("rinv", [128, 1], F32)
    ppad = sb("ppad", [128, 4, 132], F32)
    imp = sb("imp", [128, 32], F32)
    m8 = sb("m8", [128, 16], F32)
    wk = sb("wk", [128, 32], F32)
    negm = sb("negm", [128, 32], BF16)
    negmT = sb("negmT", [32, 2, TT], BF16)
    den = sb("den", [128, 1], F32)
    den4 = sb("den4", [128, 4], F32)
    coef = sb("coef", [128, 1], F32)
    acc = sb("acc", [128, 4, 128], F32)
    accb = sb("accb", [128, 4, 128], BF16)
    angA, angB, angC, ta, tb2 = r1, r2, tg, r1, r2
    sbuf_used = base[0]

    pp = [psum("pp%d" % i, [128, 512], F32) for i in range(2)]
    psc = [psum("psc%d" % i, [128, 512], F32) for i in range(2)]
    po = [psum("po%d" % i, [128, 2, 256], F32) for i in range(2)]
    ptrs = [psum("ptr%d" % i, [128, 1024], BF16) for i in range(2)]

    pm = po[1][:, :, :].rearrange("p a c -> p (a c)")
    pofl = [po[i][:, :, :].rearrange("p a c -> p (a c)") for i in range(2)]
    ptrf = [ptrs[i][:, :].bitcast(F32) for i in range(2)]
    br_i = [0]

    class _Ptr:
        def __getitem__(self, idx):
            p_, h_, c_ = idx
            return ptrs[h_][p_, c_] if not (isinstance(c_, slice) and c_ == slice(None)) else ptrs[h_][p_, 0:512]
    ptr = _Ptr()

    class _M2:
        def __getitem__(self, idx):
            p_, cc, t_ = idx
            return xn[p_, cc // 4, (cc % 4) * TT + (t_.start or 0):(cc % 4) * TT + (t_.stop if t_.stop is not None else TT)]
    m2 = _M2()

    def cB(off, n):
        return cbf[:, off:off + n]

    ident_b = cB(CB_ID, 128)
    swap_b = cB(CB_SW, 128)
    tri_b = cB(CB_TRI, 128)
    anti_b = cB(CB_ANTI, 128)

    def bc(ap2d, n):
        a = ap2d
        return bass.AP(tensor=a.tensor, offset=a.offset, ap=[list(a.ap[0]), [0, n]] + [list(z) for z in a.ap[1:]])

    class WS:
        def __init__(self):
            self.specs = []
            self.specsB = []
            self.pos = 0
            self.issued = 0
            self.posB = 0
            self.issuedB = 0
            self.collect = True
            self.gid = {}

        def _issue(self, k):
            spec = self.specs[k]
            slot = k % 2
            kind = spec[0]
            st_t, bf_t = wst[slot], wbf[slot]
            if kind == "cols":
                _, w, col0, ncols, nk = spec
                w = wmap[w]
                dst = st_t[:, 0:nk * ncols].rearrange("p (k c) -> p k c", c=ncols)
                src = w[0:nk * 128, col0:col0 + ncols].rearrange("(k p) c -> p k c", p=128)
                half = nk // 2
                def f(e, s, dst=dst, src=src, half=half, nk=nk):
                    e.dma_start(out=dst[:, 0:half, :], in_=src[:, 0:half, :]).then_inc(s, 16)
                    e.dma_start(out=dst[:, half:nk, :], in_=src[:, half:nk, :]).then_inc(s, 16)
                P.dma("sp", f, writes=["wst%d" % slot], semkey="w%d" % slot, n=2)
                n = nk * ncols
            elif kind == "two":
                _, col0 = spec
                d0 = st_t[:, 0:1024].rearrange("p (k c) -> p k c", c=128)
                d1 = st_t[:, 1024:2048].rearrange("p (k c) -> p k c", c=128)
                s0 = w_ur[:, col0:col0 + 128].rearrange("(k p) c -> p k c", p=128)
                s1 = w_un[:, col0:col0 + 128].rearrange("(k p) c -> p k c", p=128)
                def f(e, s, d0=d0, d1=d1, s0=s0, s1=s1):
                    e.dma_start(out=d0, in_=s0).then_inc(s, 16)
                    e.dma_start(out=d1, in_=s1).then_inc(s, 16)
                P.dma("sp", f, writes=["wst%d" % slot], semkey="w%d" % slot, n=2)
                n = 2048
            elif kind == "wout":
                _, ct, pc = spec
                dst = st_t[:, :].rearrange("p (k c) -> p k c", c=512)
                src = w_out[pc * 512:(pc + 1) * 512, ct * 512:(ct + 1) * 512].rearrange("(k p) c -> p k c", p=128)
                def f(e, s, dst=dst, src=src):
                    e.dma_start(out=dst[:, 0:2, :], in_=src[:, 0:2, :]).then_inc(s, 16)
                    e.dma_start(out=dst[:, 2:4, :], in_=src[:, 2:4, :]).then_inc(s, 16)
                P.dma("sp", f, writes=["wst%d" % slot], semkey="w%d" % slot, n=2)
                n = 2048
            elif kind == "w1":
                _, w, pc = spec
                w = wmap[w]
                dst = st_t[:, :].rearrange("p (i c) -> p i c", c=256)
                src = w[pc * 1024:(pc + 1) * 1024, :].rearrange("(i p) c -> p i c", p=128)
                def f(e, s, dst=dst, src=src):
                    e.dma_start(out=dst[:, 0:4, :], in_=src[:, 0:4, :]).then_inc(s, 16)
                    e.dma_start(out=dst[:, 4:8, :], in_=src[:, 4:8, :]).then_inc(s, 16)
                P.dma("sp", f, writes=["wst%d" % slot], semkey="w%d" % slot, n=2)
                n = 2048
            if k % 2 == 0:
                P.op("dve", lambda e, bf_t=bf_t, st_t=st_t, n=n: e.tensor_copy(out=bf_t[:, 0:n], in_=st_t[:, 0:n]),
                     reads=["wst%d" % slot], writes=["wbf%d" % slot])
            else:
                P.op("act", lambda e, bf_t=bf_t, st_t=st_t, n=n: e.activation(out=bf_t[:, 0:n], in_=st_t[:, 0:n], func=AF.Identity),
                     reads=["wst%d" % slot], writes=["wbf%d" % slot])

        def nextA(self, spec):
            k = self.pos
            assert self.specs[k] == spec, (k, self.specs[k], spec)
            while self.issued < min(k + 2, len(self.specs)):
                self._issue(self.issued)
                self.issued += 1
            self.pos += 1
            slot = k % 2
            return wbf[slot], "wbf%d" % slot

        def finish_collect(self):
            for sp_ in self.specsB:
                if sp_ not in self.gid:
                    self.gid[sp_] = len(self.gid)
            self.uniq = sorted(self.gid, key=lambda z: self.gid[z])
            self.specs = self.specs + self.uniq
            self.scr_keys = ["scr%d" % i for i in range(len(self.uniq))]

        def preconvert(self, wscr):
            self.wscr = wscr
            for u in self.uniq:
                k = self.pos
                wt, wk_ = self.nextA(u)
                gid_ = self.gid[u]
                P.dma("sp", lambda e, s, wt=wt, gid_=gid_: e.dma_start(out=wscr[gid_, :, :], in_=wt[:, :]).then_inc(s, 16),
                      reads=[wk_], writes=["scr%d" % gid_], semkey="ws%d" % (k % 2))

        def _issueB(self, k):
            gid_ = self.gid[self.specsB[k]]
            slot = k % NSLOT
            wscr = self.wscr
            extra = ["wst0", "wst1", "wbf0", "wbf1"] if k < len(self.uniq) + NSLOT else []
            P.dma("sp", lambda e, s, slot=slot, gid_=gid_: e.dma_start(out=wsl[slot][:, :], in_=wscr[gid_, :, :]).then_inc(s, 16),
                  reads=self.scr_keys, writes=["wsl%d" % slot] + extra, semkey="wl%d" % slot)

        def next(self, spec):
            is_a = (spec[0] == "cols" and spec[1] == "w_ada")
            if self.collect:
                (self.specs if is_a else self.specsB).append(spec)
                return None, None
            if is_a:
                return self.nextA(spec)
            k = self.posB
            assert self.specsB[k] == spec, (k, self.specsB[k], spec)
            nu = len(self.uniq)
            if k < nu:
                assert self.uniq[k] == spec
                ka = self.pos
                wt, wk_ = self.nextA(spec)
                gid_ = self.gid[spec]
                wscr = self.wscr
                P.dma("sp", lambda e, s, wt=wt, gid_=gid_: e.dma_start(out=wscr[gid_, :, :], in_=wt[:, :]).then_inc(s, 16),
                      reads=[wk_], writes=["scr%d" % gid_], semkey="ws%d" % (ka % 2))
                self.posB += 1
                self.issuedB = max(self.issuedB, nu)
                return wt, wk_
            while self.issuedB < min(k + NSLOT, len(self.specsB)):
                self._issueB(self.issuedB)
                self.issuedB += 1
            self.posB += 1
            slot = k % NSLOT
            return wsl[slot], "wsl%d" % slot

    W = WS()
    wmap = {"w_in": w_in, "w_ada": w_ada, "w_ck1": w_ck1, "w_cv1": w_cv1}
    pp_i = [0]
    psc_i = [0]

    def next_pp():
        i = pp_i[0] % 2
        pp_i[0] += 1
        return pp[i], "pp%d" % i

    def next_psc():
        i = psc_i[0] % 2
        psc_i[0] += 1
        return psc[i], "psc%d" % i

    def proj_fm(col0, dst_fn):
        wt, wk_ = W.next(("cols", "w_in", col0, 128, 16))
        if wt is None:
            flush_pending()
            dst_fn(None, None)
            return
        ps_t, ps_k = next_pp()
        w3 = wt[:, :].rearrange("p (k c) -> p k c", c=128)
        def f(e, w3=w3, ps_t=ps_t):
            ins = None
            for kc in range(16):
                ins = e.matmul(ps_t[:, :], lhsT=w3[:, kc, :], rhs=hT[:, kc, :], start=(kc == 0), stop=(kc == 15))
            return ins
        P.op("pe", f, reads=[wk_, "hT"], writes=[ps_k])
        flush_pending()
        dst_fn(ps_t, ps_k)

    pending = []

    def flush_pending():
        while pending:
            pending.pop(0)()

    def rope_to(ps_t, ps_k, dst_ap, dst_key):
        if ps_t is None:
            return
        P.op("act", lambda e: e.activation(out=xb[:, :], in_=ps_t[:, :], func=AF.Identity), reads=[ps_k], writes=["xb"])
        P.op("pool", lambda e: e.tensor_tensor(out=r1[:, :], in0=xb[:, :], in1=cosT[:, :], op=ALU.mult), reads=["xb", "cosT"], writes=["r1"])

        def rest():
            P.op("pe", lambda e: e.matmul(pm[:, :], lhsT=swap_b, rhs=xb[:, :], start=True, stop=True), reads=["xb", "cbf"], writes=["po1"])
            P.op("dve", lambda e: e.tensor_tensor(out=r2[:, :], in0=pm[:, :], in1=sinT[:, :], op=ALU.mult), reads=["po1", "sinT"], writes=["r2"])
            P.op("pool", lambda e: e.tensor_tensor(out=dst_ap, in0=r1[:, :], in1=r2[:, :], op=ALU.add), reads=["r1", "r2"], writes=[dst_key])
        pending.append(rest)

    def copy_to(ps_t, ps_k, dst_ap, dst_key):
        if ps_t is None:
            return
        P.op("act", lambda e: e.activation(out=dst_ap, in_=ps_t[:, :], func=AF.Identity), reads=[ps_k], writes=[dst_key])

    def transpose4(src_fn, src_keys, half, pre=()):
        def f(e):
            ins = None
            for j in range(4):
                ins = e.transpose(ptr[:, half, j * 128:(j + 1) * 128], src_fn(j), ident_b)
            return ins
        P.op("pe", f, reads=list(src_keys) + ["cbf"], writes=["ptr%d" % half])

    tr_i = [0]

    def next_tr():
        i = tr_i[0] % 2
        tr_i[0] += 1
        return i

    def gate_a(col0):
        def g(ps_t, ps_k):
            if ps_t is None:
                return
            P.op("act", lambda e: e.activation(out=tg[:, :], in_=ps_t[:, :], func=AF.Tanh, scale=0.5), reads=[ps_k], writes=["tg"])
            P.op("dve", lambda e: e.scalar_tensor_tensor(out=gg[:, :], in0=tg[:, :], scalar=1.0, in1=ps_t[:, :], op0=ALU.add, op1=ALU.mult),
                 reads=["tg", ps_k], writes=["gg"])
        proj_fm(col0, g)

    def gate_b(ydst, ykey):
        P.op("pool", lambda e: e.tensor_tensor(out=ydst, in0=ydst, in1=gg[:, :], op=ALU.mult), reads=["gg", ykey], writes=[ykey])

    def setup():
        def ld(dst, src, key):
            P.dma("sp", lambda e, s: e.dma_start(out=dst, in_=src).then_inc(s, 16), writes=[key], semkey="ld_" + key)
        ld(cbf[:, :], cbf_d, "cbf")
        ld(cf[:, :], cf_d, "cf")
        ld(cT[:, :], cT_d, "cT")
        ld(badaT[:, :], b_adaT, "badaT")
        ld(gnT[:, :], g_normT, "gnT")
        ld(grT[:, :], g_retT, "grT")
        ld(peT[0][:, :], pe_ckT, "peT0")
        ld(peT[1][:, :], pe_cvT, "peT1")
        ld(w2st[0][:, :, :], w_ck2.rearrange("(m p) c -> p m c", p=128), "w2st0")
        ld(w2st[1][:, :, :], w_cv2.rearrange("(m p) c -> p m c", p=128), "w2st1")
        P.op("pool", lambda e: e.memset(cpi[:, :], float(np.pi)), writes=["cpi"])
        P.op("pool", lambda e: e.memset(ones_b[:, :], 1.0), writes=["ones_b"])
        P.op("pool", lambda e: e.memset(ceps[:, :], EPS), writes=["ceps"])
        P.op("pool", lambda e: e.memset(Vs[:, :, :, 128:130], 1.0), writes=["Vs"])
        P.op("pool", lambda e: e.memset(Vw[:, :, :, 128:130], 1.0), writes=["Vw"])
        P.op("pool", lambda e: e.memset(Vc[:, :, :], 0.0), writes=["Vc"])
        P.op("pool", lambda e: e.memset(vct[:, :, 128:130], 1.0), writes=["vct"])
        P.op("pool", lambda e: e.memset(KcT[:, :, :], 0.0), writes=["KcT"])
        P.op("pool", lambda e: e.memset(ppad[:, :, :], 0.0), writes=["ppad"])
        for i in range(2):
            P.op("pool", lambda e, i=i: e.tensor_copy(out=peTb[i][:, :], in_=peT[i][:, :]), reads=["peT%d" % i], writes=["peTb%d" % i])
            P.op("pool", lambda e, i=i: e.tensor_copy(out=w2bf[i][:, :, :], in_=w2st[i][:, :, :]), reads=["w2st%d" % i], writes=["w2bf%d" % i])
        P.op("pool", lambda e: e.tensor_scalar(out=grT[:, :], in0=grT[:, :], scalar1=0.5, scalar2=None, op0=ALU.mult), reads=["grT"], writes=["grT"])
        P.op("act", lambda e: e.activation(out=cTt[:, :], in_=cT[:, :], func=AF.Tanh, scale=0.5), reads=["cT"], writes=["cTt"])
        P.op("dve", lambda e: e.scalar_tensor_tensor(out=cTt[:, :], in0=cTt[:, :], scalar=1.0, in1=cT[:, :], op0=ALU.add, op1=ALU.mult),
             reads=["cTt", "cT"], writes=["cTt"])
        for b in range(nseq):
            P.op("dve", lambda e, b=b: e.tensor_scalar(out=siluc[:, :, b], in0=cTt[:, b * 16:(b + 1) * 16], scalar1=0.5, scalar2=None, op0=ALU.mult),
                 reads=["cTt"], writes=["siluc"])
        for grp in range(32):
            wt, wk_ = W.next(("cols", "w_ada", grp * 128, 128, 16))
            if wt is None:
                continue
            w3 = wt[:, :].rearrange("p (k c) -> p k c", c=128)
            def f(e, w3=w3):
                ins = None
                for kc in range(16):
                    ins = e.matmul(pm[:, 0:nseq], lhsT=w3[:, kc, :], rhs=siluc[:, kc, :], start=(kc == 0), stop=(kc == 15))
                return ins
            P.op("pe", f, reads=[wk_, "siluc"], writes=["po1"])
            P.op("dve", lambda e, grp=grp: e.tensor_scalar(out=modT[:, grp, :], in0=pm[:, 0:nseq], scalar1=badaT[:, grp:grp + 1], scalar2=None, op0=ALU.add),
                 reads=["po1", "badaT"], writes=["modT"])
        if W.collect:
            return
        for b in range(nseq):
            P.op("dve", lambda e, b=b: e.scalar_tensor_tensor(out=sc1[:, :, b], in0=modT[:, 16:32, b], scalar=1.0, in1=gnT[:, :], op0=ALU.add, op1=ALU.mult),
                 reads=["modT", "gnT"], writes=["sc1"])

    def gate_rows():
        grow = xt[0:nseq, 1, :]
        for grp in range(16):
            wt, wk_ = W.next(("cols", "w_ada", (32 + grp) * 128, 128, 16))
            if wt is None:
                continue
            w3 = wt[:, :].rearrange("p (k c) -> p k c", c=128)
            def f(e, w3=w3):
                ins = None
                for kc in range(16):
                    ins = e.matmul(pm[0:nseq, 128:256], lhsT=siluc[:, kc, :], rhs=w3[:, kc, :], start=(kc == 0), stop=(kc == 15))
                return ins
            P.op("pe", f, reads=[wk_, "siluc"], writes=["po1"])
            P.op("act", lambda e, grp=grp: e.activation(out=grow[:, grp * 128:(grp + 1) * 128], in_=pm[0:nseq, 128:256], func=AF.Identity),
                 reads=["po1"], writes=["xt1"])
        if W.collect:
            return
        P.dma("sp", lambda e, s: e.dma_start(out=gsc_d, in_=grow).then_inc(s, 16), reads=["xt1"], writes=["gsc"], semkey="gs")

    def seq_start(b):
        if W.collect:
            return
        P.dma("sp", lambda e, s: e.dma_start(out=gate_bc1[:, :], in_=gsc_d[b, :].partition_broadcast(128)).then_inc(s, 16), reads=["gsc"], writes=["gate_bc"], semkey="gb")
        P.dma("sp", lambda e, s: e.dma_start(out=xt[:, 0, :], in_=b_gate[0, :].partition_broadcast(128)).then_inc(s, 16), writes=["xt0"], semkey="x0")
        P.op("dve", lambda e: e.tensor_tensor(out=gate_bc1[:, :], in0=gate_bc1[:, :], in1=xt[:, 0, :], op=ALU.add), reads=["gate_bc", "xt0"], writes=["gate_bc"])
        P.op("pool", lambda e: e.tensor_scalar(out=gate_bc1[:, :], in0=gate_bc1[:, :], scalar1=0.5, scalar2=None, op0=ALU.mult), reads=["gate_bc"], writes=["gate_bc"])

    def sincos(src_add, dst, dkey, fold_sign):
        P.op("dve", lambda e: e.tensor_scalar(out=angB[:, :], in0=angA[:, :], scalar1=float(src_add), scalar2=float(1.0 / (2 * np.pi)), op0=ALU.add, op1=ALU.mult),
             reads=["r1"], writes=["r2"])
        P.op("dve", lambda e: e.tensor_copy(out=angI[:, :], in_=angB[:, :]), reads=["r2"], writes=["angI"])
        P.op("dve", lambda e: e.tensor_copy(out=angB[:, :], in_=angI[:, :]), reads=["angI"], writes=["r2"])
        P.op("dve", lambda e: e.tensor_scalar(out=angC[:, :], in0=angA[:, :], scalar1=float(src_add), scalar2=None, op0=ALU.add), reads=["r1"], writes=["tg"])
        P.op("dve", lambda e: e.scalar_tensor_tensor(out=angC[:, :], in0=angB[:, :], scalar=-float(2 * np.pi), in1=angC[:, :], op0=ALU.mult, op1=ALU.add),
             reads=["r2", "tg"], writes=["tg"])
        P.op("dve", lambda e: e.tensor_scalar(out=angB[:, :], in0=angC[:, :], scalar1=0.0, scalar2=float(2 * np.pi), op0=ALU.is_lt, op1=ALU.mult),
             reads=["tg"], writes=["r2"])
        P.op("dve", lambda e: e.tensor_tensor(out=angC[:, :], in0=angC[:, :], in1=angB[:, :], op=ALU.add), reads=["tg", "r2"], writes=["tg"])
        P.op("dve", lambda e: e.tensor_scalar(out=angC[:, :], in0=angC[:, :], scalar1=float(2 * np.pi), scalar2=None, op0=ALU.min), reads=["tg"], writes=["tg"])
        P.op("act", lambda e: e.activation(out=dst[:, :], in_=angC[:, :], func=AF.Sin, bias=cpi[:, 0:1], scale=-1.0), reads=["tg", "cpi"], writes=[dkey])
        if fold_sign:
            P.op("dve", lambda e: e.tensor_scalar(out=dst[:, :], in0=dst[:, :], scalar1=cf[:, CF_SGN:CF_SGN + 1], scalar2=None, op0=ALU.mult),
                 reads=[dkey, "cf"], writes=[dkey])

    def body(b, T):
        tok0 = T * TT
        dry = W.collect
        if not dry:
            for st in range(4):
                P.dma("sp", lambda e, s, st=st: e.dma_start(out=xt[:, st, :], in_=x_d[b, tok0 + st * 128:tok0 + (st + 1) * 128, :]).then_inc(s, 16),
                      writes=["xt%d" % st], semkey="x%d" % st)
            P.dma("sp", lambda e, s: e.dma_start(out=pos_i[:, :], in_=pos_d[b, tok0:tok0 + TT].partition_broadcast(128)).then_inc(s, 16),
                  writes=["angI"], semkey="pos")
            for st in range(4):
                P.op("act", lambda e, st=st: e.activation(out=xn[:, st, :], in_=xt[:, st, :], func=AF.Square, accum_out=ss4[:, st:st + 1]),
                     reads=["xt%d" % st], writes=["xn%d" % st, "ss4"])
            if STAGE <= 1.1:
                return
            P.op("act", lambda e: e.activation(out=rs4[:, :], in_=ss4[:, :], func=AF.Sqrt, bias=ceps[:, 0:1], scale=1.0 / D), reads=["ss4", "ceps"], writes=["rs4"])
            P.op("dve", lambda e: e.reciprocal(out=rs4[:, :], in_=rs4[:, :]), reads=["rs4"], writes=["rs4"])
            if STAGE <= 1.2:
                return
            for st in range(4):
                P.op("act", lambda e, st=st: e.activation(out=xn[:, st, :], in_=xt[:, st, :], func=AF.Identity, scale=rs4[:, st:st + 1]),
                     reads=["xt%d" % st, "rs4"], writes=["xn%d" % st])
            if STAGE <= 1.3:
                return
            for fc in range(int(_os.environ.get('KFC', '16'))):
                h = next_tr()
                transpose4(lambda j, fc=fc: xn[:, j, fc * 128:(fc + 1) * 128], ["xn0", "xn1", "xn2", "xn3"], h)
                if _os.environ.get('KNODVE'):
                    continue
                kvar = _os.environ.get('KVAR', '3')
                if kvar == '1':
                    P.op("dve", lambda e, fc=fc, h=h: e.tensor_scalar(out=hT[:, fc, :], in0=ptr[:, h, :], scalar1=sc1[:, fc, b:b + 1], scalar2=None, op0=ALU.mult),
                         reads=["ptr%d" % h, "sc1", "modT"], writes=["hT"])
                    continue
                if kvar == '2':
                    P.op("dve", lambda e, fc=fc, h=h: e.tensor_copy(out=hT[:, fc, :], in_=ptr[:, h, :]),
                         reads=["ptr%d" % h, "sc1", "modT"], writes=["hT"])
                    continue
                if kvar == '3':
                    P.op("act", lambda e, fc=fc, h=h: e.activation(out=hT[:, fc, :], in_=ptr[:, h, :], func=AF.Identity, scale=sc1[:, fc, b:b + 1], bias=modT[:, fc, b:b + 1]),
                         reads=["ptr%d" % h, "sc1", "modT"], writes=["hT"])
                    continue
                P.op("dve", lambda e, fc=fc, h=h: e.tensor_scalar(out=hT[:, fc, :], in0=ptr[:, h, :], scalar1=sc1[:, fc, b:b + 1], scalar2=modT[:, fc, b:b + 1],
                                                                 op0=ALU.mult, op1=ALU.add),
                     reads=["ptr%d" % h, "sc1", "modT"], writes=["hT"])
            if debug and b == 0 and T == dbgT:
                P.op("dve", lambda e: e.tensor_copy(out=xt[:, 3, :].rearrange("p (a c) -> p a c", c=TT)[:, 0:4, :], in_=hT[:, 0:4, :]), reads=["hT"], writes=["xt3"])
            if STAGE <= 1.4:
                return
            P.op("dve", lambda e: e.tensor_copy(out=angA[:, :], in_=pos_i[:, :]), reads=["angI"], writes=["r1"])
            P.op("dve", lambda e: e.tensor_scalar(out=angA[:, :], in0=angA[:, :], scalar1=cf[:, CF_INV:CF_INV + 1], scalar2=None, op0=ALU.mult),
                 reads=["r1", "cf"], writes=["r1"])
            sincos(0.0, sinT, "sinT", True)
            sincos(np.pi / 2, cosT, "cosT", False)
            if T == 0:
                P.op("pool", lambda e: e.memset(state[:, :, :], 0.0), writes=["state"])
                P.op("pool", lambda e: e.memset(state_b[:, :, :], 0.0), writes=["state_b"])
                P.op("pool", lambda e: e.memset(kroll[:, :, :], 0.0), writes=["kroll"])
                P.op("pool", lambda e: e.memset(vroll[:, :, :], 0.0), writes=["vroll"])
            else:
                P.op("pool", lambda e: e.tensor_copy(out=kroll[:, :, 0:16], in_=kroll[:, :, TT:TT + 16]), reads=["kroll"], writes=["kroll"])
                P.op("pool", lambda e: e.tensor_copy(out=vroll[:, :, 0:16], in_=vroll[:, :, TT:TT + 16]), reads=["vroll"], writes=["vroll"])

        if STAGE <= 2:
            return
        for h in range(8):
            proj_fm(O_RQ + h * 128, lambda p_, k_: rope_to(p_, k_, qT[0][:, :], "qT0"))
            proj_fm(O_RK + h * 128, lambda p_, k_: rope_to(p_, k_, kT[:, :], "kT"))
            proj_fm(O_RV + h * 128, lambda p_, k_: copy_to(p_, k_, vT[:, :], "vT"))
            if not dry:
                xi_ap = bc(cbf[:, CB_XI + h * 128:CB_XI + (h + 1) * 128], 4)
                P.op("pool", lambda e, xi_ap=xi_ap: e.tensor_tensor(out=qxT[:, :].rearrange("p (a c) -> p a c", c=128), in0=qT[0][:, :].rearrange("p (a c) -> p a c", c=128),
                                                                   in1=xi_ap, op=ALU.mult), reads=["qT0", "cbf"], writes=["qxT"])
                hh = next_tr()
                transpose4(lambda j: vT[:, j * 128:(j + 1) * 128], ["vT"], hh)
                P.op("act", lambda e, hh=hh: e.activation(out=Vr[:, :, :], in_=ptr[:, hh, :].rearrange("p (a c) -> p a c", c=128), func=AF.Identity),
                     reads=["ptr%d" % hh], writes=["Vr"])
                hh = next_tr()
                transpose4(lambda j: kT[:, j * 128:(j + 1) * 128], ["kT"], hh)
                P.op("act", lambda e, hh=hh, h=h: e.activation(out=Kz[:, :, :], in_=ptr[:, hh, :].rearrange("p (a c) -> p a c", c=128), func=AF.Identity,
                                                              scale=cf[:, CF_ZETA + h:CF_ZETA + h + 1]),
                     reads=["ptr%d" % hh, "cf"], writes=["Kz"])
                dm4 = bc(cbf[:, CB_DM + h * 128:CB_DM + (h + 1) * 128], 4)
                psA, kA = next_psc()
                psB, kB = next_psc()
                def f_sc(e, psA=psA):
                    ins = None
                    for c in range(4):
                        cs = slice(c * 128, (c + 1) * 128)
                        ins = e.matmul(psA[:, cs], lhsT=kT[:, cs], rhs=qT[0][:, cs], start=True, stop=True)
                    return ins
                P.op("pe", f_sc, reads=["kT", "qT0"], writes=[kA])
                def f_kv(e, psB=psB):
                    ins = None
                    for c in range(4):
                        ins = e.matmul(psB[:, c * 128:(c + 1) * 128], lhsT=Kz[:, c, :], rhs=Vr[:, c, :], start=True, stop=True)
                    return ins
                P.op("pe", f_kv, reads=["Kz", "Vr"], writes=[kB])
                P.op("dve", lambda e, psA=psA, dm4=dm4: e.tensor_tensor(out=scm4[:, :, :], in0=psA[:, :].rearrange("p (a c) -> p a c", c=128), in1=dm4, op=ALU.mult),
                     reads=[kA, "cbf"], writes=["scm4"])
                for c in range(3):
                    P.op("dve", lambda e, h=h, c=c, psB=psB: e.scalar_tensor_tensor(out=state[:, h, :], in0=state[:, h, :], scalar=DECAY[h], in1=psB[:, c * 128:(c + 1) * 128],
                                                                                  op0=ALU.mult, op1=ALU.add), reads=["state", kB], writes=["state"])
                    P.op("act", lambda e, h=h, c=c: e.activation(out=sb3[:, c, :], in_=state[:, h, :], func=AF.Identity), reads=["state"], writes=["sb3"])
                def f_o(e, h=h):
                    ins = None
                    for c in range(4):
                        cs = slice(c * 128, (c + 1) * 128)
                        e.matmul(pofl[0][:, cs], lhsT=scm4[:, c, :], rhs=Vr[:, c, :], start=True, stop=False)
                        ins = e.matmul(pofl[0][:, cs], lhsT=qxT[:, cs], rhs=(state_b[:, h, :] if c == 0 else sb3[:, c - 1, :]), start=False, stop=True)
                    return ins
                P.op("pe", f_o, reads=["scm4", "Vr", "qxT", "state_b", "sb3"], writes=["po0"])
                P.op("dve", lambda e, h=h, psB=psB: e.scalar_tensor_tensor(out=state[:, h, :], in0=state[:, h, :], scalar=DECAY[h], in1=psB[:, 384:512],
                                                                          op0=ALU.mult, op1=ALU.add), reads=["state", kB], writes=["state"])
                P.op("act", lambda e, h=h: e.activation(out=state_b[:, h, :], in_=state[:, h, :], func=AF.Identity), reads=["state"], writes=["state_b"])
                for c in range(4):
                    P.op("dve", lambda e, c=c: e.bn_stats(out=bst[:, c, :], in_=pofl[0][:, c * 128:(c + 1) * 128]), reads=["po0"], writes=["bst"])
                    P.op("dve", lambda e, c=c: e.bn_aggr(out=mv[:, c, :], in_=bst[:, c, :]), reads=["bst"], writes=["mv"])
                P.op("act", lambda e: e.activation(out=rs4[:, :], in_=mv[:, :, 1], func=AF.Sqrt, bias=ceps[:, 0:1], scale=1.0), reads=["mv", "ceps"], writes=["rs4"])
                P.op("dve", lambda e: e.reciprocal(out=rs4[:, :], in_=rs4[:, :]), reads=["rs4"], writes=["rs4"])
                P.op("dve", lambda e: e.scalar_tensor_tensor(out=nmr[:, :], in0=mv[:, :, 0], scalar=-1.0, in1=rs4[:, :], op0=ALU.mult, op1=ALU.mult),
                     reads=["mv", "rs4"], writes=["nmr"])
                for c in range(4):
                    P.op("act", lambda e, c=c: e.activation(out=on[:, c, :], in_=po[0][:, c // 2, (c % 2) * 128:(c % 2) * 128 + 128], func=AF.Identity,
                                                           scale=rs4[:, c:c + 1], bias=nmr[:, c:c + 1]),
                         reads=["po0", "nmr", "rs4"], writes=["on"])
            gate_a(O_RG + h * 128)
            if not dry:
                def tail(h=h):
                    hh = next_tr()
                    transpose4(lambda j: on[:, j, :], ["on"], hh)
                    P.op("act", lambda e, hh=hh, h=h: e.activation(out=yT[:, h, :], in_=ptr[:, hh, :], func=AF.Identity, scale=grT[:, h:h + 1]),
                         reads=["ptr%d" % hh, "grT"], writes=["yT%d" % h])
                    gate_b(yT[:, h, :], "yT%d" % h)
                pending.append(tail)

        if STAGE <= 3:
            return
        for g in range(2):
            proj_fm(O_KC + g * 128, lambda p_, k_, g=g: copy_to(p_, k_, kroll[:, g, 16:16 + TT], "kroll"))
            proj_fm(O_VC + g * 128, lambda p_, k_, g=g: copy_to(p_, k_, vroll[:, g, 16:16 + TT], "vroll"))
            proj_fm(O_KS + g * 128, lambda p_, k_, g=g: rope_to(p_, k_, ksT[:, g, tok0:tok0 + TT], "ksT"))
            proj_fm(O_KW + g * 128, lambda p_, k_, g=g: rope_to(p_, k_, kwT[:, g, (T % 2) * TT:(T % 2 + 1) * TT], "kwT"))
            for (off, Vd, vk) in ((O_VS, Vs, "Vs"), (O_VW, Vw, "Vw")):
                proj_fm(off + g * 128, lambda p_, k_: copy_to(p_, k_, vT[:, :], "vT"))
                if not dry:
                    hh = next_tr()
                    vb0 = 4 * T if vk == "Vs" else 4 * (T % 2)
                    transpose4(lambda j: vT[:, j * 128:(j + 1) * 128], ["vT"], hh)
                    P.op("act", lambda e, hh=hh, Vd=Vd, g=g, vb0=vb0: e.activation(out=Vd[:, g, vb0:vb0 + 4, 0:128], in_=ptr[:, hh, :].rearrange("p (a c) -> p a c", c=128), func=AF.Identity),
                         reads=["ptr%d" % hh], writes=[vk])
        nl0 = 1 if T == 0 else 0
        ncol = 32 - nl0
        for kv in range(2):
            roll = kroll if kv == 0 else vroll
            rkey = "kroll" if kv == 0 else "vroll"
            w1 = w_ck1 if kv == 0 else w_cv1
            for pc in range(4):
                wt, wk_ = W.next(("w1", "w_ck1" if kv == 0 else "w_cv1", pc))
                if wt is None:
                    continue
                w3 = wt[:, :].rearrange("p (i c) -> p i c", c=256)
                def f(e, w3=w3, pc=pc, roll=roll, kv=kv):
                    ins = None
                    for il in range(8):
                        i = pc * 8 + il
                        for mh in range(2):
                            rhs = roll[:, :, i:i + 497:16]
                            e.matmul(psc[0][:, mh * 64:mh * 64 + 64].rearrange("p (g n) -> p g n", n=32), lhsT=w3[:, il, mh * 128:(mh + 1) * 128], rhs=rhs,
                                     start=(i == 0 and mh == 0), stop=(i == 31))
                            ins = e.matmul(psc[1][:, mh:mh + 1], lhsT=w3[:, il, mh * 128:(mh + 1) * 128], rhs=peTb[kv][:, i:i + 1],
                                           start=(i == 0 and mh == 0), stop=(i == 31))
                    return ins
                P.op("pe", f, reads=[wk_, rkey, "peTb%d" % kv], writes=["psc0", "psc1"])
            if dry:
                continue
            P.op("dve", lambda e, kv=kv: e.tensor_copy(out=hb[:, kv, :], in_=psc[1][:, 0:2]), reads=["psc1"], writes=["hb"])
            P.op("dve", lambda e, kv=kv: e.tensor_scalar(out=hbh[:, kv, :], in0=hb[:, kv, :], scalar1=0.5, scalar2=None, op0=ALU.mult), reads=["hb"], writes=["hbh"])
            for mh in range(2):
                P.op("act", lambda e, mh=mh, kv=kv: e.activation(out=ht[:, :], in_=psc[0][:, mh * 64:mh * 64 + 64], func=AF.Tanh, bias=hbh[:, kv, mh:mh + 1], scale=0.5),
                     reads=["psc0", "hbh"], writes=["ht"])
                P.op("dve", lambda e, mh=mh, kv=kv: e.tensor_scalar(out=hx[:, :], in0=psc[0][:, mh * 64:mh * 64 + 64], scalar1=hb[:, kv, mh:mh + 1], scalar2=None, op0=ALU.add),
                     reads=["psc0", "hb"], writes=["hx"])
                P.op("dve", lambda e, mh=mh: e.scalar_tensor_tensor(out=shT[:, mh, :], in0=ht[:, :], scalar=1.0, in1=hx[:, :], op0=ALU.add, op1=ALU.mult),
                     reads=["ht", "hx"], writes=["shT"])
            if kv == 0:
                def f2(e):
                    e.matmul(pm[:, 0:64], lhsT=w2bf[0][:, 0, :], rhs=shT[:, 0, :], start=True, stop=False)
                    return e.matmul(pm[:, 0:64], lhsT=w2bf[0][:, 1, :], rhs=shT[:, 1, :], start=False, stop=True)
                P.op("pe", f2, reads=["w2bf0", "shT"], writes=["po1"])
                P.op("act", lambda e: e.activation(out=kcx[:, :], in_=pm[:, 0:64], func=AF.Identity, scale=0.5), reads=["po1"], writes=["kcx"])
                P.op("pe", lambda e: e.matmul(pm[:, 64:128], lhsT=swap_b, rhs=kcx[:, :], start=True, stop=True), reads=["kcx", "cbf"], writes=["po1"])
                cos_c = bc(cosT[:, 15:TT:16], 2)
                sin_c = bc(sinT[:, 15:TT:16], 2)
                P.op("dve", lambda e, cos_c=cos_c: e.tensor_tensor(out=hx[:, :].rearrange("p (g n) -> p g n", n=32), in0=kcx[:, :].rearrange("p (g n) -> p g n", n=32), in1=cos_c, op=ALU.mult),
                     reads=["kcx", "cosT"], writes=["hx"])
                P.op("dve", lambda e, sin_c=sin_c: e.tensor_tensor(out=ht[:, :].rearrange("p (g n) -> p g n", n=32), in0=pm[:, 64:128].rearrange("p (g n) -> p g n", n=32), in1=sin_c, op=ALU.mult),
                     reads=["po1", "sinT"], writes=["ht"])
                P.op("dve", lambda e: e.tensor_tensor(out=KcT[:, :, 32 * T:32 * T + 32], in0=hx[:, :].rearrange("p (g n) -> p g n", n=32), in1=ht[:, :].rearrange("p (g n) -> p g n", n=32), op=ALU.add),
                     reads=["hx", "ht"], writes=["KcT"])
            else:
                for g in range(2):
                    def f3(e, g=g):
                        e.matmul(pm[0:32, 128 + g * 128:256 + g * 128], lhsT=shT[:, 0, g * 32:(g + 1) * 32], rhs=w2bf[1][:, 0, :], start=True, stop=False)
                        return e.matmul(pm[0:32, 128 + g * 128:256 + g * 128], lhsT=shT[:, 1, g * 32:(g + 1) * 32], rhs=w2bf[1][:, 1, :], start=False, stop=True)
                    P.op("pe", f3, reads=["w2bf1", "shT"], writes=["po1"])
                    P.op("act", lambda e, g=g: e.activation(out=vct[:, g, 0:128], in_=pm[0:32, 128 + g * 128:256 + g * 128], func=AF.Identity, scale=0.5), reads=["po1"], writes=["vct"])
                P.dma("sp", lambda e, s: e.dma_start(out=Vc[32 * T:32 * T + 32, :, :], in_=vct[:, :, :]).then_inc(s, 16), reads=["vct"], writes=["Vc"], semkey="vc")
        wt, wk_ = W.next(("cols", "w_in", O_BG, 24, 16))
        if wt is not None:
            w3 = wt[:, 0:16 * 24].rearrange("p (k c) -> p k c", c=24)
            for st in range(4):
                def f(e, st=st, w3=w3):
                    ins = None
                    for kc in range(16):
                        ins = e.matmul(pm[:, 384 + st * 24:384 + (st + 1) * 24], lhsT=hT[:, kc, st * 128:(st + 1) * 128], rhs=w3[:, kc, :], start=(kc == 0), stop=(kc == 15))
                    return ins
                P.op("pe", f, reads=[wk_, "hT"], writes=["po1"])
            P.op("act", lambda e: e.activation(out=gates[:, :, :], in_=pm[:, 384:480].rearrange("p (a c) -> p a c", c=24), func=AF.Tanh, scale=0.5), reads=["po1"], writes=["gates"])
            P.op("dve", lambda e: e.tensor_scalar(out=gates[:, :, :], in0=gates[:, :, :], scalar1=0.5, scalar2=0.5, op0=ALU.mult, op1=ALU.add), reads=["gates"], writes=["gates"])

        if STAGE <= 4:
            return
        for g in range(2):
            for r in range(4):
                hq = 4 * g + r
                proj_fm(O_NQ + hq * 128, lambda p_, k_, r=r: rope_to(p_, k_, qT[r][:, :], "qT%d" % r))
            flush_pending()
            for r in range(4):
                if dry or T < 2:
                    continue
                for st in range(4):
                    ps_t, ps_k = next_psc()
                    P.op("pe", lambda e, ps_t=ps_t, st=st, r=r, g=g: e.matmul(ps_t[:, 0:128], lhsT=qT[r][:, st * 128:(st + 1) * 128], rhs=KcT[:, g, :], start=True, stop=True),
                         reads=["qT%d" % r, "KcT"], writes=[ps_k])
                    nb_ap = cbf[:, CB_NB + ((T - 2) * 4 + st) * 128:CB_NB + ((T - 2) * 4 + st + 1) * 128]
                    P.op("dve", lambda e, ps_t=ps_t, nb_ap=nb_ap: e.scalar_tensor_tensor(out=stmp[:, :], in0=ps_t[:, 0:128], scalar=SCALE, in1=nb_ap, op0=ALU.mult, op1=ALU.add),
                         reads=[ps_k, "cbf"], writes=["stmp"])
                    P.op("act", lambda e: e.activation(out=etmp[:, :], in_=stmp[:, :], func=AF.Exp, accum_out=rsum[:, 0:1]), reads=["stmp"], writes=["etmp", "rsum"])
                    P.op("dve", lambda e: e.reciprocal(out=rinv[:, :], in_=rsum[:, :]), reads=["rsum"], writes=["rinv"])
                    if r == 0:
                        P.op("dve", lambda e, st=st: e.tensor_scalar(out=ppad[:, st, 0:128], in0=etmp[:, :], scalar1=rinv[:, 0:1], scalar2=None, op0=ALU.mult),
                             reads=["etmp", "rinv"], writes=["ppad"])
                    else:
                        P.op("dve", lambda e, st=st: e.scalar_tensor_tensor(out=ppad[:, st, 0:128], in0=etmp[:, :], scalar=rinv[:, 0:1], in1=ppad[:, st, 0:128], op0=ALU.mult, op1=ALU.add),
                             reads=["etmp", "rinv", "ppad"], writes=["ppad"])
            if not dry and T >= 2:
                for st in range(4):
                    def v(k0, st=st):
                        return ppad[:, st, k0:k0 + 128:4]
                    P.op("dve", lambda e, v=v: e.tensor_tensor(out=imp[:, :], in0=v(0), in1=v(4), op=ALU.add), reads=["ppad"], writes=["imp"])
                    P.op("dve", lambda e, v=v: e.scalar_tensor_tensor(out=imp[:, :], in0=imp[:, :], scalar=0.5, in1=v(1), op0=ALU.mult, op1=ALU.add), reads=["imp", "ppad"], writes=["imp"])
                    P.op("dve", lambda e, v=v: e.tensor_tensor(out=imp[:, :], in0=imp[:, :], in1=v(2), op=ALU.add), reads=["imp", "ppad"], writes=["imp"])
                    P.op("dve", lambda e, v=v: e.tensor_tensor(out=imp[:, :], in0=imp[:, :], in1=v(3), op=ALU.add), reads=["imp", "ppad"], writes=["imp"])
                    bo = CF_BON + ((T - 2) * 4 + st) * 32
                    P.op("dve", lambda e, bo=bo: e.tensor_tensor(out=imp[:, :], in0=imp[:, :], in1=cf[:, bo:bo + 32], op=ALU.add), reads=["imp", "cf"], writes=["imp"])
                    P.op("dve", lambda e: e.max(out=m8[:, 0:8], in_=imp[:, :]), reads=["imp"], writes=["m8"])
                    P.op("dve", lambda e: e.match_replace(out=wk[:, :], in_to_replace=m8[:, 0:8], in_values=imp[:, :], imm_value=-3.0e38), reads=["imp", "m8"], writes=["wk"])
                    P.op("dve", lambda e: e.max(out=m8[:, 8:16], in_=wk[:, :]), reads=["wk"], writes=["m8"])
                    P.op("dve", lambda e: e.tensor_scalar(out=wk[:, :], in0=imp[:, :], scalar1=m8[:, 15:16], scalar2=None, op0=ALU.is_ge), reads=["imp", "m8"], writes=["wk"])
                    P.op("dve", lambda e: e.tensor_scalar(out=negm[:, :], in0=wk[:, :], scalar1=-NEGM, scalar2=NEGM, op0=ALU.mult, op1=ALU.add), reads=["wk"], writes=["negm"])
                    hh = next_tr()
                    P.op("pe", lambda e, hh=hh: e.transpose(ptr[0:32, hh, 0:128], negm[:, :], ident_b), reads=["negm", "cbf"], writes=["ptr%d" % hh])
                    P.op("act", lambda e, hh=hh, st=st, g=g: e.activation(out=negmT[:, g, st * 128:(st + 1) * 128], in_=ptr[0:32, hh, 0:128], func=AF.Identity),
                         reads=["ptr%d" % hh], writes=["negmT"])
            for r in range(4):
                hq = 4 * g + r
                if not dry:
                    nsa_head(b, T, g, r, hq)
                gate_a(O_NG + hq * 128)
                if not dry:
                    def tail2(hq=hq):
                        hh = next_tr()
                        transpose4(lambda j: accb[:, j, :], ["accb"], hh)
                        P.op("act", lambda e, hh=hh, hq=hq: e.activation(out=yT[:, 8 + hq, :], in_=ptr[:, hh, :], func=AF.Identity, scale=0.5),
                             reads=["ptr%d" % hh], writes=["yT%d" % (8 + hq)])
                        gate_b(yT[:, 8 + hq, :], "yT%d" % (8 + hq))
                    pending.append(tail2)

        flush_pending()
        if debug and b == 0 and T == dbgT:
            if dry:
                return
            P.op("dve", lambda e: e.tensor_copy(out=xt[:, 2, :].rearrange("p (a c) -> p a c", c=TT)[:, 0:4, :], in_=yT[:, 0:4, :]), reads=["yT%d" % i for i in range(16)], writes=["xt2"])
            P.op("dve", lambda e: e.tensor_copy(out=xt[:, 1, :].rearrange("p (a c) -> p a c", c=TT)[:, 0:4, :], in_=yT[:, 8:12, :]), reads=["yT%d" % i for i in range(16)], writes=["xt1"])
            P.op("dve", lambda e: e.tensor_copy(out=xt[0:32, 0, 0:1024], in_=negmT[:, :, :].rearrange("p g t -> p (g t)")), reads=["negmT"], writes=["xt0"])
            P.op("dve", lambda e: e.tensor_copy(out=xt[:, 0, 1024:1280], in_=KcT[:, :, :].rearrange("p g t -> p (g t)")), reads=["KcT"], writes=["xt0"])
            P.dma("sp", lambda e, s: e.dma_start(out=dbg["d_m2"][:, 0:2048], in_=xt[:, 0, :]).then_inc(s, 16), reads=["xt0"], semkey="dbg")
            P.dma("sp", lambda e, s: e.dma_start(out=dbg["d_hT"][:, 0:4 * TT], in_=xt[:, 3, :]).then_inc(s, 16), reads=["xt3"], semkey="dbg")
            P.dma("sp", lambda e, s: e.dma_start(out=dbg["d_yT"][:, 0:4 * TT], in_=xt[:, 2, :]).then_inc(s, 16), reads=["xt2"], semkey="dbg")
            P.dma("sp", lambda e, s: e.dma_start(out=dbg["d_yT"][:, 4 * TT:8 * TT], in_=xt[:, 1, :]).then_inc(s, 16), reads=["xt1"], semkey="dbg")
            return
        ykeys = ["yT%d" % i for i in range(16)]
        for cc in range(16):
            if cc % 2 == 0:
                bA, kA, bB, kB, bC, kC, bD, kD = pp[0], "pp0", pp[1], "pp1", psc[0], "psc0", psc[1], "psc1"
            else:
                bA, kA, bB, kB, bC, kC, bD, kD = pofl[0], "po0", pofl[1], "po1", ptrf[0], "ptr0", ptrf[1], "ptr1"
            wt2, wk2 = W.next(("two", cc * 128))
            if wt2 is not None:
                w2v = wt2[:, :].rearrange("p (u k c) -> p u k c", u=2, c=128)
                def fA(e, w2v=w2v, bA=bA, bB=bB):
                    ins = None
                    for u, bt in ((0, bA), (1, bB)):
                        for kc in range(8):
                            ins = e.matmul(bt[:, :], lhsT=w2v[:, u, kc, :], rhs=yT[:, u * 8 + kc, :], start=(kc == 0), stop=(kc == 7))
                    return ins
                P.op("pe", fA, reads=[wk2] + ykeys, writes=[kA, kB])
            wta, wka = W.next(("cols", "w_in", O_MA + cc * 128, 128, 16))
            if wta is not None:
                wa3 = wta[:, :].rearrange("p (k c) -> p k c", c=128)
                def fC(e, wa3=wa3, bC=bC):
                    ins = None
                    for kc in range(16):
                        ins = e.matmul(bC[:, :], lhsT=wa3[:, kc, :], rhs=hT[:, kc, :], start=(kc == 0), stop=(kc == 15))
                    return ins
                P.op("pe", fC, reads=[wka, "hT"], writes=[kC])
            wtb, wkb = W.next(("cols", "w_in", O_MB + cc * 128, 128, 16))
            if wtb is None:
                continue
            wb3 = wtb[:, :].rearrange("p (k c) -> p k c", c=128)
            def fD(e, wb3=wb3, bD=bD):
                ins = None
                for kc in range(16):
                    ins = e.matmul(bD[:, :], lhsT=wb3[:, kc, :], rhs=hT[:, kc, :], start=(kc == 0), stop=(kc == 15))
                return ins
            P.op("pe", fD, reads=[wkb, "hT"], writes=[kD])
            P.op("act", lambda e, bC=bC: e.activation(out=ta[:, :], in_=bC[:, :], func=AF.Tanh, scale=0.5), reads=[kC], writes=["r1"])
            P.op("dve", lambda e, bA=bA: e.scalar_tensor_tensor(out=ta[:, :], in0=ta[:, :], scalar=1.0, in1=bA[:, :], op0=ALU.add, op1=ALU.mult), reads=["r1", kA], writes=["r1"])
            P.op("act", lambda e, bD=bD: e.activation(out=tb2[:, :], in_=bD[:, :], func=AF.Tanh, scale=0.5), reads=[kD], writes=["r2"])
            P.op("dve", lambda e, bB=bB: e.scalar_tensor_tensor(out=tb2[:, :], in0=tb2[:, :], scalar=1.0, in1=bB[:, :], op0=ALU.add, op1=ALU.mult), reads=["r2", kB], writes=["r2"])
            P.op("pool", lambda e, cc=cc: e.tensor_tensor(out=m2[:, cc, :], in0=ta[:, :], in1=tb2[:, :], op=ALU.add), reads=["r1", "r2"], writes=["xn%d" % (cc // 4)])
        if dry:
            for ct in range(4):
                for pc in range(4):
                    W.next(("wout", ct, pc))
            return
        P.dma("sp", lambda e, s: e.dma_start(out=gfin_bc[:, :], in_=g_fin[0, :].partition_broadcast(128)).then_inc(s, 16), writes=["hT"], semkey="gf")
        banks = [(pp[0], "pp0"), (pp[1], "pp1"), (psc[0], "psc0"), (psc[1], "psc1")]
        for ct in range(4):
            for pc in range(4):
                wt, wk_ = W.next(("wout", ct, pc))
                w3 = wt[:, :].rearrange("p (k c) -> p k c", c=512)
                for st in range(4):
                    bt, bk = banks[st]
                    def f(e, w3=w3, bt=bt, st=st, pc=pc):
                        ins = None
                        for kl in range(4):
                            kc = pc * 4 + kl
                            ins = e.matmul(bt[:, :], lhsT=m2[:, kc, st * 128:(st + 1) * 128], rhs=w3[:, kl, :], start=(kc == 0), stop=(kc == 15))
                        return ins
                    P.op("pe", f, reads=[wk_, "xn0", "xn1", "xn2", "xn3"], writes=[bk])
            for st in range(4):
                bt, bk = banks[st]
                P.op("dve", lambda e, bt=bt, ct=ct: e.tensor_tensor(out=r1[:, :], in0=bt[:, :], in1=gate_bc1[:, ct * 512:(ct + 1) * 512], op=ALU.mult),
                     reads=[bk, "gate_bc"], writes=["r1"])
                P.op("pool", lambda e, st=st, ct=ct: e.tensor_tensor(out=xt[:, st, ct * 512:(ct + 1) * 512], in0=xt[:, st, ct * 512:(ct + 1) * 512], in1=r1[:, :], op=ALU.add),
                     reads=["r1", "xt%d" % st], writes=["xt%d" % st])
        for st in range(4):
            P.op("act", lambda e, st=st: e.activation(out=xn[:, st, :], in_=xt[:, st, :], func=AF.Square, accum_out=ss4[:, st:st + 1]),
                 reads=["xt%d" % st], writes=["xn%d" % st, "ss4"])
        P.op("act", lambda e: e.activation(out=rs4[:, :], in_=ss4[:, :], func=AF.Sqrt, bias=ceps[:, 0:1], scale=1.0 / D), reads=["ss4", "ceps"], writes=["rs4"])
        P.op("dve", lambda e: e.reciprocal(out=rs4[:, :], in_=rs4[:, :]), reads=["rs4"], writes=["rs4"])
        for st in range(4):
            P.op("dve", lambda e, st=st: e.scalar_tensor_tensor(out=xt[:, st, :], in0=xt[:, st, :], scalar=rs4[:, st:st + 1], in1=gfin_bc[:, :], op0=ALU.mult, op1=ALU.mult),
                 reads=["xt%d" % st, "rs4", "hT"], writes=["xt%d" % st])
            P.dma("sp", lambda e, s, st=st: e.dma_start(out=out_d[b, tok0 + st * 128:tok0 + (st + 1) * 128, :], in_=xt[:, st, :]).then_inc(s, 16),
                  reads=["xt%d" % st], semkey="o%d" % st)

    pt_i = [0]

    def nsa_head(b, T, g, r, hq):
        qh = qT[r]
        qk = "qT%d" % r
        first = [True]

        def combine(br, bi):
            pk, dk = "po%d" % bi, "pp%d" % bi
            pof = pofl[bi]
            P.op("dve", lambda e: e.tensor_scalar(out=den4[:, :], in0=pp[bi][:, 0:4], scalar1=1e-30, scalar2=None, op0=ALU.max), reads=[dk], writes=["den4"])
            P.op("dve", lambda e: e.reciprocal(out=den4[:, :], in_=den4[:, :]), reads=["den4"], writes=["den4"])
            P.op("dve", lambda e: e.tensor_tensor(out=den4[:, :], in0=den4[:, :], in1=gates[:, :, hq * 3 + br], op=ALU.mult), reads=["den4", "gates"], writes=["den4"])
            for qi in range(4):
                o_ap = pof[:, qi * 128:(qi + 1) * 128]
                if br == 0:
                    P.op("act", lambda e, o_ap=o_ap, qi=qi: e.activation(out=acc[:, qi, :], in_=o_ap, func=AF.Identity, scale=den4[:, qi:qi + 1]), reads=[pk, "den4"], writes=["acc"])
                elif br == 1:
                    P.op("dve", lambda e, o_ap=o_ap, qi=qi: e.scalar_tensor_tensor(out=acc[:, qi, :], in0=o_ap, scalar=den4[:, qi:qi + 1], in1=acc[:, qi, :], op0=ALU.mult, op1=ALU.add),
                         reads=[pk, "den4", "acc"], writes=["acc"])
                else:
                    P.op("dve", lambda e, o_ap=o_ap, qi=qi: e.scalar_tensor_tensor(out=accb[:, qi, :], in0=o_ap, scalar=den4[:, qi:qi + 1], in1=acc[:, qi, :], op0=ALU.mult, op1=ALU.add),
                         reads=[pk, "den4", "acc"], writes=["accb"])

        pv_q = []
        npush = [0]

        def push_pv(fn):
            while pv_q:
                pv_q.pop(0)()
            pv_q.append(fn)
            npush[0] += 1
            if npush[0] == 3:
                flush_pending()

        nk = 32 * (T + 1)
        ps_t, ps_k = next_psc()
        def fsc(e, ps_t=ps_t):
            e.matmul(ps_t[0:nk, :], lhsT=KcT[:, g, 0:nk], rhs=qh[:, :], start=True, stop=False)
            return e.matmul(ps_t[0:nk, :], lhsT=cbf[0:nk, CB_ID:CB_ID + nk], rhs=cbf[0:nk, CB_VT + T * TT:CB_VT + (T + 1) * TT], start=False, stop=True)
        P.op("pe", fsc, reads=["KcT", qk, "cbf"], writes=[ps_k])
        pi = pt_i[0] % 2
        pt_i[0] += 1
        ptile, pkey = PT[pi], "PT%d" % pi
        P.op("act", lambda e, ps_t=ps_t, ptile=ptile: e.activation(out=ptile[0:nk, :], in_=ps_t[0:nk, :], func=AF.Exp, scale=SCALE), reads=[ps_k], writes=[pkey])
        bi0 = br_i[0] % 2
        br_i[0] += 1
        def fpv(e, ptile=ptile, bi0=bi0):
            ins = None
            for qi in range(4):
                e.matmul(pofl[bi0][:, qi * 128:(qi + 1) * 128], lhsT=ptile[0:nk, qi * 128:(qi + 1) * 128], rhs=Vc[0:nk, g, 0:128], start=(qi == 0), stop=True)
                ins = e.matmul(pp[bi0][:, qi:qi + 1], lhsT=ptile[0:nk, qi * 128:(qi + 1) * 128], rhs=ones_b[0:nk, 0:1], start=(qi == 0), stop=True)
            return ins
        def pv0(fpv=fpv, pkey=pkey, bi0=bi0):
            P.op("pe", fpv, reads=[pkey, "Vc", "ones_b"], writes=["po%d" % bi0, "pp%d" % bi0])
            combine(0, bi0)
        push_pv(pv0)

        for br, (kTt, kkey, Vt, vkey) in ((1, (ksT, "ksT", Vs, "Vs")), (2, (kwT, "kwT", Vw, "Vw"))):
            kts = list(range(0, 4 * T + 4)) if br == 1 else list(range(max(0, 4 * T - 4), 4 * T + 4))
            bi = br_i[0] % 2
            br_i[0] += 1
            for kt in kts:
                i = kt - 4 * T
                if br == 1:
                    qlo, qhi = max(i, 0), 3
                else:
                    qlo, qhi = max(i, 0), min(i + 4, 3)
                c0, c1 = qlo * 128, (qhi + 1) * 128
                ps_t, ps_k = next_psc()
                use_sel = (br == 1 and T >= 2)
                tri_q = i if i >= 0 else None
                anti_q = (i + 4) if (br == 2 and i < 0 and i + 4 <= 3) else None
                def fs(e, ps_t=ps_t, kt=kt, c0=c0, c1=c1, use_sel=use_sel, tri_q=tri_q, anti_q=anti_q, kTt=kTt, br_=br):
                    more = use_sel or (tri_q is not None) or (anti_q is not None)
                    kcol = kt * 128 if br_ == 1 else ((kt // 4) % 2) * TT + (kt % 4) * 128
                    ins = e.matmul(ps_t[:, c0:c1], lhsT=kTt[:, g, kcol:kcol + 128], rhs=qh[:, c0:c1], start=True, stop=not more)
                    if use_sel:
                        m2_ = (tri_q is not None) or (anti_q is not None)
                        ins = e.matmul(ps_t[:, c0:c1], lhsT=cbf[0:32, CB_E + kt * 128:CB_E + (kt + 1) * 128], rhs=negmT[0:32, g, c0:c1], start=False, stop=not m2_)
                    if tri_q is not None:
                        ins = e.matmul(ps_t[:, tri_q * 128:(tri_q + 1) * 128], lhsT=ident_b, rhs=tri_b, start=False, stop=(anti_q is None))
                    if anti_q is not None:
                        ins = e.matmul(ps_t[:, anti_q * 128:(anti_q + 1) * 128], lhsT=ident_b, rhs=anti_b, start=False, stop=True)
                    return ins
                P.op("pe", fs, reads=[kkey, qk, "cbf", "negmT"], writes=[ps_k])
                pi = pt_i[0] % 2
                pt_i[0] += 1
                ptile, pkey = PT[pi], "PT%d" % pi
                P.op("act", lambda e, ps_t=ps_t, ptile=ptile, c0=c0, c1=c1: e.activation(out=ptile[:, c0:c1], in_=ps_t[:, c0:c1], func=AF.Exp, scale=SCALE),
                     reads=[ps_k], writes=[pkey])
                def fpv2(e, ptile=ptile, kt=kt, qlo=qlo, qhi=qhi, Vt=Vt, br=br, bi=bi, kt0=kts[0]):
                    ins = None
                    for qi in range(qlo, qhi + 1):
                        klast = 4 * T + qi
                        vslot = kt if br == 1 else ((kt // 4) % 2) * 4 + (kt % 4)
                        st_ = (kt == kt0 and qi == 0)
                        e.matmul(pofl[bi][:, qi * 128:(qi + 1) * 128], lhsT=ptile[:, qi * 128:(qi + 1) * 128], rhs=Vt[:, g, vslot, 0:128],
                                 start=st_, stop=(kt == klast))
                        ins = e.matmul(pp[bi][:, qi:qi + 1], lhsT=ptile[:, qi * 128:(qi + 1) * 128], rhs=ones_b[:, 0:1], start=st_, stop=(kt == klast))
                    return ins
                def pvk(fpv2=fpv2, pkey=pkey, vkey=vkey, last=(kt == kts[-1]), br=br, bi=bi):
                    P.op("pe", fpv2, reads=[pkey, vkey, "ones_b"], writes=["po%d" % bi, "pp%d" % bi])
                    if last:
                        combine(br, bi)
                push_pv(pvk)
        while pv_q:
            pv_q.pop(0)()

    import os as _os
    STAGE = float(_os.environ.get('KSTAGE', '99'))
    dbgT = int(_os.environ.get('KDBGT', '2'))

    wscr_box = [None]

    def whole():
        setup()
        gate_rows()
        if not W.collect:
            W.wscr = wscr_box[0]
        if STAGE <= 0:
            return
        for b in range(nseq):
            seq_start(b)
            if STAGE <= 1:
                return
            for T in range(NT):
                body(b, T)
                if STAGE < 99:
                    return
                if debug and b == 0 and T == dbgT:
                    return

    P.dry = True
    W.collect = True
    whole()
    W.finish_collect()
    wscr_box[0] = nc.dram_tensor("wscr", [len(W.uniq), 128, 2048], BF16, kind="Internal").ap()
    P.dry = False
    W.collect = False
    W.pos = 0
    pp_i[0] = psc_i[0] = tr_i[0] = pt_i[0] = 0
    whole()
    assert W.pos == len(W.specs) and W.posB == len(W.specsB), (W.pos, len(W.specs), W.posB, len(W.specsB))
    P.emit()
    nc._kstats = dict(n_ops=len(P.ops), sbuf=sbuf_used, sems=P.stats)
    return nc


def _in_maps(inputs, nseq, ncores):
    cbf, cf, _ = _consts()
    f = lambda a: np.ascontiguousarray(np.asarray(a, dtype=np.float32))
    x = f(inputs["x"])
    c = f(inputs["c"])
    pos = np.ascontiguousarray(np.asarray(inputs["positions"], dtype=np.int32))
    b_ada = f(inputs["b_ada"])[0]
    shared = {
        "w_ada": f(inputs["w_ada"])[0],
        "b_adaT": np.ascontiguousarray(b_ada.reshape(48, 128).T),
        "b_gate": np.ascontiguousarray(b_ada[None, 4096:6144]),
        "g_normT": np.ascontiguousarray(f(inputs["g_norm"])[0].reshape(16, 128).T),
        "w_in": f(inputs["w_in"])[0],
        "g_retT": np.ascontiguousarray(f(inputs["g_ret"])[0].reshape(8, 128).T),
        "w_ck1": f(inputs["w_ck1"])[0],
        "w_ck2": f(inputs["w_ck2"])[0],
        "pe_ckT": np.ascontiguousarray(f(inputs["pe_ck"])[0].T),
        "w_cv1": f(inputs["w_cv1"])[0],
        "w_cv2": f(inputs["w_cv2"])[0],
        "pe_cvT": np.ascontiguousarray(f(inputs["pe_cv"])[0].T),
        "w_up_ret": f(inputs["w_up_ret"])[0],
        "w_up_nsa": f(inputs["w_up_nsa"])[0],
        "w_out": f(inputs["w_out"])[0],
        "g_final": np.ascontiguousarray(f(inputs["g_final"])[None, :]),
        "cbf": cbf,
        "cf": cf,
    }
    maps = []
    for i in range(ncores):
        sl = slice(i * nseq, (i + 1) * nseq)
        cT = np.concatenate([c[i * nseq + b].reshape(16, 128).T for b in range(nseq)], axis=1)
        m = dict(shared)
        m["x"] = np.ascontiguousarray(x[sl])
        m["cT"] = np.ascontiguousarray(cT)
        m["pos"] = np.ascontiguousarray(pos[sl])
        maps.append(m)
    return maps


def kernel(**inputs):
    nc = build_nc(SEQ_PER_CORE)
    maps = _in_maps(inputs, SEQ_PER_CORE, NCORES)
    res = run_bass_kernel_spmd(nc, maps, core_ids=list(range(NCORES)))
    out = np.concatenate([np.asarray(r["out"]) for r in res.results], axis=0)
    return out.astype(np.float32)
```

```python
import contextlib
import math
import numpy as np
import ml_dtypes
import concourse.bass as bass
import concourse.mybir as mybir
from concourse.bass_utils import run_bass_kernel_spmd

F32 = mybir.dt.float32
BF16 = mybir.dt.bfloat16
I32 = mybir.dt.int32
ALU = mybir.AluOpType
AF = mybir.ActivationFunctionType

D = 2048
S = 2048
NB = 16
NCORES = 8
SEQ_PER_CORE = 2
TT = 512
NT = S // TT
INW = 11800
O_RQ, O_RK, O_RV, O_RG, O_NQ = 0, 1024, 2048, 3072, 4096
O_KC, O_VC, O_KS, O_VS, O_KW, O_VW = 5120, 5376, 5632, 5888, 6144, 6400
O_NG, O_BG, O_MA, O_MB = 6656, 7680, 7704, 9752
EPS = 1e-6
SCALE = 128.0 ** -0.5
NEGM = -30000.0
ENGS = ("pe", "act", "dve", "pool", "sp")
PSUM_KEYS = {"pp0", "pp1", "psc0", "psc1", "po0", "po1", "ptr0", "ptr1"}


class _Op:
    __slots__ = ("eng", "fn", "deps", "sig", "semkey", "inc", "tick")


class Prog:
    def __init__(self, nc):
        self.nc = nc
        self.ops = []
        self.last_w = {}
        self.readers = {}
        self.dry = False

    def _add(self, eng, fn, reads, writes, semkey, inc, sig):
        if self.dry:
            return
        writes = list(writes) + [k for k in reads if k in PSUM_KEYS]
        deps = set()
        lw = self.last_w
        for k in reads:
            w = lw.get(k)
            if w is not None:
                deps.add(w)
        for k in writes:
            w = lw.get(k)
            if w is not None:
                deps.add(w)
            for r in self.readers.get(k, ()):
                deps.add(r)
        o = _Op()
        o.eng, o.fn, o.deps, o.sig, o.semkey, o.inc = eng, fn, deps, sig, semkey, inc
        idx = len(self.ops)
        self.ops.append(o)
        for k in reads:
            self.readers.setdefault(k, []).append(idx)
        for k in writes:
            lw[k] = idx
            self.readers[k] = []

    def op(self, eng, fn, reads=(), writes=()):
        self._add(eng, fn, reads, writes, eng, 1, False)

    def dma(self, eng, fn, reads=(), writes=(), semkey=None, n=1):
        self._add(eng, fn, reads, writes, semkey, 16 * n, True)

    def emit(self):
        nc = self.nc
        ops = self.ops
        for o in ops:
            for d in o.deps:
                ops[d].sig = True
        counts = {}
        for o in ops:
            if o.sig:
                counts[o.semkey] = counts.get(o.semkey, 0) + o.inc
                o.tick = counts[o.semkey]
            else:
                o.tick = None
        semkeys = list(counts.keys())
        for e in ENGS:
            if e not in semkeys:
                semkeys.append(e)
        self.stats = dict(counts)
        with contextlib.ExitStack() as st:
            sems = {k: st.enter_context(nc.semaphore("s_" + str(k))) for k in semkeys}
            block = st.enter_context(nc.Block())
            per_eng = {e: [o for o in ops if o.eng == e] for e in ENGS}

            def run(engine, ename):
                waited = {}
                for o in per_eng[ename]:
                    need = {}
                    for d in o.deps:
                        dd = ops[d]
                        if dd.tick > need.get(dd.semkey, 0):
                            need[dd.semkey] = dd.tick
                    for k, v in need.items():
                        if v > waited.get(k, 0):
                            engine.wait_ge(sems[k], v)
                            waited[k] = v
                    if o.semkey == ename:
                        ins = o.fn(engine)
                        if o.sig:
                            ins.then_inc(sems[ename], 1)
                    else:
                        o.fn(engine, sems[o.semkey])
                if ename == "sp":
                    for k, v in counts.items():
                        if v > waited.get(k, 0):
                            engine.wait_ge(sems[k], v)

            block.tensor(lambda e: run(e, "pe"))
            block.scalar(lambda e: run(e, "act"))
            block.vector(lambda e: run(e, "dve"))
            block.gpsimd(lambda e: run(e, "pool"))
            block.sync(lambda e: run(e, "sp"))


def _consts():
    bf = ml_dtypes.bfloat16
    H = 8
    lg = np.log1p(-np.exp2(-5.0 - np.arange(H, dtype=np.float64)))
    idx = np.arange(128, dtype=np.float64)
    cb = np.zeros((128, 0), np.float32)
    parts = {}
    ident = np.eye(128, dtype=np.float32)
    swp = np.zeros((128, 128), np.float32)
    for d in range(128):
        swp[(d + 64) % 128, d] = 1.0
    k = idx[:, None]
    q = idx[None, :]
    tri = np.where(k <= q, 0.0, NEGM).astype(np.float32)
    anti = np.where(k > q, 0.0, NEGM).astype(np.float32)
    dm = np.zeros((128, H, 128), np.float32)
    xi = np.zeros((128, H, 128), np.float32)
    for h in range(H):
        diff = idx[None, :] - idx[:, None]
        dm[:, h, :] = np.where(diff >= 0, np.exp(lg[h] * np.maximum(diff, 0.0)), 0.0) * SCALE
        xi[:, h, :] = np.exp(lg[h] * (idx + 1))[None, :]
    E = np.zeros((128, 16, 128), np.float32)
    for kt in range(16):
        for kk in range(128):
            E[2 * kt + kk // 64, kt, kk] = 1.0
    t = np.arange(S)[None, :]
    n1 = np.arange(128)[:, None]
    validT = np.where((n1 >= 1) & (16 * n1 + 15 <= t), 0.0, NEGM).astype(np.float32)
    negb = np.zeros((128, 2, 4, 128), np.float32)
    bonus = np.zeros((128, 2, 4, 32), np.float32)
    for tm in range(2):
        for st in range(4):
            tt = 1024 + tm * 512 + st * 128 + np.arange(128)
            nn = np.arange(128)[None, :]
            v = (nn >= 1) & (16 * nn + 15 <= tt[:, None])
            negb[:, tm, st, :] = np.where(v, 0.0, NEGM)
            tb = (tt // 64)[:, None]
            j = np.arange(32)[None, :]
            forced = (j == 0) | (j == tb) | (j == tb - 1)
            bonus[:, tm, st, :] = np.where(j <= tb, np.where(forced, 1.0e4, 0.0), -1.0e30)
    cbf = np.concatenate([ident, swp, tri, anti, dm.reshape(128, -1), xi.reshape(128, -1),
                          E.reshape(128, -1), validT, negb.reshape(128, -1)], axis=1).astype(bf)
    inv = np.exp(np.arange(0, 128, 2, dtype=np.float32) * np.float32(-math.log(10000.0) / 128)).astype(np.float32)
    inv = np.concatenate([inv, inv])
    sgn = np.concatenate([-np.ones(64), np.ones(64)]).astype(np.float32)
    zeta = np.zeros((128, H), np.float32)
    for h in range(H):
        zeta[:, h] = np.exp(lg[h] * (127 - idx)) * SCALE
    oh = np.zeros((128, 256), np.float32)
    oh[0, 0:128] = 1.0
    oh[1, 128:256] = 1.0
    cf = np.concatenate([inv[:, None], sgn[:, None], zeta, bonus.reshape(128, -1), oh], axis=1).astype(np.float32)
    decay = [float(np.exp(lg[h] * 128)) for h in range(H)]
    return cbf, cf, decay


CB_ID, CB_SW, CB_TRI, CB_ANTI, CB_DM, CB_XI, CB_E, CB_VT, CB_NB = 0, 128, 256, 384, 512, 1536, 2560, 4608, 6656
CB_N = 6656 + 1024
CF_INV, CF_SGN, CF_ZETA, CF_BON, CF_OH = 0, 1, 2, 10, 266
CF_N = 266 + 256


def build_nc(nseq=SEQ_PER_CORE, debug=False):
    nc = bass.Bass("TRN2", target_bir_lowering=False)
    _, _, DECAY = _consts()

    def din(name, shape, dt=F32):
        return nc.dram_tensor(name, list(shape), dt, kind="ExternalInput").ap()

    x_d = din("x", [nseq, S, D])
    cT_d = din("cT", [128, nseq * 16])
    pos_d = din("pos", [nseq, S], I32)
    w_ada = din("w_ada", [D, 3 * D])
    b_adaT = din("b_adaT", [128, 48])
    b_gate = din("b_gate", [1, D])
    g_normT = din("g_normT", [128, 16])
    w_in = din("w_in", [D, INW])
    g_retT = din("g_retT", [128, 8])
    w_ck1 = din("w_ck1", [4096, 256])
    w_ck2 = din("w_ck2", [256, 128])
    pe_ckT = din("pe_ckT", [128, 32])
    w_cv1 = din("w_cv1", [4096, 256])
    w_cv2 = din("w_cv2", [256, 128])
    pe_cvT = din("pe_cvT", [128, 32])
    w_ur = din("w_up_ret", [1024, D])
    w_un = din("w_up_nsa", [1024, D])
    w_out = din("w_out", [D, D])
    g_fin = din("g_final", [1, D])
    cbf_d = din("cbf", [128, CB_N], BF16)
    cf_d = din("cf", [128, CF_N])
    out_d = nc.dram_tensor("out", [nseq, S, D], F32, kind="ExternalOutput").ap()
    gsc_d = nc.dram_tensor("gsc", [nseq, D], F32, kind="Internal").ap()
    dbg = {}
    if debug:
        for nm, shp in (("d_hT", [128, 16 * TT]), ("d_yT", [128, 16 * TT]), ("d_m2", [128, 16 * TT])):
            dbg[nm] = nc.dram_tensor(nm, shp, F32, kind="ExternalOutput").ap()

    P = Prog(nc)
    base = [16512]

    def sb(name, shape, dt):
        nbytes = int(np.prod(shape[1:])) * (4 if dt in (F32, I32) else 2)
        off = base[0]
        base[0] = (off + nbytes + 31) // 32 * 32
        assert base[0] <= 229344, (name, base[0])
        return nc.alloc_sbuf_tensor_at(name, list(shape), dt, offset=off)

    def psum(name, shape, dt):
        return nc.alloc_psum_tensor(name, list(shape), dt)

    cbf = sb("cbf", [128, CB_N], BF16)
    cf = sb("cf", [128, CF_N], F32)
    hT_off = base[0]
    hT = sb("hT", [128, 16, TT], BF16)
    yT = sb("yT", [128, 16, TT], BF16)
    xt = sb("xt", [128, 4, D], F32)
    xn = sb("xn", [128, 4, D], BF16)
    wreg = base[0]
    wst = [sb("wst%d" % i, [128, 2048], F32) for i in range(2)]
    wbf = [sb("wbf%d" % i, [128, 2048], BF16) for i in range(2)]
    NSLOT = 6
    assert base[0] - wreg == NSLOT * 4096
    wsl = [nc.alloc_sbuf_tensor_at("wsl%d" % i, [128, 2048], BF16, offset=wreg + i * 4096) for i in range(NSLOT)]
    gate_bc1 = sb("gate_bc", [128, D], F32)
    gate_bc = [gate_bc1 for b in range(nseq)]
    gfin_bc = nc.alloc_sbuf_tensor_at("gfin_bc", [128, D], F32, offset=hT_off)
    cosT = sb("cosT", [128, TT], F32)
    sinT = sb("sinT", [128, TT], F32)
    ksT = sb("ksT", [128, 2, S], BF16)
    kwT = sb("kwT", [128, 2, 2 * TT], BF16)
    Vs = sb("Vs", [128, 2, 16, 130], BF16)
    Vw = sb("Vw", [128, 2, 8, 130], BF16)
    kroll = sb("kroll", [128, 2, 16 + TT], BF16)
    vroll = sb("vroll", [128, 2, 16 + TT], BF16)
    KcT = sb("KcT", [128, 2, 128], BF16)
    Vc = sb("Vc", [128, 2, 130], BF16)
    state = sb("state", [128, 8, 128], F32)
    state_b = sb("state_b", [128, 8, 128], BF16)
    modT = sb("modT", [128, 32, nseq], F32)
    sc1 = sb("sc1", [128, 16, nseq], F32)
    gnT = sb("gnT", [128, 16], F32)
    grT = sb("grT", [128, 8], F32)
    badaT = sb("badaT", [128, 48], F32)
    cT = sb("cT", [128, nseq * 16], F32)
    cTt = sb("cTt", [128, nseq * 16], F32)
    siluc = sb("siluc", [128, 16, nseq], BF16)
    peT = [sb("peT%d" % i, [128, 32], F32) for i in range(2)]
    peTb = [sb("peTb%d" % i, [128, 32], BF16) for i in range(2)]
    w2st = [sb("w2st%d" % i, [128, 2, 128], F32) for i in range(2)]
    w2bf = [sb("w2bf%d" % i, [128, 2, 128], BF16) for i in range(2)]
    cpi = sb("cpi", [128, 1], F32)
    ones_b = sb("ones_b", [128, 128], BF16)
    ceps = sb("ceps", [128, 1], F32)
    ss4 = sb("ss4", [128, 4], F32)
    rs4 = sb("rs4", [128, 4], F32)
    angI = sb("angI", [128, TT], I32)
    pos_i = angI
    qT = [sb("qT%d" % i, [128, TT], BF16) for i in range(4)]
    kT = sb("kT", [128, TT], BF16)
    qxT = sb("qxT", [128, TT], BF16)
    vT = sb("vT", [128, TT], BF16)
    xb = sb("xb", [128, TT], BF16)
    r1 = sb("r1", [128, TT], F32)
    r2 = sb("r2", [128, TT], F32)
    Vr = sb("Vr", [128, 4, 128], BF16)
    Kz = sb("Kz", [128, 4, 128], BF16)
    scm4 = sb("scm4", [128, 4, 128], BF16)
    sb3 = sb("sb3", [128, 3, 128], BF16)
    bst = sb("bst", [128, 4, 6], F32)
    mv = sb("mv", [128, 4, 2], F32)
    nmr = sb("nmr", [128, 4], F32)
    on = sb("on", [128, 4, 128], BF16)
    tg = sb("tg", [128, TT], F32)
    gg = sb("gg", [128, TT], BF16)
    gates = sb("gates", [128, 4, 24], F32)
    hb = sb("hb", [128, 2, 2], F32)
    hbh = sb("hbh", [128, 2, 2], F32)
    hx = sb("hx", [128, 64], F32)
    ht = sb("ht", [128, 64], F32)
    shT = sb("shT", [128, 2, 64], BF16)
    kcx = sb("kcx", [128, 64], BF16)
    vct = sb("vct", [32, 2, 130], BF16)
    PT = [sb("PT%d" % i, [128, TT], BF16) for i in range(2)]
    stmp = sb("stmp", [128, 128], F32)
    etmp = sb("etmp", [128, 128], F32)
    rsum = sb("rsum", [128, 1], F32)
    rinv = sb("rinv", [128, 1], F32)
    ppad = sb("ppad", [128, 4, 132], F32)
    imp = sb("imp", [128, 32], F32)
    m8 = sb("m8", [128, 16], F32)
    wk = sb("wk", [128, 32], F32)
    negm = sb("negm", [128, 32], BF16)
    negmT = sb("negmT", [32, 2, TT], BF16)
    den = sb("den", [128, 1], F32)
    den4 = sb("den4", [128, 4], F32)
    coef = sb("coef", [128, 1], F32)
    acc = sb("acc", [128, 4, 128], F32)
    accb = sb("accb", [128, 4, 128], BF16)
    angA, angB, angC, ta, tb2 = r1, r2, tg, r1, r2
    sbuf_used = base[0]

    pp = [psum("pp%d" % i, [128, 512], F32) for i in range(2)]
    psc = [psum("psc%d" % i, [128, 512], F32) for i in range(2)]
    po = [psum("po%d" % i, [128, 2, 256], F32) for i in range(2)]
    ptrs = [psum("ptr%d" % i, [128, 1024], BF16) for i in range(2)]

    pm = po[1][:, :, :].rearrange("p a c -> p (a c)")
    pofl = [po[i][:, :, :].rearrange("p a c -> p (a c)") for i in range(2)]
    ptrf = [ptrs[i][:, :].bitcast(F32) for i in range(2)]
    br_i = [0]

    class _Ptr:
        def __getitem__(self, idx):
            p_, h_, c_ = idx
            return ptrs[h_][p_, c_] if not (isinstance(c_, slice) and c_ == slice(None)) else ptrs[h_][p_, 0:512]
    ptr = _Ptr()

    class _M2:
        def __getitem__(self, idx):
            p_, cc, t_ = idx
            return xn[p_, cc // 4, (cc % 4) * TT + (t_.start or 0):(cc % 4) * TT + (t_.stop if t_.stop is not None else TT)]
    m2 = _M2()

    def cB(off, n):
        return cbf[:, off:off + n]

    ident_b = cB(CB_ID, 128)
    swap_b = cB(CB_SW, 128)
    tri_b = cB(CB_TRI, 128)
    anti_b = cB(CB_ANTI, 128)

    def bc(ap2d, n):
        a = ap2d
        return bass.AP(tensor=a.tensor, offset=a.offset, ap=[list(a.ap[0]), [0, n]] + [list(z) for z in a.ap[1:]])

    class WS:
        def __init__(self):
            self.specs = []
            self.specsB = []
            self.pos = 0
            self.issued = 0
            self.posB = 0
            self.issuedB = 0
            self.collect = True
            self.gid = {}

        def _issue(self, k):
            spec = self.specs[k]
            slot = k % 2
            kind = spec[0]
            st_t, bf_t = wst[slot], wbf[slot]
            if kind == "cols":
                _, w, col0, ncols, nk = spec
                w = wmap[w]
                dst = st_t[:, 0:nk * ncols].rearrange("p (k c) -> p k c", c=ncols)
                src = w[0:nk * 128, col0:col0 + ncols].rearrange("(k p) c -> p k c", p=128)
                half = nk // 2
                def f(e, s, dst=dst, src=src, half=half, nk=nk):
                    e.dma_start(out=dst[:, 0:half, :], in_=src[:, 0:half, :]).then_inc(s, 16)
                    e.dma_start(out=dst[:, half:nk, :], in_=src[:, half:nk, :]).then_inc(s, 16)
                P.dma("sp", f, writes=["wst%d" % slot], semkey="w%d" % slot, n=2)
                n = nk * ncols
            elif kind == "two":
                _, col0 = spec
                d0 = st_t[:, 0:1024].rearrange("p (k c) -> p k c", c=128)
                d1 = st_t[:, 1024:2048].rearrange("p (k c) -> p k c", c=128)
                s0 = w_ur[:, col0:col0 + 128].rearrange("(k p) c -> p k c", p=128)
                s1 = w_un[:, col0:col0 + 128].rearrange("(k p) c -> p k c", p=128)
                def f(e, s, d0=d0, d1=d1, s0=s0, s1=s1):
                    e.dma_start(out=d0, in_=s0).then_inc(s, 16)
                    e.dma_start(out=d1, in_=s1).then_inc(s, 16)
                P.dma("sp", f, writes=["wst%d" % slot], semkey="w%d" % slot, n=2)
                n = 2048
            elif kind == "wout":
                _, ct, pc = spec
                dst = st_t[:, :].rearrange("p (k c) -> p k c", c=512)
                src = w_out[pc * 512:(pc + 1) * 512, ct * 512:(ct + 1) * 512].rearrange("(k p) c -> p k c", p=128)
                def f(e, s, dst=dst, src=src):
                    e.dma_start(out=dst[:, 0:2, :], in_=src[:, 0:2, :]).then_inc(s, 16)
                    e.dma_start(out=dst[:, 2:4, :], in_=src[:, 2:4, :]).then_inc(s, 16)
                P.dma("sp", f, writes=["wst%d" % slot], semkey="w%d" % slot, n=2)
                n = 2048
            elif kind == "w1":
                _, w, pc = spec
                w = wmap[w]
                dst = st_t[:, :].rearrange("p (i c) -> p i c", c=256)
                src = w[pc * 1024:(pc + 1) * 1024, :].rearrange("(i p) c -> p i c", p=128)
                def f(e, s, dst=dst, src=src):
                    e.dma_start(out=dst[:, 0:4, :], in_=src[:, 0:4, :]).then_inc(s, 16)
                    e.dma_start(out=dst[:, 4:8, :], in_=src[:, 4:8, :]).then_inc(s, 16)
                P.dma("sp", f, writes=["wst%d" % slot], semkey="w%d" % slot, n=2)
                n = 2048
            if k % 2 == 0:
                P.op("dve", lambda e, bf_t=bf_t, st_t=st_t, n=n: e.tensor_copy(out=bf_t[:, 0:n], in_=st_t[:, 0:n]),
                     reads=["wst%d" % slot], writes=["wbf%d" % slot])
            else:
                P.op("act", lambda e, bf_t=bf_t, st_t=st_t, n=n: e.activation(out=bf_t[:, 0:n], in_=st_t[:, 0:n], func=AF.Identity),
                     reads=["wst%d" % slot], writes=["wbf%d" % slot])

        def nextA(self, spec):
            k = self.pos
            assert self.specs[k] == spec, (k, self.specs[k], spec)
            while self.issued < min(k + 2, len(self.specs)):
                self._issue(self.issued)
                self.issued += 1
            self.pos += 1
            slot = k % 2
            return wbf[slot], "wbf%d" % slot

        def finish_collect(self):
            for sp_ in self.specsB:
                if sp_ not in self.gid:
                    self.gid[sp_] = len(self.gid)
            self.uniq = sorted(self.gid, key=lambda z: self.gid[z])
            self.specs = self.specs + self.uniq
            self.scr_keys = ["scr%d" % i for i in range(len(self.uniq))]

        def preconvert(self, wscr):
            self.wscr = wscr
            for u in self.uniq:
                k = self.pos
                wt, wk_ = self.nextA(u)
                gid_ = self.gid[u]
                P.dma("sp", lambda e, s, wt=wt, gid_=gid_: e.dma_start(out=wscr[gid_, :, :], in_=wt[:, :]).then_inc(s, 16),
                      reads=[wk_], writes=["scr%d" % gid_], semkey="ws%d" % (k % 2))

        def _issueB(self, k):
            gid_ = self.gid[self.specsB[k]]
            slot = k % NSLOT
            wscr = self.wscr
            extra = ["wst0", "wst1", "wbf0", "wbf1"] if k < len(self.uniq) + NSLOT else []
            P.dma("sp", lambda e, s, slot=slot, gid_=gid_: e.dma_start(out=wsl[slot][:, :], in_=wscr[gid_, :, :]).then_inc(s, 16),
                  reads=self.scr_keys, writes=["wsl%d" % slot] + extra, semkey="wl%d" % slot)

        def next(self, spec):
            is_a = (spec[0] == "cols" and spec[1] == "w_ada")
            if self.collect:
                (self.specs if is_a else self.specsB).append(spec)
                return None, None
            if is_a:
                return self.nextA(spec)
            k = self.posB
            assert self.specsB[k] == spec, (k, self.specsB[k], spec)
            nu = len(self.uniq)
            if k < nu:
                assert self.uniq[k] == spec
                ka = self.pos
                wt, wk_ = self.nextA(spec)
                gid_ = self.gid[spec]
                wscr = self.wscr
                P.dma("sp", lambda e, s, wt=wt, gid_=gid_: e.dma_start(out=wscr[gid_, :, :], in_=wt[:, :]).then_inc(s, 16),
                      reads=[wk_], writes=["scr%d" % gid_], semkey="ws%d" % (ka % 2))
                self.posB += 1
                self.issuedB = max(self.issuedB, nu)
                return wt, wk_
            while self.issuedB < min(k + NSLOT, len(self.specsB)):
                self._issueB(self.issuedB)
                self.issuedB += 1
            self.posB += 1
            slot = k % NSLOT
            return wsl[slot], "wsl%d" % slot

    W = WS()
    wmap = {"w_in": w_in, "w_ada": w_ada, "w_ck1": w_ck1, "w_cv1": w_cv1}
    pp_i = [0]
    psc_i = [0]

    def next_pp():
        i = pp_i[0] % 2
        pp_i[0] += 1
        return pp[i], "pp%d" % i

    def next_psc():
        i = psc_i[0] % 2
        psc_i[0] += 1
        return psc[i], "psc%d" % i

    def proj_fm(col0, dst_fn):
        wt, wk_ = W.next(("cols", "w_in", col0, 128, 16))
        if wt is None:
            flush_pending()
            dst_fn(None, None)
            return
        ps_t, ps_k = next_pp()
        w3 = wt[:, :].rearrange("p (k c) -> p k c", c=128)
        def f(e, w3=w3, ps_t=ps_t):
            ins = None
            for kc in range(16):
                ins = e.matmul(ps_t[:, :], lhsT=w3[:, kc, :], rhs=hT[:, kc, :], start=(kc == 0), stop=(kc == 15))
            return ins
        P.op("pe", f, reads=[wk_, "hT"], writes=[ps_k])
        flush_pending()
        dst_fn(ps_t, ps_k)

    pending = []

    def flush_pending():
        while pending:
            pending.pop(0)()

    def rope_to(ps_t, ps_k, dst_ap, dst_key):
        if ps_t is None:
            return
        P.op("act", lambda e: e.activation(out=xb[:, :], in_=ps_t[:, :], func=AF.Identity), reads=[ps_k], writes=["xb"])
        P.op("pool", lambda e: e.tensor_tensor(out=r1[:, :], in0=xb[:, :], in1=cosT[:, :], op=ALU.mult), reads=["xb", "cosT"], writes=["r1"])

        def rest():
            P.op("pe", lambda e: e.matmul(pm[:, :], lhsT=swap_b, rhs=xb[:, :], start=True, stop=True), reads=["xb", "cbf"], writes=["po1"])
            P.op("dve", lambda e: e.tensor_tensor(out=r2[:, :], in0=pm[:, :], in1=sinT[:, :], op=ALU.mult), reads=["po1", "sinT"], writes=["r2"])
            P.op("pool", lambda e: e.tensor_tensor(out=dst_ap, in0=r1[:, :], in1=r2[:, :], op=ALU.add), reads=["r1", "r2"], writes=[dst_key])
        pending.append(rest)

    def copy_to(ps_t, ps_k, dst_ap, dst_key):
        if ps_t is None:
            return
        P.op("act", lambda e: e.activation(out=dst_ap, in_=ps_t[:, :], func=AF.Identity), reads=[ps_k], writes=[dst_key])

    def transpose4(src_fn, src_keys, half, pre=()):
        def f(e):
            ins = None
            for j in range(4):
                ins = e.transpose(ptr[:, half, j * 128:(j + 1) * 128], src_fn(j), ident_b)
            return ins
        P.op("pe", f, reads=list(src_keys) + ["cbf"], writes=["ptr%d" % half])

    tr_i = [0]

    def next_tr():
        i = tr_i[0] % 2
        tr_i[0] += 1
        return i

    def gate_a(col0):
        def g(ps_t, ps_k):
            if ps_t is None:
                return
            P.op("act", lambda e: e.activation(out=tg[:, :], in_=ps_t[:, :], func=AF.Tanh, scale=0.5), reads=[ps_k], writes=["tg"])
            P.op("dve", lambda e: e.scalar_tensor_tensor(out=gg[:, :], in0=tg[:, :], scalar=1.0, in1=ps_t[:, :], op0=ALU.add, op1=ALU.mult),
                 reads=["tg", ps_k], writes=["gg"])
        proj_fm(col0, g)

    def gate_b(ydst, ykey):
        P.op("pool", lambda e: e.tensor_tensor(out=ydst, in0=ydst, in1=gg[:, :], op=ALU.mult), reads=["gg", ykey], writes=[ykey])

    def setup():
        def ld(dst, src, key):
            P.dma("sp", lambda e, s: e.dma_start(out=dst, in_=src).then_inc(s, 16), writes=[key], semkey="ld_" + key)
        ld(cbf[:, :], cbf_d, "cbf")
        ld(cf[:, :], cf_d, "cf")
        ld(cT[:, :], cT_d, "cT")
        ld(badaT[:, :], b_adaT, "badaT")
        ld(gnT[:, :], g_normT, "gnT")
        ld(grT[:, :], g_retT, "grT")
        ld(peT[0][:, :], pe_ckT, "peT0")
        ld(peT[1][:, :], pe_cvT, "peT1")
        ld(w2st[0][:, :, :], w_ck2.rearrange("(m p) c -> p m c", p=128), "w2st0")
        ld(w2st[1][:, :, :], w_cv2.rearrange("(m p) c -> p m c", p=128), "w2st1")
        P.op("pool", lambda e: e.memset(cpi[:, :], float(np.pi)), writes=["cpi"])
        P.op("pool", lambda e: e.memset(ones_b[:, :], 1.0), writes=["ones_b"])
        P.op("pool", lambda e: e.memset(ceps[:, :], EPS), writes=["ceps"])
        P.op("pool", lambda e: e.memset(Vs[:, :, :, 128:130], 1.0), writes=["Vs"])
        P.op("pool", lambda e: e.memset(Vw[:, :, :, 128:130], 1.0), writes=["Vw"])
        P.op("pool", lambda e: e.memset(Vc[:, :, :], 0.0), writes=["Vc"])
        P.op("pool", lambda e: e.memset(vct[:, :, 128:130], 1.0), writes=["vct"])
        P.op("pool", lambda e: e.memset(KcT[:, :, :], 0.0), writes=["KcT"])
        P.op("pool", lambda e: e.memset(ppad[:, :, :], 0.0), writes=["ppad"])
        for i in range(2):
            P.op("pool", lambda e, i=i: e.tensor_copy(out=peTb[i][:, :], in_=peT[i][:, :]), reads=["peT%d" % i], writes=["peTb%d" % i])
            P.op("pool", lambda e, i=i: e.tensor_copy(out=w2bf[i][:, :, :], in_=w2st[i][:, :, :]), reads=["w2st%d" % i], writes=["w2bf%d" % i])
        P.op("pool", lambda e: e.tensor_scalar(out=grT[:, :], in0=grT[:, :], scalar1=0.5, scalar2=None, op0=ALU.mult), reads=["grT"], writes=["grT"])
        P.op("act", lambda e: e.activation(out=cTt[:, :], in_=cT[:, :], func=AF.Tanh, scale=0.5), reads=["cT"], writes=["cTt"])
        P.op("dve", lambda e: e.scalar_tensor_tensor(out=cTt[:, :], in0=cTt[:, :], scalar=1.0, in1=cT[:, :], op0=ALU.add, op1=ALU.mult),
             reads=["cTt", "cT"], writes=["cTt"])
        for b in range(nseq):
            P.op("dve", lambda e, b=b: e.tensor_scalar(out=siluc[:, :, b], in0=cTt[:, b * 16:(b + 1) * 16], scalar1=0.5, scalar2=None, op0=ALU.mult),
                 reads=["cTt"], writes=["siluc"])
        for grp in range(32):
            wt, wk_ = W.next(("cols", "w_ada", grp * 128, 128, 16))
            if wt is None:
                continue
            w3 = wt[:, :].rearrange("p (k c) -> p k c", c=128)
            def f(e, w3=w3):
                ins = None
                for kc in range(16):
                    ins = e.matmul(pm[:, 0:nseq], lhsT=w3[:, kc, :], rhs=siluc[:, kc, :], start=(kc == 0), stop=(kc == 15))
                return ins
            P.op("pe", f, reads=[wk_, "siluc"], writes=["po1"])
            P.op("dve", lambda e, grp=grp: e.tensor_scalar(out=modT[:, grp, :], in0=pm[:, 0:nseq], scalar1=badaT[:, grp:grp + 1], scalar2=None, op0=ALU.add),
                 reads=["po1", "badaT"], writes=["modT"])
        if W.collect:
            return
        for b in range(nseq):
            P.op("dve", lambda e, b=b: e.scalar_tensor_tensor(out=sc1[:, :, b], in0=modT[:, 16:32, b], scalar=1.0, in1=gnT[:, :], op0=ALU.add, op1=ALU.mult),
                 reads=["modT", "gnT"], writes=["sc1"])

    def gate_rows():
        grow = xt[0:nseq, 1, :]
        for grp in range(16):
            wt, wk_ = W.next(("cols", "w_ada", (32 + grp) * 128, 128, 16))
            if wt is None:
                continue
            w3 = wt[:, :].rearrange("p (k c) -> p k c", c=128)
            def f(e, w3=w3):
                ins = None
                for kc in range(16):
                    ins = e.matmul(pm[0:nseq, 128:256], lhsT=siluc[:, kc, :], rhs=w3[:, kc, :], start=(kc == 0), stop=(kc == 15))
                return ins
            P.op("pe", f, reads=[wk_, "siluc"], writes=["po1"])
            P.op("act", lambda e, grp=grp: e.activation(out=grow[:, grp * 128:(grp + 1) * 128], in_=pm[0:nseq, 128:256], func=AF.Identity),
                 reads=["po1"], writes=["xt1"])
        if W.collect:
            return
        P.dma("sp", lambda e, s: e.dma_start(out=gsc_d, in_=grow).then_inc(s, 16), reads=["xt1"], writes=["gsc"], semkey="gs")

    def seq_start(b):
        if W.collect:
            return
        P.dma("sp", lambda e, s: e.dma_start(out=gate_bc1[:, :], in_=gsc_d[b, :].partition_broadcast(128)).then_inc(s, 16), reads=["gsc"], writes=["gate_bc"], semkey="gb")
        P.dma("sp", lambda e, s: e.dma_start(out=xt[:, 0, :], in_=b_gate[0, :].partition_broadcast(128)).then_inc(s, 16), writes=["xt0"], semkey="x0")
        P.op("dve", lambda e: e.tensor_tensor(out=gate_bc1[:, :], in0=gate_bc1[:, :], in1=xt[:, 0, :], op=ALU.add), reads=["gate_bc", "xt0"], writes=["gate_bc"])
        P.op("pool", lambda e: e.tensor_scalar(out=gate_bc1[:, :], in0=gate_bc1[:, :], scalar1=0.5, scalar2=None, op0=ALU.mult), reads=["gate_bc"], writes=["gate_bc"])

    def sincos(src_add, dst, dkey, fold_sign):
        P.op("dve", lambda e: e.tensor_scalar(out=angB[:, :], in0=angA[:, :], scalar1=float(src_add), scalar2=float(1.0 / (2 * np.pi)), op0=ALU.add, op1=ALU.mult),
             reads=["r1"], writes=["r2"])
        P.op("dve", lambda e: e.tensor_copy(out=angI[:, :], in_=angB[:, :]), reads=["r2"], writes=["angI"])
        P.op("dve", lambda e: e.tensor_copy(out=angB[:, :], in_=angI[:, :]), reads=["angI"], writes=["r2"])
        P.op("dve", lambda e: e.tensor_scalar(out=angC[:, :], in0=angA[:, :], scalar1=float(src_add), scalar2=None, op0=ALU.add), reads=["r1"], writes=["tg"])
        P.op("dve", lambda e: e.scalar_tensor_tensor(out=angC[:, :], in0=angB[:, :], scalar=-float(2 * np.pi), in1=angC[:, :], op0=ALU.mult, op1=ALU.add),
             reads=["r2", "tg"], writes=["tg"])
        P.op("dve", lambda e: e.tensor_scalar(out=angB[:, :], in0=angC[:, :], scalar1=0.0, scalar2=float(2 * np.pi), op0=ALU.is_lt, op1=ALU.mult),
             reads=["tg"], writes=["r2"])
        P.op("dve", lambda e: e.tensor_tensor(out=angC[:, :], in0=angC[:, :], in1=angB[:, :], op=ALU.add), reads=["tg", "r2"], writes=["tg"])
        P.op("dve", lambda e: e.tensor_scalar(out=angC[:, :], in0=angC[:, :], scalar1=float(2 * np.pi), scalar2=None, op0=ALU.min), reads=["tg"], writes=["tg"])
        P.op("act", lambda e: e.activation(out=dst[:, :], in_=angC[:, :], func=AF.Sin, bias=cpi[:, 0:1], scale=-1.0), reads=["tg", "cpi"], writes=[dkey])
        if fold_sign:
            P.op("dve", lambda e: e.tensor_scalar(out=dst[:, :], in0=dst[:, :], scalar1=cf[:, CF_SGN:CF_SGN + 1], scalar2=None, op0=ALU.mult),
                 reads=[dkey, "cf"], writes=[dkey])

    def body(b, T):
        tok0 = T * TT
        dry = W.collect
        if not dry:
            if not xpre[0]:
                for st in range(4):
                    P.dma("sp", lambda e, s, st=st: e.dma_start(out=xt[:, st, :], in_=x_d[b, tok0 + st * 128:tok0 + (st + 1) * 128, :]).then_inc(s, 16),
                          writes=["xt%d" % st], semkey="x%d" % st)
            xpre[0] = False
            P.dma("sp", lambda e, s: e.dma_start(out=pos_i[:, :], in_=pos_d[b, tok0:tok0 + TT].partition_broadcast(128)).then_inc(s, 16),
                  writes=["angI"], semkey="pos")
            P.op("dve", lambda e: e.tensor_copy(out=angA[:, :], in_=pos_i[:, :]), reads=["angI"], writes=["r1"])
            P.op("dve", lambda e: e.tensor_scalar(out=angA[:, :], in0=angA[:, :], scalar1=cf[:, CF_INV:CF_INV + 1], scalar2=None, op0=ALU.mult),
                 reads=["r1", "cf"], writes=["r1"])
            sincos(0.0, sinT, "sinT", True)
            sincos(np.pi / 2, cosT, "cosT", False)
            for st in range(4):
                P.op("act", lambda e, st=st: e.activation(out=xn[:, st, :], in_=xt[:, st, :], func=AF.Square, accum_out=ss4[:, st:st + 1]),
                     reads=["xt%d" % st], writes=["xn%d" % st, "ss4"])
            if STAGE <= 1.1:
                return
            P.op("act", lambda e: e.activation(out=rs4[:, :], in_=ss4[:, :], func=AF.Sqrt, bias=ceps[:, 0:1], scale=1.0 / D), reads=["ss4", "ceps"], writes=["rs4"])
            P.op("dve", lambda e: e.reciprocal(out=rs4[:, :], in_=rs4[:, :]), reads=["rs4"], writes=["rs4"])
            if STAGE <= 1.2:
                return
            for st in range(4):
                if st % 2 == 0:
                    P.op("act", lambda e, st=st: e.activation(out=xn[:, st, :], in_=xt[:, st, :], func=AF.Identity, scale=rs4[:, st:st + 1]),
                         reads=["xt%d" % st, "rs4"], writes=["xn%d" % st])
                else:
                    P.op("dve", lambda e, st=st: e.tensor_scalar(out=xn[:, st, :], in0=xt[:, st, :], scalar1=rs4[:, st:st + 1], scalar2=None, op0=ALU.mult),
                         reads=["xt%d" % st, "rs4"], writes=["xn%d" % st])
            if STAGE <= 1.3:
                return
            for fc in range(int(_os.environ.get('KFC', '16'))):
                h = next_tr()
                transpose4(lambda j, fc=fc: xn[:, j, fc * 128:(fc + 1) * 128], ["xn0", "xn1", "xn2", "xn3"], h)
                if _os.environ.get('KNODVE'):
                    continue
                kvar = _os.environ.get('KVAR', '3')
                if kvar == '1':
                    P.op("dve", lambda e, fc=fc, h=h: e.tensor_scalar(out=hT[:, fc, :], in0=ptr[:, h, :], scalar1=sc1[:, fc, b:b + 1], scalar2=None, op0=ALU.mult),
                         reads=["ptr%d" % h, "sc1", "modT"], writes=["hT"])
                    continue
                if kvar == '2':
                    P.op("dve", lambda e, fc=fc, h=h: e.tensor_copy(out=hT[:, fc, :], in_=ptr[:, h, :]),
                         reads=["ptr%d" % h, "sc1", "modT"], writes=["hT"])
                    continue
                if kvar == '3':
                    P.op("act", lambda e, fc=fc, h=h: e.activation(out=hT[:, fc, :], in_=ptr[:, h, :], func=AF.Identity, scale=sc1[:, fc, b:b + 1], bias=modT[:, fc, b:b + 1]),
                         reads=["ptr%d" % h, "sc1", "modT"], writes=["hT"])
                    continue
                P.op("dve", lambda e, fc=fc, h=h: e.tensor_scalar(out=hT[:, fc, :], in0=ptr[:, h, :], scalar1=sc1[:, fc, b:b + 1], scalar2=modT[:, fc, b:b + 1],
                                                                 op0=ALU.mult, op1=ALU.add),
                     reads=["ptr%d" % h, "sc1", "modT"], writes=["hT"])
            if debug and b == 0 and T == dbgT:
                P.op("dve", lambda e: e.tensor_copy(out=xt[:, 3, :].rearrange("p (a c) -> p a c", c=TT)[:, 0:4, :], in_=hT[:, 0:4, :]), reads=["hT"], writes=["xt3"])
            if STAGE <= 1.4:
                return
            if T == 0:
                P.op("pool", lambda e: e.memset(state[:, :, :], 0.0), writes=["state"])
                P.op("pool", lambda e: e.memset(state_b[:, :, :], 0.0), writes=["state_b"])
                P.op("pool", lambda e: e.memset(kroll[:, :, :], 0.0), writes=["kroll"])
                P.op("pool", lambda e: e.memset(vroll[:, :, :], 0.0), writes=["vroll"])
            else:
                P.op("pool", lambda e: e.tensor_copy(out=kroll[:, :, 0:16], in_=kroll[:, :, TT:TT + 16]), reads=["kroll"], writes=["kroll"])
                P.op("pool", lambda e: e.tensor_copy(out=vroll[:, :, 0:16], in_=vroll[:, :, TT:TT + 16]), reads=["vroll"], writes=["vroll"])

        if STAGE <= 2:
            return
        for h in range(8):
            proj_fm(O_RQ + h * 128, lambda p_, k_: rope_to(p_, k_, qT[0][:, :], "qT0"))
            proj_fm(O_RK + h * 128, lambda p_, k_: rope_to(p_, k_, kT[:, :], "kT"))
            proj_fm(O_RV + h * 128, lambda p_, k_: copy_to(p_, k_, vT[:, :], "vT"))
            if not dry:
                xi_ap = bc(cbf[:, CB_XI + h * 128:CB_XI + (h + 1) * 128], 4)
                P.op("pool", lambda e, xi_ap=xi_ap: e.tensor_tensor(out=qxT[:, :].rearrange("p (a c) -> p a c", c=128), in0=qT[0][:, :].rearrange("p (a c) -> p a c", c=128),
                                                                   in1=xi_ap, op=ALU.mult), reads=["qT0", "cbf"], writes=["qxT"])
                hh = next_tr()
                transpose4(lambda j: vT[:, j * 128:(j + 1) * 128], ["vT"], hh)
                P.op("act", lambda e, hh=hh: e.activation(out=Vr[:, :, :], in_=ptr[:, hh, :].rearrange("p (a c) -> p a c", c=128), func=AF.Identity),
                     reads=["ptr%d" % hh], writes=["Vr"])
                hh = next_tr()
                transpose4(lambda j: kT[:, j * 128:(j + 1) * 128], ["kT"], hh)
                P.op("act", lambda e, hh=hh, h=h: e.activation(out=Kz[:, :, :], in_=ptr[:, hh, :].rearrange("p (a c) -> p a c", c=128), func=AF.Identity,
                                                              scale=cf[:, CF_ZETA + h:CF_ZETA + h + 1]),
                     reads=["ptr%d" % hh, "cf"], writes=["Kz"])
                dm4 = bc(cbf[:, CB_DM + h * 128:CB_DM + (h + 1) * 128], 4)
                psA, kA = next_psc()
                psB, kB = next_psc()
                def f_sc(e, psA=psA):
                    ins = None
                    for c in range(4):
                        cs = slice(c * 128, (c + 1) * 128)
                        ins = e.matmul(psA[:, cs], lhsT=kT[:, cs], rhs=qT[0][:, cs], start=True, stop=True)
                    return ins
                P.op("pe", f_sc, reads=["kT", "qT0"], writes=[kA])
                def f_kv(e, psB=psB):
                    ins = None
                    for c in range(4):
                        ins = e.matmul(psB[:, c * 128:(c + 1) * 128], lhsT=Kz[:, c, :], rhs=Vr[:, c, :], start=True, stop=True)
                    return ins
                P.op("pe", f_kv, reads=["Kz", "Vr"], writes=[kB])
                P.op("dve", lambda e, psA=psA, dm4=dm4: e.tensor_tensor(out=scm4[:, :, :], in0=psA[:, :].rearrange("p (a c) -> p a c", c=128), in1=dm4, op=ALU.mult),
                     reads=[kA, "cbf"], writes=["scm4"])
                for c in range(3):
                    P.op("dve", lambda e, h=h, c=c, psB=psB: e.scalar_tensor_tensor(out=state[:, h, :], in0=state[:, h, :], scalar=DECAY[h], in1=psB[:, c * 128:(c + 1) * 128],
                                                                                  op0=ALU.mult, op1=ALU.add), reads=["state", kB], writes=["state"])
                    P.op("act", lambda e, h=h, c=c: e.activation(out=sb3[:, c, :], in_=state[:, h, :], func=AF.Identity), reads=["state"], writes=["sb3"])
                def f_o(e, h=h):
                    ins = None
                    for c in range(4):
                        cs = slice(c * 128, (c + 1) * 128)
                        e.matmul(pofl[0][:, cs], lhsT=scm4[:, c, :], rhs=Vr[:, c, :], start=True, stop=False)
                        ins = e.matmul(pofl[0][:, cs], lhsT=qxT[:, cs], rhs=(state_b[:, h, :] if c == 0 else sb3[:, c - 1, :]), start=False, stop=True)
                    return ins
                P.op("pe", f_o, reads=["scm4", "Vr", "qxT", "state_b", "sb3"], writes=["po0"])
                P.op("dve", lambda e, h=h, psB=psB: e.scalar_tensor_tensor(out=state[:, h, :], in0=state[:, h, :], scalar=DECAY[h], in1=psB[:, 384:512],
                                                                          op0=ALU.mult, op1=ALU.add), reads=["state", kB], writes=["state"])
                P.op("act", lambda e, h=h: e.activation(out=state_b[:, h, :], in_=state[:, h, :], func=AF.Identity), reads=["state"], writes=["state_b"])
                for c in range(4):
                    P.op("dve", lambda e, c=c: e.bn_stats(out=bst[:, c, :], in_=pofl[0][:, c * 128:(c + 1) * 128]), reads=["po0"], writes=["bst"])
                    P.op("dve", lambda e, c=c: e.bn_aggr(out=mv[:, c, :], in_=bst[:, c, :]), reads=["bst"], writes=["mv"])
                P.op("act", lambda e: e.activation(out=rs4[:, :], in_=mv[:, :, 1], func=AF.Sqrt, bias=ceps[:, 0:1], scale=1.0), reads=["mv", "ceps"], writes=["rs4"])
                P.op("dve", lambda e: e.reciprocal(out=rs4[:, :], in_=rs4[:, :]), reads=["rs4"], writes=["rs4"])
                P.op("dve", lambda e: e.scalar_tensor_tensor(out=nmr[:, :], in0=mv[:, :, 0], scalar=-1.0, in1=rs4[:, :], op0=ALU.mult, op1=ALU.mult),
                     reads=["mv", "rs4"], writes=["nmr"])
                for c in range(4):
                    P.op("act", lambda e, c=c: e.activation(out=on[:, c, :], in_=po[0][:, c // 2, (c % 2) * 128:(c % 2) * 128 + 128], func=AF.Identity,
                                                           scale=rs4[:, c:c + 1], bias=nmr[:, c:c + 1]),
                         reads=["po0", "nmr", "rs4"], writes=["on"])
            gate_a(O_RG + h * 128)
            if not dry:
                def tail(h=h):
                    hh = next_tr()
                    transpose4(lambda j: on[:, j, :], ["on"], hh)
                    P.op("act", lambda e, hh=hh, h=h: e.activation(out=yT[:, h, :], in_=ptr[:, hh, :], func=AF.Identity, scale=grT[:, h:h + 1]),
                         reads=["ptr%d" % hh, "grT"], writes=["yT%d" % h])
                    gate_b(yT[:, h, :], "yT%d" % h)
                pending.append(tail)

        if STAGE <= 3:
            return
        for g in range(2):
            proj_fm(O_KC + g * 128, lambda p_, k_, g=g: copy_to(p_, k_, kroll[:, g, 16:16 + TT], "kroll"))
            proj_fm(O_VC + g * 128, lambda p_, k_, g=g: copy_to(p_, k_, vroll[:, g, 16:16 + TT], "vroll"))
            proj_fm(O_KS + g * 128, lambda p_, k_, g=g: rope_to(p_, k_, ksT[:, g, tok0:tok0 + TT], "ksT"))
            proj_fm(O_KW + g * 128, lambda p_, k_, g=g: rope_to(p_, k_, kwT[:, g, (T % 2) * TT:(T % 2 + 1) * TT], "kwT"))
            for (off, Vd, vk) in ((O_VS, Vs, "Vs"), (O_VW, Vw, "Vw")):
                proj_fm(off + g * 128, lambda p_, k_: copy_to(p_, k_, vT[:, :], "vT"))
                if not dry:
                    hh = next_tr()
                    vb0 = 4 * T if vk == "Vs" else 4 * (T % 2)
                    transpose4(lambda j: vT[:, j * 128:(j + 1) * 128], ["vT"], hh)
                    P.op("act", lambda e, hh=hh, Vd=Vd, g=g, vb0=vb0: e.activation(out=Vd[:, g, vb0:vb0 + 4, 0:128], in_=ptr[:, hh, :].rearrange("p (a c) -> p a c", c=128), func=AF.Identity),
                         reads=["ptr%d" % hh], writes=[vk])
        nl0 = 1 if T == 0 else 0
        ncol = 32 - nl0
        for kv in range(2):
            roll = kroll if kv == 0 else vroll
            rkey = "kroll" if kv == 0 else "vroll"
            w1 = w_ck1 if kv == 0 else w_cv1
            for pc in range(4):
                wt, wk_ = W.next(("w1", "w_ck1" if kv == 0 else "w_cv1", pc))
                if wt is None:
                    continue
                w3 = wt[:, :].rearrange("p (i c) -> p i c", c=256)
                def f(e, w3=w3, pc=pc, roll=roll, kv=kv):
                    ins = None
                    for il in range(8):
                        i = pc * 8 + il
                        for mh in range(2):
                            rhs = roll[:, :, i:i + 497:16]
                            e.matmul(psc[0][:, mh * 64:mh * 64 + 64].rearrange("p (g n) -> p g n", n=32), lhsT=w3[:, il, mh * 128:(mh + 1) * 128], rhs=rhs,
                                     start=(i == 0 and mh == 0), stop=(i == 31))
                            ins = e.matmul(psc[1][:, mh:mh + 1], lhsT=w3[:, il, mh * 128:(mh + 1) * 128], rhs=peTb[kv][:, i:i + 1],
                                           start=(i == 0 and mh == 0), stop=(i == 31))
                    return ins
                P.op("pe", f, reads=[wk_, rkey, "peTb%d" % kv], writes=["psc0", "psc1"])
            if dry:
                continue
            P.op("dve", lambda e, kv=kv: e.tensor_copy(out=hb[:, kv, :], in_=psc[1][:, 0:2]), reads=["psc1"], writes=["hb"])
            P.op("dve", lambda e, kv=kv: e.tensor_scalar(out=hbh[:, kv, :], in0=hb[:, kv, :], scalar1=0.5, scalar2=None, op0=ALU.mult), reads=["hb"], writes=["hbh"])
            for mh in range(2):
                P.op("act", lambda e, mh=mh, kv=kv: e.activation(out=ht[:, :], in_=psc[0][:, mh * 64:mh * 64 + 64], func=AF.Tanh, bias=hbh[:, kv, mh:mh + 1], scale=0.5),
                     reads=["psc0", "hbh"], writes=["ht"])
                P.op("dve", lambda e, mh=mh, kv=kv: e.tensor_scalar(out=hx[:, :], in0=psc[0][:, mh * 64:mh * 64 + 64], scalar1=hb[:, kv, mh:mh + 1], scalar2=None, op0=ALU.add),
                     reads=["psc0", "hb"], writes=["hx"])
                P.op("dve", lambda e, mh=mh: e.scalar_tensor_tensor(out=shT[:, mh, :], in0=ht[:, :], scalar=1.0, in1=hx[:, :], op0=ALU.add, op1=ALU.mult),
                     reads=["ht", "hx"], writes=["shT"])
            if kv == 0:
                def f2(e):
                    e.matmul(pm[:, 0:64], lhsT=w2bf[0][:, 0, :], rhs=shT[:, 0, :], start=True, stop=False)
                    return e.matmul(pm[:, 0:64], lhsT=w2bf[0][:, 1, :], rhs=shT[:, 1, :], start=False, stop=True)
                P.op("pe", f2, reads=["w2bf0", "shT"], writes=["po1"])
                P.op("act", lambda e: e.activation(out=kcx[:, :], in_=pm[:, 0:64], func=AF.Identity, scale=0.5), reads=["po1"], writes=["kcx"])
                P.op("pe", lambda e: e.matmul(pm[:, 64:128], lhsT=swap_b, rhs=kcx[:, :], start=True, stop=True), reads=["kcx", "cbf"], writes=["po1"])
                cos_c = bc(cosT[:, 15:TT:16], 2)
                sin_c = bc(sinT[:, 15:TT:16], 2)
                P.op("dve", lambda e, cos_c=cos_c: e.tensor_tensor(out=hx[:, :].rearrange("p (g n) -> p g n", n=32), in0=kcx[:, :].rearrange("p (g n) -> p g n", n=32), in1=cos_c, op=ALU.mult),
                     reads=["kcx", "cosT"], writes=["hx"])
                P.op("dve", lambda e, sin_c=sin_c: e.tensor_tensor(out=ht[:, :].rearrange("p (g n) -> p g n", n=32), in0=pm[:, 64:128].rearrange("p (g n) -> p g n", n=32), in1=sin_c, op=ALU.mult),
                     reads=["po1", "sinT"], writes=["ht"])
                P.op("dve", lambda e: e.tensor_tensor(out=KcT[:, :, 32 * T:32 * T + 32], in0=hx[:, :].rearrange("p (g n) -> p g n", n=32), in1=ht[:, :].rearrange("p (g n) -> p g n", n=32), op=ALU.add),
                     reads=["hx", "ht"], writes=["KcT"])
            else:
                for g in range(2):
                    def f3(e, g=g):
                        e.matmul(pm[0:32, 128 + g * 128:256 + g * 128], lhsT=shT[:, 0, g * 32:(g + 1) * 32], rhs=w2bf[1][:, 0, :], start=True, stop=False)
                        return e.matmul(pm[0:32, 128 + g * 128:256 + g * 128], lhsT=shT[:, 1, g * 32:(g + 1) * 32], rhs=w2bf[1][:, 1, :], start=False, stop=True)
                    P.op("pe", f3, reads=["w2bf1", "shT"], writes=["po1"])
                    P.op("act", lambda e, g=g: e.activation(out=vct[:, g, 0:128], in_=pm[0:32, 128 + g * 128:256 + g * 128], func=AF.Identity, scale=0.5), reads=["po1"], writes=["vct"])
                P.dma("sp", lambda e, s: e.dma_start(out=Vc[32 * T:32 * T + 32, :, :], in_=vct[:, :, :]).then_inc(s, 16), reads=["vct"], writes=["Vc"], semkey="vc")
        wt, wk_ = W.next(("cols", "w_in", O_BG, 24, 16))
        if wt is not None:
            w3 = wt[:, 0:16 * 24].rearrange("p (k c) -> p k c", c=24)
            for st in range(4):
                def f(e, st=st, w3=w3):
                    ins = None
                    for kc in range(16):
                        ins = e.matmul(pm[:, 384 + st * 24:384 + (st + 1) * 24], lhsT=hT[:, kc, st * 128:(st + 1) * 128], rhs=w3[:, kc, :], start=(kc == 0), stop=(kc == 15))
                    return ins
                P.op("pe", f, reads=[wk_, "hT"], writes=["po1"])
            P.op("act", lambda e: e.activation(out=gates[:, :, :], in_=pm[:, 384:480].rearrange("p (a c) -> p a c", c=24), func=AF.Tanh, scale=0.5), reads=["po1"], writes=["gates"])
            P.op("dve", lambda e: e.tensor_scalar(out=gates[:, :, :], in0=gates[:, :, :], scalar1=0.5, scalar2=0.5, op0=ALU.mult, op1=ALU.add), reads=["gates"], writes=["gates"])

        if STAGE <= 4:
            return
        for g in range(2):
            for r in range(4):
                hq = 4 * g + r
                proj_fm(O_NQ + hq * 128, lambda p_, k_, r=r: rope_to(p_, k_, qT[r][:, :], "qT%d" % r))
            flush_pending()
            for r in range(4):
                if dry or T < 2:
                    continue
                for st in range(4):
                    ps_t, ps_k = next_psc()
                    P.op("pe", lambda e, ps_t=ps_t, st=st, r=r, g=g: e.matmul(ps_t[:, 0:128], lhsT=qT[r][:, st * 128:(st + 1) * 128], rhs=KcT[:, g, :], start=True, stop=True),
                         reads=["qT%d" % r, "KcT"], writes=[ps_k])
                    nb_ap = cbf[:, CB_NB + ((T - 2) * 4 + st) * 128:CB_NB + ((T - 2) * 4 + st + 1) * 128]
                    P.op("dve", lambda e, ps_t=ps_t, nb_ap=nb_ap: e.scalar_tensor_tensor(out=stmp[:, :], in0=ps_t[:, 0:128], scalar=SCALE, in1=nb_ap, op0=ALU.mult, op1=ALU.add),
                         reads=[ps_k, "cbf"], writes=["stmp"])
                    P.op("act", lambda e: e.activation(out=etmp[:, :], in_=stmp[:, :], func=AF.Exp, accum_out=rsum[:, 0:1]), reads=["stmp"], writes=["etmp", "rsum"])
                    P.op("dve", lambda e: e.reciprocal(out=rinv[:, :], in_=rsum[:, :]), reads=["rsum"], writes=["rinv"])
                    if r == 0:
                        P.op("dve", lambda e, st=st: e.tensor_scalar(out=ppad[:, st, 0:128], in0=etmp[:, :], scalar1=rinv[:, 0:1], scalar2=None, op0=ALU.mult),
                             reads=["etmp", "rinv"], writes=["ppad"])
                    else:
                        P.op("dve", lambda e, st=st: e.scalar_tensor_tensor(out=ppad[:, st, 0:128], in0=etmp[:, :], scalar=rinv[:, 0:1], in1=ppad[:, st, 0:128], op0=ALU.mult, op1=ALU.add),
                             reads=["etmp", "rinv", "ppad"], writes=["ppad"])
            if not dry and T >= 2:
                for st in range(4):
                    def v(k0, st=st):
                        return ppad[:, st, k0:k0 + 128:4]
                    P.op("dve", lambda e, v=v: e.tensor_tensor(out=imp[:, :], in0=v(0), in1=v(4), op=ALU.add), reads=["ppad"], writes=["imp"])
                    P.op("dve", lambda e, v=v: e.scalar_tensor_tensor(out=imp[:, :], in0=imp[:, :], scalar=0.5, in1=v(1), op0=ALU.mult, op1=ALU.add), reads=["imp", "ppad"], writes=["imp"])
                    P.op("dve", lambda e, v=v: e.tensor_tensor(out=imp[:, :], in0=imp[:, :], in1=v(2), op=ALU.add), reads=["imp", "ppad"], writes=["imp"])
                    P.op("dve", lambda e, v=v: e.tensor_tensor(out=imp[:, :], in0=imp[:, :], in1=v(3), op=ALU.add), reads=["imp", "ppad"], writes=["imp"])
                    bo = CF_BON + ((T - 2) * 4 + st) * 32
                    P.op("dve", lambda e, bo=bo: e.tensor_tensor(out=imp[:, :], in0=imp[:, :], in1=cf[:, bo:bo + 32], op=ALU.add), reads=["imp", "cf"], writes=["imp"])
                    P.op("dve", lambda e: e.max(out=m8[:, 0:8], in_=imp[:, :]), reads=["imp"], writes=["m8"])
                    P.op("dve", lambda e: e.match_replace(out=wk[:, :], in_to_replace=m8[:, 0:8], in_values=imp[:, :], imm_value=-3.0e38), reads=["imp", "m8"], writes=["wk"])
                    P.op("dve", lambda e: e.max(out=m8[:, 8:16], in_=wk[:, :]), reads=["wk"], writes=["m8"])
                    P.op("dve", lambda e: e.tensor_scalar(out=wk[:, :], in0=imp[:, :], scalar1=m8[:, 15:16], scalar2=None, op0=ALU.is_ge), reads=["imp", "m8"], writes=["wk"])
                    P.op("dve", lambda e: e.tensor_scalar(out=negm[:, :], in0=wk[:, :], scalar1=-NEGM, scalar2=NEGM, op0=ALU.mult, op1=ALU.add), reads=["wk"], writes=["negm"])
                    hh = next_tr()
                    P.op("pe", lambda e, hh=hh: e.transpose(ptr[0:32, hh, 0:128], negm[:, :], ident_b), reads=["negm", "cbf"], writes=["ptr%d" % hh])
                    P.op("act", lambda e, hh=hh, st=st, g=g: e.activation(out=negmT[:, g, st * 128:(st + 1) * 128], in_=ptr[0:32, hh, 0:128], func=AF.Identity),
                         reads=["ptr%d" % hh], writes=["negmT"])
            for r in range(4):
                hq = 4 * g + r
                if not dry:
                    nsa_head(b, T, g, r, hq)
                gate_a(O_NG + hq * 128)
                if not dry:
                    def tail2(hq=hq):
                        hh = next_tr()
                        transpose4(lambda j: accb[:, j, :], ["accb"], hh)
                        P.op("act", lambda e, hh=hh, hq=hq: e.activation(out=yT[:, 8 + hq, :], in_=ptr[:, hh, :], func=AF.Identity, scale=0.5),
                             reads=["ptr%d" % hh], writes=["yT%d" % (8 + hq)])
                        gate_b(yT[:, 8 + hq, :], "yT%d" % (8 + hq))
                    pending.append(tail2)

        flush_pending()
        if debug and b == 0 and T == dbgT:
            if dry:
                return
            P.op("dve", lambda e: e.tensor_copy(out=xt[:, 2, :].rearrange("p (a c) -> p a c", c=TT)[:, 0:4, :], in_=yT[:, 0:4, :]), reads=["yT%d" % i for i in range(16)], writes=["xt2"])
            P.op("dve", lambda e: e.tensor_copy(out=xt[:, 1, :].rearrange("p (a c) -> p a c", c=TT)[:, 0:4, :], in_=yT[:, 8:12, :]), reads=["yT%d" % i for i in range(16)], writes=["xt1"])
            P.op("dve", lambda e: e.tensor_copy(out=xt[0:32, 0, 0:1024], in_=negmT[:, :, :].rearrange("p g t -> p (g t)")), reads=["negmT"], writes=["xt0"])
            P.op("dve", lambda e: e.tensor_copy(out=xt[:, 0, 1024:1280], in_=KcT[:, :, :].rearrange("p g t -> p (g t)")), reads=["KcT"], writes=["xt0"])
            P.dma("sp", lambda e, s: e.dma_start(out=dbg["d_m2"][:, 0:2048], in_=xt[:, 0, :]).then_inc(s, 16), reads=["xt0"], semkey="dbg")
            P.dma("sp", lambda e, s: e.dma_start(out=dbg["d_hT"][:, 0:4 * TT], in_=xt[:, 3, :]).then_inc(s, 16), reads=["xt3"], semkey="dbg")
            P.dma("sp", lambda e, s: e.dma_start(out=dbg["d_yT"][:, 0:4 * TT], in_=xt[:, 2, :]).then_inc(s, 16), reads=["xt2"], semkey="dbg")
            P.dma("sp", lambda e, s: e.dma_start(out=dbg["d_yT"][:, 4 * TT:8 * TT], in_=xt[:, 1, :]).then_inc(s, 16), reads=["xt1"], semkey="dbg")
            return
        ykeys = ["yT%d" % i for i in range(16)]
        for cc in range(16):
            if cc % 2 == 0:
                bA, kA, bB, kB, bC, kC, bD, kD = pp[0], "pp0", pp[1], "pp1", psc[0], "psc0", psc[1], "psc1"
            else:
                bA, kA, bB, kB, bC, kC, bD, kD = pofl[0], "po0", pofl[1], "po1", ptrf[0], "ptr0", ptrf[1], "ptr1"
            wt2, wk2 = W.next(("two", cc * 128))
            if wt2 is not None:
                w2v = wt2[:, :].rearrange("p (u k c) -> p u k c", u=2, c=128)
                def fA(e, w2v=w2v, bA=bA, bB=bB):
                    ins = None
                    for u, bt in ((0, bA), (1, bB)):
                        for kc in range(8):
                            ins = e.matmul(bt[:, :], lhsT=w2v[:, u, kc, :], rhs=yT[:, u * 8 + kc, :], start=(kc == 0), stop=(kc == 7))
                    return ins
                P.op("pe", fA, reads=[wk2] + ykeys, writes=[kA, kB])
            wta, wka = W.next(("cols", "w_in", O_MA + cc * 128, 128, 16))
            if wta is not None:
                wa3 = wta[:, :].rearrange("p (k c) -> p k c", c=128)
                def fC(e, wa3=wa3, bC=bC):
                    ins = None
                    for kc in range(16):
                        ins = e.matmul(bC[:, :], lhsT=wa3[:, kc, :], rhs=hT[:, kc, :], start=(kc == 0), stop=(kc == 15))
                    return ins
                P.op("pe", fC, reads=[wka, "hT"], writes=[kC])
            wtb, wkb = W.next(("cols", "w_in", O_MB + cc * 128, 128, 16))
            if wtb is None:
                continue
            wb3 = wtb[:, :].rearrange("p (k c) -> p k c", c=128)
            def fD(e, wb3=wb3, bD=bD):
                ins = None
                for kc in range(16):
                    ins = e.matmul(bD[:, :], lhsT=wb3[:, kc, :], rhs=hT[:, kc, :], start=(kc == 0), stop=(kc == 15))
                return ins
            P.op("pe", fD, reads=[wkb, "hT"], writes=[kD])
            P.op("act", lambda e, bC=bC: e.activation(out=ta[:, :], in_=bC[:, :], func=AF.Tanh, scale=0.5), reads=[kC], writes=["r1"])
            P.op("dve", lambda e, bA=bA: e.scalar_tensor_tensor(out=ta[:, :], in0=ta[:, :], scalar=1.0, in1=bA[:, :], op0=ALU.add, op1=ALU.mult), reads=["r1", kA], writes=["r1"])
            P.op("act", lambda e, bD=bD: e.activation(out=tb2[:, :], in_=bD[:, :], func=AF.Tanh, scale=0.5), reads=[kD], writes=["r2"])
            P.op("dve", lambda e, bB=bB: e.scalar_tensor_tensor(out=tb2[:, :], in0=tb2[:, :], scalar=1.0, in1=bB[:, :], op0=ALU.add, op1=ALU.mult), reads=["r2", kB], writes=["r2"])
            P.op("pool", lambda e, cc=cc: e.tensor_tensor(out=m2[:, cc, :], in0=ta[:, :], in1=tb2[:, :], op=ALU.add), reads=["r1", "r2"], writes=["xn%d" % (cc // 4)])
        if dry:
            for ct in range(4):
                for pc in range(4):
                    W.next(("wout", ct, pc))
            return
        P.dma("sp", lambda e, s: e.dma_start(out=gfin_bc[:, :], in_=g_fin[0, :].partition_broadcast(128)).then_inc(s, 16), writes=["hT"], semkey="gf")
        banks = [(pp[0], "pp0"), (pp[1], "pp1"), (psc[0], "psc0"), (psc[1], "psc1")]
        for ct in range(4):
            for pc in range(4):
                wt, wk_ = W.next(("wout", ct, pc))
                w3 = wt[:, :].rearrange("p (k c) -> p k c", c=512)
                for st in range(4):
                    bt, bk = banks[st]
                    def f(e, w3=w3, bt=bt, st=st, pc=pc):
                        ins = None
                        for kl in range(4):
                            kc = pc * 4 + kl
                            ins = e.matmul(bt[:, :], lhsT=m2[:, kc, st * 128:(st + 1) * 128], rhs=w3[:, kl, :], start=(kc == 0), stop=(kc == 15))
                        return ins
                    P.op("pe", f, reads=[wk_, "xn0", "xn1", "xn2", "xn3"], writes=[bk])
            for st in range(4):
                bt, bk = banks[st]
                P.op("dve", lambda e, bt=bt, ct=ct: e.tensor_tensor(out=r1[:, :], in0=bt[:, :], in1=gate_bc1[:, ct * 512:(ct + 1) * 512], op=ALU.mult),
                     reads=[bk, "gate_bc"], writes=["r1"])
                P.op("pool", lambda e, st=st, ct=ct: e.tensor_tensor(out=xt[:, st, ct * 512:(ct + 1) * 512], in0=xt[:, st, ct * 512:(ct + 1) * 512], in1=r1[:, :], op=ALU.add),
                     reads=["r1", "xt%d" % st], writes=["xt%d" % st])
        for st in range(4):
            P.op("act", lambda e, st=st: e.activation(out=xn[:, st, :], in_=xt[:, st, :], func=AF.Square, accum_out=ss4[:, st:st + 1]),
                 reads=["xt%d" % st], writes=["xn%d" % st, "ss4"])
        P.op("act", lambda e: e.activation(out=rs4[:, :], in_=ss4[:, :], func=AF.Sqrt, bias=ceps[:, 0:1], scale=1.0 / D), reads=["ss4", "ceps"], writes=["rs4"])
        P.op("dve", lambda e: e.reciprocal(out=rs4[:, :], in_=rs4[:, :]), reads=["rs4"], writes=["rs4"])
        for st in range(4):
            P.op("dve", lambda e, st=st: e.scalar_tensor_tensor(out=xt[:, st, :], in0=xt[:, st, :], scalar=rs4[:, st:st + 1], in1=gfin_bc[:, :], op0=ALU.mult, op1=ALU.mult),
                 reads=["xt%d" % st, "rs4", "hT"], writes=["xt%d" % st])
            P.dma("sp", lambda e, s, st=st: e.dma_start(out=out_d[b, tok0 + st * 128:tok0 + (st + 1) * 128, :], in_=xt[:, st, :]).then_inc(s, 16),
                  reads=["xt%d" % st], semkey="o%d" % st)
            if T + 1 < NT and not debug and STAGE >= 99:
                nt0 = tok0 + TT
                P.dma("sp", lambda e, s, st=st, nt0=nt0: e.dma_start(out=xt[:, st, :], in_=x_d[b, nt0 + st * 128:nt0 + (st + 1) * 128, :]).then_inc(s, 16),
                      writes=["xt%d" % st], semkey="x%d" % st)
                xpre[0] = True

    pt_i = [0]
    xpre = [False]

    def nsa_head(b, T, g, r, hq):
        qh = qT[r]
        qk = "qT%d" % r
        first = [True]

        def combine(br, bi):
            pk, dk = "po%d" % bi, "pp%d" % bi
            pof = pofl[bi]
            P.op("dve", lambda e: e.tensor_scalar(out=den4[:, :], in0=pp[bi][:, 0:4], scalar1=1e-30, scalar2=None, op0=ALU.max), reads=[dk], writes=["den4"])
            P.op("dve", lambda e: e.reciprocal(out=den4[:, :], in_=den4[:, :]), reads=["den4"], writes=["den4"])
            P.op("dve", lambda e: e.tensor_tensor(out=den4[:, :], in0=den4[:, :], in1=gates[:, :, hq * 3 + br], op=ALU.mult), reads=["den4", "gates"], writes=["den4"])
            for qi in range(4):
                o_ap = pof[:, qi * 128:(qi + 1) * 128]
                if br == 0:
                    P.op("act", lambda e, o_ap=o_ap, qi=qi: e.activation(out=acc[:, qi, :], in_=o_ap, func=AF.Identity, scale=den4[:, qi:qi + 1]), reads=[pk, "den4"], writes=["acc"])
                elif br == 1:
                    P.op("dve", lambda e, o_ap=o_ap, qi=qi: e.scalar_tensor_tensor(out=acc[:, qi, :], in0=o_ap, scalar=den4[:, qi:qi + 1], in1=acc[:, qi, :], op0=ALU.mult, op1=ALU.add),
                         reads=[pk, "den4", "acc"], writes=["acc"])
                else:
                    P.op("dve", lambda e, o_ap=o_ap, qi=qi: e.scalar_tensor_tensor(out=accb[:, qi, :], in0=o_ap, scalar=den4[:, qi:qi + 1], in1=acc[:, qi, :], op0=ALU.mult, op1=ALU.add),
                         reads=[pk, "den4", "acc"], writes=["accb"])

        pv_q = []
        npush = [0]

        def push_pv(fn):
            while pv_q:
                pv_q.pop(0)()
            pv_q.append(fn)
            npush[0] += 1
            if npush[0] == 3:
                flush_pending()

        nk = 32 * (T + 1)
        ps_t, ps_k = next_psc()
        def fsc(e, ps_t=ps_t):
            e.matmul(ps_t[0:nk, :], lhsT=KcT[:, g, 0:nk], rhs=qh[:, :], start=True, stop=False)
            return e.matmul(ps_t[0:nk, :], lhsT=cbf[0:nk, CB_ID:CB_ID + nk], rhs=cbf[0:nk, CB_VT + T * TT:CB_VT + (T + 1) * TT], start=False, stop=True)
        P.op("pe", fsc, reads=["KcT", qk, "cbf"], writes=[ps_k])
        pi = pt_i[0] % 2
        pt_i[0] += 1
        ptile, pkey = PT[pi], "PT%d" % pi
        P.op("act", lambda e, ps_t=ps_t, ptile=ptile: e.activation(out=ptile[0:nk, :], in_=ps_t[0:nk, :], func=AF.Exp, scale=SCALE), reads=[ps_k], writes=[pkey])
        bi0 = br_i[0] % 2
        br_i[0] += 1
        def fpv(e, ptile=ptile, bi0=bi0):
            ins = None
            for qi in range(4):
                e.matmul(pofl[bi0][:, qi * 128:(qi + 1) * 128], lhsT=ptile[0:nk, qi * 128:(qi + 1) * 128], rhs=Vc[0:nk, g, 0:128], start=(qi == 0), stop=True)
                ins = e.matmul(pp[bi0][:, qi:qi + 1], lhsT=ptile[0:nk, qi * 128:(qi + 1) * 128], rhs=ones_b[0:nk, 0:1], start=(qi == 0), stop=True)
            return ins
        def pv0(fpv=fpv, pkey=pkey, bi0=bi0):
            P.op("pe", fpv, reads=[pkey, "Vc", "ones_b"], writes=["po%d" % bi0, "pp%d" % bi0])
            combine(0, bi0)
        push_pv(pv0)

        for br, (kTt, kkey, Vt, vkey) in ((1, (ksT, "ksT", Vs, "Vs")), (2, (kwT, "kwT", Vw, "Vw"))):
            kts = list(range(0, 4 * T + 4)) if br == 1 else list(range(max(0, 4 * T - 4), 4 * T + 4))
            bi = br_i[0] % 2
            br_i[0] += 1
            for kt in kts:
                i = kt - 4 * T
                if br == 1:
                    qlo, qhi = max(i, 0), 3
                else:
                    qlo, qhi = max(i, 0), min(i + 4, 3)
                c0, c1 = qlo * 128, (qhi + 1) * 128
                ps_t, ps_k = next_psc()
                use_sel = (br == 1 and T >= 2)
                tri_q = i if i >= 0 else None
                anti_q = (i + 4) if (br == 2 and i < 0 and i + 4 <= 3) else None
                def fs(e, ps_t=ps_t, kt=kt, c0=c0, c1=c1, use_sel=use_sel, tri_q=tri_q, anti_q=anti_q, kTt=kTt, br_=br):
                    more = use_sel or (tri_q is not None) or (anti_q is not None)
                    kcol = kt * 128 if br_ == 1 else ((kt // 4) % 2) * TT + (kt % 4) * 128
                    ins = e.matmul(ps_t[:, c0:c1], lhsT=kTt[:, g, kcol:kcol + 128], rhs=qh[:, c0:c1], start=True, stop=not more)
                    if use_sel:
                        m2_ = (tri_q is not None) or (anti_q is not None)
                        ins = e.matmul(ps_t[:, c0:c1], lhsT=cbf[0:32, CB_E + kt * 128:CB_E + (kt + 1) * 128], rhs=negmT[0:32, g, c0:c1], start=False, stop=not m2_)
                    if tri_q is not None:
                        ins = e.matmul(ps_t[:, tri_q * 128:(tri_q + 1) * 128], lhsT=ident_b, rhs=tri_b, start=False, stop=(anti_q is None))
                    if anti_q is not None:
                        ins = e.matmul(ps_t[:, anti_q * 128:(anti_q + 1) * 128], lhsT=ident_b, rhs=anti_b, start=False, stop=True)
                    return ins
                P.op("pe", fs, reads=[kkey, qk, "cbf", "negmT"], writes=[ps_k])
                pi = pt_i[0] % 2
                pt_i[0] += 1
                ptile, pkey = PT[pi], "PT%d" % pi
                P.op("act", lambda e, ps_t=ps_t, ptile=ptile, c0=c0, c1=c1: e.activation(out=ptile[:, c0:c1], in_=ps_t[:, c0:c1], func=AF.Exp, scale=SCALE),
                     reads=[ps_k], writes=[pkey])
                def fpv2(e, ptile=ptile, kt=kt, qlo=qlo, qhi=qhi, Vt=Vt, br=br, bi=bi, kt0=kts[0]):
                    ins = None
                    for qi in range(qlo, qhi + 1):
                        klast = 4 * T + qi
                        vslot = kt if br == 1 else ((kt // 4) % 2) * 4 + (kt % 4)
                        st_ = (kt == kt0 and qi == 0)
                        e.matmul(pofl[bi][:, qi * 128:(qi + 1) * 128], lhsT=ptile[:, qi * 128:(qi + 1) * 128], rhs=Vt[:, g, vslot, 0:128],
                                 start=st_, stop=(kt == klast))
                        ins = e.matmul(pp[bi][:, qi:qi + 1], lhsT=ptile[:, qi * 128:(qi + 1) * 128], rhs=ones_b[:, 0:1], start=st_, stop=(kt == klast))
                    return ins
                def pvk(fpv2=fpv2, pkey=pkey, vkey=vkey, last=(kt == kts[-1]), br=br, bi=bi):
                    P.op("pe", fpv2, reads=[pkey, vkey, "ones_b"], writes=["po%d" % bi, "pp%d" % bi])
                    if last:
                        combine(br, bi)
                push_pv(pvk)
        while pv_q:
            pv_q.pop(0)()

    import os as _os
    STAGE = float(_os.environ.get('KSTAGE', '99'))
    dbgT = int(_os.environ.get('KDBGT', '2'))

    wscr_box = [None]

    def whole():
        setup()
        gate_rows()
        if not W.collect:
            W.wscr = wscr_box[0]
        if STAGE <= 0:
            return
        for b in range(nseq):
            seq_start(b)
            if STAGE <= 1:
                return
            for T in range(NT):
                body(b, T)
                if STAGE < 99:
                    return
                if debug and b == 0 and T == dbgT:
                    return

    P.dry = True
    W.collect = True
    whole()
    W.finish_collect()
    wscr_box[0] = nc.dram_tensor("wscr", [len(W.uniq), 128, 2048], BF16, kind="Internal").ap()
    P.dry = False
    W.collect = False
    W.pos = 0
    pp_i[0] = psc_i[0] = tr_i[0] = pt_i[0] = 0
    whole()
    assert W.pos == len(W.specs) and W.posB == len(W.specsB), (W.pos, len(W.specs), W.posB, len(W.specsB))
    P.emit()
    nc._kstats = dict(n_ops=len(P.ops), sbuf=sbuf_used, sems=P.stats)
    return nc


def _in_maps(inputs, nseq, ncores):
    cbf, cf, _ = _consts()
    f = lambda a: np.ascontiguousarray(np.asarray(a, dtype=np.float32))
    x = f(inputs["x"])
    c = f(inputs["c"])
    pos = np.ascontiguousarray(np.asarray(inputs["positions"], dtype=np.int32))
    b_ada = f(inputs["b_ada"])[0]
    shared = {
        "w_ada": f(inputs["w_ada"])[0],
        "b_adaT": np.ascontiguousarray(b_ada.reshape(48, 128).T),
        "b_gate": np.ascontiguousarray(b_ada[None, 4096:6144]),
        "g_normT": np.ascontiguousarray(f(inputs["g_norm"])[0].reshape(16, 128).T),
        "w_in": f(inputs["w_in"])[0],
        "g_retT": np.ascontiguousarray(f(inputs["g_ret"])[0].reshape(8, 128).T),
        "w_ck1": f(inputs["w_ck1"])[0],
        "w_ck2": f(inputs["w_ck2"])[0],
        "pe_ckT": np.ascontiguousarray(f(inputs["pe_ck"])[0].T),
        "w_cv1": f(inputs["w_cv1"])[0],
        "w_cv2": f(inputs["w_cv2"])[0],
        "pe_cvT": np.ascontiguousarray(f(inputs["pe_cv"])[0].T),
        "w_up_ret": f(inputs["w_up_ret"])[0],
        "w_up_nsa": f(inputs["w_up_nsa"])[0],
        "w_out": f(inputs["w_out"])[0],
        "g_final": np.ascontiguousarray(f(inputs["g_final"])[None, :]),
        "cbf": cbf,
        "cf": cf,
    }
    maps = []
    for i in range(ncores):
        sl = slice(i * nseq, (i + 1) * nseq)
        cT = np.concatenate([c[i * nseq + b].reshape(16, 128).T for b in range(nseq)], axis=1)
        m = dict(shared)
        m["x"] = np.ascontiguousarray(x[sl])
        m["cT"] = np.ascontiguousarray(cT)
        m["pos"] = np.ascontiguousarray(pos[sl])
        maps.append(m)
    return maps


def kernel(**inputs):
    nc = build_nc(SEQ_PER_CORE)
    maps = _in_maps(inputs, SEQ_PER_CORE, NCORES)
    res = run_bass_kernel_spmd(nc, maps, core_ids=list(range(NCORES)))
    out = np.concatenate([np.asarray(r["out"]) for r in res.results], axis=0)
    return out.astype(np.float32)
```

```python
import contextlib
import math
import numpy as np
import ml_dtypes
import concourse.bass as bass
import concourse.mybir as mybir
from concourse.bass_utils import run_bass_kernel_spmd

F32 = mybir.dt.float32
BF16 = mybir.dt.bfloat16
I32 = mybir.dt.int32
ALU = mybir.AluOpType
AF = mybir.ActivationFunctionType

D = 2048
S = 2048
NB = 16
NCORES = 8
SEQ_PER_CORE = 2
TT = 512
NT = S // TT
INW = 11800
O_RQ, O_RK, O_RV, O_RG, O_NQ = 0, 1024, 2048, 3072, 4096
O_KC, O_VC, O_KS, O_VS, O_KW, O_VW = 5120, 5376, 5632, 5888, 6144, 6400
O_NG, O_BG, O_MA, O_MB = 6656, 7680, 7704, 9752
EPS = 1e-6
SCALE = 128.0 ** -0.5
NEGM = -30000.0
ENGS = ("pe", "act", "dve", "pool", "sp")
PSUM_KEYS = {"pp0", "pp1", "psc0", "psc1", "po0", "po1", "ptr0", "ptr1"}


class _Op:
    __slots__ = ("eng", "fn", "deps", "sig", "semkey", "inc", "tick")


class Prog:
    def __init__(self, nc):
        self.nc = nc
        self.ops = []
        self.last_w = {}
        self.readers = {}
        self.dry = False

    def _add(self, eng, fn, reads, writes, semkey, inc, sig):
        if self.dry:
            return
        writes = list(writes) + [k for k in reads if k in PSUM_KEYS]
        deps = set()
        lw = self.last_w
        for k in reads:
            w = lw.get(k)
            if w is not None:
                deps.add(w)
        for k in writes:
            w = lw.get(k)
            if w is not None:
                deps.add(w)
            for r in self.readers.get(k, ()):
                deps.add(r)
        o = _Op()
        o.eng, o.fn, o.deps, o.sig, o.semkey, o.inc = eng, fn, deps, sig, semkey, inc
        idx = len(self.ops)
        self.ops.append(o)
        for k in reads:
            self.readers.setdefault(k, []).append(idx)
        for k in writes:
            lw[k] = idx
            self.readers[k] = []

    def op(self, eng, fn, reads=(), writes=()):
        self._add(eng, fn, reads, writes, eng, 1, False)

    def dma(self, eng, fn, reads=(), writes=(), semkey=None, n=1):
        self._add(eng, fn, reads, writes, semkey, 16 * n, True)

    def emit(self):
        nc = self.nc
        ops = self.ops
        for o in ops:
            for d in o.deps:
                ops[d].sig = True
        counts = {}
        for o in ops:
            if o.sig:
                counts[o.semkey] = counts.get(o.semkey, 0) + o.inc
                o.tick = counts[o.semkey]
            else:
                o.tick = None
        semkeys = list(counts.keys())
        for e in ENGS:
            if e not in semkeys:
                semkeys.append(e)
        self.stats = dict(counts)
        with contextlib.ExitStack() as st:
            sems = {k: st.enter_context(nc.semaphore("s_" + str(k))) for k in semkeys}
            block = st.enter_context(nc.Block())
            per_eng = {e: [o for o in ops if o.eng == e] for e in ENGS}

            def run(engine, ename):
                waited = {}
                for o in per_eng[ename]:
                    need = {}
                    for d in o.deps:
                        dd = ops[d]
                        if dd.tick > need.get(dd.semkey, 0):
                            need[dd.semkey] = dd.tick
                    for k, v in need.items():
                        if v > waited.get(k, 0):
                            engine.wait_ge(sems[k], v)
                            waited[k] = v
                    if o.semkey == ename:
                        ins = o.fn(engine)
                        if o.sig:
                            ins.then_inc(sems[ename], 1)
                    else:
                        o.fn(engine, sems[o.semkey])
                if ename == "sp":
                    for k, v in counts.items():
                        if v > waited.get(k, 0):
                            engine.wait_ge(sems[k], v)

            block.tensor(lambda e: run(e, "pe"))
            block.scalar(lambda e: run(e, "act"))
            block.vector(lambda e: run(e, "dve"))
            block.gpsimd(lambda e: run(e, "pool"))
            block.sync(lambda e: run(e, "sp"))


def _consts():
    bf = ml_dtypes.bfloat16
    H = 8
    lg = np.log1p(-np.exp2(-5.0 - np.arange(H, dtype=np.float64)))
    idx = np.arange(128, dtype=np.float64)
    cb = np.zeros((128, 0), np.float32)
    parts = {}
    ident = np.eye(128, dtype=np.float32)
    swp = np.zeros((128, 128), np.float32)
    for d in range(128):
        swp[(d + 64) % 128, d] = 1.0
    k = idx[:, None]
    q = idx[None, :]
    tri = np.where(k <= q, 0.0, NEGM).astype(np.float32)
    anti = np.where(k > q, 0.0, NEGM).astype(np.float32)
    dm = np.zeros((128, H, 128), np.float32)
    xi = np.zeros((128, H, 128), np.float32)
    for h in range(H):
        diff = idx[None, :] - idx[:, None]
        dm[:, h, :] = np.where(diff >= 0, np.exp(lg[h] * np.maximum(diff, 0.0)), 0.0) * SCALE
        xi[:, h, :] = np.exp(lg[h] * (idx + 1))[None, :]
    E = np.zeros((128, 16, 128), np.float32)
    for kt in range(16):
        for kk in range(128):
            E[2 * kt + kk // 64, kt, kk] = 1.0
    t = np.arange(S)[None, :]
    n1 = np.arange(128)[:, None]
    validT = np.where((n1 >= 1) & (16 * n1 + 15 <= t), 0.0, NEGM).astype(np.float32)
    negb = np.zeros((128, 2, 4, 128), np.float32)
    bonus = np.zeros((128, 2, 4, 32), np.float32)
    for tm in range(2):
        for st in range(4):
            tt = 1024 + tm * 512 + st * 128 + np.arange(128)
            nn = np.arange(128)[None, :]
            v = (nn >= 1) & (16 * nn + 15 <= tt[:, None])
            negb[:, tm, st, :] = np.where(v, 0.0, NEGM)
            tb = (tt // 64)[:, None]
            j = np.arange(32)[None, :]
            forced = (j == 0) | (j == tb) | (j == tb - 1)
            bonus[:, tm, st, :] = np.where(j <= tb, np.where(forced, 1.0e4, 0.0), -1.0e30)
    cbf = np.concatenate([ident, swp, tri, anti, dm.reshape(128, -1), xi.reshape(128, -1),
                          E.reshape(128, -1), validT, negb.reshape(128, -1)], axis=1).astype(bf)
    inv = np.exp(np.arange(0, 128, 2, dtype=np.float32) * np.float32(-math.log(10000.0) / 128)).astype(np.float32)
    inv = np.concatenate([inv, inv])
    sgn = np.concatenate([-np.ones(64), np.ones(64)]).astype(np.float32)
    zeta = np.zeros((128, H), np.float32)
    for h in range(H):
        zeta[:, h] = np.exp(lg[h] * (127 - idx)) * SCALE
    oh = np.zeros((128, 256), np.float32)
    oh[0, 0:128] = 1.0
    oh[1, 128:256] = 1.0
    cf = np.concatenate([inv[:, None], sgn[:, None], zeta, bonus.reshape(128, -1), oh], axis=1).astype(np.float32)
    decay = [float(np.exp(lg[h] * 128)) for h in range(H)]
    return cbf, cf, decay


CB_ID, CB_SW, CB_TRI, CB_ANTI, CB_DM, CB_XI, CB_E, CB_VT, CB_NB = 0, 128, 256, 384, 512, 1536, 2560, 4608, 6656
CB_N = 6656 + 1024
CF_INV, CF_SGN, CF_ZETA, CF_BON, CF_OH = 0, 1, 2, 10, 266
CF_N = 266 + 256


def build_nc(nseq=SEQ_PER_CORE, debug=False):
    nc = bass.Bass("TRN2", target_bir_lowering=False)
    _, _, DECAY = _consts()

    def din(name, shape, dt=F32):
        return nc.dram_tensor(name, list(shape), dt, kind="ExternalInput").ap()

    x_d = din("x", [nseq, S, D])
    cT_d = din("cT", [128, nseq * 16])
    pos_d = din("pos", [nseq, S], I32)
    w_ada = din("w_ada", [D, 3 * D])
    b_adaT = din("b_adaT", [128, 48])
    b_gate = din("b_gate", [1, D])
    g_normT = din("g_normT", [128, 16])
    w_in = din("w_in", [D, INW])
    g_retT = din("g_retT", [128, 8])
    w_ck1 = din("w_ck1", [4096, 256])
    w_ck2 = din("w_ck2", [256, 128])
    pe_ckT = din("pe_ckT", [128, 32])
    w_cv1 = din("w_cv1", [4096, 256])
    w_cv2 = din("w_cv2", [256, 128])
    pe_cvT = din("pe_cvT", [128, 32])
    w_ur = din("w_up_ret", [1024, D])
    w_un = din("w_up_nsa", [1024, D])
    w_out = din("w_out", [D, D])
    g_fin = din("g_final", [1, D])
    cbf_d = din("cbf", [128, CB_N], BF16)
    cf_d = din("cf", [128, CF_N])
    out_d = nc.dram_tensor("out", [nseq, S, D], F32, kind="ExternalOutput").ap()
    gsc_d = nc.dram_tensor("gsc", [nseq, D], F32, kind="Internal").ap()
    dbg = {}
    if debug:
        for nm, shp in (("d_hT", [128, 16 * TT]), ("d_yT", [128, 16 * TT]), ("d_m2", [128, 16 * TT])):
            dbg[nm] = nc.dram_tensor(nm, shp, F32, kind="ExternalOutput").ap()

    P = Prog(nc)
    base = [16512]

    def sb(name, shape, dt):
        nbytes = int(np.prod(shape[1:])) * (4 if dt in (F32, I32) else 2)
        off = base[0]
        base[0] = (off + nbytes + 31) // 32 * 32
        assert base[0] <= 229344, (name, base[0])
        return nc.alloc_sbuf_tensor_at(name, list(shape), dt, offset=off)

    def psum(name, shape, dt):
        return nc.alloc_psum_tensor(name, list(shape), dt)

    cbf = sb("cbf", [128, CB_N], BF16)
    cf = sb("cf", [128, CF_N], F32)
    hT_off = base[0]
    hT = sb("hT", [128, 16, TT], BF16)
    yT = sb("yT", [128, 16, TT], BF16)
    xt = sb("xt", [128, 4, D], F32)
    xn = sb("xn", [128, 4, D], BF16)
    wreg = base[0]
    wst = [sb("wst%d" % i, [128, 2048], F32) for i in range(2)]
    wbf = [sb("wbf%d" % i, [128, 2048], BF16) for i in range(2)]
    NSLOT = 6
    assert base[0] - wreg == NSLOT * 4096
    wsl = [nc.alloc_sbuf_tensor_at("wsl%d" % i, [128, 2048], BF16, offset=wreg + i * 4096) for i in range(NSLOT)]
    gate_bc1 = sb("gate_bc", [128, D], F32)
    gate_bc = [gate_bc1 for b in range(nseq)]
    gfin_bc = nc.alloc_sbuf_tensor_at("gfin_bc", [128, D], F32, offset=hT_off)
    cosT = sb("cosT", [128, TT], F32)
    sinT = sb("sinT", [128, TT], F32)
    ksT = sb("ksT", [128, 2, S], BF16)
    kwT = sb("kwT", [128, 2, 2 * TT], BF16)
    Vs = sb("Vs", [128, 2, 16, 130], BF16)
    Vw = sb("Vw", [128, 2, 8, 130], BF16)
    kroll = sb("kroll", [128, 2, 16 + TT], BF16)
    vroll = sb("vroll", [128, 2, 16 + TT], BF16)
    KcT = sb("KcT", [128, 2, 128], BF16)
    Vc = sb("Vc", [128, 2, 130], BF16)
    state = sb("state", [128, 8, 128], F32)
    state_b = sb("state_b", [128, 8, 128], BF16)
    modT = sb("modT", [128, 32, nseq], F32)
    sc1 = sb("sc1", [128, 16, nseq], F32)
    gnT = sb("gnT", [128, 16], F32)
    grT = sb("grT", [128, 8], F32)
    badaT = sb("badaT", [128, 48], F32)
    cT = sb("cT", [128, nseq * 16], F32)
    cTt = sb("cTt", [128, nseq * 16], F32)
    siluc = sb("siluc", [128, 16, nseq], BF16)
    peT = [sb("peT%d" % i, [128, 32], F32) for i in range(2)]
    peTb = [sb("peTb%d" % i, [128, 32], BF16) for i in range(2)]
    w2st = [sb("w2st%d" % i, [128, 2, 128], F32) for i in range(2)]
    w2bf = [sb("w2bf%d" % i, [128, 2, 128], BF16) for i in range(2)]
    cpi = sb("cpi", [128, 1], F32)
    ones_b = sb("ones_b", [128, 128], BF16)
    ceps = sb("ceps", [128, 1], F32)
    ss4 = sb("ss4", [128, 4], F32)
    rs4 = sb("rs4", [128, 4], F32)
    angI = sb("angI", [128, TT], I32)
    pos_i = angI
    qT = [sb("qT%d" % i, [128, TT], BF16) for i in range(4)]
    kT = sb("kT", [128, TT], BF16)
    qxT = sb("qxT", [128, TT], BF16)
    vT = sb("vT", [128, TT], BF16)
    xb = sb("xb", [128, TT], BF16)
    r1 = sb("r1", [128, TT], F32)
    r2 = sb("r2", [128, TT], F32)
    Vr = sb("Vr", [128, 4, 128], BF16)
    Kz = sb("Kz", [128, 4, 128], BF16)
    scm4 = sb("scm4", [128, 4, 128], BF16)
    sb3 = sb("sb3", [128, 3, 128], BF16)
    bst = sb("bst", [128, 4, 6], F32)
    mv = sb("mv", [128, 4, 2], F32)
    nmr = sb("nmr", [128, 4], F32)
    on = sb("on", [128, 4, 128], BF16)
    tg = sb("tg", [128, TT], F32)
    gg = sb("gg", [128, TT], BF16)
    gates = sb("gates", [128, 4, 24], F32)
    hb = sb("hb", [128, 2, 2], F32)
    hbh = sb("hbh", [128, 2, 2], F32)
    hx = sb("hx", [128, 64], F32)
    ht = sb("ht", [128, 64], F32)
    shT = sb("shT", [128, 2, 64], BF16)
    kcx = sb("kcx", [128, 64], BF16)
    vct = sb("vct", [32, 2, 130], BF16)
    PT = [sb("PT%d" % i, [128, TT], BF16) for i in range(2)]
    stmp = sb("stmp", [128, 128], F32)
    etmp = sb("etmp", [128, 128], F32)
    rsum = sb("rsum", [128, 1], F32)
    rinv = sb("rinv", [128, 1], F32)
    ppad = sb("ppad", [128, 4, 132], F32)
    imp = sb("imp", [128, 32], F32)
    m8 = sb("m8", [128, 16], F32)
    wk = sb("wk", [128, 32], F32)
    negm = sb("negm", [128, 32], BF16)
    negmT = sb("negmT", [32, 2, TT], BF16)
    den = sb("den", [128, 1], F32)
    den4 = sb("den4", [128, 4], F32)
    coef = sb("coef", [128, 1], F32)
    acc = sb("acc", [128, 4, 128], F32)
    accb = sb("accb", [128, 4, 128], BF16)
    angA, angB, angC, ta, tb2 = r1, r2, tg, r1, r2
    sbuf_used = base[0]

    pp = [psum("pp%d" % i, [128, 512], F32) for i in range(2)]
    psc = [psum("psc%d" % i, [128, 512], F32) for i in range(2)]
    po = [psum("po%d" % i, [128, 2, 256], F32) for i in range(2)]
    ptrs = [psum("ptr%d" % i, [128, 1024], BF16) for i in range(2)]

    pm = po[1][:, :, :].rearrange("p a c -> p (a c)")
    pofl = [po[i][:, :, :].rearrange("p a c -> p (a c)") for i in range(2)]
    ptrf = [ptrs[i][:, :].bitcast(F32) for i in range(2)]
    br_i = [0]

    class _Ptr:
        def __getitem__(self, idx):
            p_, h_, c_ = idx
            return ptrs[h_][p_, c_] if not (isinstance(c_, slice) and c_ == slice(None)) else ptrs[h_][p_, 0:512]
    ptr = _Ptr()

    class _M2:
        def __getitem__(self, idx):
            p_, cc, t_ = idx
            return xn[p_, cc // 4, (cc % 4) * TT + (t_.start or 0):(cc % 4) * TT + (t_.stop if t_.stop is not None else TT)]
    m2 = _M2()

    def cB(off, n):
        return cbf[:, off:off + n]

    ident_b = cB(CB_ID, 128)
    swap_b = cB(CB_SW, 128)
    tri_b = cB(CB_TRI, 128)
    anti_b = cB(CB_ANTI, 128)

    def bc(ap2d, n):
        a = ap2d
        return bass.AP(tensor=a.tensor, offset=a.offset, ap=[list(a.ap[0]), [0, n]] + [list(z) for z in a.ap[1:]])

    class WS:
        def __init__(self):
            self.specs = []
            self.specsB = []
            self.pos = 0
            self.issued = 0
            self.posB = 0
            self.issuedB = 0
            self.collect = True
            self.gid = {}

        def _issue(self, k):
            spec = self.specs[k]
            slot = k % 2
            kind = spec[0]
            st_t, bf_t = wst[slot], wbf[slot]
            if kind == "cols":
                _, w, col0, ncols, nk = spec
                w = wmap[w]
                dst = st_t[:, 0:nk * ncols].rearrange("p (k c) -> p k c", c=ncols)
                src = w[0:nk * 128, col0:col0 + ncols].rearrange("(k p) c -> p k c", p=128)
                half = nk // 2
                def f(e, s, dst=dst, src=src, half=half, nk=nk):
                    e.dma_start(out=dst[:, 0:half, :], in_=src[:, 0:half, :]).then_inc(s, 16)
                    e.dma_start(out=dst[:, half:nk, :], in_=src[:, half:nk, :]).then_inc(s, 16)
                P.dma("sp", f, writes=["wst%d" % slot], semkey="w%d" % slot, n=2)
                n = nk * ncols
            elif kind == "two":
                _, col0 = spec
                d0 = st_t[:, 0:1024].rearrange("p (k c) -> p k c", c=128)
                d1 = st_t[:, 1024:2048].rearrange("p (k c) -> p k c", c=128)
                s0 = w_ur[:, col0:col0 + 128].rearrange("(k p) c -> p k c", p=128)
                s1 = w_un[:, col0:col0 + 128].rearrange("(k p) c -> p k c", p=128)
                def f(e, s, d0=d0, d1=d1, s0=s0, s1=s1):
                    e.dma_start(out=d0, in_=s0).then_inc(s, 16)
                    e.dma_start(out=d1, in_=s1).then_inc(s, 16)
                P.dma("sp", f, writes=["wst%d" % slot], semkey="w%d" % slot, n=2)
                n = 2048
            elif kind == "wout":
                _, ct, pc = spec
                dst = st_t[:, :].rearrange("p (k c) -> p k c", c=512)
                src = w_out[pc * 512:(pc + 1) * 512, ct * 512:(ct + 1) * 512].rearrange("(k p) c -> p k c", p=128)
                def f(e, s, dst=dst, src=src):
                    e.dma_start(out=dst[:, 0:2, :], in_=src[:, 0:2, :]).then_inc(s, 16)
                    e.dma_start(out=dst[:, 2:4, :], in_=src[:, 2:4, :]).then_inc(s, 16)
                P.dma("sp", f, writes=["wst%d" % slot], semkey="w%d" % slot, n=2)
                n = 2048
            elif kind == "w1":
                _, w, pc = spec
                w = wmap[w]
                dst = st_t[:, :].rearrange("p (i c) -> p i c", c=256)
                src = w[pc * 1024:(pc + 1) * 1024, :].rearrange("(i p) c -> p i c", p=128)
                def f(e, s, dst=dst, src=src):
                    e.dma_start(out=dst[:, 0:4, :], in_=src[:, 0:4, :]).then_inc(s, 16)
                    e.dma_start(out=dst[:, 4:8, :], in_=src[:, 4:8, :]).then_inc(s, 16)
                P.dma("sp", f, writes=["wst%d" % slot], semkey="w%d" % slot, n=2)
                n = 2048
            if k % 2 == 0:
                P.op("dve", lambda e, bf_t=bf_t, st_t=st_t, n=n: e.tensor_copy(out=bf_t[:, 0:n], in_=st_t[:, 0:n]),
                     reads=["wst%d" % slot], writes=["wbf%d" % slot])
            else:
                P.op("act", lambda e, bf_t=bf_t, st_t=st_t, n=n: e.activation(out=bf_t[:, 0:n], in_=st_t[:, 0:n], func=AF.Identity),
                     reads=["wst%d" % slot], writes=["wbf%d" % slot])

        def nextA(self, spec):
            k = self.pos
            assert self.specs[k] == spec, (k, self.specs[k], spec)
            while self.issued < min(k + 2, len(self.specs)):
                self._issue(self.issued)
                self.issued += 1
            self.pos += 1
            slot = k % 2
            return wbf[slot], "wbf%d" % slot

        def finish_collect(self):
            for sp_ in self.specsB:
                if sp_ not in self.gid:
                    self.gid[sp_] = len(self.gid)
            self.uniq = sorted(self.gid, key=lambda z: self.gid[z])
            self.specs = self.specs + self.uniq
            self.scr_keys = ["scr%d" % i for i in range(len(self.uniq))]

        def preconvert(self, wscr):
            self.wscr = wscr
            for u in self.uniq:
                k = self.pos
                wt, wk_ = self.nextA(u)
                gid_ = self.gid[u]
                P.dma("sp", lambda e, s, wt=wt, gid_=gid_: e.dma_start(out=wscr[gid_, :, :], in_=wt[:, :]).then_inc(s, 16),
                      reads=[wk_], writes=["scr%d" % gid_], semkey="ws%d" % (k % 2))

        def _issueB(self, k):
            gid_ = self.gid[self.specsB[k]]
            slot = k % NSLOT
            wscr = self.wscr
            extra = ["wst0", "wst1", "wbf0", "wbf1"] if k < len(self.uniq) + NSLOT else []
            P.dma("sp", lambda e, s, slot=slot, gid_=gid_: e.dma_start(out=wsl[slot][:, :], in_=wscr[gid_, :, :]).then_inc(s, 16),
                  reads=self.scr_keys, writes=["wsl%d" % slot] + extra, semkey="wl%d" % slot)

        def next(self, spec):
            is_a = (spec[0] == "cols" and spec[1] == "w_ada")
            if self.collect:
                (self.specs if is_a else self.specsB).append(spec)
                return None, None
            if is_a:
                return self.nextA(spec)
            k = self.posB
            assert self.specsB[k] == spec, (k, self.specsB[k], spec)
            nu = len(self.uniq)
            if k < nu:
                assert self.uniq[k] == spec
                ka = self.pos
                wt, wk_ = self.nextA(spec)
                gid_ = self.gid[spec]
                wscr = self.wscr
                P.dma("sp", lambda e, s, wt=wt, gid_=gid_: e.dma_start(out=wscr[gid_, :, :], in_=wt[:, :]).then_inc(s, 16),
                      reads=[wk_], writes=["scr%d" % gid_], semkey="ws%d" % (ka % 2))
                self.posB += 1
                self.issuedB = max(self.issuedB, nu)
                return wt, wk_
            while self.issuedB < min(k + NSLOT, len(self.specsB)):
                self._issueB(self.issuedB)
                self.issuedB += 1
            self.posB += 1
            slot = k % NSLOT
            return wsl[slot], "wsl%d" % slot

    W = WS()
    wmap = {"w_in": w_in, "w_ada": w_ada, "w_ck1": w_ck1, "w_cv1": w_cv1}
    pp_i = [0]
    psc_i = [0]

    def next_pp():
        i = pp_i[0] % 2
        pp_i[0] += 1
        return pp[i], "pp%d" % i

    def next_psc():
        i = psc_i[0] % 2
        psc_i[0] += 1
        return psc[i], "psc%d" % i

    def proj_fm(col0, dst_fn):
        wt, wk_ = W.next(("cols", "w_in", col0, 128, 16))
        if wt is None:
            flush_pending()
            dst_fn(None, None)
            return
        ps_t, ps_k = next_pp()
        w3 = wt[:, :].rearrange("p (k c) -> p k c", c=128)
        def f(e, w3=w3, ps_t=ps_t):
            ins = None
            for kc in range(16):
                ins = e.matmul(ps_t[:, :], lhsT=w3[:, kc, :], rhs=hT[:, kc, :], start=(kc == 0), stop=(kc == 15))
            return ins
        P.op("pe", f, reads=[wk_, "hT"], writes=[ps_k])
        flush_pending()
        dst_fn(ps_t, ps_k)

    pending = []

    def flush_pending():
        while pending:
            pending.pop(0)()

    def rope_to(ps_t, ps_k, dst_ap, dst_key):
        if ps_t is None:
            return
        P.op("act", lambda e: e.activation(out=xb[:, :], in_=ps_t[:, :], func=AF.Identity), reads=[ps_k], writes=["xb"])
        P.op("pool", lambda e: e.tensor_tensor(out=r1[:, :], in0=xb[:, :], in1=cosT[:, :], op=ALU.mult), reads=["xb", "cosT"], writes=["r1"])

        def rest():
            P.op("pe", lambda e: e.matmul(pm[:, :], lhsT=swap_b, rhs=xb[:, :], start=True, stop=True), reads=["xb", "cbf"], writes=["po1"])
            P.op("dve", lambda e: e.tensor_tensor(out=r2[:, :], in0=pm[:, :], in1=sinT[:, :], op=ALU.mult), reads=["po1", "sinT"], writes=["r2"])
            P.op("pool", lambda e: e.tensor_tensor(out=dst_ap, in0=r1[:, :], in1=r2[:, :], op=ALU.add), reads=["r1", "r2"], writes=[dst_key])
        pending.append(rest)

    def copy_to(ps_t, ps_k, dst_ap, dst_key):
        if ps_t is None:
            return
        P.op("act", lambda e: e.activation(out=dst_ap, in_=ps_t[:, :], func=AF.Identity), reads=[ps_k], writes=[dst_key])

    def transpose4(src_fn, src_keys, half, pre=()):
        def f(e):
            ins = None
            for j in range(4):
                ins = e.transpose(ptr[:, half, j * 128:(j + 1) * 128], src_fn(j), ident_b)
            return ins
        P.op("pe", f, reads=list(src_keys) + ["cbf"], writes=["ptr%d" % half])

    tr_i = [0]

    def next_tr():
        i = tr_i[0] % 2
        tr_i[0] += 1
        return i

    def gate_a(col0):
        def g(ps_t, ps_k):
            if ps_t is None:
                return
            P.op("act", lambda e: e.activation(out=tg[:, :], in_=ps_t[:, :], func=AF.Tanh, scale=0.5), reads=[ps_k], writes=["tg"])
            P.op("dve", lambda e: e.scalar_tensor_tensor(out=gg[:, :], in0=tg[:, :], scalar=1.0, in1=ps_t[:, :], op0=ALU.add, op1=ALU.mult),
                 reads=["tg", ps_k], writes=["gg"])
        proj_fm(col0, g)

    def gate_b(ydst, ykey):
        P.op("pool", lambda e: e.tensor_tensor(out=ydst, in0=ydst, in1=gg[:, :], op=ALU.mult), reads=["gg", ykey], writes=[ykey])

    def setup():
        def ld(dst, src, key):
            P.dma("sp", lambda e, s: e.dma_start(out=dst, in_=src).then_inc(s, 16), writes=[key], semkey="ld_" + key)
        ld(cbf[:, :], cbf_d, "cbf")
        ld(cf[:, :], cf_d, "cf")
        ld(cT[:, :], cT_d, "cT")
        ld(badaT[:, :], b_adaT, "badaT")
        ld(gnT[:, :], g_normT, "gnT")
        ld(grT[:, :], g_retT, "grT")
        ld(peT[0][:, :], pe_ckT, "peT0")
        ld(peT[1][:, :], pe_cvT, "peT1")
        ld(w2st[0][:, :, :], w_ck2.rearrange("(m p) c -> p m c", p=128), "w2st0")
        ld(w2st[1][:, :, :], w_cv2.rearrange("(m p) c -> p m c", p=128), "w2st1")
        P.op("pool", lambda e: e.memset(cpi[:, :], float(np.pi)), writes=["cpi"])
        P.op("pool", lambda e: e.memset(ones_b[:, :], 1.0), writes=["ones_b"])
        P.op("pool", lambda e: e.memset(ceps[:, :], EPS), writes=["ceps"])
        P.op("pool", lambda e: e.memset(Vs[:, :, :, 128:130], 1.0), writes=["Vs"])
        P.op("pool", lambda e: e.memset(Vw[:, :, :, 128:130], 1.0), writes=["Vw"])
        P.op("pool", lambda e: e.memset(Vc[:, :, :], 0.0), writes=["Vc"])
        P.op("pool", lambda e: e.memset(vct[:, :, 128:130], 1.0), writes=["vct"])
        P.op("pool", lambda e: e.memset(KcT[:, :, :], 0.0), writes=["KcT"])
        P.op("pool", lambda e: e.memset(ppad[:, :, :], 0.0), writes=["ppad"])
        for i in range(2):
            P.op("pool", lambda e, i=i: e.tensor_copy(out=peTb[i][:, :], in_=peT[i][:, :]), reads=["peT%d" % i], writes=["peTb%d" % i])
            P.op("pool", lambda e, i=i: e.tensor_copy(out=w2bf[i][:, :, :], in_=w2st[i][:, :, :]), reads=["w2st%d" % i], writes=["w2bf%d" % i])
        P.op("pool", lambda e: e.tensor_scalar(out=grT[:, :], in0=grT[:, :], scalar1=0.5, scalar2=None, op0=ALU.mult), reads=["grT"], writes=["grT"])
        P.op("act", lambda e: e.activation(out=cTt[:, :], in_=cT[:, :], func=AF.Tanh, scale=0.5), reads=["cT"], writes=["cTt"])
        P.op("dve", lambda e: e.scalar_tensor_tensor(out=cTt[:, :], in0=cTt[:, :], scalar=1.0, in1=cT[:, :], op0=ALU.add, op1=ALU.mult),
             reads=["cTt", "cT"], writes=["cTt"])
        for b in range(nseq):
            P.op("dve", lambda e, b=b: e.tensor_scalar(out=siluc[:, :, b], in0=cTt[:, b * 16:(b + 1) * 16], scalar1=0.5, scalar2=None, op0=ALU.mult),
                 reads=["cTt"], writes=["siluc"])
        for grp in range(32):
            wt, wk_ = W.next(("cols", "w_ada", grp * 128, 128, 16))
            if wt is None:
                continue
            w3 = wt[:, :].rearrange("p (k c) -> p k c", c=128)
            def f(e, w3=w3):
                ins = None
                for kc in range(16):
                    ins = e.matmul(pm[:, 0:nseq], lhsT=w3[:, kc, :], rhs=siluc[:, kc, :], start=(kc == 0), stop=(kc == 15))
                return ins
            P.op("pe", f, reads=[wk_, "siluc"], writes=["po1"])
            P.op("dve", lambda e, grp=grp: e.tensor_scalar(out=modT[:, grp, :], in0=pm[:, 0:nseq], scalar1=badaT[:, grp:grp + 1], scalar2=None, op0=ALU.add),
                 reads=["po1", "badaT"], writes=["modT"])
        if W.collect:
            return
        for b in range(nseq):
            P.op("dve", lambda e, b=b: e.scalar_tensor_tensor(out=sc1[:, :, b], in0=modT[:, 16:32, b], scalar=1.0, in1=gnT[:, :], op0=ALU.add, op1=ALU.mult),
                 reads=["modT", "gnT"], writes=["sc1"])

    def gate_rows():
        grow = xt[0:nseq, 1, :]
        for grp in range(16):
            wt, wk_ = W.next(("cols", "w_ada", (32 + grp) * 128, 128, 16))
            if wt is None:
                continue
            w3 = wt[:, :].rearrange("p (k c) -> p k c", c=128)
            def f(e, w3=w3):
                ins = None
                for kc in range(16):
                    ins = e.matmul(pm[0:nseq, 128:256], lhsT=siluc[:, kc, :], rhs=w3[:, kc, :], start=(kc == 0), stop=(kc == 15))
                return ins
            P.op("pe", f, reads=[wk_, "siluc"], writes=["po1"])
            P.op("act", lambda e, grp=grp: e.activation(out=grow[:, grp * 128:(grp + 1) * 128], in_=pm[0:nseq, 128:256], func=AF.Identity),
                 reads=["po1"], writes=["xt1"])
        if W.collect:
            return
        P.dma("sp", lambda e, s: e.dma_start(out=gsc_d, in_=grow).then_inc(s, 16), reads=["xt1"], writes=["gsc"], semkey="gs")

    def seq_start(b):
        if W.collect:
            return
        P.dma("sp", lambda e, s: e.dma_start(out=gate_bc1[:, :], in_=gsc_d[b, :].partition_broadcast(128)).then_inc(s, 16), reads=["gsc"], writes=["gate_bc"], semkey="gb")
        P.dma("sp", lambda e, s: e.dma_start(out=xt[:, 0, :], in_=b_gate[0, :].partition_broadcast(128)).then_inc(s, 16), writes=["xt0"], semkey="x0")
        P.op("dve", lambda e: e.tensor_tensor(out=gate_bc1[:, :], in0=gate_bc1[:, :], in1=xt[:, 0, :], op=ALU.add), reads=["gate_bc", "xt0"], writes=["gate_bc"])
        P.op("pool", lambda e: e.tensor_scalar(out=gate_bc1[:, :], in0=gate_bc1[:, :], scalar1=0.5, scalar2=None, op0=ALU.mult), reads=["gate_bc"], writes=["gate_bc"])

    def sincos(src_add, dst, dkey, fold_sign):
        P.op("dve", lambda e: e.tensor_scalar(out=angB[:, :], in0=angA[:, :], scalar1=float(src_add), scalar2=float(1.0 / (2 * np.pi)), op0=ALU.add, op1=ALU.mult),
             reads=["r1"], writes=["r2"])
        P.op("dve", lambda e: e.tensor_copy(out=angI[:, :], in_=angB[:, :]), reads=["r2"], writes=["angI"])
        P.op("dve", lambda e: e.tensor_copy(out=angB[:, :], in_=angI[:, :]), reads=["angI"], writes=["r2"])
        P.op("dve", lambda e: e.tensor_scalar(out=angC[:, :], in0=angA[:, :], scalar1=float(src_add), scalar2=None, op0=ALU.add), reads=["r1"], writes=["tg"])
        P.op("dve", lambda e: e.scalar_tensor_tensor(out=angC[:, :], in0=angB[:, :], scalar=-float(2 * np.pi), in1=angC[:, :], op0=ALU.mult, op1=ALU.add),
             reads=["r2", "tg"], writes=["tg"])
        P.op("dve", lambda e: e.tensor_scalar(out=angB[:, :], in0=angC[:, :], scalar1=0.0, scalar2=float(2 * np.pi), op0=ALU.is_lt, op1=ALU.mult),
             reads=["tg"], writes=["r2"])
        P.op("dve", lambda e: e.tensor_tensor(out=angC[:, :], in0=angC[:, :], in1=angB[:, :], op=ALU.add), reads=["tg", "r2"], writes=["tg"])
        P.op("dve", lambda e: e.tensor_scalar(out=angC[:, :], in0=angC[:, :], scalar1=float(2 * np.pi), scalar2=None, op0=ALU.min), reads=["tg"], writes=["tg"])
        P.op("act", lambda e: e.activation(out=dst[:, :], in_=angC[:, :], func=AF.Sin, bias=cpi[:, 0:1], scale=-1.0), reads=["tg", "cpi"], writes=[dkey])
        if fold_sign:
            P.op("dve", lambda e: e.tensor_scalar(out=dst[:, :], in0=dst[:, :], scalar1=cf[:, CF_SGN:CF_SGN + 1], scalar2=None, op0=ALU.mult),
                 reads=[dkey, "cf"], writes=[dkey])

    def body(b, T):
        tok0 = T * TT
        dry = W.collect
        if not dry:
            if not xpre[0]:
                for st in range(4):
                    P.dma("sp", lambda e, s, st=st: e.dma_start(out=xt[:, st, :], in_=x_d[b, tok0 + st * 128:tok0 + (st + 1) * 128, :]).then_inc(s, 16),
                          writes=["xt%d" % st], semkey="x%d" % st)
            xpre[0] = False
            P.dma("sp", lambda e, s: e.dma_start(out=pos_i[:, :], in_=pos_d[b, tok0:tok0 + TT].partition_broadcast(128)).then_inc(s, 16),
                  writes=["angI"], semkey="pos")
            for st in range(4):
                P.op("act", lambda e, st=st: e.activation(out=xn[:, st, :], in_=xt[:, st, :], func=AF.Square, accum_out=ss4[:, st:st + 1]),
                     reads=["xt%d" % st], writes=["xn%d" % st, "ss4"])
            if STAGE <= 1.1:
                return
            P.op("act", lambda e: e.activation(out=rs4[:, :], in_=ss4[:, :], func=AF.Sqrt, bias=ceps[:, 0:1], scale=1.0 / D), reads=["ss4", "ceps"], writes=["rs4"])
            P.op("dve", lambda e: e.reciprocal(out=rs4[:, :], in_=rs4[:, :]), reads=["rs4"], writes=["rs4"])
            if STAGE <= 1.2:
                return
            for st in range(4):
                if st % 2 == 0:
                    P.op("act", lambda e, st=st: e.activation(out=xn[:, st, :], in_=xt[:, st, :], func=AF.Identity, scale=rs4[:, st:st + 1]),
                         reads=["xt%d" % st, "rs4"], writes=["xn%d" % st])
                else:
                    P.op("dve", lambda e, st=st: e.tensor_scalar(out=xn[:, st, :], in0=xt[:, st, :], scalar1=rs4[:, st:st + 1], scalar2=None, op0=ALU.mult),
                         reads=["xt%d" % st, "rs4"], writes=["xn%d" % st])
            if STAGE <= 1.3:
                return
            for fc in range(int(_os.environ.get('KFC', '16'))):
                h = next_tr()
                transpose4(lambda j, fc=fc: xn[:, j, fc * 128:(fc + 1) * 128], ["xn0", "xn1", "xn2", "xn3"], h)
                if _os.environ.get('KNODVE'):
                    continue
                kvar = _os.environ.get('KVAR', '3')
                if kvar == '1':
                    P.op("dve", lambda e, fc=fc, h=h: e.tensor_scalar(out=hT[:, fc, :], in0=ptr[:, h, :], scalar1=sc1[:, fc, b:b + 1], scalar2=None, op0=ALU.mult),
                         reads=["ptr%d" % h, "sc1", "modT"], writes=["hT"])
                    continue
                if kvar == '2':
                    P.op("dve", lambda e, fc=fc, h=h: e.tensor_copy(out=hT[:, fc, :], in_=ptr[:, h, :]),
                         reads=["ptr%d" % h, "sc1", "modT"], writes=["hT"])
                    continue
                if kvar == '3':
                    P.op("act", lambda e, fc=fc, h=h: e.activation(out=hT[:, fc, :], in_=ptr[:, h, :], func=AF.Identity, scale=sc1[:, fc, b:b + 1], bias=modT[:, fc, b:b + 1]),
                         reads=["ptr%d" % h, "sc1", "modT"], writes=["hT"])
                    continue
                P.op("dve", lambda e, fc=fc, h=h: e.tensor_scalar(out=hT[:, fc, :], in0=ptr[:, h, :], scalar1=sc1[:, fc, b:b + 1], scalar2=modT[:, fc, b:b + 1],
                                                                 op0=ALU.mult, op1=ALU.add),
                     reads=["ptr%d" % h, "sc1", "modT"], writes=["hT"])
            if debug and b == 0 and T == dbgT:
                P.op("dve", lambda e: e.tensor_copy(out=xt[:, 3, :].rearrange("p (a c) -> p a c", c=TT)[:, 0:4, :], in_=hT[:, 0:4, :]), reads=["hT"], writes=["xt3"])
            if STAGE <= 1.4:
                return
            P.op("dve", lambda e: e.tensor_copy(out=angA[:, :], in_=pos_i[:, :]), reads=["angI"], writes=["r1"])
            P.op("dve", lambda e: e.tensor_scalar(out=angA[:, :], in0=angA[:, :], scalar1=cf[:, CF_INV:CF_INV + 1], scalar2=None, op0=ALU.mult),
                 reads=["r1", "cf"], writes=["r1"])
            sincos(0.0, sinT, "sinT", True)
            sincos(np.pi / 2, cosT, "cosT", False)
            if T == 0:
                P.op("pool", lambda e: e.memset(state[:, :, :], 0.0), writes=["state"])
                P.op("pool", lambda e: e.memset(state_b[:, :, :], 0.0), writes=["state_b"])
                P.op("pool", lambda e: e.memset(kroll[:, :, :], 0.0), writes=["kroll"])
                P.op("pool", lambda e: e.memset(vroll[:, :, :], 0.0), writes=["vroll"])
            else:
                P.op("pool", lambda e: e.tensor_copy(out=kroll[:, :, 0:16], in_=kroll[:, :, TT:TT + 16]), reads=["kroll"], writes=["kroll"])
                P.op("pool", lambda e: e.tensor_copy(out=vroll[:, :, 0:16], in_=vroll[:, :, TT:TT + 16]), reads=["vroll"], writes=["vroll"])

        if STAGE <= 2:
            return
        for h in range(8):
            proj_fm(O_RQ + h * 128, lambda p_, k_: rope_to(p_, k_, qT[0][:, :], "qT0"))
            proj_fm(O_RK + h * 128, lambda p_, k_: rope_to(p_, k_, kT[:, :], "kT"))
            proj_fm(O_RV + h * 128, lambda p_, k_: copy_to(p_, k_, vT[:, :], "vT"))
            if not dry:
                xi_ap = bc(cbf[:, CB_XI + h * 128:CB_XI + (h + 1) * 128], 4)
                P.op("pool", lambda e, xi_ap=xi_ap: e.tensor_tensor(out=qxT[:, :].rearrange("p (a c) -> p a c", c=128), in0=qT[0][:, :].rearrange("p (a c) -> p a c", c=128),
                                                                   in1=xi_ap, op=ALU.mult), reads=["qT0", "cbf"], writes=["qxT"])
                hh = next_tr()
                transpose4(lambda j: vT[:, j * 128:(j + 1) * 128], ["vT"], hh)
                P.op("act", lambda e, hh=hh: e.activation(out=Vr[:, :, :], in_=ptr[:, hh, :].rearrange("p (a c) -> p a c", c=128), func=AF.Identity),
                     reads=["ptr%d" % hh], writes=["Vr"])
                hh = next_tr()
                transpose4(lambda j: kT[:, j * 128:(j + 1) * 128], ["kT"], hh)
                P.op("act", lambda e, hh=hh, h=h: e.activation(out=Kz[:, :, :], in_=ptr[:, hh, :].rearrange("p (a c) -> p a c", c=128), func=AF.Identity,
                                                              scale=cf[:, CF_ZETA + h:CF_ZETA + h + 1]),
                     reads=["ptr%d" % hh, "cf"], writes=["Kz"])
                dm4 = bc(cbf[:, CB_DM + h * 128:CB_DM + (h + 1) * 128], 4)
                psA, kA = next_psc()
                psB, kB = next_psc()
                def f_sc(e, psA=psA):
                    ins = None
                    for c in range(4):
                        cs = slice(c * 128, (c + 1) * 128)
                        ins = e.matmul(psA[:, cs], lhsT=kT[:, cs], rhs=qT[0][:, cs], start=True, stop=True)
                    return ins
                P.op("pe", f_sc, reads=["kT", "qT0"], writes=[kA])
                def f_kv(e, psB=psB):
                    ins = None
                    for c in range(4):
                        ins = e.matmul(psB[:, c * 128:(c + 1) * 128], lhsT=Kz[:, c, :], rhs=Vr[:, c, :], start=True, stop=True)
                    return ins
                P.op("pe", f_kv, reads=["Kz", "Vr"], writes=[kB])
                P.op("dve", lambda e, psA=psA, dm4=dm4: e.tensor_tensor(out=scm4[:, :, :], in0=psA[:, :].rearrange("p (a c) -> p a c", c=128), in1=dm4, op=ALU.mult),
                     reads=[kA, "cbf"], writes=["scm4"])
                for c in range(3):
                    P.op("dve", lambda e, h=h, c=c, psB=psB: e.scalar_tensor_tensor(out=state[:, h, :], in0=state[:, h, :], scalar=DECAY[h], in1=psB[:, c * 128:(c + 1) * 128],
                                                                                  op0=ALU.mult, op1=ALU.add), reads=["state", kB], writes=["state"])
                    P.op("act", lambda e, h=h, c=c: e.activation(out=sb3[:, c, :], in_=state[:, h, :], func=AF.Identity), reads=["state"], writes=["sb3"])
                def f_o(e, h=h):
                    ins = None
                    for c in range(4):
                        cs = slice(c * 128, (c + 1) * 128)
                        e.matmul(pofl[0][:, cs], lhsT=scm4[:, c, :], rhs=Vr[:, c, :], start=True, stop=False)
                        ins = e.matmul(pofl[0][:, cs], lhsT=qxT[:, cs], rhs=(state_b[:, h, :] if c == 0 else sb3[:, c - 1, :]), start=False, stop=True)
                    return ins
                P.op("pe", f_o, reads=["scm4", "Vr", "qxT", "state_b", "sb3"], writes=["po0"])
                P.op("dve", lambda e, h=h, psB=psB: e.scalar_tensor_tensor(out=state[:, h, :], in0=state[:, h, :], scalar=DECAY[h], in1=psB[:, 384:512],
                                                                          op0=ALU.mult, op1=ALU.add), reads=["state", kB], writes=["state"])
                P.op("act", lambda e, h=h: e.activation(out=state_b[:, h, :], in_=state[:, h, :], func=AF.Identity), reads=["state"], writes=["state_b"])
                for c in range(4):
                    P.op("dve", lambda e, c=c: e.bn_stats(out=bst[:, c, :], in_=pofl[0][:, c * 128:(c + 1) * 128]), reads=["po0"], writes=["bst"])
                    P.op("dve", lambda e, c=c: e.bn_aggr(out=mv[:, c, :], in_=bst[:, c, :]), reads=["bst"], writes=["mv"])
                P.op("act", lambda e: e.activation(out=rs4[:, :], in_=mv[:, :, 1], func=AF.Sqrt, bias=ceps[:, 0:1], scale=1.0), reads=["mv", "ceps"], writes=["rs4"])
                P.op("dve", lambda e: e.reciprocal(out=rs4[:, :], in_=rs4[:, :]), reads=["rs4"], writes=["rs4"])
                P.op("dve", lambda e: e.scalar_tensor_tensor(out=nmr[:, :], in0=mv[:, :, 0], scalar=-1.0, in1=rs4[:, :], op0=ALU.mult, op1=ALU.mult),
                     reads=["mv", "rs4"], writes=["nmr"])
                for c in range(4):
                    P.op("act", lambda e, c=c: e.activation(out=on[:, c, :], in_=po[0][:, c // 2, (c % 2) * 128:(c % 2) * 128 + 128], func=AF.Identity,
                                                           scale=rs4[:, c:c + 1], bias=nmr[:, c:c + 1]),
                         reads=["po0", "nmr", "rs4"], writes=["on"])
            gate_a(O_RG + h * 128)
            if not dry:
                def tail(h=h):
                    hh = next_tr()
                    transpose4(lambda j: on[:, j, :], ["on"], hh)
                    P.op("act", lambda e, hh=hh, h=h: e.activation(out=yT[:, h, :], in_=ptr[:, hh, :], func=AF.Identity, scale=grT[:, h:h + 1]),
                         reads=["ptr%d" % hh, "grT"], writes=["yT%d" % h])
                    gate_b(yT[:, h, :], "yT%d" % h)
                pending.append(tail)

        if STAGE <= 3:
            return
        for g in range(2):
            proj_fm(O_KC + g * 128, lambda p_, k_, g=g: copy_to(p_, k_, kroll[:, g, 16:16 + TT], "kroll"))
            proj_fm(O_VC + g * 128, lambda p_, k_, g=g: copy_to(p_, k_, vroll[:, g, 16:16 + TT], "vroll"))
            proj_fm(O_KS + g * 128, lambda p_, k_, g=g: rope_to(p_, k_, ksT[:, g, tok0:tok0 + TT], "ksT"))
            proj_fm(O_KW + g * 128, lambda p_, k_, g=g: rope_to(p_, k_, kwT[:, g, (T % 2) * TT:(T % 2 + 1) * TT], "kwT"))
            for (off, Vd, vk) in ((O_VS, Vs, "Vs"), (O_VW, Vw, "Vw")):
                proj_fm(off + g * 128, lambda p_, k_: copy_to(p_, k_, vT[:, :], "vT"))
                if not dry:
                    hh = next_tr()
                    vb0 = 4 * T if vk == "Vs" else 4 * (T % 2)
                    transpose4(lambda j: vT[:, j * 128:(j + 1) * 128], ["vT"], hh)
                    P.op("act", lambda e, hh=hh, Vd=Vd, g=g, vb0=vb0: e.activation(out=Vd[:, g, vb0:vb0 + 4, 0:128], in_=ptr[:, hh, :].rearrange("p (a c) -> p a c", c=128), func=AF.Identity),
                         reads=["ptr%d" % hh], writes=[vk])
        nl0 = 1 if T == 0 else 0
        ncol = 32 - nl0
        for kv in range(2):
            roll = kroll if kv == 0 else vroll
            rkey = "kroll" if kv == 0 else "vroll"
            w1 = w_ck1 if kv == 0 else w_cv1
            for pc in range(4):
                wt, wk_ = W.next(("w1", "w_ck1" if kv == 0 else "w_cv1", pc))
                if wt is None:
                    continue
                w3 = wt[:, :].rearrange("p (i c) -> p i c", c=256)
                def f(e, w3=w3, pc=pc, roll=roll, kv=kv):
                    ins = None
                    for il in range(8):
                        i = pc * 8 + il
                        for mh in range(2):
                            rhs = roll[:, :, i:i + 497:16]
                            e.matmul(psc[0][:, mh * 64:mh * 64 + 64].rearrange("p (g n) -> p g n", n=32), lhsT=w3[:, il, mh * 128:(mh + 1) * 128], rhs=rhs,
                                     start=(i == 0 and mh == 0), stop=(i == 31), skip_group_check=True)
                            ins = e.matmul(psc[1][:, mh:mh + 1], lhsT=w3[:, il, mh * 128:(mh + 1) * 128], rhs=peTb[kv][:, i:i + 1],
                                           start=(i == 0 and mh == 0), stop=(i == 31), skip_group_check=True)
                    return ins
                P.op("pe", f, reads=[wk_, rkey, "peTb%d" % kv], writes=["psc0", "psc1"])
            if dry:
                continue
            P.op("dve", lambda e, kv=kv: e.tensor_copy(out=hb[:, kv, :], in_=psc[1][:, 0:2]), reads=["psc1"], writes=["hb"])
            P.op("dve", lambda e, kv=kv: e.tensor_scalar(out=hbh[:, kv, :], in0=hb[:, kv, :], scalar1=0.5, scalar2=None, op0=ALU.mult), reads=["hb"], writes=["hbh"])
            for mh in range(2):
                P.op("act", lambda e, mh=mh, kv=kv: e.activation(out=ht[:, :], in_=psc[0][:, mh * 64:mh * 64 + 64], func=AF.Tanh, bias=hbh[:, kv, mh:mh + 1], scale=0.5),
                     reads=["psc0", "hbh"], writes=["ht"])
                P.op("dve", lambda e, mh=mh, kv=kv: e.tensor_scalar(out=hx[:, :], in0=psc[0][:, mh * 64:mh * 64 + 64], scalar1=hb[:, kv, mh:mh + 1], scalar2=None, op0=ALU.add),
                     reads=["psc0", "hb"], writes=["hx"])
                P.op("dve", lambda e, mh=mh: e.scalar_tensor_tensor(out=shT[:, mh, :], in0=ht[:, :], scalar=1.0, in1=hx[:, :], op0=ALU.add, op1=ALU.mult),
                     reads=["ht", "hx"], writes=["shT"])
            if kv == 0:
                def f2(e):
                    e.matmul(pm[:, 0:64], lhsT=w2bf[0][:, 0, :], rhs=shT[:, 0, :], start=True, stop=False)
                    return e.matmul(pm[:, 0:64], lhsT=w2bf[0][:, 1, :], rhs=shT[:, 1, :], start=False, stop=True)
                P.op("pe", f2, reads=["w2bf0", "shT"], writes=["po1"])
                P.op("act", lambda e: e.activation(out=kcx[:, :], in_=pm[:, 0:64], func=AF.Identity, scale=0.5), reads=["po1"], writes=["kcx"])
                P.op("pe", lambda e: e.matmul(pm[:, 64:128], lhsT=swap_b, rhs=kcx[:, :], start=True, stop=True), reads=["kcx", "cbf"], writes=["po1"])
                cos_c = bc(cosT[:, 15:TT:16], 2)
                sin_c = bc(sinT[:, 15:TT:16], 2)
                P.op("dve", lambda e, cos_c=cos_c: e.tensor_tensor(out=hx[:, :].rearrange("p (g n) -> p g n", n=32), in0=kcx[:, :].rearrange("p (g n) -> p g n", n=32), in1=cos_c, op=ALU.mult),
                     reads=["kcx", "cosT"], writes=["hx"])
                P.op("dve", lambda e, sin_c=sin_c: e.tensor_tensor(out=ht[:, :].rearrange("p (g n) -> p g n", n=32), in0=pm[:, 64:128].rearrange("p (g n) -> p g n", n=32), in1=sin_c, op=ALU.mult),
                     reads=["po1", "sinT"], writes=["ht"])
                P.op("dve", lambda e: e.tensor_tensor(out=KcT[:, :, 32 * T:32 * T + 32], in0=hx[:, :].rearrange("p (g n) -> p g n", n=32), in1=ht[:, :].rearrange("p (g n) -> p g n", n=32), op=ALU.add),
                     reads=["hx", "ht"], writes=["KcT"])
            else:
                for g in range(2):
                    def f3(e, g=g):
                        e.matmul(pm[0:32, 128 + g * 128:256 + g * 128], lhsT=shT[:, 0, g * 32:(g + 1) * 32], rhs=w2bf[1][:, 0, :], start=True, stop=False)
                        return e.matmul(pm[0:32, 128 + g * 128:256 + g * 128], lhsT=shT[:, 1, g * 32:(g + 1) * 32], rhs=w2bf[1][:, 1, :], start=False, stop=True)
                    P.op("pe", f3, reads=["w2bf1", "shT"], writes=["po1"])
                    P.op("act", lambda e, g=g: e.activation(out=vct[:, g, 0:128], in_=pm[0:32, 128 + g * 128:256 + g * 128], func=AF.Identity, scale=0.5), reads=["po1"], writes=["vct"])
                P.dma("sp", lambda e, s: e.dma_start(out=Vc[32 * T:32 * T + 32, :, :], in_=vct[:, :, :]).then_inc(s, 16), reads=["vct"], writes=["Vc"], semkey="vc")
        wt, wk_ = W.next(("cols", "w_in", O_BG, 24, 16))
        if wt is not None:
            w3 = wt[:, 0:16 * 24].rearrange("p (k c) -> p k c", c=24)
            for st in range(4):
                def f(e, st=st, w3=w3):
                    ins = None
                    for kc in range(16):
                        ins = e.matmul(pm[:, 384 + st * 24:384 + (st + 1) * 24], lhsT=hT[:, kc, st * 128:(st + 1) * 128], rhs=w3[:, kc, :], start=(kc == 0), stop=(kc == 15))
                    return ins
                P.op("pe", f, reads=[wk_, "hT"], writes=["po1"])
            P.op("act", lambda e: e.activation(out=gates[:, :, :], in_=pm[:, 384:480].rearrange("p (a c) -> p a c", c=24), func=AF.Tanh, scale=0.5), reads=["po1"], writes=["gates"])
            P.op("dve", lambda e: e.tensor_scalar(out=gates[:, :, :], in0=gates[:, :, :], scalar1=0.5, scalar2=0.5, op0=ALU.mult, op1=ALU.add), reads=["gates"], writes=["gates"])

        if STAGE <= 4:
            return
        for g in range(2):
            for r in range(4):
                hq = 4 * g + r
                proj_fm(O_NQ + hq * 128, lambda p_, k_, r=r: rope_to(p_, k_, qT[r][:, :], "qT%d" % r))
            flush_pending()
            for r in range(4):
                if dry or T < 2:
                    continue
                for st in range(4):
                    ps_t, ps_k = next_psc()
                    P.op("pe", lambda e, ps_t=ps_t, st=st, r=r, g=g: e.matmul(ps_t[:, 0:128], lhsT=qT[r][:, st * 128:(st + 1) * 128], rhs=KcT[:, g, :], start=True, stop=True),
                         reads=["qT%d" % r, "KcT"], writes=[ps_k])
                    nb_ap = cbf[:, CB_NB + ((T - 2) * 4 + st) * 128:CB_NB + ((T - 2) * 4 + st + 1) * 128]
                    P.op("dve", lambda e, ps_t=ps_t, nb_ap=nb_ap: e.scalar_tensor_tensor(out=stmp[:, :], in0=ps_t[:, 0:128], scalar=SCALE, in1=nb_ap, op0=ALU.mult, op1=ALU.add),
                         reads=[ps_k, "cbf"], writes=["stmp"])
                    P.op("act", lambda e: e.activation(out=etmp[:, :], in_=stmp[:, :], func=AF.Exp, accum_out=rsum[:, 0:1]), reads=["stmp"], writes=["etmp", "rsum"])
                    P.op("dve", lambda e: e.reciprocal(out=rinv[:, :], in_=rsum[:, :]), reads=["rsum"], writes=["rinv"])
                    if r == 0:
                        P.op("dve", lambda e, st=st: e.tensor_scalar(out=ppad[:, st, 0:128], in0=etmp[:, :], scalar1=rinv[:, 0:1], scalar2=None, op0=ALU.mult),
                             reads=["etmp", "rinv"], writes=["ppad"])
                    else:
                        P.op("dve", lambda e, st=st: e.scalar_tensor_tensor(out=ppad[:, st, 0:128], in0=etmp[:, :], scalar=rinv[:, 0:1], in1=ppad[:, st, 0:128], op0=ALU.mult, op1=ALU.add),
                             reads=["etmp", "rinv", "ppad"], writes=["ppad"])
            if not dry and T >= 2:
                for st in range(4):
                    def v(k0, st=st):
                        return ppad[:, st, k0:k0 + 128:4]
                    P.op("dve", lambda e, v=v: e.tensor_tensor(out=imp[:, :], in0=v(0), in1=v(4), op=ALU.add), reads=["ppad"], writes=["imp"])
                    P.op("dve", lambda e, v=v: e.scalar_tensor_tensor(out=imp[:, :], in0=imp[:, :], scalar=0.5, in1=v(1), op0=ALU.mult, op1=ALU.add), reads=["imp", "ppad"], writes=["imp"])
                    P.op("dve", lambda e, v=v: e.tensor_tensor(out=imp[:, :], in0=imp[:, :], in1=v(2), op=ALU.add), reads=["imp", "ppad"], writes=["imp"])
                    P.op("dve", lambda e, v=v: e.tensor_tensor(out=imp[:, :], in0=imp[:, :], in1=v(3), op=ALU.add), reads=["imp", "ppad"], writes=["imp"])
                    bo = CF_BON + ((T - 2) * 4 + st) * 32
                    P.op("dve", lambda e, bo=bo: e.tensor_tensor(out=imp[:, :], in0=imp[:, :], in1=cf[:, bo:bo + 32], op=ALU.add), reads=["imp", "cf"], writes=["imp"])
                    P.op("dve", lambda e: e.max(out=m8[:, 0:8], in_=imp[:, :]), reads=["imp"], writes=["m8"])
                    P.op("dve", lambda e: e.match_replace(out=wk[:, :], in_to_replace=m8[:, 0:8], in_values=imp[:, :], imm_value=-3.0e38), reads=["imp", "m8"], writes=["wk"])
                    P.op("dve", lambda e: e.max(out=m8[:, 8:16], in_=wk[:, :]), reads=["wk"], writes=["m8"])
                    P.op("dve", lambda e: e.tensor_scalar(out=wk[:, :], in0=imp[:, :], scalar1=m8[:, 15:16], scalar2=None, op0=ALU.is_ge), reads=["imp", "m8"], writes=["wk"])
                    P.op("dve", lambda e: e.tensor_scalar(out=negm[:, :], in0=wk[:, :], scalar1=-NEGM, scalar2=NEGM, op0=ALU.mult, op1=ALU.add), reads=["wk"], writes=["negm"])
                    hh = next_tr()
                    P.op("pe", lambda e, hh=hh: e.transpose(ptr[0:32, hh, 0:128], negm[:, :], ident_b), reads=["negm", "cbf"], writes=["ptr%d" % hh])
                    P.op("act", lambda e, hh=hh, st=st, g=g: e.activation(out=negmT[:, g, st * 128:(st + 1) * 128], in_=ptr[0:32, hh, 0:128], func=AF.Identity),
                         reads=["ptr%d" % hh], writes=["negmT"])
            for r in range(4):
                hq = 4 * g + r
                if not dry:
                    nsa_head(b, T, g, r, hq)
                gate_a(O_NG + hq * 128)
                if not dry:
                    def tail2(hq=hq):
                        hh = next_tr()
                        transpose4(lambda j: accb[:, j, :], ["accb"], hh)
                        P.op("act", lambda e, hh=hh, hq=hq: e.activation(out=yT[:, 8 + hq, :], in_=ptr[:, hh, :], func=AF.Identity, scale=0.5),
                             reads=["ptr%d" % hh], writes=["yT%d" % (8 + hq)])
                        gate_b(yT[:, 8 + hq, :], "yT%d" % (8 + hq))
                    pending.append(tail2)

        flush_pending()
        if debug and b == 0 and T == dbgT:
            if dry:
                return
            P.op("dve", lambda e: e.tensor_copy(out=xt[:, 2, :].rearrange("p (a c) -> p a c", c=TT)[:, 0:4, :], in_=yT[:, 0:4, :]), reads=["yT%d" % i for i in range(16)], writes=["xt2"])
            P.op("dve", lambda e: e.tensor_copy(out=xt[:, 1, :].rearrange("p (a c) -> p a c", c=TT)[:, 0:4, :], in_=yT[:, 8:12, :]), reads=["yT%d" % i for i in range(16)], writes=["xt1"])
            P.op("dve", lambda e: e.tensor_copy(out=xt[0:32, 0, 0:1024], in_=negmT[:, :, :].rearrange("p g t -> p (g t)")), reads=["negmT"], writes=["xt0"])
            P.op("dve", lambda e: e.tensor_copy(out=xt[:, 0, 1024:1280], in_=KcT[:, :, :].rearrange("p g t -> p (g t)")), reads=["KcT"], writes=["xt0"])
            P.dma("sp", lambda e, s: e.dma_start(out=dbg["d_m2"][:, 0:2048], in_=xt[:, 0, :]).then_inc(s, 16), reads=["xt0"], semkey="dbg")
            P.dma("sp", lambda e, s: e.dma_start(out=dbg["d_hT"][:, 0:4 * TT], in_=xt[:, 3, :]).then_inc(s, 16), reads=["xt3"], semkey="dbg")
            P.dma("sp", lambda e, s: e.dma_start(out=dbg["d_yT"][:, 0:4 * TT], in_=xt[:, 2, :]).then_inc(s, 16), reads=["xt2"], semkey="dbg")
            P.dma("sp", lambda e, s: e.dma_start(out=dbg["d_yT"][:, 4 * TT:8 * TT], in_=xt[:, 1, :]).then_inc(s, 16), reads=["xt1"], semkey="dbg")
            return
        ykeys = ["yT%d" % i for i in range(16)]
        for cc in range(16):
            if cc % 2 == 0:
                bA, kA, bB, kB, bC, kC, bD, kD = pp[0], "pp0", pp[1], "pp1", psc[0], "psc0", psc[1], "psc1"
            else:
                bA, kA, bB, kB, bC, kC, bD, kD = pofl[0], "po0", pofl[1], "po1", ptrf[0], "ptr0", ptrf[1], "ptr1"
            wt2, wk2 = W.next(("two", cc * 128))
            if wt2 is not None:
                w2v = wt2[:, :].rearrange("p (u k c) -> p u k c", u=2, c=128)
                def fA(e, w2v=w2v, bA=bA, bB=bB):
                    ins = None
                    for u, bt in ((0, bA), (1, bB)):
                        for kc in range(8):
                            ins = e.matmul(bt[:, :], lhsT=w2v[:, u, kc, :], rhs=yT[:, u * 8 + kc, :], start=(kc == 0), stop=(kc == 7))
                    return ins
                P.op("pe", fA, reads=[wk2] + ykeys, writes=[kA, kB])
            wta, wka = W.next(("cols", "w_in", O_MA + cc * 128, 128, 16))
            if wta is not None:
                wa3 = wta[:, :].rearrange("p (k c) -> p k c", c=128)
                def fC(e, wa3=wa3, bC=bC):
                    ins = None
                    for kc in range(16):
                        ins = e.matmul(bC[:, :], lhsT=wa3[:, kc, :], rhs=hT[:, kc, :], start=(kc == 0), stop=(kc == 15))
                    return ins
                P.op("pe", fC, reads=[wka, "hT"], writes=[kC])
            wtb, wkb = W.next(("cols", "w_in", O_MB + cc * 128, 128, 16))
            if wtb is None:
                continue
            wb3 = wtb[:, :].rearrange("p (k c) -> p k c", c=128)
            def fD(e, wb3=wb3, bD=bD):
                ins = None
                for kc in range(16):
                    ins = e.matmul(bD[:, :], lhsT=wb3[:, kc, :], rhs=hT[:, kc, :], start=(kc == 0), stop=(kc == 15))
                return ins
            P.op("pe", fD, reads=[wkb, "hT"], writes=[kD])
            P.op("act", lambda e, bC=bC: e.activation(out=ta[:, :], in_=bC[:, :], func=AF.Tanh, scale=0.5), reads=[kC], writes=["r1"])
            P.op("dve", lambda e, bA=bA: e.scalar_tensor_tensor(out=ta[:, :], in0=ta[:, :], scalar=1.0, in1=bA[:, :], op0=ALU.add, op1=ALU.mult), reads=["r1", kA], writes=["r1"])
            P.op("act", lambda e, bD=bD: e.activation(out=tb2[:, :], in_=bD[:, :], func=AF.Tanh, scale=0.5), reads=[kD], writes=["r2"])
            P.op("dve", lambda e, bB=bB: e.scalar_tensor_tensor(out=tb2[:, :], in0=tb2[:, :], scalar=1.0, in1=bB[:, :], op0=ALU.add, op1=ALU.mult), reads=["r2", kB], writes=["r2"])
            P.op("pool", lambda e, cc=cc: e.tensor_tensor(out=m2[:, cc, :], in0=ta[:, :], in1=tb2[:, :], op=ALU.add), reads=["r1", "r2"], writes=["xn%d" % (cc // 4)])
        if dry:
            for ct in range(4):
                for pc in range(4):
                    W.next(("wout", ct, pc))
            return
        P.dma("sp", lambda e, s: e.dma_start(out=gfin_bc[:, :], in_=g_fin[0, :].partition_broadcast(128)).then_inc(s, 16), writes=["hT"], semkey="gf")
        banks = [(pp[0], "pp0"), (pp[1], "pp1"), (psc[0], "psc0"), (psc[1], "psc1")]
        for ct in range(4):
            for pc in range(4):
                wt, wk_ = W.next(("wout", ct, pc))
                w3 = wt[:, :].rearrange("p (k c) -> p k c", c=512)
                for st in range(4):
                    bt, bk = banks[st]
                    def f(e, w3=w3, bt=bt, st=st, pc=pc):
                        ins = None
                        for kl in range(4):
                            kc = pc * 4 + kl
                            ins = e.matmul(bt[:, :], lhsT=m2[:, kc, st * 128:(st + 1) * 128], rhs=w3[:, kl, :], start=(kc == 0), stop=(kc == 15))
                        return ins
                    P.op("pe", f, reads=[wk_, "xn0", "xn1", "xn2", "xn3"], writes=[bk])
            for st in range(4):
                bt, bk = banks[st]
                P.op("dve", lambda e, bt=bt, ct=ct: e.tensor_tensor(out=r1[:, :], in0=bt[:, :], in1=gate_bc1[:, ct * 512:(ct + 1) * 512], op=ALU.mult),
                     reads=[bk, "gate_bc"], writes=["r1"])
                P.op("pool", lambda e, st=st, ct=ct: e.tensor_tensor(out=xt[:, st, ct * 512:(ct + 1) * 512], in0=xt[:, st, ct * 512:(ct + 1) * 512], in1=r1[:, :], op=ALU.add),
                     reads=["r1", "xt%d" % st], writes=["xt%d" % st])
        for st in range(4):
            P.op("act", lambda e, st=st: e.activation(out=xn[:, st, :], in_=xt[:, st, :], func=AF.Square, accum_out=ss4[:, st:st + 1]),
                 reads=["xt%d" % st], writes=["xn%d" % st, "ss4"])
        P.op("act", lambda e: e.activation(out=rs4[:, :], in_=ss4[:, :], func=AF.Sqrt, bias=ceps[:, 0:1], scale=1.0 / D), reads=["ss4", "ceps"], writes=["rs4"])
        P.op("dve", lambda e: e.reciprocal(out=rs4[:, :], in_=rs4[:, :]), reads=["rs4"], writes=["rs4"])
        for st in range(4):
            P.op("dve", lambda e, st=st: e.scalar_tensor_tensor(out=xt[:, st, :], in0=xt[:, st, :], scalar=rs4[:, st:st + 1], in1=gfin_bc[:, :], op0=ALU.mult, op1=ALU.mult),
                 reads=["xt%d" % st, "rs4", "hT"], writes=["xt%d" % st])
            P.dma("sp", lambda e, s, st=st: e.dma_start(out=out_d[b, tok0 + st * 128:tok0 + (st + 1) * 128, :], in_=xt[:, st, :]).then_inc(s, 16),
                  reads=["xt%d" % st], semkey="o%d" % st)
            if T + 1 < NT and not debug and STAGE >= 99:
                nt0 = tok0 + TT
                P.dma("sp", lambda e, s, st=st, nt0=nt0: e.dma_start(out=xt[:, st, :], in_=x_d[b, nt0 + st * 128:nt0 + (st + 1) * 128, :]).then_inc(s, 16),
                      writes=["xt%d" % st], semkey="x%d" % st)
                xpre[0] = True

    pt_i = [0]
    xpre = [False]

    def nsa_head(b, T, g, r, hq):
        qh = qT[r]
        qk = "qT%d" % r
        first = [True]

        def combine(br, bi):
            pk, dk = "po%d" % bi, "pp%d" % bi
            pof = pofl[bi]
            P.op("dve", lambda e: e.tensor_scalar(out=den4[:, :], in0=pp[bi][:, 0:4], scalar1=1e-30, scalar2=None, op0=ALU.max), reads=[dk], writes=["den4"])
            P.op("dve", lambda e: e.reciprocal(out=den4[:, :], in_=den4[:, :]), reads=["den4"], writes=["den4"])
            P.op("dve", lambda e: e.tensor_tensor(out=den4[:, :], in0=den4[:, :], in1=gates[:, :, hq * 3 + br], op=ALU.mult), reads=["den4", "gates"], writes=["den4"])
            for qi in range(4):
                o_ap = pof[:, qi * 128:(qi + 1) * 128]
                if br == 0:
                    P.op("act", lambda e, o_ap=o_ap, qi=qi: e.activation(out=acc[:, qi, :], in_=o_ap, func=AF.Identity, scale=den4[:, qi:qi + 1]), reads=[pk, "den4"], writes=["acc"])
                elif br == 1:
                    P.op("dve", lambda e, o_ap=o_ap, qi=qi: e.scalar_tensor_tensor(out=acc[:, qi, :], in0=o_ap, scalar=den4[:, qi:qi + 1], in1=acc[:, qi, :], op0=ALU.mult, op1=ALU.add),
                         reads=[pk, "den4", "acc"], writes=["acc"])
                else:
                    P.op("dve", lambda e, o_ap=o_ap, qi=qi: e.scalar_tensor_tensor(out=accb[:, qi, :], in0=o_ap, scalar=den4[:, qi:qi + 1], in1=acc[:, qi, :], op0=ALU.mult, op1=ALU.add),
                         reads=[pk, "den4", "acc"], writes=["accb"])

        pv_q = []
        npush = [0]

        def push_pv(fn):
            while pv_q:
                pv_q.pop(0)()
            pv_q.append(fn)
            npush[0] += 1
            if npush[0] == 3:
                flush_pending()

        nk = 32 * (T + 1)
        ps_t, ps_k = next_psc()
        def fsc(e, ps_t=ps_t):
            e.matmul(ps_t[0:nk, :], lhsT=KcT[:, g, 0:nk], rhs=qh[:, :], start=True, stop=False)
            return e.matmul(ps_t[0:nk, :], lhsT=cbf[0:nk, CB_ID:CB_ID + nk], rhs=cbf[0:nk, CB_VT + T * TT:CB_VT + (T + 1) * TT], start=False, stop=True)
        P.op("pe", fsc, reads=["KcT", qk, "cbf"], writes=[ps_k])
        pi = pt_i[0] % 2
        pt_i[0] += 1
        ptile, pkey = PT[pi], "PT%d" % pi
        P.op("act", lambda e, ps_t=ps_t, ptile=ptile: e.activation(out=ptile[0:nk, :], in_=ps_t[0:nk, :], func=AF.Exp, scale=SCALE), reads=[ps_k], writes=[pkey])
        bi0 = br_i[0] % 2
        br_i[0] += 1
        def fpv(e, ptile=ptile, bi0=bi0):
            ins = None
            for qi in range(4):
                e.matmul(pofl[bi0][:, qi * 128:(qi + 1) * 128], lhsT=ptile[0:nk, qi * 128:(qi + 1) * 128], rhs=Vc[0:nk, g, 0:128], start=(qi == 0), stop=True, skip_group_check=True)
                ins = e.matmul(pp[bi0][:, qi:qi + 1], lhsT=ptile[0:nk, qi * 128:(qi + 1) * 128], rhs=ones_b[0:nk, 0:1], start=(qi == 0), stop=True, skip_group_check=True)
            return ins
        def pv0(fpv=fpv, pkey=pkey, bi0=bi0):
            P.op("pe", fpv, reads=[pkey, "Vc", "ones_b"], writes=["po%d" % bi0, "pp%d" % bi0])
            combine(0, bi0)
        push_pv(pv0)

        for br, (kTt, kkey, Vt, vkey) in ((1, (ksT, "ksT", Vs, "Vs")), (2, (kwT, "kwT", Vw, "Vw"))):
            kts = list(range(0, 4 * T + 4)) if br == 1 else list(range(max(0, 4 * T - 4), 4 * T + 4))
            bi = br_i[0] % 2
            br_i[0] += 1
            for kt in kts:
                i = kt - 4 * T
                if br == 1:
                    qlo, qhi = max(i, 0), 3
                else:
                    qlo, qhi = max(i, 0), min(i + 4, 3)
                c0, c1 = qlo * 128, (qhi + 1) * 128
                ps_t, ps_k = next_psc()
                use_sel = (br == 1 and T >= 2)
                tri_q = i if i >= 0 else None
                anti_q = (i + 4) if (br == 2 and i < 0 and i + 4 <= 3) else None
                def fs(e, ps_t=ps_t, kt=kt, c0=c0, c1=c1, use_sel=use_sel, tri_q=tri_q, anti_q=anti_q, kTt=kTt, br_=br):
                    more = use_sel or (tri_q is not None) or (anti_q is not None)
                    kcol = kt * 128 if br_ == 1 else ((kt // 4) % 2) * TT + (kt % 4) * 128
                    ins = e.matmul(ps_t[:, c0:c1], lhsT=kTt[:, g, kcol:kcol + 128], rhs=qh[:, c0:c1], start=True, stop=not more)
                    if use_sel:
                        m2_ = (tri_q is not None) or (anti_q is not None)
                        ins = e.matmul(ps_t[:, c0:c1], lhsT=cbf[0:32, CB_E + kt * 128:CB_E + (kt + 1) * 128], rhs=negmT[0:32, g, c0:c1], start=False, stop=not m2_)
                    if tri_q is not None:
                        ins = e.matmul(ps_t[:, tri_q * 128:(tri_q + 1) * 128], lhsT=ident_b, rhs=tri_b, start=False, stop=(anti_q is None))
                    if anti_q is not None:
                        ins = e.matmul(ps_t[:, anti_q * 128:(anti_q + 1) * 128], lhsT=ident_b, rhs=anti_b, start=False, stop=True)
                    return ins
                P.op("pe", fs, reads=[kkey, qk, "cbf", "negmT"], writes=[ps_k])
                pi = pt_i[0] % 2
                pt_i[0] += 1
                ptile, pkey = PT[pi], "PT%d" % pi
                P.op("act", lambda e, ps_t=ps_t, ptile=ptile, c0=c0, c1=c1: e.activation(out=ptile[:, c0:c1], in_=ps_t[:, c0:c1], func=AF.Exp, scale=SCALE),
                     reads=[ps_k], writes=[pkey])
                def fpv2(e, ptile=ptile, kt=kt, qlo=qlo, qhi=qhi, Vt=Vt, br=br, bi=bi, kt0=kts[0]):
                    ins = None
                    for qi in range(qlo, qhi + 1):
                        klast = 4 * T + qi
                        vslot = kt if br == 1 else ((kt // 4) % 2) * 4 + (kt % 4)
                        st_ = (kt == kt0 and qi == 0)
                        e.matmul(pofl[bi][:, qi * 128:(qi + 1) * 128], lhsT=ptile[:, qi * 128:(qi + 1) * 128], rhs=Vt[:, g, vslot, 0:128],
                                 start=st_, stop=(kt == klast), skip_group_check=True)
                        ins = e.matmul(pp[bi][:, qi:qi + 1], lhsT=ptile[:, qi * 128:(qi + 1) * 128], rhs=ones_b[:, 0:1], start=st_, stop=(kt == klast), skip_group_check=True)
                    return ins
                def pvk(fpv2=fpv2, pkey=pkey, vkey=vkey, last=(kt == kts[-1]), br=br, bi=bi):
                    P.op("pe", fpv2, reads=[pkey, vkey, "ones_b"], writes=["po%d" % bi, "pp%d" % bi])
                    if last:
                        combine(br, bi)
                push_pv(pvk)
        while pv_q:
            pv_q.pop(0)()

    import os as _os
    STAGE = float(_os.environ.get('KSTAGE', '99'))
    dbgT = int(_os.environ.get('KDBGT', '2'))

    wscr_box = [None]

    def whole():
        setup()
        gate_rows()
        if not W.collect:
            W.wscr = wscr_box[0]
        if STAGE <= 0:
            return
        for b in range(nseq):
            seq_start(b)
            if STAGE <= 1:
                return
            for T in range(NT):
                body(b, T)
                if STAGE < 99:
                    return
                if debug and b == 0 and T == dbgT:
                    return

    P.dry = True
    W.collect = True
    whole()
    W.finish_collect()
    wscr_box[0] = nc.dram_tensor("wscr", [len(W.uniq), 128, 2048], BF16, kind="Internal").ap()
    P.dry = False
    W.collect = False
    W.pos = 0
    pp_i[0] = psc_i[0] = tr_i[0] = pt_i[0] = 0
    whole()
    assert W.pos == len(W.specs) and W.posB == len(W.specsB), (W.pos, len(W.specs), W.posB, len(W.specsB))
    P.emit()
    nc._kstats = dict(n_ops=len(P.ops), sbuf=sbuf_used, sems=P.stats)
    return nc


def _in_maps(inputs, nseq, ncores):
    cbf, cf, _ = _consts()
    f = lambda a: np.ascontiguousarray(np.asarray(a, dtype=np.float32))
    x = f(inputs["x"])
    c = f(inputs["c"])
    pos = np.ascontiguousarray(np.asarray(inputs["positions"], dtype=np.int32))
    b_ada = f(inputs["b_ada"])[0]
    shared = {
        "w_ada": f(inputs["w_ada"])[0],
        "b_adaT": np.ascontiguousarray(b_ada.reshape(48, 128).T),
        "b_gate": np.ascontiguousarray(b_ada[None, 4096:6144]),
        "g_normT": np.ascontiguousarray(f(inputs["g_norm"])[0].reshape(16, 128).T),
        "w_in": f(inputs["w_in"])[0],
        "g_retT": np.ascontiguousarray(f(inputs["g_ret"])[0].reshape(8, 128).T),
        "w_ck1": f(inputs["w_ck1"])[0],
        "w_ck2": f(inputs["w_ck2"])[0],
        "pe_ckT": np.ascontiguousarray(f(inputs["pe_ck"])[0].T),
        "w_cv1": f(inputs["w_cv1"])[0],
        "w_cv2": f(inputs["w_cv2"])[0],
        "pe_cvT": np.ascontiguousarray(f(inputs["pe_cv"])[0].T),
        "w_up_ret": f(inputs["w_up_ret"])[0],
        "w_up_nsa": f(inputs["w_up_nsa"])[0],
        "w_out": f(inputs["w_out"])[0],
        "g_final": np.ascontiguousarray(f(inputs["g_final"])[None, :]),
        "cbf": cbf,
        "cf": cf,
    }
    maps = []
    for i in range(ncores):
        sl = slice(i * nseq, (i + 1) * nseq)
        cT = np.concatenate([c[i * nseq + b].reshape(16, 128).T for b in range(nseq)], axis=1)
        m = dict(shared)
        m["x"] = np.ascontiguousarray(x[sl])
        m["cT"] = np.ascontiguousarray(cT)
        m["pos"] = np.ascontiguousarray(pos[sl])
        maps.append(m)
    return maps


def kernel(**inputs):
    nc = build_nc(SEQ_PER_CORE)
    maps = _in_maps(inputs, SEQ_PER_CORE, NCORES)
    res = run_bass_kernel_spmd(nc, maps, core_ids=list(range(NCORES)))
    out = np.concatenate([np.asarray(r["out"]) for r in res.results], axis=0)
    return out.astype(np.float32)
```

```python
import contextlib
import math
import numpy as np
import ml_dtypes
import concourse.bass as bass
import concourse.mybir as mybir
from concourse.bass_utils import run_bass_kernel_spmd

F32 = mybir.dt.float32
BF16 = mybir.dt.bfloat16
I32 = mybir.dt.int32
ALU = mybir.AluOpType
AF = mybir.ActivationFunctionType

D = 2048
S = 2048
NB = 16
NCORES = 8
SEQ_PER_CORE = 2
TT = 512
NT = S // TT
INW = 11800
O_RQ, O_RK, O_RV, O_RG, O_NQ = 0, 1024, 2048, 3072, 4096
O_KC, O_VC, O_KS, O_VS, O_KW, O_VW = 5120, 5376, 5632, 5888, 6144, 6400
O_NG, O_BG, O_MA, O_MB = 6656, 7680, 7704, 9752
EPS = 1e-6
SCALE = 128.0 ** -0.5
NEGM = -30000.0
ENGS = ("pe", "act", "dve", "pool", "sp")
PSUM_KEYS = {"pp0", "pp1", "psc0", "psc1", "po0", "po1", "ptr0", "ptr1"}


class _Op:
    __slots__ = ("eng", "fn", "deps", "sig", "semkey", "inc", "tick")


class Prog:
    def __init__(self, nc):
        self.nc = nc
        self.ops = []
        self.last_w = {}
        self.readers = {}
        self.dry = False

    def _add(self, eng, fn, reads, writes, semkey, inc, sig):
        if self.dry:
            return
        writes = list(writes) + [k for k in reads if k in PSUM_KEYS]
        deps = set()
        lw = self.last_w
        for k in reads:
            w = lw.get(k)
            if w is not None:
                deps.add(w)
        for k in writes:
            w = lw.get(k)
            if w is not None:
                deps.add(w)
            for r in self.readers.get(k, ()):
                deps.add(r)
        o = _Op()
        o.eng, o.fn, o.deps, o.sig, o.semkey, o.inc = eng, fn, deps, sig, semkey, inc
        idx = len(self.ops)
        self.ops.append(o)
        for k in reads:
            self.readers.setdefault(k, []).append(idx)
        for k in writes:
            lw[k] = idx
            self.readers[k] = []

    def op(self, eng, fn, reads=(), writes=()):
        self._add(eng, fn, reads, writes, eng, 1, False)

    def dma(self, eng, fn, reads=(), writes=(), semkey=None, n=1):
        self._add(eng, fn, reads, writes, semkey, 16 * n, True)

    def emit(self):
        nc = self.nc
        ops = self.ops
        for o in ops:
            for d in o.deps:
                ops[d].sig = True
        counts = {}
        for o in ops:
            if o.sig:
                counts[o.semkey] = counts.get(o.semkey, 0) + o.inc
                o.tick = counts[o.semkey]
            else:
                o.tick = None
        semkeys = list(counts.keys())
        for e in ENGS:
            if e not in semkeys:
                semkeys.append(e)
        self.stats = dict(counts)
        with contextlib.ExitStack() as st:
            sems = {k: st.enter_context(nc.semaphore("s_" + str(k))) for k in semkeys}
            block = st.enter_context(nc.Block())
            per_eng = {e: [o for o in ops if o.eng == e] for e in ENGS}

            def run(engine, ename):
                waited = {}
                for o in per_eng[ename]:
                    need = {}
                    for d in o.deps:
                        dd = ops[d]
                        if dd.tick > need.get(dd.semkey, 0):
                            need[dd.semkey] = dd.tick
                    for k, v in need.items():
                        if v > waited.get(k, 0):
                            engine.wait_ge(sems[k], v)
                            waited[k] = v
                    if o.semkey == ename:
                        ins = o.fn(engine)
                        if o.sig:
                            ins.then_inc(sems[ename], 1)
                    else:
                        o.fn(engine, sems[o.semkey])
                if ename == "sp":
                    for k, v in counts.items():
                        if v > waited.get(k, 0):
                            engine.wait_ge(sems[k], v)

            block.tensor(lambda e: run(e, "pe"))
            block.scalar(lambda e: run(e, "act"))
            block.vector(lambda e: run(e, "dve"))
            block.gpsimd(lambda e: run(e, "pool"))
            block.sync(lambda e: run(e, "sp"))


def _consts():
    bf = ml_dtypes.bfloat16
    H = 8
    lg = np.log1p(-np.exp2(-5.0 - np.arange(H, dtype=np.float64)))
    idx = np.arange(128, dtype=np.float64)
    cb = np.zeros((128, 0), np.float32)
    parts = {}
    ident = np.eye(128, dtype=np.float32)
    swp = np.zeros((128, 128), np.float32)
    for d in range(128):
        swp[(d + 64) % 128, d] = 1.0
    k = idx[:, None]
    q = idx[None, :]
    tri = np.where(k <= q, 0.0, NEGM).astype(np.float32)
    anti = np.where(k > q, 0.0, NEGM).astype(np.float32)
    dm = np.zeros((128, H, 128), np.float32)
    xi = np.zeros((128, H, 128), np.float32)
    for h in range(H):
        diff = idx[None, :] - idx[:, None]
        dm[:, h, :] = np.where(diff >= 0, np.exp(lg[h] * np.maximum(diff, 0.0)), 0.0) * SCALE
        xi[:, h, :] = np.exp(lg[h] * (idx + 1))[None, :]
    E = np.zeros((128, 16, 128), np.float32)
    for kt in range(16):
        for kk in range(128):
            E[2 * kt + kk // 64, kt, kk] = 1.0
    t = np.arange(S)[None, :]
    n1 = np.arange(128)[:, None]
    validT = np.where((n1 >= 1) & (16 * n1 + 15 <= t), 0.0, NEGM).astype(np.float32)
    negb = np.zeros((128, 2, 4, 128), np.float32)
    bonus = np.zeros((128, 2, 4, 32), np.float32)
    for tm in range(2):
        for st in range(4):
            tt = 1024 + tm * 512 + st * 128 + np.arange(128)
            nn = np.arange(128)[None, :]
            v = (nn >= 1) & (16 * nn + 15 <= tt[:, None])
            negb[:, tm, st, :] = np.where(v, 0.0, NEGM)
            tb = (tt // 64)[:, None]
            j = np.arange(32)[None, :]
            forced = (j == 0) | (j == tb) | (j == tb - 1)
            bonus[:, tm, st, :] = np.where(j <= tb, np.where(forced, 1.0e4, 0.0), -1.0e30)
    cbf = np.concatenate([ident, swp, tri, anti, dm.reshape(128, -1), xi.reshape(128, -1),
                          E.reshape(128, -1), validT, negb.reshape(128, -1)], axis=1).astype(bf)
    inv = np.exp(np.arange(0, 128, 2, dtype=np.float32) * np.float32(-math.log(10000.0) / 128)).astype(np.float32)
    inv = np.concatenate([inv, inv])
    sgn = np.concatenate([-np.ones(64), np.ones(64)]).astype(np.float32)
    zeta = np.zeros((128, H), np.float32)
    for h in range(H):
        zeta[:, h] = np.exp(lg[h] * (127 - idx)) * SCALE
    oh = np.zeros((128, 256), np.float32)
    oh[0, 0:128] = 1.0
    oh[1, 128:256] = 1.0
    cf = np.concatenate([inv[:, None], sgn[:, None], zeta, bonus.reshape(128, -1), oh], axis=1).astype(np.float32)
    decay = [float(np.exp(lg[h] * 128)) for h in range(H)]
    return cbf, cf, decay


CB_ID, CB_SW, CB_TRI, CB_ANTI, CB_DM, CB_XI, CB_E, CB_VT, CB_NB = 0, 128, 256, 384, 512, 1536, 2560, 4608, 6656
CB_N = 6656 + 1024
CF_INV, CF_SGN, CF_ZETA, CF_BON, CF_OH = 0, 1, 2, 10, 266
CF_N = 266 + 256


def build_nc(nseq=SEQ_PER_CORE, debug=False):
    nc = bass.Bass("TRN2", target_bir_lowering=False)
    _, _, DECAY = _consts()

    def din(name, shape, dt=F32):
        return nc.dram_tensor(name, list(shape), dt, kind="ExternalInput").ap()

    x_d = din("x", [nseq, S, D])
    cT_d = din("cT", [128, nseq * 16])
    pos_d = din("pos", [nseq, S], I32)
    w_ada = din("w_ada", [D, 3 * D])
    b_adaT = din("b_adaT", [128, 48])
    b_gate = din("b_gate", [1, D])
    g_normT = din("g_normT", [128, 16])
    w_in = din("w_in", [D, INW])
    g_retT = din("g_retT", [128, 8])
    w_ck1 = din("w_ck1", [4096, 256])
    w_ck2 = din("w_ck2", [256, 128])
    pe_ckT = din("pe_ckT", [128, 32])
    w_cv1 = din("w_cv1", [4096, 256])
    w_cv2 = din("w_cv2", [256, 128])
    pe_cvT = din("pe_cvT", [128, 32])
    w_ur = din("w_up_ret", [1024, D])
    w_un = din("w_up_nsa", [1024, D])
    w_out = din("w_out", [D, D])
    g_fin = din("g_final", [1, D])
    cbf_d = din("cbf", [128, CB_N], BF16)
    cf_d = din("cf", [128, CF_N])
    out_d = nc.dram_tensor("out", [nseq, S, D], F32, kind="ExternalOutput").ap()
    gsc_d = nc.dram_tensor("gsc", [nseq, D], F32, kind="Internal").ap()
    dbg = {}
    if debug:
        for nm, shp in (("d_hT", [128, 16 * TT]), ("d_yT", [128, 16 * TT]), ("d_m2", [128, 16 * TT])):
            dbg[nm] = nc.dram_tensor(nm, shp, F32, kind="ExternalOutput").ap()

    P = Prog(nc)
    base = [16512]

    def sb(name, shape, dt):
        nbytes = int(np.prod(shape[1:])) * (4 if dt in (F32, I32) else 2)
        off = base[0]
        base[0] = (off + nbytes + 31) // 32 * 32
        assert base[0] <= 229344, (name, base[0])
        return nc.alloc_sbuf_tensor_at(name, list(shape), dt, offset=off)

    def psum(name, shape, dt):
        return nc.alloc_psum_tensor(name, list(shape), dt)

    cbf = sb("cbf", [128, CB_N], BF16)
    cf = sb("cf", [128, CF_N], F32)
    hT_off = base[0]
    hT = sb("hT", [128, 16, TT], BF16)
    yT = sb("yT", [128, 16, TT], BF16)
    xt = sb("xt", [128, 4, D], F32)
    xn = sb("xn", [128, 4, D], BF16)
    wreg = base[0]
    wst = [sb("wst%d" % i, [128, 2048], F32) for i in range(2)]
    wbf = [sb("wbf%d" % i, [128, 2048], BF16) for i in range(2)]
    NSLOT = 6
    assert base[0] - wreg == NSLOT * 4096
    wsl = [nc.alloc_sbuf_tensor_at("wsl%d" % i, [128, 2048], BF16, offset=wreg + i * 4096) for i in range(NSLOT)]
    gate_bc1 = sb("gate_bc", [128, D], F32)
    gate_bc = [gate_bc1 for b in range(nseq)]
    gfin_bc = nc.alloc_sbuf_tensor_at("gfin_bc", [128, D], F32, offset=hT_off)
    cosT = sb("cosT", [128, TT], F32)
    sinT = sb("sinT", [128, TT], F32)
    ksT = sb("ksT", [128, 2, S], BF16)
    kwT = sb("kwT", [128, 2, 2 * TT], BF16)
    Vs = sb("Vs", [128, 2, 16, 130], BF16)
    Vw = sb("Vw", [128, 2, 8, 130], BF16)
    kroll = sb("kroll", [128, 2, 16 + TT], BF16)
    vroll = sb("vroll", [128, 2, 16 + TT], BF16)
    KcT = sb("KcT", [128, 2, 128], BF16)
    Vc = sb("Vc", [128, 2, 130], BF16)
    state = sb("state", [128, 8, 128], F32)
    state_b = sb("state_b", [128, 8, 128], BF16)
    modT = sb("modT", [128, 32, nseq], F32)
    sc1 = sb("sc1", [128, 16, nseq], F32)
    gnT = sb("gnT", [128, 16], F32)
    grT = sb("grT", [128, 8], F32)
    badaT = sb("badaT", [128, 48], F32)
    cT = sb("cT", [128, nseq * 16], F32)
    cTt = sb("cTt", [128, nseq * 16], F32)
    siluc = sb("siluc", [128, 16, nseq], BF16)
    peT = [sb("peT%d" % i, [128, 32], F32) for i in range(2)]
    peTb = [sb("peTb%d" % i, [128, 32], BF16) for i in range(2)]
    w2st = [sb("w2st%d" % i, [128, 2, 128], F32) for i in range(2)]
    w2bf = [sb("w2bf%d" % i, [128, 2, 128], BF16) for i in range(2)]
    cpi = sb("cpi", [128, 1], F32)
    ones_b = sb("ones_b", [128, 128], BF16)
    ceps = sb("ceps", [128, 1], F32)
    ss4 = sb("ss4", [128, 4], F32)
    rs4 = sb("rs4", [128, 4], F32)
    angI = sb("angI", [128, TT], I32)
    pos_i = angI
    qT = [sb("qT%d" % i, [128, TT], BF16) for i in range(4)]
    kT = sb("kT", [128, TT], BF16)
    qxT = sb("qxT", [128, TT], BF16)
    vT = sb("vT", [128, TT], BF16)
    xb = sb("xb", [128, TT], BF16)
    r1 = sb("r1", [128, TT], F32)
    r2 = sb("r2", [128, TT], F32)
    Vr = sb("Vr", [128, 4, 128], BF16)
    Kz = sb("Kz", [128, 4, 128], BF16)
    scm4 = sb("scm4", [128, 4, 128], BF16)
    sb3 = sb("sb3", [128, 3, 128], BF16)
    bst = sb("bst", [128, 4, 6], F32)
    mv = sb("mv", [128, 4, 2], F32)
    nmr = sb("nmr", [128, 4], F32)
    on = sb("on", [128, 4, 128], BF16)
    tg = sb("tg", [128, TT], F32)
    gg = sb("gg", [128, TT], BF16)
    gates = sb("gates", [128, 4, 24], F32)
    hb = sb("hb", [128, 2, 2], F32)
    hbh = sb("hbh", [128, 2, 2], F32)
    hx = sb("hx", [128, 64], F32)
    ht = sb("ht", [128, 64], F32)
    shT = sb("shT", [128, 2, 64], BF16)
    kcx = sb("kcx", [128, 64], BF16)
    vct = sb("vct", [32, 2, 130], BF16)
    PT = [sb("PT%d" % i, [128, TT], BF16) for i in range(2)]
    stmp = sb("stmp", [128, 128], F32)
    etmp = sb("etmp", [128, 128], F32)
    rsum = sb("rsum", [128, 1], F32)
    rinv = sb("rinv", [128, 1], F32)
    ppad = sb("ppad", [128, 4, 132], F32)
    imp = sb("imp", [128, 32], F32)
    m8 = sb("m8", [128, 16], F32)
    wk = sb("wk", [128, 32], F32)
    negm = sb("negm", [128, 32], BF16)
    negm4 = sb("negm4", [128, 4, 32], BF16)
    negmT = sb("negmT", [32, 2, TT], BF16)
    den = sb("den", [128, 1], F32)
    den4 = sb("den4", [128, 4], F32)
    coef = sb("coef", [128, 1], F32)
    acc = sb("acc", [128, 4, 128], F32)
    accb = sb("accb", [128, 4, 128], BF16)
    angA, angB, angC, ta, tb2 = r1, r2, tg, r1, r2
    sbuf_used = base[0]

    pp = [psum("pp%d" % i, [128, 512], F32) for i in range(2)]
    psc = [psum("psc%d" % i, [128, 512], F32) for i in range(2)]
    po = [psum("po%d" % i, [128, 2, 256], F32) for i in range(2)]
    ptrs = [psum("ptr%d" % i, [128, 1024], BF16) for i in range(2)]

    pm = po[1][:, :, :].rearrange("p a c -> p (a c)")
    pofl = [po[i][:, :, :].rearrange("p a c -> p (a c)") for i in range(2)]
    ptrf = [ptrs[i][:, :].bitcast(F32) for i in range(2)]
    br_i = [0]

    class _Ptr:
        def __getitem__(self, idx):
            p_, h_, c_ = idx
            return ptrs[h_][p_, c_] if not (isinstance(c_, slice) and c_ == slice(None)) else ptrs[h_][p_, 0:512]
    ptr = _Ptr()

    class _M2:
        def __getitem__(self, idx):
            p_, cc, t_ = idx
            return xn[p_, cc // 4, (cc % 4) * TT + (t_.start or 0):(cc % 4) * TT + (t_.stop if t_.stop is not None else TT)]
    m2 = _M2()

    def cB(off, n):
        return cbf[:, off:off + n]

    ident_b = cB(CB_ID, 128)
    swap_b = cB(CB_SW, 128)
    tri_b = cB(CB_TRI, 128)
    anti_b = cB(CB_ANTI, 128)

    def bc(ap2d, n):
        a = ap2d
        return bass.AP(tensor=a.tensor, offset=a.offset, ap=[list(a.ap[0]), [0, n]] + [list(z) for z in a.ap[1:]])

    class WS:
        def __init__(self):
            self.specs = []
            self.specsB = []
            self.pos = 0
            self.issued = 0
            self.posB = 0
            self.issuedB = 0
            self.collect = True
            self.gid = {}

        def _issue(self, k):
            spec = self.specs[k]
            slot = k % 2
            kind = spec[0]
            st_t, bf_t = wst[slot], wbf[slot]
            if kind == "cols":
                _, w, col0, ncols, nk = spec
                w = wmap[w]
                dst = st_t[:, 0:nk * ncols].rearrange("p (k c) -> p k c", c=ncols)
                src = w[0:nk * 128, col0:col0 + ncols].rearrange("(k p) c -> p k c", p=128)
                half = nk // 2
                def f(e, s, dst=dst, src=src, half=half, nk=nk):
                    e.dma_start(out=dst[:, 0:half, :], in_=src[:, 0:half, :]).then_inc(s, 16)
                    e.dma_start(out=dst[:, half:nk, :], in_=src[:, half:nk, :]).then_inc(s, 16)
                P.dma("sp", f, writes=["wst%d" % slot], semkey="w%d" % slot, n=2)
                n = nk * ncols
            elif kind == "two":
                _, col0 = spec
                d0 = st_t[:, 0:1024].rearrange("p (k c) -> p k c", c=128)
                d1 = st_t[:, 1024:2048].rearrange("p (k c) -> p k c", c=128)
                s0 = w_ur[:, col0:col0 + 128].rearrange("(k p) c -> p k c", p=128)
                s1 = w_un[:, col0:col0 + 128].rearrange("(k p) c -> p k c", p=128)
                def f(e, s, d0=d0, d1=d1, s0=s0, s1=s1):
                    e.dma_start(out=d0, in_=s0).then_inc(s, 16)
                    e.dma_start(out=d1, in_=s1).then_inc(s, 16)
                P.dma("sp", f, writes=["wst%d" % slot], semkey="w%d" % slot, n=2)
                n = 2048
            elif kind == "wout":
                _, ct, pc = spec
                dst = st_t[:, :].rearrange("p (k c) -> p k c", c=512)
                src = w_out[pc * 512:(pc + 1) * 512, ct * 512:(ct + 1) * 512].rearrange("(k p) c -> p k c", p=128)
                def f(e, s, dst=dst, src=src):
                    e.dma_start(out=dst[:, 0:2, :], in_=src[:, 0:2, :]).then_inc(s, 16)
                    e.dma_start(out=dst[:, 2:4, :], in_=src[:, 2:4, :]).then_inc(s, 16)
                P.dma("sp", f, writes=["wst%d" % slot], semkey="w%d" % slot, n=2)
                n = 2048
            elif kind == "w1":
                _, w, pc = spec
                w = wmap[w]
                dst = st_t[:, :].rearrange("p (i c) -> p i c", c=256)
                src = w[pc * 1024:(pc + 1) * 1024, :].rearrange("(i p) c -> p i c", p=128)
                def f(e, s, dst=dst, src=src):
                    e.dma_start(out=dst[:, 0:4, :], in_=src[:, 0:4, :]).then_inc(s, 16)
                    e.dma_start(out=dst[:, 4:8, :], in_=src[:, 4:8, :]).then_inc(s, 16)
                P.dma("sp", f, writes=["wst%d" % slot], semkey="w%d" % slot, n=2)
                n = 2048
            if k % 2 == 0:
                P.op("dve", lambda e, bf_t=bf_t, st_t=st_t, n=n: e.tensor_copy(out=bf_t[:, 0:n], in_=st_t[:, 0:n]),
                     reads=["wst%d" % slot], writes=["wbf%d" % slot])
            else:
                P.op("act", lambda e, bf_t=bf_t, st_t=st_t, n=n: e.activation(out=bf_t[:, 0:n], in_=st_t[:, 0:n], func=AF.Identity),
                     reads=["wst%d" % slot], writes=["wbf%d" % slot])

        def nextA(self, spec):
            k = self.pos
            assert self.specs[k] == spec, (k, self.specs[k], spec)
            while self.issued < min(k + 2, len(self.specs)):
                self._issue(self.issued)
                self.issued += 1
            self.pos += 1
            slot = k % 2
            return wbf[slot], "wbf%d" % slot

        def finish_collect(self):
            for sp_ in self.specsB:
                if sp_ not in self.gid:
                    self.gid[sp_] = len(self.gid)
            self.uniq = sorted(self.gid, key=lambda z: self.gid[z])
            self.specs = self.specs + self.uniq
            self.scr_keys = ["scr%d" % i for i in range(len(self.uniq))]

        def preconvert(self, wscr):
            self.wscr = wscr
            for u in self.uniq:
                k = self.pos
                wt, wk_ = self.nextA(u)
                gid_ = self.gid[u]
                P.dma("sp", lambda e, s, wt=wt, gid_=gid_: e.dma_start(out=wscr[gid_, :, :], in_=wt[:, :]).then_inc(s, 16),
                      reads=[wk_], writes=["scr%d" % gid_], semkey="ws%d" % (k % 2))

        def _issueB(self, k):
            gid_ = self.gid[self.specsB[k]]
            slot = k % NSLOT
            wscr = self.wscr
            extra = ["wst0", "wst1", "wbf0", "wbf1"] if k < len(self.uniq) + NSLOT else []
            P.dma("sp", lambda e, s, slot=slot, gid_=gid_: e.dma_start(out=wsl[slot][:, :], in_=wscr[gid_, :, :]).then_inc(s, 16),
                  reads=self.scr_keys, writes=["wsl%d" % slot] + extra, semkey="wl%d" % slot)

        def next(self, spec):
            is_a = (spec[0] == "cols" and spec[1] == "w_ada")
            if self.collect:
                (self.specs if is_a else self.specsB).append(spec)
                return None, None
            if is_a:
                return self.nextA(spec)
            k = self.posB
            assert self.specsB[k] == spec, (k, self.specsB[k], spec)
            nu = len(self.uniq)
            if k < nu:
                assert self.uniq[k] == spec
                ka = self.pos
                wt, wk_ = self.nextA(spec)
                gid_ = self.gid[spec]
                wscr = self.wscr
                P.dma("sp", lambda e, s, wt=wt, gid_=gid_: e.dma_start(out=wscr[gid_, :, :], in_=wt[:, :]).then_inc(s, 16),
                      reads=[wk_], writes=["scr%d" % gid_], semkey="ws%d" % (ka % 2))
                self.posB += 1
                self.issuedB = max(self.issuedB, nu)
                return wt, wk_
            while self.issuedB < min(k + NSLOT, len(self.specsB)):
                self._issueB(self.issuedB)
                self.issuedB += 1
            self.posB += 1
            slot = k % NSLOT
            return wsl[slot], "wsl%d" % slot

    W = WS()
    wmap = {"w_in": w_in, "w_ada": w_ada, "w_ck1": w_ck1, "w_cv1": w_cv1}
    pp_i = [0]
    psc_i = [0]

    def next_pp():
        i = pp_i[0] % 2
        pp_i[0] += 1
        return pp[i], "pp%d" % i

    def next_psc():
        i = psc_i[0] % 2
        psc_i[0] += 1
        return psc[i], "psc%d" % i

    def proj_fm(col0, dst_fn):
        wt, wk_ = W.next(("cols", "w_in", col0, 128, 16))
        if wt is None:
            flush_pending()
            dst_fn(None, None)
            return
        ps_t, ps_k = next_pp()
        w3 = wt[:, :].rearrange("p (k c) -> p k c", c=128)
        def f(e, w3=w3, ps_t=ps_t):
            ins = None
            for kc in range(16):
                ins = e.matmul(ps_t[:, :], lhsT=w3[:, kc, :], rhs=hT[:, kc, :], start=(kc == 0), stop=(kc == 15))
            return ins
        P.op("pe", f, reads=[wk_, "hT"], writes=[ps_k])
        flush_pending()
        dst_fn(ps_t, ps_k)

    pending = []

    def flush_pending():
        while pending:
            pending.pop(0)()

    def rope_to(ps_t, ps_k, dst_ap, dst_key):
        if ps_t is None:
            return
        P.op("act", lambda e: e.activation(out=xb[:, :], in_=ps_t[:, :], func=AF.Identity), reads=[ps_k], writes=["xb"])
        P.op("pool", lambda e: e.tensor_tensor(out=r1[:, :], in0=xb[:, :], in1=cosT[:, :], op=ALU.mult), reads=["xb", "cosT"], writes=["r1"])

        def rest():
            P.op("pe", lambda e: e.matmul(pm[:, :], lhsT=swap_b, rhs=xb[:, :], start=True, stop=True), reads=["xb", "cbf"], writes=["po1"])
            P.op("dve", lambda e: e.tensor_tensor(out=r2[:, :], in0=pm[:, :], in1=sinT[:, :], op=ALU.mult), reads=["po1", "sinT"], writes=["r2"])
            P.op("pool", lambda e: e.tensor_tensor(out=dst_ap, in0=r1[:, :], in1=r2[:, :], op=ALU.add), reads=["r1", "r2"], writes=[dst_key])
        pending.append(rest)

    def copy_to(ps_t, ps_k, dst_ap, dst_key):
        if ps_t is None:
            return
        P.op("act", lambda e: e.activation(out=dst_ap, in_=ps_t[:, :], func=AF.Identity), reads=[ps_k], writes=[dst_key])

    def transpose4(src_fn, src_keys, half, pre=()):
        def f(e):
            ins = None
            for j in range(4):
                ins = e.transpose(ptr[:, half, j * 128:(j + 1) * 128], src_fn(j), ident_b)
            return ins
        P.op("pe", f, reads=list(src_keys) + ["cbf"], writes=["ptr%d" % half])

    tr_i = [0]

    def next_tr():
        i = tr_i[0] % 2
        tr_i[0] += 1
        return i

    def gate_a(col0):
        def g(ps_t, ps_k):
            if ps_t is None:
                return
            P.op("act", lambda e: e.activation(out=tg[:, :], in_=ps_t[:, :], func=AF.Tanh, scale=0.5), reads=[ps_k], writes=["tg"])
            P.op("dve", lambda e: e.scalar_tensor_tensor(out=gg[:, :], in0=tg[:, :], scalar=1.0, in1=ps_t[:, :], op0=ALU.add, op1=ALU.mult),
                 reads=["tg", ps_k], writes=["gg"])
        proj_fm(col0, g)

    def gate_b(ydst, ykey):
        P.op("pool", lambda e: e.tensor_tensor(out=ydst, in0=ydst, in1=gg[:, :], op=ALU.mult), reads=["gg", ykey], writes=[ykey])

    def setup():
        def ld(dst, src, key):
            P.dma("sp", lambda e, s: e.dma_start(out=dst, in_=src).then_inc(s, 16), writes=[key], semkey="ld_" + key)
        ld(cbf[:, :], cbf_d, "cbf")
        ld(cf[:, :], cf_d, "cf")
        ld(cT[:, :], cT_d, "cT")
        ld(badaT[:, :], b_adaT, "badaT")
        ld(gnT[:, :], g_normT, "gnT")
        ld(grT[:, :], g_retT, "grT")
        ld(peT[0][:, :], pe_ckT, "peT0")
        ld(peT[1][:, :], pe_cvT, "peT1")
        ld(w2st[0][:, :, :], w_ck2.rearrange("(m p) c -> p m c", p=128), "w2st0")
        ld(w2st[1][:, :, :], w_cv2.rearrange("(m p) c -> p m c", p=128), "w2st1")
        P.op("pool", lambda e: e.memset(cpi[:, :], float(np.pi)), writes=["cpi"])
        P.op("pool", lambda e: e.memset(ones_b[:, :], 1.0), writes=["ones_b"])
        P.op("pool", lambda e: e.memset(ceps[:, :], EPS), writes=["ceps"])
        P.op("pool", lambda e: e.memset(Vs[:, :, :, 128:130], 1.0), writes=["Vs"])
        P.op("pool", lambda e: e.memset(Vw[:, :, :, 128:130], 1.0), writes=["Vw"])
        P.op("pool", lambda e: e.memset(Vc[:, :, :], 0.0), writes=["Vc"])
        P.op("pool", lambda e: e.memset(vct[:, :, 128:130], 1.0), writes=["vct"])
        P.op("pool", lambda e: e.memset(KcT[:, :, :], 0.0), writes=["KcT"])
        P.op("pool", lambda e: e.memset(ppad[:, :, :], 0.0), writes=["ppad"])
        for i in range(2):
            P.op("pool", lambda e, i=i: e.tensor_copy(out=peTb[i][:, :], in_=peT[i][:, :]), reads=["peT%d" % i], writes=["peTb%d" % i])
            P.op("pool", lambda e, i=i: e.tensor_copy(out=w2bf[i][:, :, :], in_=w2st[i][:, :, :]), reads=["w2st%d" % i], writes=["w2bf%d" % i])
        P.op("pool", lambda e: e.tensor_scalar(out=grT[:, :], in0=grT[:, :], scalar1=0.5, scalar2=None, op0=ALU.mult), reads=["grT"], writes=["grT"])
        P.op("act", lambda e: e.activation(out=cTt[:, :], in_=cT[:, :], func=AF.Tanh, scale=0.5), reads=["cT"], writes=["cTt"])
        P.op("dve", lambda e: e.scalar_tensor_tensor(out=cTt[:, :], in0=cTt[:, :], scalar=1.0, in1=cT[:, :], op0=ALU.add, op1=ALU.mult),
             reads=["cTt", "cT"], writes=["cTt"])
        for b in range(nseq):
            P.op("dve", lambda e, b=b: e.tensor_scalar(out=siluc[:, :, b], in0=cTt[:, b * 16:(b + 1) * 16], scalar1=0.5, scalar2=None, op0=ALU.mult),
                 reads=["cTt"], writes=["siluc"])
        for grp in range(32):
            wt, wk_ = W.next(("cols", "w_ada", grp * 128, 128, 16))
            if wt is None:
                continue
            w3 = wt[:, :].rearrange("p (k c) -> p k c", c=128)
            def f(e, w3=w3):
                ins = None
                for kc in range(16):
                    ins = e.matmul(pm[:, 0:nseq], lhsT=w3[:, kc, :], rhs=siluc[:, kc, :], start=(kc == 0), stop=(kc == 15))
                return ins
            P.op("pe", f, reads=[wk_, "siluc"], writes=["po1"])
            P.op("dve", lambda e, grp=grp: e.tensor_scalar(out=modT[:, grp, :], in0=pm[:, 0:nseq], scalar1=badaT[:, grp:grp + 1], scalar2=None, op0=ALU.add),
                 reads=["po1", "badaT"], writes=["modT"])
        if W.collect:
            return
        for b in range(nseq):
            P.op("dve", lambda e, b=b: e.scalar_tensor_tensor(out=sc1[:, :, b], in0=modT[:, 16:32, b], scalar=1.0, in1=gnT[:, :], op0=ALU.add, op1=ALU.mult),
                 reads=["modT", "gnT"], writes=["sc1"])

    def gate_rows():
        grow = xt[0:nseq, 1, :]
        for grp in range(16):
            wt, wk_ = W.next(("cols", "w_ada", (32 + grp) * 128, 128, 16))
            if wt is None:
                continue
            w3 = wt[:, :].rearrange("p (k c) -> p k c", c=128)
            def f(e, w3=w3):
                ins = None
                for kc in range(16):
                    ins = e.matmul(pm[0:nseq, 128:256], lhsT=siluc[:, kc, :], rhs=w3[:, kc, :], start=(kc == 0), stop=(kc == 15))
                return ins
            P.op("pe", f, reads=[wk_, "siluc"], writes=["po1"])
            P.op("act", lambda e, grp=grp: e.activation(out=grow[:, grp * 128:(grp + 1) * 128], in_=pm[0:nseq, 128:256], func=AF.Identity),
                 reads=["po1"], writes=["xt1"])
        if W.collect:
            return
        P.dma("sp", lambda e, s: e.dma_start(out=gsc_d, in_=grow).then_inc(s, 16), reads=["xt1"], writes=["gsc"], semkey="gs")

    def seq_start(b):
        if W.collect:
            return
        P.dma("sp", lambda e, s: e.dma_start(out=gate_bc1[:, :], in_=gsc_d[b, :].partition_broadcast(128)).then_inc(s, 16), reads=["gsc"], writes=["gate_bc"], semkey="gb")
        P.dma("sp", lambda e, s: e.dma_start(out=xt[:, 0, :], in_=b_gate[0, :].partition_broadcast(128)).then_inc(s, 16), writes=["xt0"], semkey="x0")
        P.op("dve", lambda e: e.tensor_tensor(out=gate_bc1[:, :], in0=gate_bc1[:, :], in1=xt[:, 0, :], op=ALU.add), reads=["gate_bc", "xt0"], writes=["gate_bc"])
        P.op("pool", lambda e: e.tensor_scalar(out=gate_bc1[:, :], in0=gate_bc1[:, :], scalar1=0.5, scalar2=None, op0=ALU.mult), reads=["gate_bc"], writes=["gate_bc"])

    def sincos(src_add, dst, dkey, fold_sign):
        P.op("dve", lambda e: e.tensor_scalar(out=angB[:, :], in0=angA[:, :], scalar1=float(src_add), scalar2=float(1.0 / (2 * np.pi)), op0=ALU.add, op1=ALU.mult),
             reads=["r1"], writes=["r2"])
        P.op("dve", lambda e: e.tensor_copy(out=angI[:, :], in_=angB[:, :]), reads=["r2"], writes=["angI"])
        P.op("dve", lambda e: e.tensor_copy(out=angB[:, :], in_=angI[:, :]), reads=["angI"], writes=["r2"])
        P.op("dve", lambda e: e.tensor_scalar(out=angC[:, :], in0=angA[:, :], scalar1=float(src_add), scalar2=None, op0=ALU.add), reads=["r1"], writes=["tg"])
        P.op("dve", lambda e: e.scalar_tensor_tensor(out=angC[:, :], in0=angB[:, :], scalar=-float(2 * np.pi), in1=angC[:, :], op0=ALU.mult, op1=ALU.add),
             reads=["r2", "tg"], writes=["tg"])
        P.op("dve", lambda e: e.tensor_scalar(out=angB[:, :], in0=angC[:, :], scalar1=0.0, scalar2=float(2 * np.pi), op0=ALU.is_lt, op1=ALU.mult),
             reads=["tg"], writes=["r2"])
        P.op("dve", lambda e: e.tensor_tensor(out=angC[:, :], in0=angC[:, :], in1=angB[:, :], op=ALU.add), reads=["tg", "r2"], writes=["tg"])
        P.op("dve", lambda e: e.tensor_scalar(out=angC[:, :], in0=angC[:, :], scalar1=float(2 * np.pi), scalar2=None, op0=ALU.min), reads=["tg"], writes=["tg"])
        P.op("act", lambda e: e.activation(out=dst[:, :], in_=angC[:, :], func=AF.Sin, bias=cpi[:, 0:1], scale=-1.0), reads=["tg", "cpi"], writes=[dkey])
        if fold_sign:
            P.op("dve", lambda e: e.tensor_scalar(out=dst[:, :], in0=dst[:, :], scalar1=cf[:, CF_SGN:CF_SGN + 1], scalar2=None, op0=ALU.mult),
                 reads=[dkey, "cf"], writes=[dkey])

    def body(b, T):
        tok0 = T * TT
        dry = W.collect
        if not dry:
            if not xpre[0]:
                for st in range(4):
                    P.dma("sp", lambda e, s, st=st: e.dma_start(out=xt[:, st, :], in_=x_d[b, tok0 + st * 128:tok0 + (st + 1) * 128, :]).then_inc(s, 16),
                          writes=["xt%d" % st], semkey="x%d" % st)
            xpre[0] = False
            P.dma("sp", lambda e, s: e.dma_start(out=pos_i[:, :], in_=pos_d[b, tok0:tok0 + TT].partition_broadcast(128)).then_inc(s, 16),
                  writes=["angI"], semkey="pos")
            for st in range(4):
                P.op("act", lambda e, st=st: e.activation(out=xn[:, st, :], in_=xt[:, st, :], func=AF.Square, accum_out=ss4[:, st:st + 1]),
                     reads=["xt%d" % st], writes=["xn%d" % st, "ss4"])
            if STAGE <= 1.1:
                return
            P.op("act", lambda e: e.activation(out=rs4[:, :], in_=ss4[:, :], func=AF.Sqrt, bias=ceps[:, 0:1], scale=1.0 / D), reads=["ss4", "ceps"], writes=["rs4"])
            P.op("dve", lambda e: e.reciprocal(out=rs4[:, :], in_=rs4[:, :]), reads=["rs4"], writes=["rs4"])
            if STAGE <= 1.2:
                return
            for st in range(4):
                if st % 2 == 0:
                    P.op("act", lambda e, st=st: e.activation(out=xn[:, st, :], in_=xt[:, st, :], func=AF.Identity, scale=rs4[:, st:st + 1]),
                         reads=["xt%d" % st, "rs4"], writes=["xn%d" % st])
                else:
                    P.op("dve", lambda e, st=st: e.tensor_scalar(out=xn[:, st, :], in0=xt[:, st, :], scalar1=rs4[:, st:st + 1], scalar2=None, op0=ALU.mult),
                         reads=["xt%d" % st, "rs4"], writes=["xn%d" % st])
            if STAGE <= 1.3:
                return
            for fc in range(int(_os.environ.get('KFC', '16'))):
                h = next_tr()
                transpose4(lambda j, fc=fc: xn[:, j, fc * 128:(fc + 1) * 128], ["xn0", "xn1", "xn2", "xn3"], h)
                if _os.environ.get('KNODVE'):
                    continue
                kvar = _os.environ.get('KVAR', '3')
                if kvar == '1':
                    P.op("dve", lambda e, fc=fc, h=h: e.tensor_scalar(out=hT[:, fc, :], in0=ptr[:, h, :], scalar1=sc1[:, fc, b:b + 1], scalar2=None, op0=ALU.mult),
                         reads=["ptr%d" % h, "sc1", "modT"], writes=["hT"])
                    continue
                if kvar == '2':
                    P.op("dve", lambda e, fc=fc, h=h: e.tensor_copy(out=hT[:, fc, :], in_=ptr[:, h, :]),
                         reads=["ptr%d" % h, "sc1", "modT"], writes=["hT"])
                    continue
                if kvar == '3':
                    P.op("act", lambda e, fc=fc, h=h: e.activation(out=hT[:, fc, :], in_=ptr[:, h, :], func=AF.Identity, scale=sc1[:, fc, b:b + 1], bias=modT[:, fc, b:b + 1]),
                         reads=["ptr%d" % h, "sc1", "modT"], writes=["hT"])
                    continue
                P.op("dve", lambda e, fc=fc, h=h: e.tensor_scalar(out=hT[:, fc, :], in0=ptr[:, h, :], scalar1=sc1[:, fc, b:b + 1], scalar2=modT[:, fc, b:b + 1],
                                                                 op0=ALU.mult, op1=ALU.add),
                     reads=["ptr%d" % h, "sc1", "modT"], writes=["hT"])
            if debug and b == 0 and T == dbgT:
                P.op("dve", lambda e: e.tensor_copy(out=xt[:, 3, :].rearrange("p (a c) -> p a c", c=TT)[:, 0:4, :], in_=hT[:, 0:4, :]), reads=["hT"], writes=["xt3"])
            if STAGE <= 1.4:
                return
            P.op("dve", lambda e: e.tensor_copy(out=angA[:, :], in_=pos_i[:, :]), reads=["angI"], writes=["r1"])
            P.op("dve", lambda e: e.tensor_scalar(out=angA[:, :], in0=angA[:, :], scalar1=cf[:, CF_INV:CF_INV + 1], scalar2=None, op0=ALU.mult),
                 reads=["r1", "cf"], writes=["r1"])
            sincos(0.0, sinT, "sinT", True)
            sincos(np.pi / 2, cosT, "cosT", False)
            if T == 0:
                P.op("pool", lambda e: e.memset(state[:, :, :], 0.0), writes=["state"])
                P.op("pool", lambda e: e.memset(state_b[:, :, :], 0.0), writes=["state_b"])
                P.op("pool", lambda e: e.memset(kroll[:, :, :], 0.0), writes=["kroll"])
                P.op("pool", lambda e: e.memset(vroll[:, :, :], 0.0), writes=["vroll"])
            else:
                P.op("pool", lambda e: e.tensor_copy(out=kroll[:, :, 0:16], in_=kroll[:, :, TT:TT + 16]), reads=["kroll"], writes=["kroll"])
                P.op("pool", lambda e: e.tensor_copy(out=vroll[:, :, 0:16], in_=vroll[:, :, TT:TT + 16]), reads=["vroll"], writes=["vroll"])

        if STAGE <= 2:
            return
        for h in range(8):
            proj_fm(O_RQ + h * 128, lambda p_, k_: rope_to(p_, k_, qT[0][:, :], "qT0"))
            proj_fm(O_RK + h * 128, lambda p_, k_: rope_to(p_, k_, kT[:, :], "kT"))
            proj_fm(O_RV + h * 128, lambda p_, k_: copy_to(p_, k_, vT[:, :], "vT"))
            if not dry:
                xi_ap = bc(cbf[:, CB_XI + h * 128:CB_XI + (h + 1) * 128], 4)
                P.op("pool", lambda e, xi_ap=xi_ap: e.tensor_tensor(out=qxT[:, :].rearrange("p (a c) -> p a c", c=128), in0=qT[0][:, :].rearrange("p (a c) -> p a c", c=128),
                                                                   in1=xi_ap, op=ALU.mult), reads=["qT0", "cbf"], writes=["qxT"])
                hh = next_tr()
                transpose4(lambda j: vT[:, j * 128:(j + 1) * 128], ["vT"], hh)
                P.op("act", lambda e, hh=hh: e.activation(out=Vr[:, :, :], in_=ptr[:, hh, :].rearrange("p (a c) -> p a c", c=128), func=AF.Identity),
                     reads=["ptr%d" % hh], writes=["Vr"])
                hh = next_tr()
                transpose4(lambda j: kT[:, j * 128:(j + 1) * 128], ["kT"], hh)
                P.op("act", lambda e, hh=hh, h=h: e.activation(out=Kz[:, :, :], in_=ptr[:, hh, :].rearrange("p (a c) -> p a c", c=128), func=AF.Identity,
                                                              scale=cf[:, CF_ZETA + h:CF_ZETA + h + 1]),
                     reads=["ptr%d" % hh, "cf"], writes=["Kz"])
                dm4 = bc(cbf[:, CB_DM + h * 128:CB_DM + (h + 1) * 128], 4)
                psA, kA = next_psc()
                psB, kB = next_psc()
                def f_sc(e, psA=psA):
                    ins = None
                    for c in range(4):
                        cs = slice(c * 128, (c + 1) * 128)
                        ins = e.matmul(psA[:, cs], lhsT=kT[:, cs], rhs=qT[0][:, cs], start=True, stop=True)
                    return ins
                P.op("pe", f_sc, reads=["kT", "qT0"], writes=[kA])
                def f_kv(e, psB=psB):
                    ins = None
                    for c in range(4):
                        ins = e.matmul(psB[:, c * 128:(c + 1) * 128], lhsT=Kz[:, c, :], rhs=Vr[:, c, :], start=True, stop=True)
                    return ins
                P.op("pe", f_kv, reads=["Kz", "Vr"], writes=[kB])
                P.op("dve", lambda e, psA=psA, dm4=dm4: e.tensor_tensor(out=scm4[:, :, :], in0=psA[:, :].rearrange("p (a c) -> p a c", c=128), in1=dm4, op=ALU.mult),
                     reads=[kA, "cbf"], writes=["scm4"])
                for c in range(3):
                    P.op("dve", lambda e, h=h, c=c, psB=psB: e.scalar_tensor_tensor(out=state[:, h, :], in0=state[:, h, :], scalar=DECAY[h], in1=psB[:, c * 128:(c + 1) * 128],
                                                                                  op0=ALU.mult, op1=ALU.add), reads=["state", kB], writes=["state"])
                    P.op("act", lambda e, h=h, c=c: e.activation(out=sb3[:, c, :], in_=state[:, h, :], func=AF.Identity), reads=["state"], writes=["sb3"])
                def f_o(e, h=h):
                    ins = None
                    for c in range(4):
                        cs = slice(c * 128, (c + 1) * 128)
                        e.matmul(pofl[0][:, cs], lhsT=scm4[:, c, :], rhs=Vr[:, c, :], start=True, stop=False)
                        ins = e.matmul(pofl[0][:, cs], lhsT=qxT[:, cs], rhs=(state_b[:, h, :] if c == 0 else sb3[:, c - 1, :]), start=False, stop=True)
                    return ins
                P.op("pe", f_o, reads=["scm4", "Vr", "qxT", "state_b", "sb3"], writes=["po0"])
                P.op("dve", lambda e, h=h, psB=psB: e.scalar_tensor_tensor(out=state[:, h, :], in0=state[:, h, :], scalar=DECAY[h], in1=psB[:, 384:512],
                                                                          op0=ALU.mult, op1=ALU.add), reads=["state", kB], writes=["state"])
                P.op("act", lambda e, h=h: e.activation(out=state_b[:, h, :], in_=state[:, h, :], func=AF.Identity), reads=["state"], writes=["state_b"])
                for c in range(4):
                    P.op("dve", lambda e, c=c: e.bn_stats(out=bst[:, c, :], in_=pofl[0][:, c * 128:(c + 1) * 128]), reads=["po0"], writes=["bst"])
                    P.op("dve", lambda e, c=c: e.bn_aggr(out=mv[:, c, :], in_=bst[:, c, :]), reads=["bst"], writes=["mv"])
                P.op("act", lambda e: e.activation(out=rs4[:, :], in_=mv[:, :, 1], func=AF.Sqrt, bias=ceps[:, 0:1], scale=1.0), reads=["mv", "ceps"], writes=["rs4"])
                P.op("dve", lambda e: e.reciprocal(out=rs4[:, :], in_=rs4[:, :]), reads=["rs4"], writes=["rs4"])
                P.op("dve", lambda e: e.scalar_tensor_tensor(out=nmr[:, :], in0=mv[:, :, 0], scalar=-1.0, in1=rs4[:, :], op0=ALU.mult, op1=ALU.mult),
                     reads=["mv", "rs4"], writes=["nmr"])
                for c in range(4):
                    P.op("act", lambda e, c=c: e.activation(out=on[:, c, :], in_=po[0][:, c // 2, (c % 2) * 128:(c % 2) * 128 + 128], func=AF.Identity,
                                                           scale=rs4[:, c:c + 1], bias=nmr[:, c:c + 1]),
                         reads=["po0", "nmr", "rs4"], writes=["on"])
            gate_a(O_RG + h * 128)
            if not dry:
                def tail(h=h):
                    hh = next_tr()
                    transpose4(lambda j: on[:, j, :], ["on"], hh)
                    P.op("act", lambda e, hh=hh, h=h: e.activation(out=yT[:, h, :], in_=ptr[:, hh, :], func=AF.Identity, scale=grT[:, h:h + 1]),
                         reads=["ptr%d" % hh, "grT"], writes=["yT%d" % h])
                    gate_b(yT[:, h, :], "yT%d" % h)
                pending.append(tail)

        if STAGE <= 3:
            return
        for g in range(2):
            proj_fm(O_KC + g * 128, lambda p_, k_, g=g: copy_to(p_, k_, kroll[:, g, 16:16 + TT], "kroll"))
            proj_fm(O_VC + g * 128, lambda p_, k_, g=g: copy_to(p_, k_, vroll[:, g, 16:16 + TT], "vroll"))
            proj_fm(O_KS + g * 128, lambda p_, k_, g=g: rope_to(p_, k_, ksT[:, g, tok0:tok0 + TT], "ksT"))
            proj_fm(O_KW + g * 128, lambda p_, k_, g=g: rope_to(p_, k_, kwT[:, g, (T % 2) * TT:(T % 2 + 1) * TT], "kwT"))
            for (off, Vd, vk) in ((O_VS, Vs, "Vs"), (O_VW, Vw, "Vw")):
                proj_fm(off + g * 128, lambda p_, k_: copy_to(p_, k_, vT[:, :], "vT"))
                if not dry:
                    hh = next_tr()
                    vb0 = 4 * T if vk == "Vs" else 4 * (T % 2)
                    transpose4(lambda j: vT[:, j * 128:(j + 1) * 128], ["vT"], hh)
                    P.op("act", lambda e, hh=hh, Vd=Vd, g=g, vb0=vb0: e.activation(out=Vd[:, g, vb0:vb0 + 4, 0:128], in_=ptr[:, hh, :].rearrange("p (a c) -> p a c", c=128), func=AF.Identity),
                         reads=["ptr%d" % hh], writes=[vk])
        nl0 = 1 if T == 0 else 0
        ncol = 32 - nl0
        for kv in range(2):
            roll = kroll if kv == 0 else vroll
            rkey = "kroll" if kv == 0 else "vroll"
            w1 = w_ck1 if kv == 0 else w_cv1
            for pc in range(4):
                wt, wk_ = W.next(("w1", "w_ck1" if kv == 0 else "w_cv1", pc))
                if wt is None:
                    continue
                w3 = wt[:, :].rearrange("p (i c) -> p i c", c=256)
                def f(e, w3=w3, pc=pc, roll=roll, kv=kv):
                    ins = None
                    for il in range(8):
                        i = pc * 8 + il
                        for mh in range(2):
                            rhs = roll[:, :, i:i + 497:16]
                            e.matmul(psc[0][:, mh * 64:mh * 64 + 64].rearrange("p (g n) -> p g n", n=32), lhsT=w3[:, il, mh * 128:(mh + 1) * 128], rhs=rhs,
                                     start=(i == 0 and mh == 0), stop=(i == 31), skip_group_check=True)
                            ins = e.matmul(psc[1][:, mh:mh + 1], lhsT=w3[:, il, mh * 128:(mh + 1) * 128], rhs=peTb[kv][:, i:i + 1],
                                           start=(i == 0 and mh == 0), stop=(i == 31), skip_group_check=True)
                    return ins
                P.op("pe", f, reads=[wk_, rkey, "peTb%d" % kv], writes=["psc0", "psc1"])
            if dry:
                continue
            P.op("dve", lambda e, kv=kv: e.tensor_copy(out=hb[:, kv, :], in_=psc[1][:, 0:2]), reads=["psc1"], writes=["hb"])
            P.op("dve", lambda e, kv=kv: e.tensor_scalar(out=hbh[:, kv, :], in0=hb[:, kv, :], scalar1=0.5, scalar2=None, op0=ALU.mult), reads=["hb"], writes=["hbh"])
            for mh in range(2):
                P.op("act", lambda e, mh=mh, kv=kv: e.activation(out=ht[:, :], in_=psc[0][:, mh * 64:mh * 64 + 64], func=AF.Tanh, bias=hbh[:, kv, mh:mh + 1], scale=0.5),
                     reads=["psc0", "hbh"], writes=["ht"])
                P.op("dve", lambda e, mh=mh, kv=kv: e.tensor_scalar(out=hx[:, :], in0=psc[0][:, mh * 64:mh * 64 + 64], scalar1=hb[:, kv, mh:mh + 1], scalar2=None, op0=ALU.add),
                     reads=["psc0", "hb"], writes=["hx"])
                P.op("dve", lambda e, mh=mh: e.scalar_tensor_tensor(out=shT[:, mh, :], in0=ht[:, :], scalar=1.0, in1=hx[:, :], op0=ALU.add, op1=ALU.mult),
                     reads=["ht", "hx"], writes=["shT"])
            if kv == 0:
                def f2(e):
                    e.matmul(pm[:, 0:64], lhsT=w2bf[0][:, 0, :], rhs=shT[:, 0, :], start=True, stop=False)
                    return e.matmul(pm[:, 0:64], lhsT=w2bf[0][:, 1, :], rhs=shT[:, 1, :], start=False, stop=True)
                P.op("pe", f2, reads=["w2bf0", "shT"], writes=["po1"])
                P.op("act", lambda e: e.activation(out=kcx[:, :], in_=pm[:, 0:64], func=AF.Identity, scale=0.5), reads=["po1"], writes=["kcx"])
                P.op("pe", lambda e: e.matmul(pm[:, 64:128], lhsT=swap_b, rhs=kcx[:, :], start=True, stop=True), reads=["kcx", "cbf"], writes=["po1"])
                cos_c = bc(cosT[:, 15:TT:16], 2)
                sin_c = bc(sinT[:, 15:TT:16], 2)
                P.op("dve", lambda e, cos_c=cos_c: e.tensor_tensor(out=hx[:, :].rearrange("p (g n) -> p g n", n=32), in0=kcx[:, :].rearrange("p (g n) -> p g n", n=32), in1=cos_c, op=ALU.mult),
                     reads=["kcx", "cosT"], writes=["hx"])
                P.op("dve", lambda e, sin_c=sin_c: e.tensor_tensor(out=ht[:, :].rearrange("p (g n) -> p g n", n=32), in0=pm[:, 64:128].rearrange("p (g n) -> p g n", n=32), in1=sin_c, op=ALU.mult),
                     reads=["po1", "sinT"], writes=["ht"])
                P.op("dve", lambda e: e.tensor_tensor(out=KcT[:, :, 32 * T:32 * T + 32], in0=hx[:, :].rearrange("p (g n) -> p g n", n=32), in1=ht[:, :].rearrange("p (g n) -> p g n", n=32), op=ALU.add),
                     reads=["hx", "ht"], writes=["KcT"])
            else:
                for g in range(2):
                    def f3(e, g=g):
                        e.matmul(pm[0:32, 128 + g * 128:256 + g * 128], lhsT=shT[:, 0, g * 32:(g + 1) * 32], rhs=w2bf[1][:, 0, :], start=True, stop=False)
                        return e.matmul(pm[0:32, 128 + g * 128:256 + g * 128], lhsT=shT[:, 1, g * 32:(g + 1) * 32], rhs=w2bf[1][:, 1, :], start=False, stop=True)
                    P.op("pe", f3, reads=["w2bf1", "shT"], writes=["po1"])
                    P.op("act", lambda e, g=g: e.activation(out=vct[:, g, 0:128], in_=pm[0:32, 128 + g * 128:256 + g * 128], func=AF.Identity, scale=0.5), reads=["po1"], writes=["vct"])
                P.dma("sp", lambda e, s: e.dma_start(out=Vc[32 * T:32 * T + 32, :, :], in_=vct[:, :, :]).then_inc(s, 16), reads=["vct"], writes=["Vc"], semkey="vc")
        wt, wk_ = W.next(("cols", "w_in", O_BG, 24, 16))
        if wt is not None:
            w3 = wt[:, 0:16 * 24].rearrange("p (k c) -> p k c", c=24)
            for st in range(4):
                def f(e, st=st, w3=w3):
                    ins = None
                    for kc in range(16):
                        ins = e.matmul(pm[:, 384 + st * 24:384 + (st + 1) * 24], lhsT=hT[:, kc, st * 128:(st + 1) * 128], rhs=w3[:, kc, :], start=(kc == 0), stop=(kc == 15))
                    return ins
                P.op("pe", f, reads=[wk_, "hT"], writes=["po1"])
            P.op("act", lambda e: e.activation(out=gates[:, :, :], in_=pm[:, 384:480].rearrange("p (a c) -> p a c", c=24), func=AF.Tanh, scale=0.5), reads=["po1"], writes=["gates"])
            P.op("dve", lambda e: e.tensor_scalar(out=gates[:, :, :], in0=gates[:, :, :], scalar1=0.5, scalar2=0.5, op0=ALU.mult, op1=ALU.add), reads=["gates"], writes=["gates"])

        if STAGE <= 4:
            return
        for g in range(2):
            for r in range(4):
                hq = 4 * g + r
                proj_fm(O_NQ + hq * 128, lambda p_, k_, r=r: rope_to(p_, k_, qT[r][:, :], "qT%d" % r))
            flush_pending()
            if not dry and T >= 2:
                nb4 = cbf[:, CB_NB + (T - 2) * 512:CB_NB + (T - 1) * 512]
                r2v = r2[:, :].rearrange("p (a c) -> p a c", c=128)
                rsa = rs4[:, :]
                rb = bass.AP(tensor=rsa.tensor, offset=rsa.offset, ap=[list(rsa.ap[0]), [1, 4], [0, 128]])
                for r in range(4):
                    ps_t, ps_k = next_psc()
                    def fsel(e, ps_t=ps_t, r=r, g=g):
                        ins = None
                        for st in range(4):
                            ins = e.matmul(ps_t[:, st * 128:(st + 1) * 128], lhsT=qT[r][:, st * 128:(st + 1) * 128], rhs=KcT[:, g, :], start=True, stop=True)
                        return ins
                    P.op("pe", fsel, reads=["qT%d" % r, "KcT"], writes=[ps_k])
                    P.op("dve", lambda e, ps_t=ps_t: e.scalar_tensor_tensor(out=r1[:, :], in0=ps_t[:, :], scalar=SCALE, in1=nb4, op0=ALU.mult, op1=ALU.add),
                         reads=[ps_k, "cbf"], writes=["r1"])
                    P.op("act", lambda e: e.activation(out=r2[:, :], in_=r1[:, :], func=AF.Exp), reads=["r1"], writes=["r2"])
                    P.op("dve", lambda e: e.reduce_sum(out=rs4[:, :], in_=r2v, axis=mybir.AxisListType.X), reads=["r2"], writes=["rs4"])
                    P.op("dve", lambda e: e.reciprocal(out=rs4[:, :], in_=rs4[:, :]), reads=["rs4"], writes=["rs4"])
                    if r == 0:
                        P.op("dve", lambda e: e.tensor_tensor(out=ppad[:, :, 0:128], in0=r2v, in1=rb, op=ALU.mult), reads=["r2", "rs4"], writes=["ppad"])
                    else:
                        P.op("dve", lambda e: e.tensor_tensor(out=r2v, in0=r2v, in1=rb, op=ALU.mult), reads=["r2", "rs4"], writes=["r2"])
                        P.op("pool", lambda e: e.tensor_tensor(out=ppad[:, :, 0:128], in0=ppad[:, :, 0:128], in1=r2v, op=ALU.add), reads=["r2", "ppad"], writes=["ppad"])
                imp4 = stmp[:, :].rearrange("p (a c) -> p a c", c=32)
                wk4 = etmp[:, :].rearrange("p (a c) -> p a c", c=32)
                def v(k0):
                    return ppad[:, :, k0:k0 + 128:4]
                P.op("dve", lambda e: e.tensor_tensor(out=imp4, in0=v(0), in1=v(4), op=ALU.add), reads=["ppad"], writes=["stmp"])
                P.op("dve", lambda e: e.scalar_tensor_tensor(out=imp4, in0=imp4, scalar=0.5, in1=v(1), op0=ALU.mult, op1=ALU.add), reads=["stmp", "ppad"], writes=["stmp"])
                P.op("dve", lambda e: e.tensor_tensor(out=imp4, in0=imp4, in1=v(2), op=ALU.add), reads=["stmp", "ppad"], writes=["stmp"])
                P.op("dve", lambda e: e.tensor_tensor(out=imp4, in0=imp4, in1=v(3), op=ALU.add), reads=["stmp", "ppad"], writes=["stmp"])
                bon4 = cf[:, CF_BON + (T - 2) * 128:CF_BON + (T - 1) * 128].rearrange("p (a c) -> p a c", c=32)
                P.op("dve", lambda e: e.tensor_tensor(out=imp4, in0=imp4, in1=bon4, op=ALU.add), reads=["stmp", "cf"], writes=["stmp"])
                for st in range(4):
                    P.op("dve", lambda e, st=st: e.max(out=m8[:, 0:8], in_=imp4[:, st, :]), reads=["stmp"], writes=["m8"])
                    P.op("dve", lambda e, st=st: e.match_replace(out=wk4[:, st, :], in_to_replace=m8[:, 0:8], in_values=imp4[:, st, :], imm_value=-3.0e38),
                         reads=["stmp", "m8"], writes=["etmp"])
                    P.op("dve", lambda e, st=st: e.max(out=m8[:, 8:16], in_=wk4[:, st, :]), reads=["etmp"], writes=["m8"])
                    P.op("dve", lambda e, st=st: e.tensor_scalar(out=wk4[:, st, :], in0=imp4[:, st, :], scalar1=m8[:, 15:16], scalar2=None, op0=ALU.is_ge),
                         reads=["stmp", "m8"], writes=["etmp"])
                P.op("dve", lambda e: e.tensor_scalar(out=negm4[:, :, :], in0=wk4, scalar1=-NEGM, scalar2=NEGM, op0=ALU.mult, op1=ALU.add), reads=["etmp"], writes=["negm"])
                hh = next_tr()
                def ftr(e, hh=hh):
                    ins = None
                    for st in range(4):
                        ins = e.transpose(ptr[0:32, hh, st * 128:(st + 1) * 128], negm4[:, st, :], ident_b)
                    return ins
                P.op("pe", ftr, reads=["negm", "cbf"], writes=["ptr%d" % hh])
                P.op("act", lambda e, hh=hh, g=g: e.activation(out=negmT[:, g, :], in_=ptr[0:32, hh, 0:512], func=AF.Identity), reads=["ptr%d" % hh], writes=["negmT"])
            for r in range(4):
                hq = 4 * g + r
                if not dry:
                    nsa_head(b, T, g, r, hq)
                gate_a(O_NG + hq * 128)
                if not dry:
                    def tail2(hq=hq):
                        hh = next_tr()
                        transpose4(lambda j: accb[:, j, :], ["accb"], hh)
                        P.op("act", lambda e, hh=hh, hq=hq: e.activation(out=yT[:, 8 + hq, :], in_=ptr[:, hh, :], func=AF.Identity, scale=0.5),
                             reads=["ptr%d" % hh], writes=["yT%d" % (8 + hq)])
                        gate_b(yT[:, 8 + hq, :], "yT%d" % (8 + hq))
                    pending.append(tail2)

        flush_pending()
        if debug and b == 0 and T == dbgT:
            if dry:
                return
            P.op("dve", lambda e: e.tensor_copy(out=xt[:, 2, :].rearrange("p (a c) -> p a c", c=TT)[:, 0:4, :], in_=yT[:, 0:4, :]), reads=["yT%d" % i for i in range(16)], writes=["xt2"])
            P.op("dve", lambda e: e.tensor_copy(out=xt[:, 1, :].rearrange("p (a c) -> p a c", c=TT)[:, 0:4, :], in_=yT[:, 8:12, :]), reads=["yT%d" % i for i in range(16)], writes=["xt1"])
            P.op("dve", lambda e: e.tensor_copy(out=xt[0:32, 0, 0:1024], in_=negmT[:, :, :].rearrange("p g t -> p (g t)")), reads=["negmT"], writes=["xt0"])
            P.op("dve", lambda e: e.tensor_copy(out=xt[:, 0, 1024:1280], in_=KcT[:, :, :].rearrange("p g t -> p (g t)")), reads=["KcT"], writes=["xt0"])
            P.dma("sp", lambda e, s: e.dma_start(out=dbg["d_m2"][:, 0:2048], in_=xt[:, 0, :]).then_inc(s, 16), reads=["xt0"], semkey="dbg")
            P.dma("sp", lambda e, s: e.dma_start(out=dbg["d_hT"][:, 0:4 * TT], in_=xt[:, 3, :]).then_inc(s, 16), reads=["xt3"], semkey="dbg")
            P.dma("sp", lambda e, s: e.dma_start(out=dbg["d_yT"][:, 0:4 * TT], in_=xt[:, 2, :]).then_inc(s, 16), reads=["xt2"], semkey="dbg")
            P.dma("sp", lambda e, s: e.dma_start(out=dbg["d_yT"][:, 4 * TT:8 * TT], in_=xt[:, 1, :]).then_inc(s, 16), reads=["xt1"], semkey="dbg")
            return
        ykeys = ["yT%d" % i for i in range(16)]
        for cc in range(16):
            if cc % 2 == 0:
                bA, kA, bB, kB, bC, kC, bD, kD = pp[0], "pp0", pp[1], "pp1", psc[0], "psc0", psc[1], "psc1"
            else:
                bA, kA, bB, kB, bC, kC, bD, kD = pofl[0], "po0", pofl[1], "po1", ptrf[0], "ptr0", ptrf[1], "ptr1"
            wt2, wk2 = W.next(("two", cc * 128))
            if wt2 is not None:
                w2v = wt2[:, :].rearrange("p (u k c) -> p u k c", u=2, c=128)
                def fA(e, w2v=w2v, bA=bA, bB=bB):
                    ins = None
                    for u, bt in ((0, bA), (1, bB)):
                        for kc in range(8):
                            ins = e.matmul(bt[:, :], lhsT=w2v[:, u, kc, :], rhs=yT[:, u * 8 + kc, :], start=(kc == 0), stop=(kc == 7))
                    return ins
                P.op("pe", fA, reads=[wk2] + ykeys, writes=[kA, kB])
            wta, wka = W.next(("cols", "w_in", O_MA + cc * 128, 128, 16))
            if wta is not None:
                wa3 = wta[:, :].rearrange("p (k c) -> p k c", c=128)
                def fC(e, wa3=wa3, bC=bC):
                    ins = None
                    for kc in range(16):
                        ins = e.matmul(bC[:, :], lhsT=wa3[:, kc, :], rhs=hT[:, kc, :], start=(kc == 0), stop=(kc == 15))
                    return ins
                P.op("pe", fC, reads=[wka, "hT"], writes=[kC])
            wtb, wkb = W.next(("cols", "w_in", O_MB + cc * 128, 128, 16))
            if wtb is None:
                continue
            wb3 = wtb[:, :].rearrange("p (k c) -> p k c", c=128)
            def fD(e, wb3=wb3, bD=bD):
                ins = None
                for kc in range(16):
                    ins = e.matmul(bD[:, :], lhsT=wb3[:, kc, :], rhs=hT[:, kc, :], start=(kc == 0), stop=(kc == 15))
                return ins
            P.op("pe", fD, reads=[wkb, "hT"], writes=[kD])
            P.op("act", lambda e, bC=bC: e.activation(out=ta[:, :], in_=bC[:, :], func=AF.Tanh, scale=0.5), reads=[kC], writes=["r1"])
            P.op("dve", lambda e, bA=bA: e.scalar_tensor_tensor(out=ta[:, :], in0=ta[:, :], scalar=1.0, in1=bA[:, :], op0=ALU.add, op1=ALU.mult), reads=["r1", kA], writes=["r1"])
            P.op("act", lambda e, bD=bD: e.activation(out=tb2[:, :], in_=bD[:, :], func=AF.Tanh, scale=0.5), reads=[kD], writes=["r2"])
            P.op("dve", lambda e, bB=bB: e.scalar_tensor_tensor(out=tb2[:, :], in0=tb2[:, :], scalar=1.0, in1=bB[:, :], op0=ALU.add, op1=ALU.mult), reads=["r2", kB], writes=["r2"])
            P.op("pool", lambda e, cc=cc: e.tensor_tensor(out=m2[:, cc, :], in0=ta[:, :], in1=tb2[:, :], op=ALU.add), reads=["r1", "r2"], writes=["xn%d" % (cc // 4)])
        if dry:
            for ct in range(4):
                for pc in range(4):
                    W.next(("wout", ct, pc))
            return
        P.dma("sp", lambda e, s: e.dma_start(out=gfin_bc[:, :], in_=g_fin[0, :].partition_broadcast(128)).then_inc(s, 16), writes=["hT"], semkey="gf")
        banks = [(pp[0], "pp0"), (pp[1], "pp1"), (psc[0], "psc0"), (psc[1], "psc1")]
        for ct in range(4):
            for pc in range(4):
                wt, wk_ = W.next(("wout", ct, pc))
                w3 = wt[:, :].rearrange("p (k c) -> p k c", c=512)
                for st in range(4):
                    bt, bk = banks[st]
                    def f(e, w3=w3, bt=bt, st=st, pc=pc):
                        ins = None
                        for kl in range(4):
                            kc = pc * 4 + kl
                            ins = e.matmul(bt[:, :], lhsT=m2[:, kc, st * 128:(st + 1) * 128], rhs=w3[:, kl, :], start=(kc == 0), stop=(kc == 15))
                        return ins
                    P.op("pe", f, reads=[wk_, "xn0", "xn1", "xn2", "xn3"], writes=[bk])
            for st in range(4):
                bt, bk = banks[st]
                P.op("dve", lambda e, bt=bt, ct=ct: e.tensor_tensor(out=r1[:, :], in0=bt[:, :], in1=gate_bc1[:, ct * 512:(ct + 1) * 512], op=ALU.mult),
                     reads=[bk, "gate_bc"], writes=["r1"])
                P.op("pool", lambda e, st=st, ct=ct: e.tensor_tensor(out=xt[:, st, ct * 512:(ct + 1) * 512], in0=xt[:, st, ct * 512:(ct + 1) * 512], in1=r1[:, :], op=ALU.add),
                     reads=["r1", "xt%d" % st], writes=["xt%d" % st])
        for st in range(4):
            P.op("act", lambda e, st=st: e.activation(out=xn[:, st, :], in_=xt[:, st, :], func=AF.Square, accum_out=ss4[:, st:st + 1]),
                 reads=["xt%d" % st], writes=["xn%d" % st, "ss4"])
        P.op("act", lambda e: e.activation(out=rs4[:, :], in_=ss4[:, :], func=AF.Sqrt, bias=ceps[:, 0:1], scale=1.0 / D), reads=["ss4", "ceps"], writes=["rs4"])
        P.op("dve", lambda e: e.reciprocal(out=rs4[:, :], in_=rs4[:, :]), reads=["rs4"], writes=["rs4"])
        for st in range(4):
            P.op("dve", lambda e, st=st: e.scalar_tensor_tensor(out=xt[:, st, :], in0=xt[:, st, :], scalar=rs4[:, st:st + 1], in1=gfin_bc[:, :], op0=ALU.mult, op1=ALU.mult),
                 reads=["xt%d" % st, "rs4", "hT"], writes=["xt%d" % st])
            P.dma("sp", lambda e, s, st=st: e.dma_start(out=out_d[b, tok0 + st * 128:tok0 + (st + 1) * 128, :], in_=xt[:, st, :]).then_inc(s, 16),
                  reads=["xt%d" % st], semkey="o%d" % st)
            if T + 1 < NT and not debug and STAGE >= 99:
                nt0 = tok0 + TT
                P.dma("sp", lambda e, s, st=st, nt0=nt0: e.dma_start(out=xt[:, st, :], in_=x_d[b, nt0 + st * 128:nt0 + (st + 1) * 128, :]).then_inc(s, 16),
                      writes=["xt%d" % st], semkey="x%d" % st)
                xpre[0] = True

    pt_i = [0]
    xpre = [False]

    def nsa_head(b, T, g, r, hq):
        qh = qT[r]
        qk = "qT%d" % r
        first = [True]

        def combine(br, bi):
            pk, dk = "po%d" % bi, "pp%d" % bi
            pof = pofl[bi]
            P.op("dve", lambda e: e.tensor_scalar(out=den4[:, :], in0=pp[bi][:, 0:4], scalar1=1e-30, scalar2=None, op0=ALU.max), reads=[dk], writes=["den4"])
            P.op("dve", lambda e: e.reciprocal(out=den4[:, :], in_=den4[:, :]), reads=["den4"], writes=["den4"])
            P.op("dve", lambda e: e.tensor_tensor(out=den4[:, :], in0=den4[:, :], in1=gates[:, :, hq * 3 + br], op=ALU.mult), reads=["den4", "gates"], writes=["den4"])
            for qi in range(4):
                o_ap = pof[:, qi * 128:(qi + 1) * 128]
                if br == 0:
                    P.op("act", lambda e, o_ap=o_ap, qi=qi: e.activation(out=acc[:, qi, :], in_=o_ap, func=AF.Identity, scale=den4[:, qi:qi + 1]), reads=[pk, "den4"], writes=["acc"])
                elif br == 1:
                    P.op("dve", lambda e, o_ap=o_ap, qi=qi: e.scalar_tensor_tensor(out=acc[:, qi, :], in0=o_ap, scalar=den4[:, qi:qi + 1], in1=acc[:, qi, :], op0=ALU.mult, op1=ALU.add),
                         reads=[pk, "den4", "acc"], writes=["acc"])
                else:
                    P.op("dve", lambda e, o_ap=o_ap, qi=qi: e.scalar_tensor_tensor(out=accb[:, qi, :], in0=o_ap, scalar=den4[:, qi:qi + 1], in1=acc[:, qi, :], op0=ALU.mult, op1=ALU.add),
                         reads=[pk, "den4", "acc"], writes=["accb"])

        pv_q = []
        npush = [0]

        def push_pv(fn):
            while pv_q:
                pv_q.pop(0)()
            pv_q.append(fn)
            npush[0] += 1
            if npush[0] == 3:
                flush_pending()

        nk = 32 * (T + 1)
        ps_t, ps_k = next_psc()
        def fsc(e, ps_t=ps_t):
            e.matmul(ps_t[0:nk, :], lhsT=KcT[:, g, 0:nk], rhs=qh[:, :], start=True, stop=False)
            return e.matmul(ps_t[0:nk, :], lhsT=cbf[0:nk, CB_ID:CB_ID + nk], rhs=cbf[0:nk, CB_VT + T * TT:CB_VT + (T + 1) * TT], start=False, stop=True)
        P.op("pe", fsc, reads=["KcT", qk, "cbf"], writes=[ps_k])
        pi = pt_i[0] % 2
        pt_i[0] += 1
        ptile, pkey = PT[pi], "PT%d" % pi
        P.op("act", lambda e, ps_t=ps_t, ptile=ptile: e.activation(out=ptile[0:nk, :], in_=ps_t[0:nk, :], func=AF.Exp, scale=SCALE), reads=[ps_k], writes=[pkey])
        bi0 = br_i[0] % 2
        br_i[0] += 1
        def fpv(e, ptile=ptile, bi0=bi0):
            ins = None
            for qi in range(4):
                e.matmul(pofl[bi0][:, qi * 128:(qi + 1) * 128], lhsT=ptile[0:nk, qi * 128:(qi + 1) * 128], rhs=Vc[0:nk, g, 0:128], start=(qi == 0), stop=True, skip_group_check=True)
                ins = e.matmul(pp[bi0][:, qi:qi + 1], lhsT=ptile[0:nk, qi * 128:(qi + 1) * 128], rhs=ones_b[0:nk, 0:1], start=(qi == 0), stop=True, skip_group_check=True)
            return ins
        def pv0(fpv=fpv, pkey=pkey, bi0=bi0):
            P.op("pe", fpv, reads=[pkey, "Vc", "ones_b"], writes=["po%d" % bi0, "pp%d" % bi0])
            combine(0, bi0)
        push_pv(pv0)

        for br, (kTt, kkey, Vt, vkey) in ((1, (ksT, "ksT", Vs, "Vs")), (2, (kwT, "kwT", Vw, "Vw"))):
            kts = list(range(0, 4 * T + 4)) if br == 1 else list(range(max(0, 4 * T - 4), 4 * T + 4))
            bi = br_i[0] % 2
            br_i[0] += 1
            for kt in kts:
                i = kt - 4 * T
                if br == 1:
                    qlo, qhi = max(i, 0), 3
                else:
                    qlo, qhi = max(i, 0), min(i + 4, 3)
                c0, c1 = qlo * 128, (qhi + 1) * 128
                ps_t, ps_k = next_psc()
                use_sel = (br == 1 and T >= 2)
                tri_q = i if i >= 0 else None
                anti_q = (i + 4) if (br == 2 and i < 0 and i + 4 <= 3) else None
                def fs(e, ps_t=ps_t, kt=kt, c0=c0, c1=c1, use_sel=use_sel, tri_q=tri_q, anti_q=anti_q, kTt=kTt, br_=br):
                    more = use_sel or (tri_q is not None) or (anti_q is not None)
                    kcol = kt * 128 if br_ == 1 else ((kt // 4) % 2) * TT + (kt % 4) * 128
                    ins = e.matmul(ps_t[:, c0:c1], lhsT=kTt[:, g, kcol:kcol + 128], rhs=qh[:, c0:c1], start=True, stop=not more)
                    if use_sel:
                        m2_ = (tri_q is not None) or (anti_q is not None)
                        ins = e.matmul(ps_t[:, c0:c1], lhsT=cbf[0:32, CB_E + kt * 128:CB_E + (kt + 1) * 128], rhs=negmT[0:32, g, c0:c1], start=False, stop=not m2_)
                    if tri_q is not None:
                        ins = e.matmul(ps_t[:, tri_q * 128:(tri_q + 1) * 128], lhsT=ident_b, rhs=tri_b, start=False, stop=(anti_q is None))
                    if anti_q is not None:
                        ins = e.matmul(ps_t[:, anti_q * 128:(anti_q + 1) * 128], lhsT=ident_b, rhs=anti_b, start=False, stop=True)
                    return ins
                P.op("pe", fs, reads=[kkey, qk, "cbf", "negmT"], writes=[ps_k])
                pi = pt_i[0] % 2
                pt_i[0] += 1
                ptile, pkey = PT[pi], "PT%d" % pi
                P.op("act", lambda e, ps_t=ps_t, ptile=ptile, c0=c0, c1=c1: e.activation(out=ptile[:, c0:c1], in_=ps_t[:, c0:c1], func=AF.Exp, scale=SCALE),
                     reads=[ps_k], writes=[pkey])
                def fpv2(e, ptile=ptile, kt=kt, qlo=qlo, qhi=qhi, Vt=Vt, br=br, bi=bi, kt0=kts[0]):
                    ins = None
                    for qi in range(qlo, qhi + 1):
                        klast = 4 * T + qi
                        vslot = kt if br == 1 else ((kt // 4) % 2) * 4 + (kt % 4)
                        st_ = (kt == kt0 and qi == 0)
                        e.matmul(pofl[bi][:, qi * 128:(qi + 1) * 128], lhsT=ptile[:, qi * 128:(qi + 1) * 128], rhs=Vt[:, g, vslot, 0:128],
                                 start=st_, stop=(kt == klast), skip_group_check=True)
                        ins = e.matmul(pp[bi][:, qi:qi + 1], lhsT=ptile[:, qi * 128:(qi + 1) * 128], rhs=ones_b[:, 0:1], start=st_, stop=(kt == klast), skip_group_check=True)
                    return ins
                def pvk(fpv2=fpv2, pkey=pkey, vkey=vkey, last=(kt == kts[-1]), br=br, bi=bi):
                    P.op("pe", fpv2, reads=[pkey, vkey, "ones_b"], writes=["po%d" % bi, "pp%d" % bi])
                    if last:
                        combine(br, bi)
                push_pv(pvk)
        while pv_q:
            pv_q.pop(0)()

    import os as _os
    STAGE = float(_os.environ.get('KSTAGE', '99'))
    dbgT = int(_os.environ.get('KDBGT', '2'))

    wscr_box = [None]

    def whole():
        setup()
        gate_rows()
        if not W.collect:
            W.wscr = wscr_box[0]
        if STAGE <= 0:
            return
        for b in range(nseq):
            seq_start(b)
            if STAGE <= 1:
                return
            for T in range(NT):
                body(b, T)
                if STAGE < 99:
                    return
                if debug and b == 0 and T == dbgT:
                    return

    P.dry = True
    W.collect = True
    whole()
    W.finish_collect()
    wscr_box[0] = nc.dram_tensor("wscr", [len(W.uniq), 128, 2048], BF16, kind="Internal").ap()
    P.dry = False
    W.collect = False
    W.pos = 0
    pp_i[0] = psc_i[0] = tr_i[0] = pt_i[0] = 0
    whole()
    assert W.pos == len(W.specs) and W.posB == len(W.specsB), (W.pos, len(W.specs), W.posB, len(W.specsB))
    P.emit()
    nc._kstats = dict(n_ops=len(P.ops), sbuf=sbuf_used, sems=P.stats)
    return nc


def _in_maps(inputs, nseq, ncores):
    cbf, cf, _ = _consts()
    f = lambda a: np.ascontiguousarray(np.asarray(a, dtype=np.float32))
    x = f(inputs["x"])
    c = f(inputs["c"])
    pos = np.ascontiguousarray(np.asarray(inputs["positions"], dtype=np.int32))
    b_ada = f(inputs["b_ada"])[0]
    shared = {
        "w_ada": f(inputs["w_ada"])[0],
        "b_adaT": np.ascontiguousarray(b_ada.reshape(48, 128).T),
        "b_gate": np.ascontiguousarray(b_ada[None, 4096:6144]),
        "g_normT": np.ascontiguousarray(f(inputs["g_norm"])[0].reshape(16, 128).T),
        "w_in": f(inputs["w_in"])[0],
        "g_retT": np.ascontiguousarray(f(inputs["g_ret"])[0].reshape(8, 128).T),
        "w_ck1": f(inputs["w_ck1"])[0],
        "w_ck2": f(inputs["w_ck2"])[0],
        "pe_ckT": np.ascontiguousarray(f(inputs["pe_ck"])[0].T),
        "w_cv1": f(inputs["w_cv1"])[0],
        "w_cv2": f(inputs["w_cv2"])[0],
        "pe_cvT": np.ascontiguousarray(f(inputs["pe_cv"])[0].T),
        "w_up_ret": f(inputs["w_up_ret"])[0],
        "w_up_nsa": f(inputs["w_up_nsa"])[0],
        "w_out": f(inputs["w_out"])[0],
        "g_final": np.ascontiguousarray(f(inputs["g_final"])[None, :]),
        "cbf": cbf,
        "cf": cf,
    }
    maps = []
    for i in range(ncores):
        sl = slice(i * nseq, (i + 1) * nseq)
        cT = np.concatenate([c[i * nseq + b].reshape(16, 128).T for b in range(nseq)], axis=1)
        m = dict(shared)
        m["x"] = np.ascontiguousarray(x[sl])
        m["cT"] = np.ascontiguousarray(cT)
        m["pos"] = np.ascontiguousarray(pos[sl])
        maps.append(m)
    return maps


def kernel(**inputs):
    nc = build_nc(SEQ_PER_CORE)
    maps = _in_maps(inputs, SEQ_PER_CORE, NCORES)
    res = run_bass_kernel_spmd(nc, maps, core_ids=list(range(NCORES)))
    out = np.concatenate([np.asarray(r["out"]) for r in res.results], axis=0)
    return out.astype(np.float32)
```

```python
import contextlib
import math
import numpy as np
import ml_dtypes
import concourse.bass as bass
import concourse.mybir as mybir
from concourse.bass_utils import run_bass_kernel_spmd

F32 = mybir.dt.float32
BF16 = mybir.dt.bfloat16
I32 = mybir.dt.int32
ALU = mybir.AluOpType
AF = mybir.ActivationFunctionType

D = 2048
S = 2048
NB = 16
NCORES = 8
SEQ_PER_CORE = 2
TT = 512
NT = S // TT
INW = 11800
O_RQ, O_RK, O_RV, O_RG, O_NQ = 0, 1024, 2048, 3072, 4096
O_KC, O_VC, O_KS, O_VS, O_KW, O_VW = 5120, 5376, 5632, 5888, 6144, 6400
O_NG, O_BG, O_MA, O_MB = 6656, 7680, 7704, 9752
EPS = 1e-6
SCALE = 128.0 ** -0.5
NEGM = -30000.0
ENGS = ("pe", "act", "dve", "pool", "sp")
PSUM_KEYS = {"pp0", "pp1", "psc0", "psc1", "po0", "po1", "ptr0", "ptr1"}


class _Op:
    __slots__ = ("eng", "fn", "deps", "sig", "semkey", "inc", "tick")


class Prog:
    def __init__(self, nc):
        self.nc = nc
        self.ops = []
        self.last_w = {}
        self.readers = {}
        self.dry = False

    def _add(self, eng, fn, reads, writes, semkey, inc, sig):
        if self.dry:
            return
        writes = list(writes) + [k for k in reads if k in PSUM_KEYS]
        deps = set()
        lw = self.last_w
        for k in reads:
            w = lw.get(k)
            if w is not None:
                deps.add(w)
        for k in writes:
            w = lw.get(k)
            if w is not None:
                deps.add(w)
            for r in self.readers.get(k, ()):
                deps.add(r)
        o = _Op()
        o.eng, o.fn, o.deps, o.sig, o.semkey, o.inc = eng, fn, deps, sig, semkey, inc
        idx = len(self.ops)
        self.ops.append(o)
        for k in reads:
            self.readers.setdefault(k, []).append(idx)
        for k in writes:
            lw[k] = idx
            self.readers[k] = []

    def op(self, eng, fn, reads=(), writes=()):
        self._add(eng, fn, reads, writes, eng, 1, False)

    def dma(self, eng, fn, reads=(), writes=(), semkey=None, n=1):
        self._add(eng, fn, reads, writes, semkey, 16 * n, True)

    def emit(self):
        nc = self.nc
        ops = self.ops
        for o in ops:
            for d in o.deps:
                ops[d].sig = True
        counts = {}
        for o in ops:
            if o.sig:
                counts[o.semkey] = counts.get(o.semkey, 0) + o.inc
                o.tick = counts[o.semkey]
            else:
                o.tick = None
        semkeys = list(counts.keys())
        for e in ENGS:
            if e not in semkeys:
                semkeys.append(e)
        self.stats = dict(counts)
        with contextlib.ExitStack() as st:
            sems = {k: st.enter_context(nc.semaphore("s_" + str(k))) for k in semkeys}
            block = st.enter_context(nc.Block())
            per_eng = {e: [o for o in ops if o.eng == e] for e in ENGS}

            def run(engine, ename):
                waited = {}
                for o in per_eng[ename]:
                    need = {}
                    for d in o.deps:
                        dd = ops[d]
                        if dd.tick > need.get(dd.semkey, 0):
                            need[dd.semkey] = dd.tick
                    for k, v in need.items():
                        if v > waited.get(k, 0):
                            engine.wait_ge(sems[k], v)
                            waited[k] = v
                    if o.semkey == ename:
                        ins = o.fn(engine)
                        if o.sig:
                            ins.then_inc(sems[ename], 1)
                    else:
                        o.fn(engine, sems[o.semkey])
                if ename == "sp":
                    for k, v in counts.items():
                        if v > waited.get(k, 0):
                            engine.wait_ge(sems[k], v)

            block.tensor(lambda e: run(e, "pe"))
            block.scalar(lambda e: run(e, "act"))
            block.vector(lambda e: run(e, "dve"))
            block.gpsimd(lambda e: run(e, "pool"))
            block.sync(lambda e: run(e, "sp"))


def _consts():
    bf = ml_dtypes.bfloat16
    H = 8
    lg = np.log1p(-np.exp2(-5.0 - np.arange(H, dtype=np.float64)))
    idx = np.arange(128, dtype=np.float64)
    cb = np.zeros((128, 0), np.float32)
    parts = {}
    ident = np.eye(128, dtype=np.float32)
    swp = np.zeros((128, 128), np.float32)
    for d in range(128):
        swp[(d + 64) % 128, d] = 1.0
    k = idx[:, None]
    q = idx[None, :]
    tri = np.where(k <= q, 0.0, NEGM).astype(np.float32)
    anti = np.where(k > q, 0.0, NEGM).astype(np.float32)
    dm = np.zeros((128, H, 128), np.float32)
    xi = np.zeros((128, H, 128), np.float32)
    for h in range(H):
        diff = idx[None, :] - idx[:, None]
        dm[:, h, :] = np.where(diff >= 0, np.exp(lg[h] * np.maximum(diff, 0.0)), 0.0) * SCALE
        xi[:, h, :] = np.exp(lg[h] * (idx + 1))[None, :]
    E = np.zeros((128, 16, 128), np.float32)
    for kt in range(16):
        for kk in range(128):
            E[2 * kt + kk // 64, kt, kk] = 1.0
    t = np.arange(S)[None, :]
    n1 = np.arange(128)[:, None]
    validT = np.where((n1 >= 1) & (16 * n1 + 15 <= t), 0.0, NEGM).astype(np.float32)
    negb = np.zeros((128, 2, 4, 128), np.float32)
    bonus = np.zeros((128, 2, 4, 32), np.float32)
    for tm in range(2):
        for st in range(4):
            tt = 1024 + tm * 512 + st * 128 + np.arange(128)
            nn = np.arange(128)[None, :]
            v = (nn >= 1) & (16 * nn + 15 <= tt[:, None])
            negb[:, tm, st, :] = np.where(v, 0.0, NEGM)
            tb = (tt // 64)[:, None]
            j = np.arange(32)[None, :]
            forced = (j == 0) | (j == tb) | (j == tb - 1)
            bonus[:, tm, st, :] = np.where(j <= tb, np.where(forced, 1.0e4, 0.0), -1.0e30)
    cbf = np.concatenate([ident, swp, tri, anti, dm.reshape(128, -1), xi.reshape(128, -1),
                          E.reshape(128, -1), validT, negb.reshape(128, -1)], axis=1).astype(bf)
    inv = np.exp(np.arange(0, 128, 2, dtype=np.float32) * np.float32(-math.log(10000.0) / 128)).astype(np.float32)
    inv = np.concatenate([inv, inv])
    sgn = np.concatenate([-np.ones(64), np.ones(64)]).astype(np.float32)
    zeta = np.zeros((128, H), np.float32)
    for h in range(H):
        zeta[:, h] = np.exp(lg[h] * (127 - idx)) * SCALE
    oh = np.zeros((128, 256), np.float32)
    oh[0, 0:128] = 1.0
    oh[1, 128:256] = 1.0
    cf = np.concatenate([inv[:, None], sgn[:, None], zeta, bonus.reshape(128, -1), oh], axis=1).astype(np.float32)
    decay = [float(np.exp(lg[h] * 128)) for h in range(H)]
    return cbf, cf, decay


CB_ID, CB_SW, CB_TRI, CB_ANTI, CB_DM, CB_XI, CB_E, CB_VT, CB_NB = 0, 128, 256, 384, 512, 1536, 2560, 4608, 6656
CB_N = 6656 + 1024
CF_INV, CF_SGN, CF_ZETA, CF_BON, CF_OH = 0, 1, 2, 10, 266
CF_N = 266 + 256


def build_nc(nseq=SEQ_PER_CORE, debug=False):
    nc = bass.Bass("TRN2", target_bir_lowering=False)
    _, _, DECAY = _consts()

    def din(name, shape, dt=F32):
        return nc.dram_tensor(name, list(shape), dt, kind="ExternalInput").ap()

    x_d = din("x", [nseq, S, D])
    cT_d = din("cT", [128, nseq * 16])
    pos_d = din("pos", [nseq, S], I32)
    w_ada = din("w_ada", [D, 3 * D])
    b_adaT = din("b_adaT", [128, 48])
    b_gate = din("b_gate", [1, D])
    g_normT = din("g_normT", [128, 16])
    w_in = din("w_in", [D, INW])
    g_retT = din("g_retT", [128, 8])
    w_ck1 = din("w_ck1", [4096, 256])
    w_ck2 = din("w_ck2", [256, 128])
    pe_ckT = din("pe_ckT", [128, 32])
    w_cv1 = din("w_cv1", [4096, 256])
    w_cv2 = din("w_cv2", [256, 128])
    pe_cvT = din("pe_cvT", [128, 32])
    w_ur = din("w_up_ret", [1024, D])
    w_un = din("w_up_nsa", [1024, D])
    w_out = din("w_out", [D, D])
    g_fin = din("g_final", [1, D])
    cbf_d = din("cbf", [128, CB_N], BF16)
    cf_d = din("cf", [128, CF_N])
    out_d = nc.dram_tensor("out", [nseq, S, D], F32, kind="ExternalOutput").ap()
    gsc_d = nc.dram_tensor("gsc", [nseq, D], F32, kind="Internal").ap()
    dbg = {}
    if debug:
        for nm, shp in (("d_hT", [128, 16 * TT]), ("d_yT", [128, 16 * TT]), ("d_m2", [128, 16 * TT])):
            dbg[nm] = nc.dram_tensor(nm, shp, F32, kind="ExternalOutput").ap()

    P = Prog(nc)
    base = [16512]

    def sb(name, shape, dt):
        nbytes = int(np.prod(shape[1:])) * (4 if dt in (F32, I32) else 2)
        off = base[0]
        base[0] = (off + nbytes + 31) // 32 * 32
        assert base[0] <= 229344, (name, base[0])
        return nc.alloc_sbuf_tensor_at(name, list(shape), dt, offset=off)

    def psum(name, shape, dt):
        return nc.alloc_psum_tensor(name, list(shape), dt)

    cbf = sb("cbf", [128, CB_N], BF16)
    cf = sb("cf", [128, CF_N], F32)
    hT_off = base[0]
    hT = sb("hT", [128, 16, TT], BF16)
    yT = sb("yT", [128, 16, TT], BF16)
    xt = sb("xt", [128, 4, D], F32)
    xn = sb("xn", [128, 4, D], BF16)
    wreg = base[0]
    wst = [sb("wst%d" % i, [128, 2048], F32) for i in range(2)]
    wbf = [sb("wbf%d" % i, [128, 2048], BF16) for i in range(2)]
    NSLOT = 6
    assert base[0] - wreg == NSLOT * 4096
    wsl = [nc.alloc_sbuf_tensor_at("wsl%d" % i, [128, 2048], BF16, offset=wreg + i * 4096) for i in range(NSLOT)]
    gate_bc1 = sb("gate_bc", [128, D], F32)
    gate_bc = [gate_bc1 for b in range(nseq)]
    gfin_bc = nc.alloc_sbuf_tensor_at("gfin_bc", [128, D], F32, offset=hT_off)
    cosT = sb("cosT", [128, TT], F32)
    sinT = sb("sinT", [128, TT], F32)
    ksT = sb("ksT", [128, 2, S], BF16)
    kwT = sb("kwT", [128, 2, 2 * TT], BF16)
    Vs = sb("Vs", [128, 2, 16, 130], BF16)
    Vw = sb("Vw", [128, 2, 8, 130], BF16)
    kroll = sb("kroll", [128, 2, 16 + TT], BF16)
    vroll = sb("vroll", [128, 2, 16 + TT], BF16)
    KcT = sb("KcT", [128, 2, 128], BF16)
    Vc = sb("Vc", [128, 2, 130], BF16)
    state = sb("state", [128, 8, 128], F32)
    state_b = sb("state_b", [128, 8, 128], BF16)
    modT = sb("modT", [128, 32, nseq], F32)
    sc1 = sb("sc1", [128, 16, nseq], F32)
    gnT = sb("gnT", [128, 16], F32)
    grT = sb("grT", [128, 8], F32)
    badaT = sb("badaT", [128, 48], F32)
    cT = sb("cT", [128, nseq * 16], F32)
    cTt = sb("cTt", [128, nseq * 16], F32)
    siluc = sb("siluc", [128, 16, nseq], BF16)
    peT = [sb("peT%d" % i, [128, 32], F32) for i in range(2)]
    peTb = [sb("peTb%d" % i, [128, 32], BF16) for i in range(2)]
    w2st = [sb("w2st%d" % i, [128, 2, 128], F32) for i in range(2)]
    w2bf = [sb("w2bf%d" % i, [128, 2, 128], BF16) for i in range(2)]
    cpi = sb("cpi", [128, 1], F32)
    ones_b = sb("ones_b", [128, 128], BF16)
    ceps = sb("ceps", [128, 1], F32)
    ss4 = sb("ss4", [128, 4], F32)
    rs4 = sb("rs4", [128, 4], F32)
    angI = sb("angI", [128, TT], I32)
    pos_i = angI
    qT = [sb("qT%d" % i, [128, TT], BF16) for i in range(4)]
    kT = sb("kT", [128, TT], BF16)
    qxT = sb("qxT", [128, TT], BF16)
    vT = sb("vT", [128, TT], BF16)
    xb = sb("xb", [128, TT], BF16)
    r1 = sb("r1", [128, TT], F32)
    r2 = sb("r2", [128, TT], F32)
    Vr = sb("Vr", [128, 4, 128], BF16)
    Kz = sb("Kz", [128, 4, 128], BF16)
    scm4 = sb("scm4", [128, 4, 128], BF16)
    sb3 = sb("sb3", [128, 3, 128], BF16)
    bst = sb("bst", [128, 4, 6], F32)
    mv = sb("mv", [128, 4, 2], F32)
    nmr = sb("nmr", [128, 4], F32)
    on = sb("on", [128, 4, 128], BF16)
    tg = sb("tg", [128, TT], F32)
    gg = sb("gg", [128, TT], BF16)
    gates = sb("gates", [128, 4, 24], F32)
    hb = sb("hb", [128, 2, 2], F32)
    hbh = sb("hbh", [128, 2, 2], F32)
    hx = sb("hx", [128, 64], F32)
    ht = sb("ht", [128, 64], F32)
    shT = sb("shT", [128, 2, 64], BF16)
    kcx = sb("kcx", [128, 64], BF16)
    vct = sb("vct", [32, 2, 130], BF16)
    PT = [sb("PT%d" % i, [128, TT], BF16) for i in range(2)]
    stmp = sb("stmp", [128, 128], F32)
    etmp = sb("etmp", [128, 128], F32)
    rsum = sb("rsum", [128, 1], F32)
    rinv = sb("rinv", [128, 1], F32)
    ppad = sb("ppad", [128, 4, 132], F32)
    imp = sb("imp", [128, 32], F32)
    m8 = sb("m8", [128, 16], F32)
    wk = sb("wk", [128, 32], F32)
    negm = sb("negm", [128, 32], BF16)
    negm4 = sb("negm4", [128, 4, 32], BF16)
    negmT = sb("negmT", [32, 2, TT], BF16)
    den = sb("den", [128, 1], F32)
    den4 = sb("den4", [128, 4], F32)
    coef = sb("coef", [128, 1], F32)
    acc = sb("acc", [128, 4, 128], F32)
    accb = sb("accb", [128, 4, 128], BF16)
    angA, angB, angC, ta, tb2 = r1, r2, tg, r1, r2
    sbuf_used = base[0]

    pp = [psum("pp%d" % i, [128, 512], F32) for i in range(2)]
    psc = [psum("psc%d" % i, [128, 512], F32) for i in range(2)]
    po = [psum("po%d" % i, [128, 2, 256], F32) for i in range(2)]
    ptrs = [psum("ptr%d" % i, [128, 1024], BF16) for i in range(2)]

    pm = po[1][:, :, :].rearrange("p a c -> p (a c)")
    pofl = [po[i][:, :, :].rearrange("p a c -> p (a c)") for i in range(2)]
    ptrf = [ptrs[i][:, :].bitcast(F32) for i in range(2)]
    br_i = [0]

    class _Ptr:
        def __getitem__(self, idx):
            p_, h_, c_ = idx
            return ptrs[h_][p_, c_] if not (isinstance(c_, slice) and c_ == slice(None)) else ptrs[h_][p_, 0:512]
    ptr = _Ptr()

    class _M2:
        def __getitem__(self, idx):
            p_, cc, t_ = idx
            return xn[p_, cc // 4, (cc % 4) * TT + (t_.start or 0):(cc % 4) * TT + (t_.stop if t_.stop is not None else TT)]
    m2 = _M2()

    def cB(off, n):
        return cbf[:, off:off + n]

    ident_b = cB(CB_ID, 128)
    swap_b = cB(CB_SW, 128)
    tri_b = cB(CB_TRI, 128)
    anti_b = cB(CB_ANTI, 128)

    def bc(ap2d, n):
        a = ap2d
        return bass.AP(tensor=a.tensor, offset=a.offset, ap=[list(a.ap[0]), [0, n]] + [list(z) for z in a.ap[1:]])

    class WS:
        def __init__(self):
            self.specs = []
            self.specsB = []
            self.pos = 0
            self.issued = 0
            self.posB = 0
            self.issuedB = 0
            self.collect = True
            self.gid = {}

        def _issue(self, k):
            spec = self.specs[k]
            slot = k % 2
            kind = spec[0]
            st_t, bf_t = wst[slot], wbf[slot]
            if kind == "cols":
                _, w, col0, ncols, nk = spec
                w = wmap[w]
                dst = st_t[:, 0:nk * ncols].rearrange("p (k c) -> p k c", c=ncols)
                src = w[0:nk * 128, col0:col0 + ncols].rearrange("(k p) c -> p k c", p=128)
                half = nk // 2
                def f(e, s, dst=dst, src=src, half=half, nk=nk):
                    e.dma_start(out=dst[:, 0:half, :], in_=src[:, 0:half, :]).then_inc(s, 16)
                    e.dma_start(out=dst[:, half:nk, :], in_=src[:, half:nk, :]).then_inc(s, 16)
                P.dma("sp", f, writes=["wst%d" % slot], semkey="w%d" % slot, n=2)
                n = nk * ncols
            elif kind == "two":
                _, col0 = spec
                d0 = st_t[:, 0:1024].rearrange("p (k c) -> p k c", c=128)
                d1 = st_t[:, 1024:2048].rearrange("p (k c) -> p k c", c=128)
                s0 = w_ur[:, col0:col0 + 128].rearrange("(k p) c -> p k c", p=128)
                s1 = w_un[:, col0:col0 + 128].rearrange("(k p) c -> p k c", p=128)
                def f(e, s, d0=d0, d1=d1, s0=s0, s1=s1):
                    e.dma_start(out=d0, in_=s0).then_inc(s, 16)
                    e.dma_start(out=d1, in_=s1).then_inc(s, 16)
                P.dma("sp", f, writes=["wst%d" % slot], semkey="w%d" % slot, n=2)
                n = 2048
            elif kind == "wout":
                _, ct, pc = spec
                dst = st_t[:, :].rearrange("p (k c) -> p k c", c=512)
                src = w_out[pc * 512:(pc + 1) * 512, ct * 512:(ct + 1) * 512].rearrange("(k p) c -> p k c", p=128)
                def f(e, s, dst=dst, src=src):
                    e.dma_start(out=dst[:, 0:2, :], in_=src[:, 0:2, :]).then_inc(s, 16)
                    e.dma_start(out=dst[:, 2:4, :], in_=src[:, 2:4, :]).then_inc(s, 16)
                P.dma("sp", f, writes=["wst%d" % slot], semkey="w%d" % slot, n=2)
                n = 2048
            elif kind == "w1":
                _, w, pc = spec
                w = wmap[w]
                dst = st_t[:, :].rearrange("p (i c) -> p i c", c=256)
                src = w[pc * 1024:(pc + 1) * 1024, :].rearrange("(i p) c -> p i c", p=128)
                def f(e, s, dst=dst, src=src):
                    e.dma_start(out=dst[:, 0:4, :], in_=src[:, 0:4, :]).then_inc(s, 16)
                    e.dma_start(out=dst[:, 4:8, :], in_=src[:, 4:8, :]).then_inc(s, 16)
                P.dma("sp", f, writes=["wst%d" % slot], semkey="w%d" % slot, n=2)
                n = 2048
            if k % 2 == 0:
                P.op("dve", lambda e, bf_t=bf_t, st_t=st_t, n=n: e.tensor_copy(out=bf_t[:, 0:n], in_=st_t[:, 0:n]),
                     reads=["wst%d" % slot], writes=["wbf%d" % slot])
            else:
                P.op("act", lambda e, bf_t=bf_t, st_t=st_t, n=n: e.activation(out=bf_t[:, 0:n], in_=st_t[:, 0:n], func=AF.Identity),
                     reads=["wst%d" % slot], writes=["wbf%d" % slot])

        def nextA(self, spec):
            k = self.pos
            assert self.specs[k] == spec, (k, self.specs[k], spec)
            while self.issued < min(k + 2, len(self.specs)):
                self._issue(self.issued)
                self.issued += 1
            self.pos += 1
            slot = k % 2
            return wbf[slot], "wbf%d" % slot

        def finish_collect(self):
            for sp_ in self.specsB:
                if sp_ not in self.gid:
                    self.gid[sp_] = len(self.gid)
            self.uniq = sorted(self.gid, key=lambda z: self.gid[z])
            self.specs = self.specs + self.uniq
            self.scr_keys = ["scr%d" % i for i in range(len(self.uniq))]

        def preconvert(self, wscr):
            self.wscr = wscr
            for u in self.uniq:
                k = self.pos
                wt, wk_ = self.nextA(u)
                gid_ = self.gid[u]
                P.dma("sp", lambda e, s, wt=wt, gid_=gid_: e.dma_start(out=wscr[gid_, :, :], in_=wt[:, :]).then_inc(s, 16),
                      reads=[wk_], writes=["scr%d" % gid_], semkey="ws%d" % (k % 2))

        def _issueB(self, k):
            gid_ = self.gid[self.specsB[k]]
            slot = k % NSLOT
            wscr = self.wscr
            extra = ["wst0", "wst1", "wbf0", "wbf1"] if k < len(self.uniq) + NSLOT else []
            P.dma("sp", lambda e, s, slot=slot, gid_=gid_: e.dma_start(out=wsl[slot][:, :], in_=wscr[gid_, :, :]).then_inc(s, 16),
                  reads=self.scr_keys, writes=["wsl%d" % slot] + extra, semkey="wl%d" % slot)

        def next(self, spec):
            is_a = (spec[0] == "cols" and spec[1] == "w_ada")
            if self.collect:
                (self.specs if is_a else self.specsB).append(spec)
                return None, None
            if is_a:
                return self.nextA(spec)
            k = self.posB
            assert self.specsB[k] == spec, (k, self.specsB[k], spec)
            nu = len(self.uniq)
            if k < nu:
                assert self.uniq[k] == spec
                ka = self.pos
                wt, wk_ = self.nextA(spec)
                gid_ = self.gid[spec]
                wscr = self.wscr
                P.dma("sp", lambda e, s, wt=wt, gid_=gid_: e.dma_start(out=wscr[gid_, :, :], in_=wt[:, :]).then_inc(s, 16),
                      reads=[wk_], writes=["scr%d" % gid_], semkey="ws%d" % (ka % 2))
                self.posB += 1
                self.issuedB = max(self.issuedB, nu)
                return wt, wk_
            while self.issuedB < min(k + NSLOT, len(self.specsB)):
                self._issueB(self.issuedB)
                self.issuedB += 1
            self.posB += 1
            slot = k % NSLOT
            return wsl[slot], "wsl%d" % slot

    W = WS()
    wmap = {"w_in": w_in, "w_ada": w_ada, "w_ck1": w_ck1, "w_cv1": w_cv1}
    pp_i = [0]
    psc_i = [0]

    def next_pp():
        i = pp_i[0] % 2
        pp_i[0] += 1
        return pp[i], "pp%d" % i

    def next_psc():
        i = psc_i[0] % 2
        psc_i[0] += 1
        return psc[i], "psc%d" % i

    def proj_fm(col0, dst_fn):
        wt, wk_ = W.next(("cols", "w_in", col0, 128, 16))
        if wt is None:
            flush_pending()
            dst_fn(None, None)
            return
        ps_t, ps_k = next_pp()
        w3 = wt[:, :].rearrange("p (k c) -> p k c", c=128)
        def f(e, w3=w3, ps_t=ps_t):
            ins = None
            for kc in range(16):
                ins = e.matmul(ps_t[:, :], lhsT=w3[:, kc, :], rhs=hT[:, kc, :], start=(kc == 0), stop=(kc == 15))
            return ins
        P.op("pe", f, reads=[wk_, "hT"], writes=[ps_k])
        flush_pending()
        dst_fn(ps_t, ps_k)

    pending = []

    def flush_pending():
        while pending:
            pending.pop(0)()

    def rope_to(ps_t, ps_k, dst_ap, dst_key):
        if ps_t is None:
            return
        P.op("act", lambda e: e.activation(out=xb[:, :], in_=ps_t[:, :], func=AF.Identity), reads=[ps_k], writes=["xb"])
        P.op("pool", lambda e: e.tensor_tensor(out=r1[:, :], in0=xb[:, :], in1=cosT[:, :], op=ALU.mult), reads=["xb", "cosT"], writes=["r1"])

        def rest():
            P.op("pe", lambda e: e.matmul(pm[:, :], lhsT=swap_b, rhs=xb[:, :], start=True, stop=True), reads=["xb", "cbf"], writes=["po1"])
            P.op("dve", lambda e: e.tensor_tensor(out=r2[:, :], in0=pm[:, :], in1=sinT[:, :], op=ALU.mult), reads=["po1", "sinT"], writes=["r2"])
            P.op("pool", lambda e: e.tensor_tensor(out=dst_ap, in0=r1[:, :], in1=r2[:, :], op=ALU.add), reads=["r1", "r2"], writes=[dst_key])
        pending.append(rest)

    def copy_to(ps_t, ps_k, dst_ap, dst_key):
        if ps_t is None:
            return
        P.op("act", lambda e: e.activation(out=dst_ap, in_=ps_t[:, :], func=AF.Identity), reads=[ps_k], writes=[dst_key])

    def transpose4(src_fn, src_keys, half, pre=()):
        def f(e):
            ins = None
            for j in range(4):
                ins = e.transpose(ptr[:, half, j * 128:(j + 1) * 128], src_fn(j), ident_b)
            return ins
        P.op("pe", f, reads=list(src_keys) + ["cbf"], writes=["ptr%d" % half])

    tr_i = [0]

    def next_tr():
        i = tr_i[0] % 2
        tr_i[0] += 1
        return i

    def gate_a(col0):
        def g(ps_t, ps_k):
            if ps_t is None:
                return
            P.op("act", lambda e: e.activation(out=tg[:, :], in_=ps_t[:, :], func=AF.Tanh, scale=0.5), reads=[ps_k], writes=["tg"])
            P.op("dve", lambda e: e.scalar_tensor_tensor(out=gg[:, :], in0=tg[:, :], scalar=1.0, in1=ps_t[:, :], op0=ALU.add, op1=ALU.mult),
                 reads=["tg", ps_k], writes=["gg"])
        proj_fm(col0, g)

    def gate_b(ydst, ykey):
        P.op("pool", lambda e: e.tensor_tensor(out=ydst, in0=ydst, in1=gg[:, :], op=ALU.mult), reads=["gg", ykey], writes=[ykey])

    def setup():
        def ld(dst, src, key):
            P.dma("sp", lambda e, s: e.dma_start(out=dst, in_=src).then_inc(s, 16), writes=[key], semkey="ld_" + key)
        ld(cbf[:, :], cbf_d, "cbf")
        ld(cf[:, :], cf_d, "cf")
        ld(cT[:, :], cT_d, "cT")
        ld(badaT[:, :], b_adaT, "badaT")
        ld(gnT[:, :], g_normT, "gnT")
        ld(grT[:, :], g_retT, "grT")
        ld(peT[0][:, :], pe_ckT, "peT0")
        ld(peT[1][:, :], pe_cvT, "peT1")
        ld(w2st[0][:, :, :], w_ck2.rearrange("(m p) c -> p m c", p=128), "w2st0")
        ld(w2st[1][:, :, :], w_cv2.rearrange("(m p) c -> p m c", p=128), "w2st1")
        P.op("pool", lambda e: e.memset(cpi[:, :], float(np.pi)), writes=["cpi"])
        P.op("pool", lambda e: e.memset(ones_b[:, :], 1.0), writes=["ones_b"])
        P.op("pool", lambda e: e.memset(ceps[:, :], EPS), writes=["ceps"])
        P.op("pool", lambda e: e.memset(Vs[:, :, :, 128:130], 1.0), writes=["Vs"])
        P.op("pool", lambda e: e.memset(Vw[:, :, :, 128:130], 1.0), writes=["Vw"])
        P.op("pool", lambda e: e.memset(Vc[:, :, :], 0.0), writes=["Vc"])
        P.op("pool", lambda e: e.memset(vct[:, :, 128:130], 1.0), writes=["vct"])
        P.op("pool", lambda e: e.memset(KcT[:, :, :], 0.0), writes=["KcT"])
        P.op("pool", lambda e: e.memset(ppad[:, :, :], 0.0), writes=["ppad"])
        for i in range(2):
            P.op("pool", lambda e, i=i: e.tensor_copy(out=peTb[i][:, :], in_=peT[i][:, :]), reads=["peT%d" % i], writes=["peTb%d" % i])
            P.op("pool", lambda e, i=i: e.tensor_copy(out=w2bf[i][:, :, :], in_=w2st[i][:, :, :]), reads=["w2st%d" % i], writes=["w2bf%d" % i])
        P.op("pool", lambda e: e.tensor_scalar(out=grT[:, :], in0=grT[:, :], scalar1=0.5, scalar2=None, op0=ALU.mult), reads=["grT"], writes=["grT"])
        P.op("act", lambda e: e.activation(out=cTt[:, :], in_=cT[:, :], func=AF.Tanh, scale=0.5), reads=["cT"], writes=["cTt"])
        P.op("dve", lambda e: e.scalar_tensor_tensor(out=cTt[:, :], in0=cTt[:, :], scalar=1.0, in1=cT[:, :], op0=ALU.add, op1=ALU.mult),
             reads=["cTt", "cT"], writes=["cTt"])
        for b in range(nseq):
            P.op("dve", lambda e, b=b: e.tensor_scalar(out=siluc[:, :, b], in0=cTt[:, b * 16:(b + 1) * 16], scalar1=0.5, scalar2=None, op0=ALU.mult),
                 reads=["cTt"], writes=["siluc"])
        for grp in range(32):
            wt, wk_ = W.next(("cols", "w_ada", grp * 128, 128, 16))
            if wt is None:
                continue
            w3 = wt[:, :].rearrange("p (k c) -> p k c", c=128)
            def f(e, w3=w3):
                ins = None
                for kc in range(16):
                    ins = e.matmul(pm[:, 0:nseq], lhsT=w3[:, kc, :], rhs=siluc[:, kc, :], start=(kc == 0), stop=(kc == 15))
                return ins
            P.op("pe", f, reads=[wk_, "siluc"], writes=["po1"])
            P.op("dve", lambda e, grp=grp: e.tensor_scalar(out=modT[:, grp, :], in0=pm[:, 0:nseq], scalar1=badaT[:, grp:grp + 1], scalar2=None, op0=ALU.add),
                 reads=["po1", "badaT"], writes=["modT"])
        if W.collect:
            return
        for b in range(nseq):
            P.op("dve", lambda e, b=b: e.scalar_tensor_tensor(out=sc1[:, :, b], in0=modT[:, 16:32, b], scalar=1.0, in1=gnT[:, :], op0=ALU.add, op1=ALU.mult),
                 reads=["modT", "gnT"], writes=["sc1"])

    def gate_rows():
        grow = xt[0:nseq, 1, :]
        for grp in range(16):
            wt, wk_ = W.next(("cols", "w_ada", (32 + grp) * 128, 128, 16))
            if wt is None:
                continue
            w3 = wt[:, :].rearrange("p (k c) -> p k c", c=128)
            def f(e, w3=w3):
                ins = None
                for kc in range(16):
                    ins = e.matmul(pm[0:nseq, 128:256], lhsT=siluc[:, kc, :], rhs=w3[:, kc, :], start=(kc == 0), stop=(kc == 15))
                return ins
            P.op("pe", f, reads=[wk_, "siluc"], writes=["po1"])
            P.op("act", lambda e, grp=grp: e.activation(out=grow[:, grp * 128:(grp + 1) * 128], in_=pm[0:nseq, 128:256], func=AF.Identity),
                 reads=["po1"], writes=["xt1"])
        if W.collect:
            return
        P.dma("sp", lambda e, s: e.dma_start(out=gsc_d, in_=grow).then_inc(s, 16), reads=["xt1"], writes=["gsc"], semkey="gs")

    def seq_start(b):
        if W.collect:
            return
        P.dma("sp", lambda e, s: e.dma_start(out=gate_bc1[:, :], in_=gsc_d[b, :].partition_broadcast(128)).then_inc(s, 16), reads=["gsc"], writes=["gate_bc"], semkey="gb")
        P.dma("sp", lambda e, s: e.dma_start(out=xt[:, 0, :], in_=b_gate[0, :].partition_broadcast(128)).then_inc(s, 16), writes=["xt0"], semkey="x0")
        P.op("dve", lambda e: e.tensor_tensor(out=gate_bc1[:, :], in0=gate_bc1[:, :], in1=xt[:, 0, :], op=ALU.add), reads=["gate_bc", "xt0"], writes=["gate_bc"])
        P.op("pool", lambda e: e.tensor_scalar(out=gate_bc1[:, :], in0=gate_bc1[:, :], scalar1=0.5, scalar2=None, op0=ALU.mult), reads=["gate_bc"], writes=["gate_bc"])

    def sincos(src_add, dst, dkey, fold_sign):
        P.op("dve", lambda e: e.tensor_scalar(out=angB[:, :], in0=angA[:, :], scalar1=float(src_add), scalar2=float(1.0 / (2 * np.pi)), op0=ALU.add, op1=ALU.mult),
             reads=["r1"], writes=["r2"])
        P.op("dve", lambda e: e.tensor_copy(out=angI[:, :], in_=angB[:, :]), reads=["r2"], writes=["angI"])
        P.op("dve", lambda e: e.tensor_copy(out=angB[:, :], in_=angI[:, :]), reads=["angI"], writes=["r2"])
        P.op("dve", lambda e: e.tensor_scalar(out=angC[:, :], in0=angA[:, :], scalar1=float(src_add), scalar2=None, op0=ALU.add), reads=["r1"], writes=["tg"])
        P.op("dve", lambda e: e.scalar_tensor_tensor(out=angC[:, :], in0=angB[:, :], scalar=-float(2 * np.pi), in1=angC[:, :], op0=ALU.mult, op1=ALU.add),
             reads=["r2", "tg"], writes=["tg"])
        P.op("dve", lambda e: e.tensor_scalar(out=angB[:, :], in0=angC[:, :], scalar1=0.0, scalar2=float(2 * np.pi), op0=ALU.is_lt, op1=ALU.mult),
             reads=["tg"], writes=["r2"])
        P.op("dve", lambda e: e.tensor_tensor(out=angC[:, :], in0=angC[:, :], in1=angB[:, :], op=ALU.add), reads=["tg", "r2"], writes=["tg"])
        P.op("dve", lambda e: e.tensor_scalar(out=angC[:, :], in0=angC[:, :], scalar1=float(2 * np.pi), scalar2=None, op0=ALU.min), reads=["tg"], writes=["tg"])
        P.op("act", lambda e: e.activation(out=dst[:, :], in_=angC[:, :], func=AF.Sin, bias=cpi[:, 0:1], scale=-1.0), reads=["tg", "cpi"], writes=[dkey])
        if fold_sign:
            P.op("dve", lambda e: e.tensor_scalar(out=dst[:, :], in0=dst[:, :], scalar1=cf[:, CF_SGN:CF_SGN + 1], scalar2=None, op0=ALU.mult),
                 reads=[dkey, "cf"], writes=[dkey])

    def body(b, T):
        tok0 = T * TT
        dry = W.collect
        if not dry:
            if not xpre[0]:
                for st in range(4):
                    P.dma("sp", lambda e, s, st=st: e.dma_start(out=xt[:, st, :], in_=x_d[b, tok0 + st * 128:tok0 + (st + 1) * 128, :]).then_inc(s, 16),
                          writes=["xt%d" % st], semkey="x%d" % st)
            xpre[0] = False
            P.dma("sp", lambda e, s: e.dma_start(out=pos_i[:, :], in_=pos_d[b, tok0:tok0 + TT].partition_broadcast(128)).then_inc(s, 16),
                  writes=["angI"], semkey="pos")
            for st in range(4):
                P.op("act", lambda e, st=st: e.activation(out=xn[:, st, :], in_=xt[:, st, :], func=AF.Square, accum_out=ss4[:, st:st + 1]),
                     reads=["xt%d" % st], writes=["xn%d" % st, "ss4"])
            if STAGE <= 1.1:
                return
            P.op("act", lambda e: e.activation(out=rs4[:, :], in_=ss4[:, :], func=AF.Sqrt, bias=ceps[:, 0:1], scale=1.0 / D), reads=["ss4", "ceps"], writes=["rs4"])
            P.op("dve", lambda e: e.reciprocal(out=rs4[:, :], in_=rs4[:, :]), reads=["rs4"], writes=["rs4"])
            if STAGE <= 1.2:
                return
            for st in range(4):
                if st % 2 == 0:
                    P.op("act", lambda e, st=st: e.activation(out=xn[:, st, :], in_=xt[:, st, :], func=AF.Identity, scale=rs4[:, st:st + 1]),
                         reads=["xt%d" % st, "rs4"], writes=["xn%d" % st])
                else:
                    P.op("dve", lambda e, st=st: e.tensor_scalar(out=xn[:, st, :], in0=xt[:, st, :], scalar1=rs4[:, st:st + 1], scalar2=None, op0=ALU.mult),
                         reads=["xt%d" % st, "rs4"], writes=["xn%d" % st])
            if STAGE <= 1.3:
                return
            for fc in range(int(_os.environ.get('KFC', '16'))):
                h = next_tr()
                transpose4(lambda j, fc=fc: xn[:, j, fc * 128:(fc + 1) * 128], ["xn0", "xn1", "xn2", "xn3"], h)
                if _os.environ.get('KNODVE'):
                    continue
                kvar = _os.environ.get('KVAR', '3')
                if kvar == '1':
                    P.op("dve", lambda e, fc=fc, h=h: e.tensor_scalar(out=hT[:, fc, :], in0=ptr[:, h, :], scalar1=sc1[:, fc, b:b + 1], scalar2=None, op0=ALU.mult),
                         reads=["ptr%d" % h, "sc1", "modT"], writes=["hT"])
                    continue
                if kvar == '2':
                    P.op("dve", lambda e, fc=fc, h=h: e.tensor_copy(out=hT[:, fc, :], in_=ptr[:, h, :]),
                         reads=["ptr%d" % h, "sc1", "modT"], writes=["hT"])
                    continue
                if kvar == '3':
                    P.op("act", lambda e, fc=fc, h=h: e.activation(out=hT[:, fc, :], in_=ptr[:, h, :], func=AF.Identity, scale=sc1[:, fc, b:b + 1], bias=modT[:, fc, b:b + 1]),
                         reads=["ptr%d" % h, "sc1", "modT"], writes=["hT"])
                    continue
                P.op("dve", lambda e, fc=fc, h=h: e.tensor_scalar(out=hT[:, fc, :], in0=ptr[:, h, :], scalar1=sc1[:, fc, b:b + 1], scalar2=modT[:, fc, b:b + 1],
                                                                 op0=ALU.mult, op1=ALU.add),
                     reads=["ptr%d" % h, "sc1", "modT"], writes=["hT"])
            if debug and b == 0 and T == dbgT:
                P.op("dve", lambda e: e.tensor_copy(out=xt[:, 3, :].rearrange("p (a c) -> p a c", c=TT)[:, 0:4, :], in_=hT[:, 0:4, :]), reads=["hT"], writes=["xt3"])
            if STAGE <= 1.4:
                return
            P.op("dve", lambda e: e.tensor_copy(out=angA[:, :], in_=pos_i[:, :]), reads=["angI"], writes=["r1"])
            P.op("dve", lambda e: e.tensor_scalar(out=angA[:, :], in0=angA[:, :], scalar1=cf[:, CF_INV:CF_INV + 1], scalar2=None, op0=ALU.mult),
                 reads=["r1", "cf"], writes=["r1"])
            sincos(0.0, sinT, "sinT", True)
            sincos(np.pi / 2, cosT, "cosT", False)
            if T == 0:
                P.op("pool", lambda e: e.memset(state[:, :, :], 0.0), writes=["state"])
                P.op("pool", lambda e: e.memset(state_b[:, :, :], 0.0), writes=["state_b"])
                P.op("pool", lambda e: e.memset(kroll[:, :, :], 0.0), writes=["kroll"])
                P.op("pool", lambda e: e.memset(vroll[:, :, :], 0.0), writes=["vroll"])
            else:
                P.op("pool", lambda e: e.tensor_copy(out=kroll[:, :, 0:16], in_=kroll[:, :, TT:TT + 16]), reads=["kroll"], writes=["kroll"])
                P.op("pool", lambda e: e.tensor_copy(out=vroll[:, :, 0:16], in_=vroll[:, :, TT:TT + 16]), reads=["vroll"], writes=["vroll"])

        if STAGE <= 2:
            return
        for h in range(8):
            proj_fm(O_RQ + h * 128, lambda p_, k_: rope_to(p_, k_, qT[0][:, :], "qT0"))
            proj_fm(O_RK + h * 128, lambda p_, k_: rope_to(p_, k_, kT[:, :], "kT"))
            proj_fm(O_RV + h * 128, lambda p_, k_: copy_to(p_, k_, vT[:, :], "vT"))
            if not dry:
                xi_ap = bc(cbf[:, CB_XI + h * 128:CB_XI + (h + 1) * 128], 4)
                P.op("pool", lambda e, xi_ap=xi_ap: e.tensor_tensor(out=qxT[:, :].rearrange("p (a c) -> p a c", c=128), in0=qT[0][:, :].rearrange("p (a c) -> p a c", c=128),
                                                                   in1=xi_ap, op=ALU.mult), reads=["qT0", "cbf"], writes=["qxT"])
                hh = next_tr()
                transpose4(lambda j: vT[:, j * 128:(j + 1) * 128], ["vT"], hh)
                P.op("act", lambda e, hh=hh: e.activation(out=Vr[:, :, :], in_=ptr[:, hh, :].rearrange("p (a c) -> p a c", c=128), func=AF.Identity),
                     reads=["ptr%d" % hh], writes=["Vr"])
                hh = next_tr()
                transpose4(lambda j: kT[:, j * 128:(j + 1) * 128], ["kT"], hh)
                P.op("act", lambda e, hh=hh, h=h: e.activation(out=Kz[:, :, :], in_=ptr[:, hh, :].rearrange("p (a c) -> p a c", c=128), func=AF.Identity,
                                                              scale=cf[:, CF_ZETA + h:CF_ZETA + h + 1]),
                     reads=["ptr%d" % hh, "cf"], writes=["Kz"])
                dm4 = bc(cbf[:, CB_DM + h * 128:CB_DM + (h + 1) * 128], 4)
                psA, kA = next_psc()
                psB, kB = next_psc()
                def f_sc(e, psA=psA):
                    ins = None
                    for c in range(4):
                        cs = slice(c * 128, (c + 1) * 128)
                        ins = e.matmul(psA[:, cs], lhsT=kT[:, cs], rhs=qT[0][:, cs], start=True, stop=True)
                    return ins
                P.op("pe", f_sc, reads=["kT", "qT0"], writes=[kA])
                def f_kv(e, psB=psB):
                    ins = None
                    for c in range(4):
                        ins = e.matmul(psB[:, c * 128:(c + 1) * 128], lhsT=Kz[:, c, :], rhs=Vr[:, c, :], start=True, stop=True)
                    return ins
                P.op("pe", f_kv, reads=["Kz", "Vr"], writes=[kB])
                P.op("dve", lambda e, psA=psA, dm4=dm4: e.tensor_tensor(out=scm4[:, :, :], in0=psA[:, :].rearrange("p (a c) -> p a c", c=128), in1=dm4, op=ALU.mult),
                     reads=[kA, "cbf"], writes=["scm4"])
                for c in range(3):
                    P.op("dve", lambda e, h=h, c=c, psB=psB: e.scalar_tensor_tensor(out=state[:, h, :], in0=state[:, h, :], scalar=DECAY[h], in1=psB[:, c * 128:(c + 1) * 128],
                                                                                  op0=ALU.mult, op1=ALU.add), reads=["state", kB], writes=["state"])
                    P.op("act", lambda e, h=h, c=c: e.activation(out=sb3[:, c, :], in_=state[:, h, :], func=AF.Identity), reads=["state"], writes=["sb3"])
                def f_o(e, h=h):
                    ins = None
                    for c in range(4):
                        cs = slice(c * 128, (c + 1) * 128)
                        e.matmul(pofl[0][:, cs], lhsT=scm4[:, c, :], rhs=Vr[:, c, :], start=True, stop=False)
                        ins = e.matmul(pofl[0][:, cs], lhsT=qxT[:, cs], rhs=(state_b[:, h, :] if c == 0 else sb3[:, c - 1, :]), start=False, stop=True)
                    return ins
                P.op("pe", f_o, reads=["scm4", "Vr", "qxT", "state_b", "sb3"], writes=["po0"])
                P.op("dve", lambda e, h=h, psB=psB: e.scalar_tensor_tensor(out=state[:, h, :], in0=state[:, h, :], scalar=DECAY[h], in1=psB[:, 384:512],
                                                                          op0=ALU.mult, op1=ALU.add), reads=["state", kB], writes=["state"])
                P.op("act", lambda e, h=h: e.activation(out=state_b[:, h, :], in_=state[:, h, :], func=AF.Identity), reads=["state"], writes=["state_b"])
                for c in range(4):
                    P.op("dve", lambda e, c=c: e.bn_stats(out=bst[:, c, :], in_=pofl[0][:, c * 128:(c + 1) * 128]), reads=["po0"], writes=["bst"])
                    P.op("dve", lambda e, c=c: e.bn_aggr(out=mv[:, c, :], in_=bst[:, c, :]), reads=["bst"], writes=["mv"])
                P.op("act", lambda e: e.activation(out=rs4[:, :], in_=mv[:, :, 1], func=AF.Sqrt, bias=ceps[:, 0:1], scale=1.0), reads=["mv", "ceps"], writes=["rs4"])
                P.op("dve", lambda e: e.reciprocal(out=rs4[:, :], in_=rs4[:, :]), reads=["rs4"], writes=["rs4"])
                P.op("dve", lambda e: e.scalar_tensor_tensor(out=nmr[:, :], in0=mv[:, :, 0], scalar=-1.0, in1=rs4[:, :], op0=ALU.mult, op1=ALU.mult),
                     reads=["mv", "rs4"], writes=["nmr"])
                for c in range(4):
                    P.op("act", lambda e, c=c: e.activation(out=on[:, c, :], in_=po[0][:, c // 2, (c % 2) * 128:(c % 2) * 128 + 128], func=AF.Identity,
                                                           scale=rs4[:, c:c + 1], bias=nmr[:, c:c + 1]),
                         reads=["po0", "nmr", "rs4"], writes=["on"])
            gate_a(O_RG + h * 128)
            if not dry:
                def tail(h=h):
                    hh = next_tr()
                    transpose4(lambda j: on[:, j, :], ["on"], hh)
                    P.op("act", lambda e, hh=hh, h=h: e.activation(out=yT[:, h, :], in_=ptr[:, hh, :], func=AF.Identity, scale=grT[:, h:h + 1]),
                         reads=["ptr%d" % hh, "grT"], writes=["yT%d" % h])
                    gate_b(yT[:, h, :], "yT%d" % h)
                pending.append(tail)

        if STAGE <= 3:
            return
        for g in range(2):
            proj_fm(O_KC + g * 128, lambda p_, k_, g=g: copy_to(p_, k_, kroll[:, g, 16:16 + TT], "kroll"))
            proj_fm(O_VC + g * 128, lambda p_, k_, g=g: copy_to(p_, k_, vroll[:, g, 16:16 + TT], "vroll"))
            proj_fm(O_KS + g * 128, lambda p_, k_, g=g: rope_to(p_, k_, ksT[:, g, tok0:tok0 + TT], "ksT"))
            proj_fm(O_KW + g * 128, lambda p_, k_, g=g: rope_to(p_, k_, kwT[:, g, (T % 2) * TT:(T % 2 + 1) * TT], "kwT"))
            for (off, Vd, vk) in ((O_VS, Vs, "Vs"), (O_VW, Vw, "Vw")):
                proj_fm(off + g * 128, lambda p_, k_: copy_to(p_, k_, vT[:, :], "vT"))
                if not dry:
                    hh = next_tr()
                    vb0 = 4 * T if vk == "Vs" else 4 * (T % 2)
                    transpose4(lambda j: vT[:, j * 128:(j + 1) * 128], ["vT"], hh)
                    P.op("act", lambda e, hh=hh, Vd=Vd, g=g, vb0=vb0: e.activation(out=Vd[:, g, vb0:vb0 + 4, 0:128], in_=ptr[:, hh, :].rearrange("p (a c) -> p a c", c=128), func=AF.Identity),
                         reads=["ptr%d" % hh], writes=[vk])
        nl0 = 1 if T == 0 else 0
        ncol = 32 - nl0
        for kv in range(2):
            roll = kroll if kv == 0 else vroll
            rkey = "kroll" if kv == 0 else "vroll"
            w1 = w_ck1 if kv == 0 else w_cv1
            for pc in range(4):
                wt, wk_ = W.next(("w1", "w_ck1" if kv == 0 else "w_cv1", pc))
                if wt is None:
                    continue
                w3 = wt[:, :].rearrange("p (i c) -> p i c", c=256)
                def f(e, w3=w3, pc=pc, roll=roll, kv=kv):
                    ins = None
                    for il in range(8):
                        i = pc * 8 + il
                        for mh in range(2):
                            rhs = roll[:, :, i:i + 497:16]
                            e.matmul(psc[0][:, mh * 64:mh * 64 + 64].rearrange("p (g n) -> p g n", n=32), lhsT=w3[:, il, mh * 128:(mh + 1) * 128], rhs=rhs,
                                     start=(i == 0 and mh == 0), stop=(i == 31), skip_group_check=True)
                            ins = e.matmul(psc[1][:, mh:mh + 1], lhsT=w3[:, il, mh * 128:(mh + 1) * 128], rhs=peTb[kv][:, i:i + 1],
                                           start=(i == 0 and mh == 0), stop=(i == 31), skip_group_check=True)
                    return ins
                P.op("pe", f, reads=[wk_, rkey, "peTb%d" % kv], writes=["psc0", "psc1"])
            if dry:
                continue
            P.op("dve", lambda e, kv=kv: e.tensor_copy(out=hb[:, kv, :], in_=psc[1][:, 0:2]), reads=["psc1"], writes=["hb"])
            P.op("dve", lambda e, kv=kv: e.tensor_scalar(out=hbh[:, kv, :], in0=hb[:, kv, :], scalar1=0.5, scalar2=None, op0=ALU.mult), reads=["hb"], writes=["hbh"])
            for mh in range(2):
                P.op("act", lambda e, mh=mh, kv=kv: e.activation(out=ht[:, :], in_=psc[0][:, mh * 64:mh * 64 + 64], func=AF.Tanh, bias=hbh[:, kv, mh:mh + 1], scale=0.5),
                     reads=["psc0", "hbh"], writes=["ht"])
                P.op("dve", lambda e, mh=mh, kv=kv: e.tensor_scalar(out=hx[:, :], in0=psc[0][:, mh * 64:mh * 64 + 64], scalar1=hb[:, kv, mh:mh + 1], scalar2=None, op0=ALU.add),
                     reads=["psc0", "hb"], writes=["hx"])
                P.op("dve", lambda e, mh=mh: e.scalar_tensor_tensor(out=shT[:, mh, :], in0=ht[:, :], scalar=1.0, in1=hx[:, :], op0=ALU.add, op1=ALU.mult),
                     reads=["ht", "hx"], writes=["shT"])
            if kv == 0:
                def f2(e):
                    e.matmul(pm[:, 0:64], lhsT=w2bf[0][:, 0, :], rhs=shT[:, 0, :], start=True, stop=False)
                    return e.matmul(pm[:, 0:64], lhsT=w2bf[0][:, 1, :], rhs=shT[:, 1, :], start=False, stop=True)
                P.op("pe", f2, reads=["w2bf0", "shT"], writes=["po1"])
                P.op("act", lambda e: e.activation(out=kcx[:, :], in_=pm[:, 0:64], func=AF.Identity, scale=0.5), reads=["po1"], writes=["kcx"])
                P.op("pe", lambda e: e.matmul(pm[:, 64:128], lhsT=swap_b, rhs=kcx[:, :], start=True, stop=True), reads=["kcx", "cbf"], writes=["po1"])
                cos_c = bc(cosT[:, 15:TT:16], 2)
                sin_c = bc(sinT[:, 15:TT:16], 2)
                P.op("dve", lambda e, cos_c=cos_c: e.tensor_tensor(out=hx[:, :].rearrange("p (g n) -> p g n", n=32), in0=kcx[:, :].rearrange("p (g n) -> p g n", n=32), in1=cos_c, op=ALU.mult),
                     reads=["kcx", "cosT"], writes=["hx"])
                P.op("dve", lambda e, sin_c=sin_c: e.tensor_tensor(out=ht[:, :].rearrange("p (g n) -> p g n", n=32), in0=pm[:, 64:128].rearrange("p (g n) -> p g n", n=32), in1=sin_c, op=ALU.mult),
                     reads=["po1", "sinT"], writes=["ht"])
                P.op("dve", lambda e: e.tensor_tensor(out=KcT[:, :, 32 * T:32 * T + 32], in0=hx[:, :].rearrange("p (g n) -> p g n", n=32), in1=ht[:, :].rearrange("p (g n) -> p g n", n=32), op=ALU.add),
                     reads=["hx", "ht"], writes=["KcT"])
            else:
                for g in range(2):
                    def f3(e, g=g):
                        e.matmul(pm[0:32, 128 + g * 128:256 + g * 128], lhsT=shT[:, 0, g * 32:(g + 1) * 32], rhs=w2bf[1][:, 0, :], start=True, stop=False)
                        return e.matmul(pm[0:32, 128 + g * 128:256 + g * 128], lhsT=shT[:, 1, g * 32:(g + 1) * 32], rhs=w2bf[1][:, 1, :], start=False, stop=True)
                    P.op("pe", f3, reads=["w2bf1", "shT"], writes=["po1"])
                    P.op("act", lambda e, g=g: e.activation(out=vct[:, g, 0:128], in_=pm[0:32, 128 + g * 128:256 + g * 128], func=AF.Identity, scale=0.5), reads=["po1"], writes=["vct"])
                P.dma("sp", lambda e, s: e.dma_start(out=Vc[32 * T:32 * T + 32, :, :], in_=vct[:, :, :]).then_inc(s, 16), reads=["vct"], writes=["Vc"], semkey="vc")
        wt, wk_ = W.next(("cols", "w_in", O_BG, 24, 16))
        if wt is not None:
            w3 = wt[:, 0:16 * 24].rearrange("p (k c) -> p k c", c=24)
            for st in range(4):
                def f(e, st=st, w3=w3):
                    ins = None
                    for kc in range(16):
                        ins = e.matmul(pm[:, 384 + st * 24:384 + (st + 1) * 24], lhsT=hT[:, kc, st * 128:(st + 1) * 128], rhs=w3[:, kc, :], start=(kc == 0), stop=(kc == 15))
                    return ins
                P.op("pe", f, reads=[wk_, "hT"], writes=["po1"])
            P.op("act", lambda e: e.activation(out=gates[:, :, :], in_=pm[:, 384:480].rearrange("p (a c) -> p a c", c=24), func=AF.Tanh, scale=0.5), reads=["po1"], writes=["gates"])
            P.op("dve", lambda e: e.tensor_scalar(out=gates[:, :, :], in0=gates[:, :, :], scalar1=0.5, scalar2=0.5, op0=ALU.mult, op1=ALU.add), reads=["gates"], writes=["gates"])

        if STAGE <= 4:
            return
        for g in range(2):
            for r in range(4):
                hq = 4 * g + r
                proj_fm(O_NQ + hq * 128, lambda p_, k_, r=r: rope_to(p_, k_, qT[r][:, :], "qT%d" % r))
            flush_pending()
            if not dry and T >= 2:
                nb4 = cbf[:, CB_NB + (T - 2) * 512:CB_NB + (T - 1) * 512]
                r2v = r2[:, :].rearrange("p (a c) -> p a c", c=128)
                rsa = rs4[:, :]
                rb = bass.AP(tensor=rsa.tensor, offset=rsa.offset, ap=[list(rsa.ap[0]), [1, 4], [0, 128]])
                for r in range(4):
                    ps_t, ps_k = next_psc()
                    def fsel(e, ps_t=ps_t, r=r, g=g):
                        ins = None
                        for st in range(4):
                            ins = e.matmul(ps_t[:, st * 128:(st + 1) * 128], lhsT=qT[r][:, st * 128:(st + 1) * 128], rhs=KcT[:, g, :], start=True, stop=True)
                        return ins
                    P.op("pe", fsel, reads=["qT%d" % r, "KcT"], writes=[ps_k])
                    P.op("dve", lambda e, ps_t=ps_t: e.scalar_tensor_tensor(out=r1[:, :], in0=ps_t[:, :], scalar=SCALE, in1=nb4, op0=ALU.mult, op1=ALU.add),
                         reads=[ps_k, "cbf"], writes=["r1"])
                    P.op("act", lambda e: e.activation(out=r2[:, :], in_=r1[:, :], func=AF.Exp), reads=["r1"], writes=["r2"])
                    P.op("dve", lambda e: e.reduce_sum(out=rs4[:, :], in_=r2v, axis=mybir.AxisListType.X), reads=["r2"], writes=["rs4"])
                    P.op("dve", lambda e: e.reciprocal(out=rs4[:, :], in_=rs4[:, :]), reads=["rs4"], writes=["rs4"])
                    if r == 0:
                        P.op("dve", lambda e: e.tensor_tensor(out=ppad[:, :, 0:128], in0=r2v, in1=rb, op=ALU.mult), reads=["r2", "rs4"], writes=["ppad"])
                    else:
                        P.op("dve", lambda e: e.tensor_tensor(out=r2v, in0=r2v, in1=rb, op=ALU.mult), reads=["r2", "rs4"], writes=["r2"])
                        P.op("pool", lambda e: e.tensor_tensor(out=ppad[:, :, 0:128], in0=ppad[:, :, 0:128], in1=r2v, op=ALU.add), reads=["r2", "ppad"], writes=["ppad"])
                imp4 = stmp[:, :].rearrange("p (a c) -> p a c", c=32)
                wk4 = etmp[:, :].rearrange("p (a c) -> p a c", c=32)
                def v(k0):
                    return ppad[:, :, k0:k0 + 128:4]
                P.op("dve", lambda e: e.tensor_tensor(out=imp4, in0=v(0), in1=v(4), op=ALU.add), reads=["ppad"], writes=["stmp"])
                P.op("dve", lambda e: e.scalar_tensor_tensor(out=imp4, in0=imp4, scalar=0.5, in1=v(1), op0=ALU.mult, op1=ALU.add), reads=["stmp", "ppad"], writes=["stmp"])
                P.op("dve", lambda e: e.tensor_tensor(out=imp4, in0=imp4, in1=v(2), op=ALU.add), reads=["stmp", "ppad"], writes=["stmp"])
                P.op("dve", lambda e: e.tensor_tensor(out=imp4, in0=imp4, in1=v(3), op=ALU.add), reads=["stmp", "ppad"], writes=["stmp"])
                bon4 = cf[:, CF_BON + (T - 2) * 128:CF_BON + (T - 1) * 128].rearrange("p (a c) -> p a c", c=32)
                P.op("dve", lambda e: e.tensor_tensor(out=imp4, in0=imp4, in1=bon4, op=ALU.add), reads=["stmp", "cf"], writes=["stmp"])
                for st in range(4):
                    P.op("dve", lambda e, st=st: e.max(out=m8[:, 0:8], in_=imp4[:, st, :]), reads=["stmp"], writes=["m8"])
                    P.op("dve", lambda e, st=st: e.match_replace(out=wk4[:, st, :], in_to_replace=m8[:, 0:8], in_values=imp4[:, st, :], imm_value=-3.0e38),
                         reads=["stmp", "m8"], writes=["etmp"])
                    P.op("dve", lambda e, st=st: e.max(out=m8[:, 8:16], in_=wk4[:, st, :]), reads=["etmp"], writes=["m8"])
                    P.op("dve", lambda e, st=st: e.tensor_scalar(out=wk4[:, st, :], in0=imp4[:, st, :], scalar1=m8[:, 15:16], scalar2=None, op0=ALU.is_ge),
                         reads=["stmp", "m8"], writes=["etmp"])
                P.op("dve", lambda e: e.tensor_scalar(out=negm4[:, :, :], in0=wk4, scalar1=-NEGM, scalar2=NEGM, op0=ALU.mult, op1=ALU.add), reads=["etmp"], writes=["negm"])
                hh = next_tr()
                def ftr(e, hh=hh):
                    ins = None
                    for st in range(4):
                        ins = e.transpose(ptr[0:32, hh, st * 128:(st + 1) * 128], negm4[:, st, :], ident_b)
                    return ins
                P.op("pe", ftr, reads=["negm", "cbf"], writes=["ptr%d" % hh])
                P.op("act", lambda e, hh=hh, g=g: e.activation(out=negmT[:, g, :], in_=ptr[0:32, hh, 0:512], func=AF.Identity), reads=["ptr%d" % hh], writes=["negmT"])
            for r in range(4):
                hq = 4 * g + r
                if not dry:
                    nsa_head(b, T, g, r, hq)
                gate_a(O_NG + hq * 128)
                if not dry:
                    def tail2(hq=hq):
                        hh = next_tr()
                        transpose4(lambda j: accb[:, j, :], ["accb"], hh)
                        P.op("act", lambda e, hh=hh, hq=hq: e.activation(out=yT[:, 8 + hq, :], in_=ptr[:, hh, :], func=AF.Identity, scale=0.5),
                             reads=["ptr%d" % hh], writes=["yT%d" % (8 + hq)])
                        gate_b(yT[:, 8 + hq, :], "yT%d" % (8 + hq))
                    pending.append(tail2)

        flush_pending()
        if debug and b == 0 and T == dbgT:
            if dry:
                return
            P.op("dve", lambda e: e.tensor_copy(out=xt[:, 2, :].rearrange("p (a c) -> p a c", c=TT)[:, 0:4, :], in_=yT[:, 0:4, :]), reads=["yT%d" % i for i in range(16)], writes=["xt2"])
            P.op("dve", lambda e: e.tensor_copy(out=xt[:, 1, :].rearrange("p (a c) -> p a c", c=TT)[:, 0:4, :], in_=yT[:, 8:12, :]), reads=["yT%d" % i for i in range(16)], writes=["xt1"])
            P.op("dve", lambda e: e.tensor_copy(out=xt[0:32, 0, 0:1024], in_=negmT[:, :, :].rearrange("p g t -> p (g t)")), reads=["negmT"], writes=["xt0"])
            P.op("dve", lambda e: e.tensor_copy(out=xt[:, 0, 1024:1280], in_=KcT[:, :, :].rearrange("p g t -> p (g t)")), reads=["KcT"], writes=["xt0"])
            P.dma("sp", lambda e, s: e.dma_start(out=dbg["d_m2"][:, 0:2048], in_=xt[:, 0, :]).then_inc(s, 16), reads=["xt0"], semkey="dbg")
            P.dma("sp", lambda e, s: e.dma_start(out=dbg["d_hT"][:, 0:4 * TT], in_=xt[:, 3, :]).then_inc(s, 16), reads=["xt3"], semkey="dbg")
            P.dma("sp", lambda e, s: e.dma_start(out=dbg["d_yT"][:, 0:4 * TT], in_=xt[:, 2, :]).then_inc(s, 16), reads=["xt2"], semkey="dbg")
            P.dma("sp", lambda e, s: e.dma_start(out=dbg["d_yT"][:, 4 * TT:8 * TT], in_=xt[:, 1, :]).then_inc(s, 16), reads=["xt1"], semkey="dbg")
            return
        ykeys = ["yT%d" % i for i in range(16)]
        for cc in range(16):
            if cc % 2 == 0:
                bA, kA, bB, kB, bC, kC, bD, kD = pp[0], "pp0", pp[1], "pp1", psc[0], "psc0", psc[1], "psc1"
            else:
                bA, kA, bB, kB, bC, kC, bD, kD = pofl[0], "po0", pofl[1], "po1", ptrf[0], "ptr0", ptrf[1], "ptr1"
            wt2, wk2 = W.next(("two", cc * 128))
            if wt2 is not None:
                w2v = wt2[:, :].rearrange("p (u k c) -> p u k c", u=2, c=128)
                def fA(e, w2v=w2v, bA=bA, bB=bB):
                    ins = None
                    for u, bt in ((0, bA), (1, bB)):
                        for kc in range(8):
                            ins = e.matmul(bt[:, :], lhsT=w2v[:, u, kc, :], rhs=yT[:, u * 8 + kc, :], start=(kc == 0), stop=(kc == 7))
                    return ins
                P.op("pe", fA, reads=[wk2] + ykeys, writes=[kA, kB])
            wta, wka = W.next(("cols", "w_in", O_MA + cc * 128, 128, 16))
            if wta is not None:
                wa3 = wta[:, :].rearrange("p (k c) -> p k c", c=128)
                def fC(e, wa3=wa3, bC=bC):
                    ins = None
                    for kc in range(16):
                        ins = e.matmul(bC[:, :], lhsT=wa3[:, kc, :], rhs=hT[:, kc, :], start=(kc == 0), stop=(kc == 15))
                    return ins
                P.op("pe", fC, reads=[wka, "hT"], writes=[kC])
            wtb, wkb = W.next(("cols", "w_in", O_MB + cc * 128, 128, 16))
            if wtb is None:
                continue
            wb3 = wtb[:, :].rearrange("p (k c) -> p k c", c=128)
            def fD(e, wb3=wb3, bD=bD):
                ins = None
                for kc in range(16):
                    ins = e.matmul(bD[:, :], lhsT=wb3[:, kc, :], rhs=hT[:, kc, :], start=(kc == 0), stop=(kc == 15))
                return ins
            P.op("pe", fD, reads=[wkb, "hT"], writes=[kD])
            P.op("act", lambda e, bC=bC: e.activation(out=ta[:, :], in_=bC[:, :], func=AF.Tanh, scale=0.5), reads=[kC], writes=["r1"])
            P.op("dve", lambda e, bA=bA: e.scalar_tensor_tensor(out=ta[:, :], in0=ta[:, :], scalar=1.0, in1=bA[:, :], op0=ALU.add, op1=ALU.mult), reads=["r1", kA], writes=["r1"])
            P.op("act", lambda e, bD=bD: e.activation(out=tb2[:, :], in_=bD[:, :], func=AF.Tanh, scale=0.5), reads=[kD], writes=["r2"])
            P.op("dve", lambda e, bB=bB: e.scalar_tensor_tensor(out=tb2[:, :], in0=tb2[:, :], scalar=1.0, in1=bB[:, :], op0=ALU.add, op1=ALU.mult), reads=["r2", kB], writes=["r2"])
            P.op("pool", lambda e, cc=cc: e.tensor_tensor(out=m2[:, cc, :], in0=ta[:, :], in1=tb2[:, :], op=ALU.add), reads=["r1", "r2"], writes=["xn%d" % (cc // 4)])
        if dry:
            for ct in range(4):
                for pc in range(4):
                    W.next(("wout", ct, pc))
            return
        P.dma("sp", lambda e, s: e.dma_start(out=gfin_bc[:, :], in_=g_fin[0, :].partition_broadcast(128)).then_inc(s, 16), writes=["hT"], semkey="gf")
        banks_e = [(pp[0], "pp0"), (pp[1], "pp1"), (psc[0], "psc0"), (psc[1], "psc1")]
        banks_o = [(pofl[0], "po0"), (pofl[1], "po1"), (ptrf[0], "ptr0"), (ptrf[1], "ptr1")]
        for ct in range(4):
            banks = banks_e if ct % 2 == 0 else banks_o
            for pc in range(4):
                wt, wk_ = W.next(("wout", ct, pc))
                w3 = wt[:, :].rearrange("p (k c) -> p k c", c=512)
                for st in range(4):
                    bt, bk = banks[st]
                    def f(e, w3=w3, bt=bt, st=st, pc=pc):
                        ins = None
                        for kl in range(4):
                            kc = pc * 4 + kl
                            ins = e.matmul(bt[:, :], lhsT=m2[:, kc, st * 128:(st + 1) * 128], rhs=w3[:, kl, :], start=(kc == 0), stop=(kc == 15))
                        return ins
                    P.op("pe", f, reads=[wk_, "xn0", "xn1", "xn2", "xn3"], writes=[bk])
            for st in range(4):
                bt, bk = banks[st]
                P.op("dve", lambda e, bt=bt, ct=ct: e.tensor_tensor(out=r1[:, :], in0=bt[:, :], in1=gate_bc1[:, ct * 512:(ct + 1) * 512], op=ALU.mult),
                     reads=[bk, "gate_bc"], writes=["r1"])
                P.op("pool", lambda e, st=st, ct=ct: e.tensor_tensor(out=xt[:, st, ct * 512:(ct + 1) * 512], in0=xt[:, st, ct * 512:(ct + 1) * 512], in1=r1[:, :], op=ALU.add),
                     reads=["r1", "xt%d" % st], writes=["xt%d" % st])
        for st in range(4):
            P.op("act", lambda e, st=st: e.activation(out=xn[:, st, :], in_=xt[:, st, :], func=AF.Square, accum_out=ss4[:, st:st + 1]),
                 reads=["xt%d" % st], writes=["xn%d" % st, "ss4"])
        P.op("act", lambda e: e.activation(out=rs4[:, :], in_=ss4[:, :], func=AF.Sqrt, bias=ceps[:, 0:1], scale=1.0 / D), reads=["ss4", "ceps"], writes=["rs4"])
        P.op("dve", lambda e: e.reciprocal(out=rs4[:, :], in_=rs4[:, :]), reads=["rs4"], writes=["rs4"])
        for st in range(4):
            P.op("dve", lambda e, st=st: e.scalar_tensor_tensor(out=xt[:, st, :], in0=xt[:, st, :], scalar=rs4[:, st:st + 1], in1=gfin_bc[:, :], op0=ALU.mult, op1=ALU.mult),
                 reads=["xt%d" % st, "rs4", "hT"], writes=["xt%d" % st])
            P.dma("sp", lambda e, s, st=st: e.dma_start(out=out_d[b, tok0 + st * 128:tok0 + (st + 1) * 128, :], in_=xt[:, st, :]).then_inc(s, 16),
                  reads=["xt%d" % st], semkey="o%d" % st)
            if T + 1 < NT and not debug and STAGE >= 99:
                nt0 = tok0 + TT
                P.dma("sp", lambda e, s, st=st, nt0=nt0: e.dma_start(out=xt[:, st, :], in_=x_d[b, nt0 + st * 128:nt0 + (st + 1) * 128, :]).then_inc(s, 16),
                      writes=["xt%d" % st], semkey="x%d" % st)
                xpre[0] = True

    pt_i = [0]
    xpre = [False]

    def nsa_head(b, T, g, r, hq):
        qh = qT[r]
        qk = "qT%d" % r
        first = [True]

        def combine(br, bi):
            pk, dk = "po%d" % bi, "pp%d" % bi
            pof = pofl[bi]
            P.op("dve", lambda e: e.tensor_scalar(out=den4[:, :], in0=pp[bi][:, 0:4], scalar1=1e-30, scalar2=None, op0=ALU.max), reads=[dk], writes=["den4"])
            P.op("dve", lambda e: e.reciprocal(out=den4[:, :], in_=den4[:, :]), reads=["den4"], writes=["den4"])
            P.op("dve", lambda e: e.tensor_tensor(out=den4[:, :], in0=den4[:, :], in1=gates[:, :, hq * 3 + br], op=ALU.mult), reads=["den4", "gates"], writes=["den4"])
            for qi in range(4):
                o_ap = pof[:, qi * 128:(qi + 1) * 128]
                if br == 0:
                    P.op("act", lambda e, o_ap=o_ap, qi=qi: e.activation(out=acc[:, qi, :], in_=o_ap, func=AF.Identity, scale=den4[:, qi:qi + 1]), reads=[pk, "den4"], writes=["acc"])
                elif br == 1:
                    P.op("dve", lambda e, o_ap=o_ap, qi=qi: e.scalar_tensor_tensor(out=acc[:, qi, :], in0=o_ap, scalar=den4[:, qi:qi + 1], in1=acc[:, qi, :], op0=ALU.mult, op1=ALU.add),
                         reads=[pk, "den4", "acc"], writes=["acc"])
                else:
                    P.op("dve", lambda e, o_ap=o_ap, qi=qi: e.scalar_tensor_tensor(out=accb[:, qi, :], in0=o_ap, scalar=den4[:, qi:qi + 1], in1=acc[:, qi, :], op0=ALU.mult, op1=ALU.add),
                         reads=[pk, "den4", "acc"], writes=["accb"])

        pv_q = []
        npush = [0]

        def push_pv(fn):
            while pv_q:
                pv_q.pop(0)()
            pv_q.append(fn)
            npush[0] += 1
            if npush[0] == 3:
                flush_pending()

        nk = 32 * (T + 1)
        ps_t, ps_k = next_psc()
        def fsc(e, ps_t=ps_t):
            e.matmul(ps_t[0:nk, :], lhsT=KcT[:, g, 0:nk], rhs=qh[:, :], start=True, stop=False)
            return e.matmul(ps_t[0:nk, :], lhsT=cbf[0:nk, CB_ID:CB_ID + nk], rhs=cbf[0:nk, CB_VT + T * TT:CB_VT + (T + 1) * TT], start=False, stop=True)
        P.op("pe", fsc, reads=["KcT", qk, "cbf"], writes=[ps_k])
        pi = pt_i[0] % 2
        pt_i[0] += 1
        ptile, pkey = PT[pi], "PT%d" % pi
        P.op("act", lambda e, ps_t=ps_t, ptile=ptile: e.activation(out=ptile[0:nk, :], in_=ps_t[0:nk, :], func=AF.Exp, scale=SCALE), reads=[ps_k], writes=[pkey])
        bi0 = br_i[0] % 2
        br_i[0] += 1
        def fpv(e, ptile=ptile, bi0=bi0):
            ins = None
            for qi in range(4):
                e.matmul(pofl[bi0][:, qi * 128:(qi + 1) * 128], lhsT=ptile[0:nk, qi * 128:(qi + 1) * 128], rhs=Vc[0:nk, g, 0:128], start=(qi == 0), stop=True, skip_group_check=True)
                ins = e.matmul(pp[bi0][:, qi:qi + 1], lhsT=ptile[0:nk, qi * 128:(qi + 1) * 128], rhs=ones_b[0:nk, 0:1], start=(qi == 0), stop=True, skip_group_check=True)
            return ins
        def pv0(fpv=fpv, pkey=pkey, bi0=bi0):
            P.op("pe", fpv, reads=[pkey, "Vc", "ones_b"], writes=["po%d" % bi0, "pp%d" % bi0])
            combine(0, bi0)
        push_pv(pv0)

        for br, (kTt, kkey, Vt, vkey) in ((1, (ksT, "ksT", Vs, "Vs")), (2, (kwT, "kwT", Vw, "Vw"))):
            kts = list(range(0, 4 * T + 4)) if br == 1 else list(range(max(0, 4 * T - 4), 4 * T + 4))
            bi = br_i[0] % 2
            br_i[0] += 1
            for kt in kts:
                i = kt - 4 * T
                if br == 1:
                    qlo, qhi = max(i, 0), 3
                else:
                    qlo, qhi = max(i, 0), min(i + 4, 3)
                c0, c1 = qlo * 128, (qhi + 1) * 128
                ps_t, ps_k = next_psc()
                use_sel = (br == 1 and T >= 2)
                tri_q = i if i >= 0 else None
                anti_q = (i + 4) if (br == 2 and i < 0 and i + 4 <= 3) else None
                def fs(e, ps_t=ps_t, kt=kt, c0=c0, c1=c1, use_sel=use_sel, tri_q=tri_q, anti_q=anti_q, kTt=kTt, br_=br):
                    more = use_sel or (tri_q is not None) or (anti_q is not None)
                    kcol = kt * 128 if br_ == 1 else ((kt // 4) % 2) * TT + (kt % 4) * 128
                    ins = e.matmul(ps_t[:, c0:c1], lhsT=kTt[:, g, kcol:kcol + 128], rhs=qh[:, c0:c1], start=True, stop=not more)
                    if use_sel:
                        m2_ = (tri_q is not None) or (anti_q is not None)
                        ins = e.matmul(ps_t[:, c0:c1], lhsT=cbf[0:32, CB_E + kt * 128:CB_E + (kt + 1) * 128], rhs=negmT[0:32, g, c0:c1], start=False, stop=not m2_)
                    if tri_q is not None:
                        ins = e.matmul(ps_t[:, tri_q * 128:(tri_q + 1) * 128], lhsT=ident_b, rhs=tri_b, start=False, stop=(anti_q is None))
                    if anti_q is not None:
                        ins = e.matmul(ps_t[:, anti_q * 128:(anti_q + 1) * 128], lhsT=ident_b, rhs=anti_b, start=False, stop=True)
                    return ins
                P.op("pe", fs, reads=[kkey, qk, "cbf", "negmT"], writes=[ps_k])
                pi = pt_i[0] % 2
                pt_i[0] += 1
                ptile, pkey = PT[pi], "PT%d" % pi
                P.op("act", lambda e, ps_t=ps_t, ptile=ptile, c0=c0, c1=c1: e.activation(out=ptile[:, c0:c1], in_=ps_t[:, c0:c1], func=AF.Exp, scale=SCALE),
                     reads=[ps_k], writes=[pkey])
                def fpv2(e, ptile=ptile, kt=kt, qlo=qlo, qhi=qhi, Vt=Vt, br=br, bi=bi, kt0=kts[0]):
                    ins = None
                    for qi in range(qlo, qhi + 1):
                        klast = 4 * T + qi
                        vslot = kt if br == 1 else ((kt // 4) % 2) * 4 + (kt % 4)
                        st_ = (kt == kt0 and qi == 0)
                        e.matmul(pofl[bi][:, qi * 128:(qi + 1) * 128], lhsT=ptile[:, qi * 128:(qi + 1) * 128], rhs=Vt[:, g, vslot, 0:128],
                                 start=st_, stop=(kt == klast), skip_group_check=True)
                        ins = e.matmul(pp[bi][:, qi:qi + 1], lhsT=ptile[:, qi * 128:(qi + 1) * 128], rhs=ones_b[:, 0:1], start=st_, stop=(kt == klast), skip_group_check=True)
                    return ins
                def pvk(fpv2=fpv2, pkey=pkey, vkey=vkey, last=(kt == kts[-1]), br=br, bi=bi):
                    P.op("pe", fpv2, reads=[pkey, vkey, "ones_b"], writes=["po%d" % bi, "pp%d" % bi])
                    if last:
                        combine(br, bi)
                push_pv(pvk)
        while pv_q:
            pv_q.pop(0)()

    import os as _os
    STAGE = float(_os.environ.get('KSTAGE', '99'))
    dbgT = int(_os.environ.get('KDBGT', '2'))

    wscr_box = [None]

    def whole():
        setup()
        gate_rows()
        if not W.collect:
            W.wscr = wscr_box[0]
        if STAGE <= 0:
            return
        for b in range(nseq):
            seq_start(b)
            if STAGE <= 1:
                return
            for T in range(NT):
                body(b, T)
                if STAGE < 99:
                    return
                if debug and b == 0 and T == dbgT:
                    return

    P.dry = True
    W.collect = True
    whole()
    W.finish_collect()
    wscr_box[0] = nc.dram_tensor("wscr", [len(W.uniq), 128, 2048], BF16, kind="Internal").ap()
    P.dry = False
    W.collect = False
    W.pos = 0
    pp_i[0] = psc_i[0] = tr_i[0] = pt_i[0] = 0
    whole()
    assert W.pos == len(W.specs) and W.posB == len(W.specsB), (W.pos, len(W.specs), W.posB, len(W.specsB))
    P.emit()
    nc._kstats = dict(n_ops=len(P.ops), sbuf=sbuf_used, sems=P.stats)
    return nc


def _in_maps(inputs, nseq, ncores):
    cbf, cf, _ = _consts()
    f = lambda a: np.ascontiguousarray(np.asarray(a, dtype=np.float32))
    x = f(inputs["x"])
    c = f(inputs["c"])
    pos = np.ascontiguousarray(np.asarray(inputs["positions"], dtype=np.int32))
    b_ada = f(inputs["b_ada"])[0]
    shared = {
        "w_ada": f(inputs["w_ada"])[0],
        "b_adaT": np.ascontiguousarray(b_ada.reshape(48, 128).T),
        "b_gate": np.ascontiguousarray(b_ada[None, 4096:6144]),
        "g_normT": np.ascontiguousarray(f(inputs["g_norm"])[0].reshape(16, 128).T),
        "w_in": f(inputs["w_in"])[0],
        "g_retT": np.ascontiguousarray(f(inputs["g_ret"])[0].reshape(8, 128).T),
        "w_ck1": f(inputs["w_ck1"])[0],
        "w_ck2": f(inputs["w_ck2"])[0],
        "pe_ckT": np.ascontiguousarray(f(inputs["pe_ck"])[0].T),
        "w_cv1": f(inputs["w_cv1"])[0],
        "w_cv2": f(inputs["w_cv2"])[0],
        "pe_cvT": np.ascontiguousarray(f(inputs["pe_cv"])[0].T),
        "w_up_ret": f(inputs["w_up_ret"])[0],
        "w_up_nsa": f(inputs["w_up_nsa"])[0],
        "w_out": f(inputs["w_out"])[0],
        "g_final": np.ascontiguousarray(f(inputs["g_final"])[None, :]),
        "cbf": cbf,
        "cf": cf,
    }
    maps = []
    for i in range(ncores):
        sl = slice(i * nseq, (i + 1) * nseq)
        cT = np.concatenate([c[i * nseq + b].reshape(16, 128).T for b in range(nseq)], axis=1)
        m = dict(shared)
        m["x"] = np.ascontiguousarray(x[sl])
        m["cT"] = np.ascontiguousarray(cT)
        m["pos"] = np.ascontiguousarray(pos[sl])
        maps.append(m)
    return maps


def kernel(**inputs):
    nc = build_nc(SEQ_PER_CORE)
    maps = _in_maps(inputs, SEQ_PER_CORE, NCORES)
    res = run_bass_kernel_spmd(nc, maps, core_ids=list(range(NCORES)))
    out = np.concatenate([np.asarray(r["out"]) for r in res.results], axis=0)
    return out.astype(np.float32)
```
